# Optimizing a Trainium2 kernel written in Bass

```python
import jax, jax.numpy as jnp
from jax import lax
import numpy as np

D_MODEL = 1024
BATCH = 2
SEQ = 8192
DEPTH = 1

MIX_W = D_MODEL
POOL_W = MIX_W // 2
POOL_GROUPS = 4
POOL_GC = POOL_W // POOL_GROUPS
POOL_WINDOWS = (2, 4, 8, 16)
RWKV_C = MIX_W - POOL_W
HEAD_N = 64
N_HEAD = RWKV_C // HEAD_N
DECAY_LORA = 64
ICLR_LORA = 64
GATE_LORA = 128
SHIFT_COLS = 3 * RWKV_C + DECAY_LORA + ICLR_LORA + GATE_LORA
IN_COLS = POOL_W + SHIFT_COLS
D_FF = ((8 * D_MODEL // 3 + 255) // 256) * 256
RMS_EPS = 1e-6
GN_EPS = 64e-5

kernel_name = "hybrid_pool_rwkv7_adaln_block"


def _rmsnorm(x, g):
    xf = x.astype(jnp.float32)
    y = xf * lax.rsqrt(jnp.mean(xf * xf, axis=-1, keepdims=True) + RMS_EPS)
    return (y * g.astype(jnp.float32)).astype(x.dtype)


def _modulate(h, shift, scale):
    return h * (1.0 + scale[:, None, :]) + shift[:, None, :]


def _token_shift(z):
    return jnp.pad(z, ((0, 0), (1, 0), (0, 0)))[:, :-1]


def _pool_mixer(u, pool_w, pool_scale):
    B, T, _ = u.shape
    uf = u.astype(jnp.float32).reshape(B, T, POOL_GROUPS, POOL_GC)
    cs = jnp.cumsum(uf, axis=1)
    t = jnp.arange(T)
    pooled = []
    for gi, w in enumerate(POOL_WINDOWS):
        csg = cs[:, :, gi]
        prev = jnp.pad(csg, ((0, 0), (w, 0), (0, 0)))[:, :T]
        cnt = jnp.minimum(t + 1, w).astype(jnp.float32)[None, :, None]
        pooled.append((csg - prev) / cnt)
    d = (jnp.stack(pooled, axis=2) - uf).astype(u.dtype)
    y = jnp.einsum('btgc,gcd->btgd', d, pool_w).reshape(B, T, POOL_W)
    return y * pool_scale


def _rwkv7_step(S, inp):
    r, w, k, v, a_vec, b_vec = inp
    sa = jnp.einsum('bhij,bhj->bhi', S, a_vec)
    S = S * w[:, :, None, :] + sa[..., None] * b_vec[:, :, None, :] + v[..., None] * k[:, :, None, :]
    y = jnp.einsum('bhij,bhj->bhi', S, r)
    return S, y


def _rwkv7_mixer(z, w0, w_decay_up, a0, w_iclr_up, w_gate_up, k_k, k_a, r_k, lnx_w, lnx_b):
    B, T, _ = z.shape
    f32 = jnp.float32
    cuts = np.cumsum([RWKV_C, RWKV_C, RWKV_C, DECAY_LORA, ICLR_LORA])
    r, k, v, xw, xa, xg = jnp.split(z, [int(i) for i in cuts], axis=-1)
    w_log = -jax.nn.softplus(-(w0 + jnp.tanh(xw) @ w_decay_up)) - 0.5
    decay = jnp.exp(-jnp.exp(w_log.astype(f32)))
    a = jax.nn.sigmoid(a0 + xa @ w_iclr_up)
    g = jax.nn.sigmoid(xg) @ w_gate_up
    heads = lambda t_: t_.astype(f32).reshape(B, T, N_HEAD, HEAD_N)
    kk = heads(k * k_k)
    kk = kk / jnp.maximum(jnp.sqrt(jnp.sum(kk * kk, axis=-1, keepdims=True)), 1e-12)
    k = k * (1.0 + (a - 1.0) * k_a)
    rh, kh, vh, wh, ah = heads(r), heads(k), heads(v), heads(decay), heads(a)
    tm = lambda t_: jnp.moveaxis(t_, 1, 0)
    S0 = jnp.zeros((B, N_HEAD, HEAD_N, HEAD_N), f32)
    _, y = lax.scan(_rwkv7_step, S0, (tm(rh), tm(wh), tm(kh), tm(vh), tm(-kk), tm(kk * ah)))
    y = jnp.moveaxis(y, 0, 1)
    mu = jnp.mean(y, axis=-1, keepdims=True)
    var = jnp.mean(jnp.square(y - mu), axis=-1, keepdims=True)
    yn = ((y - mu) * lax.rsqrt(var + GN_EPS)).reshape(B, T, RWKV_C) * lnx_w + lnx_b
    bonus = jnp.sum(rh * kh * r_k.astype(f32).reshape(N_HEAD, HEAD_N), axis=-1, keepdims=True) * vh
    out = (yn + bonus.reshape(B, T, RWKV_C)) * g
    return out.astype(z.dtype)


def setup_inputs(seed: int = 0) -> dict:
    key = jax.random.key(seed)
    ks = iter(jax.random.split(key, 32))
    f32 = jnp.float32
    nrm = lambda shape, s: jax.random.normal(next(ks), shape, f32) * s
    uni = lambda shape, lo, hi: jax.random.uniform(next(ks), shape, f32, lo, hi)
    L, D, C = DEPTH, D_MODEL, RWKV_C
    return {
        "x": nrm((BATCH, SEQ, D), 1.0),
        "c": nrm((BATCH, D), 1.0),
        "ada_w": nrm((L, D, 6 * D), D ** -0.5),
        "ada_b": nrm((L, 6 * D), 0.02),
        "norm1_g": 1.0 + nrm((L, D), 0.02),
        "w_in": nrm((L, D, IN_COLS), D ** -0.5),
        "mu_shift": uni((L, SHIFT_COLS), 0.05, 0.95),
        "pool_w": nrm((L, POOL_GROUPS, POOL_GC, POOL_GC), POOL_GC ** -0.5),
        "pool_scale": 1.0 + nrm((L, POOL_W), 0.1),
        "w0": uni((L, C), -6.0, -1.0),
        "w_decay_up": nrm((L, DECAY_LORA, C), 0.5 * DECAY_LORA ** -0.5),
        "a0": nrm((L, C), 0.1),
        "w_iclr_up": nrm((L, ICLR_LORA, C), 0.5 * ICLR_LORA ** -0.5),
        "w_gate_up": nrm((L, GATE_LORA, C), GATE_LORA ** -0.5),
        "k_k": 0.85 + nrm((L, C), 0.02),
        "k_a": 1.0 + nrm((L, C), 0.02),
        "r_k": nrm((L, C), 0.1),
        "lnx_w": 1.0 + nrm((L, C), 0.02),
        "lnx_b": nrm((L, C), 0.02),
        "w_out": nrm((L, MIX_W, D), MIX_W ** -0.5),
        "norm2_g": 1.0 + nrm((L, D), 0.02),
        "w_ffn_gu": nrm((L, D, 2 * D_FF), D ** -0.5),
        "w_ffn_down": nrm((L, D_FF, D), D_FF ** -0.5),
        "final_g": 1.0 + nrm((D,), 0.02),
    }


def reference(x, c, ada_w, ada_b, norm1_g, w_in, mu_shift, pool_w, pool_scale, w0, w_decay_up, a0,
              w_iclr_up, w_gate_up, k_k, k_a, r_k, lnx_w, lnx_b, w_out, norm2_g, w_ffn_gu,
              w_ffn_down, final_g):
    c_act = jax.nn.silu(c)
    for i in range(DEPTH):
        mod = c_act @ ada_w[i] + ada_b[i]
        sh1, sc1, g1, sh2, sc2, g2 = jnp.split(mod, 6, axis=-1)
        h = _modulate(_rmsnorm(x, norm1_g[i]), sh1, sc1)
        z = h @ w_in[i]
        u_pool = z[..., :POOL_W]
        zr = z[..., POOL_W:]
        zr = zr + (_token_shift(zr) - zr) * mu_shift[i]
        y_pool = _pool_mixer(u_pool, pool_w[i], pool_scale[i])
        y_rwkv = _rwkv7_mixer(zr, w0[i], w_decay_up[i], a0[i], w_iclr_up[i], w_gate_up[i],
                              k_k[i], k_a[i], r_k[i], lnx_w[i], lnx_b[i])
        mix = jnp.concatenate([y_pool, y_rwkv], axis=-1) @ w_out[i]
        x = x + g1[:, None, :] * mix
        h2 = _modulate(_rmsnorm(x, norm2_g[i]), sh2, sc2)
        gate, up = jnp.split(h2 @ w_ffn_gu[i], 2, axis=-1)
        x = x + g2[:, None, :] * ((jax.nn.silu(gate) * up) @ w_ffn_down[i])
    return _rmsnorm(x, final_g)
```

```python
import numpy as np
import ml_dtypes
import concourse.bass as bass
import concourse.mybir as mybir
from concourse.bass_utils import run_bass_kernel_spmd

F32 = mybir.dt.float32
BF16 = mybir.dt.bfloat16
ALU = mybir.AluOpType
AF = mybir.ActivationFunctionType
AX = mybir.AxisListType

D = 1024
T = 8192
TQ = 2048
NBLK = T // 512
DFF = 2816
NFF = DFF // 128
GN_EPS = 64e-5
RMS_EPS = 1e-6
DEC = 0.6065306597126334


class Sched:
    def __init__(self, nc, es):
        self.nc = nc
        self.es = es
        self.eng = {"pe": nc.tensor, "act": nc.scalar, "dve": nc.vector, "pool": nc.gpsimd, "sp": nc.sync}
        self.sem = {}
        self.cnt = {}
        self.inc = {}
        self.waited = {}
        self.lastw = {}
        self.readers = {}
        for e in ("pe", "act", "dve", "pool"):
            self.stream(e, 1)

    def stream(self, name, inc):
        if name not in self.sem:
            self.sem[name] = self.es.enter_context(self.nc.semaphore("s_" + name))
            self.cnt[name] = 0
            self.inc[name] = inc
        return name

    def _wait(self, e, s, v):
        if s == e and e == "pe":
            return
        if self.waited.get((e, s), 0) >= v:
            return
        self.eng[e].wait_ge(self.sem[s], v)
        self.waited[(e, s)] = v

    @staticmethod
    def _bank(k):
        if isinstance(k, tuple) and k and k[0] == "TB":
            return "BANK_TB"
        if isinstance(k, str) and (k.startswith("s0_") or k.startswith("s1_")):
            return "BANK_" + k[:4]
        return None

    def _aug(self, keys):
        out = list(keys)
        for k in keys:
            b = self._bank(k)
            if b is not None and b not in out:
                out.append(b)
        return out

    def _deps(self, reads, writes):
        deps = set()
        for r in reads:
            if r in self.lastw:
                deps.add(self.lastw[r])
        for r in writes:
            if r in self.lastw:
                deps.add(self.lastw[r])
            for d in self.readers.get(r, ()):
                deps.add(d)
        return deps

    def _commit(self, s, reads, writes):
        v = self.cnt[s]
        for r in writes:
            self.lastw[r] = (s, v)
            self.readers[r] = []
        for r in reads:
            self.readers.setdefault(r, []).append((s, v))

    def op(self, e, fn, reads=(), writes=()):
        bk = [k for k in self._aug(list(reads) + list(writes)) if isinstance(k, str) and k.startswith("BANK_")]
        reads, writes = list(reads), list(writes) + bk
        for (s, v) in self._deps(reads, writes):
            self._wait(e, s, v)
        ins = fn()
        self.cnt[e] += 1
        ins.then_inc(self.sem[e], 1)
        self._commit(e, reads, writes)
        return ins

    def dma(self, q, s, out, in_, reads=(), writes=(), **kw):
        self.stream(s, 16)
        for (ss, v) in self._deps(reads, writes):
            self._wait(q, ss, v)
        ins = self.eng[q].dma_start(out=out, in_=in_, **kw)
        self.cnt[s] += 16
        ins.then_inc(self.sem[s], 16)
        self._commit(s, reads, writes)
        return ins

    def close(self, s):
        for r, (ss, v) in list(self.lastw.items()):
            if ss == s:
                self.lastw[r] = (s, self.cnt[s])

    def wait_all(self, e):
        for s in self.cnt:
            if self.cnt[s] > 0:
                self._wait(e, s, self.cnt[s])

    def barrier(self):
        for e in ("pe", "act", "dve", "pool", "sp"):
            self.wait_all(e)
        self.lastw.clear()
        self.readers.clear()


STOP = None
DEBUG = False
DBG_MAP = {}
DBG_OUT = None
_LAST_S = None


def build():
    import contextlib

    nc = bass.Bass("TRN2", target_bir_lowering=False)

    def din(name, shape, dt=F32):
        return nc.dram_tensor(name, list(shape), dt, kind="ExternalInput").ap()

    xb = din("xb", [T + 16, D])
    cvec = din("cvec", [D])
    flags = din("flags", [128, 65])
    ada_w = din("ada_w", [D, 6 * D])
    ada_b = din("ada_b", [6 * D])
    norm1_g = din("norm1_g", [D])
    xo = din("xo", [TQ + 16, D])
    w_in = din("w_in_sel", [D, 1152])
    mu_sel = din("mu_sel", [128, 5])
    pv_sel = din("pv_sel", [128, 5])
    ln2 = din("ln2", [2, 2, 64])
    pool_w = din("pool_w", [4, 128, 128])
    pool_scale = din("pool_scale", [512])
    w_decay_up = din("Wd_sel", [64, 128])
    w_iclr_up = din("Wa_sel", [64, 128])
    w_gate_up = din("Wg_sel", [128, 128])
    w_out = din("w_out", [D, D])
    norm2_g = din("norm2_g", [D])
    w_gu = din("w_ffn_gu", [D, 2 * DFF])
    w_down = din("w_ffn_down", [DFF, D])
    final_g = din("final_g", [D])
    out = nc.dram_tensor("out", [TQ, D], F32, kind="ExternalOutput").ap()
    ysrc = [nc.dram_tensor(f"ysrc{j}", [128, TQ], BF16) for j in range(4)]
    ydst = nc.dram_tensor("ydst", [4, 512, TQ], BF16)
    x1d = nc.dram_tensor("x1d", [TQ, D], F32)
    dbg = nc.dram_tensor("dbg", [128, 8192], F32, kind="ExternalOutput").ap() if DEBUG else None

    es = contextlib.ExitStack()
    with es:
        S = Sched(nc, es)
        global _LAST_S
        _LAST_S = S
        pid = nc.gpsimd.partition_id()
        q = pid % 4
        _n = [0]

        def sb(shape, dt=F32, stack=es, name=None):
            _n[0] += 1
            return stack.enter_context(nc.sbuf_tensor(name or f"t{_n[0]}", list(shape), dt))

        def ps(shape, dt=F32, name=None):
            _n[0] += 1
            return es.enter_context(nc.psum_tensor(name or f"p{_n[0]}", list(shape), dt))

        def mm(out, lhsT, rhs, r, w, start=True, stop=True):
            return S.op("pe", lambda: nc.tensor.matmul(out, lhsT, rhs, start=start, stop=stop), r, w)

        def tr(out, in_, ident, r, w):
            return S.op("pe", lambda: nc.tensor.transpose(out, in_, ident), r, w)

        def act(out, in_, func, r, w, bias=None, scale=None, accum_out=None):
            kw = {}
            if bias is not None:
                kw["bias"] = bias
            if scale is not None:
                kw["scale"] = scale
            if accum_out is not None:
                kw["accum_out"] = accum_out
            return S.op("act", lambda: nc.scalar.activation(out=out, in_=in_, func=func, **kw), r, w)

        def tt(e, out, in0, in1, op, r, w):
            eng = nc.vector if e == "dve" else nc.gpsimd
            return S.op(e, lambda: eng.tensor_tensor(out=out, in0=in0, in1=in1, op=op), r, w)

        def tsc(e, out, in0, s1, s2, op0, op1, r, w):
            eng = nc.vector if e == "dve" else nc.gpsimd
            if op1 is None:
                return S.op(e, lambda: eng.tensor_scalar(out=out, in0=in0, scalar1=s1, scalar2=None, op0=op0), r, w)
            return S.op(e, lambda: eng.tensor_scalar(out=out, in0=in0, scalar1=s1, scalar2=s2, op0=op0, op1=op1), r, w)

        def stt(out, in0, scalar, in1, op0, op1, r, w):
            return S.op("dve", lambda: nc.vector.scalar_tensor_tensor(out=out, in0=in0, scalar=scalar, in1=in1, op0=op0, op1=op1), r, w)

        def cp(e, out, in_, r, w):
            if e == "act":
                return S.op("act", lambda: nc.scalar.copy(out=out, in_=in_), r, w)
            eng = nc.vector if e == "dve" else nc.gpsimd
            return S.op(e, lambda: eng.tensor_copy(out=out, in_=in_), r, w)

        def mset(e, ap, val, w):
            eng = nc.vector if e == "dve" else nc.gpsimd
            return S.op(e, lambda: eng.memset(ap, val), (), w)

        NW = 3
        PS = [ps([128, 512], name=f"PS{i}") for i in range(8)]
        SB0 = [PS[p] for p in range(NW)]
        SB1 = [PS[NW + p] for p in range(NW)]
        TB = PS[6]
        GBL = [0, 1, 2, 3, 4, 5, 7]
        _gb = [0]

        def gbank():
            i = GBL[_gb[0] % len(GBL)]
            _gb[0] += 1
            return PS[i], f"GB{i}"

        ones_f = sb([128, 128])
        ident_f = sb([128, 128])
        ident_bf = sb([128, 128], BF16)
        flg = sb([128, 65])
        c_fm = sb([128, 8])
        n1g = sb([128, 8])
        n2g = sb([128, 8])
        mu = sb([128, 5])
        pscale = sb([128, 4])
        pv = sb([128, 8])
        fgbc = sb([128, D])
        omm = sb([128, 5])
        stage = [sb([128, 2048]) for _ in range(2)]
        g2bc = sb([128, D])
        modfm = sb([128, 48])
        gsc1 = sb([128, 8])
        gsc2 = sb([128, 8])
        xn = sb([128, D])
        junk = sb([128, D], BF16)
        ss = sb([128, 4])
        dbgt = sb([128, 512]) if DEBUG else None
        pA = contextlib.ExitStack()
        es.enter_context(pA)
        M_TS = sb([128, 2, 128], stack=pA)
        M_A3 = sb([128, 3, 128], stack=pA)
        Blk = sb([128, 128], stack=pA)
        E_bf = sb([128, 64], BF16, stack=pA)
        ones_bf = sb([128, 2], BF16, stack=pA)
        rmask = sb([128, 512], stack=pA)
        lnw_bc = sb([128, 64], stack=pA)
        lnb_bc = sb([128, 64], stack=pA)
        Wd = sb([128, 128], stack=pA)
        Wa = sb([128, 128], stack=pA)
        Wg = sb([128, 128], stack=pA)
        pw_f = sb([128, 4, 128], stack=pA)
        pw_bf = sb([128, 4, 128], BF16, stack=pA)
        g1bc = sb([128, D], stack=pA)
        xt = [sb([128, D], stack=pA) for _ in range(2)]
        mixp = sb([128, 4, TQ], BF16, stack=pA)
        mset("pool", ones_f[:], 1.0, ["ones_f"])

        def asel(out, pattern, cmul, cmp, w):
            return S.op("pool", lambda: nc.gpsimd.affine_select(out=out, in_=ones_f[:], pattern=pattern, compare_op=cmp,
                                                                fill=0.0, base=0, channel_multiplier=cmul), ["ones_f"], w)

        asel(ident_f[:], [[1, 128]], -1, ALU.is_equal, ["ident_f"])
        asel(M_TS[:, 0, :], [[1, 128]], -1, ALU.is_gt, ["M_TS"])
        asel(M_TS[:, 1, :], [[-1, 128]], 1, ALU.is_gt, ["M_TS"])
        asel(M_A3[:, 0, :], [[1, 128]], -1, ALU.is_ge, ["M_A3"])
        asel(M_A3[:, 1, :], [[1, 128]], -1, ALU.is_gt, ["M_A3"])
        asel(M_A3[:, 2, :], [[1, 128]], -1, ALU.is_ge, ["M_A3"])
        cp("pool", ident_bf[:], ident_f[:], ["ident_f"], ["ident_bf"])
        mset("pool", Blk[:], 0.0, ["Blk"])
        mset("pool", Blk[0:64, 0:64], 1.0, ["Blk"])
        mset("pool", Blk[64:128, 64:128], 1.0, ["Blk"])
        tt("pool", E_bf[:], ident_f[:, 0:64], ident_f[:, 64:128], ALU.add, ["ident_f"], ["E_bf"])
        mset("pool", ones_bf[:], 1.0, ["ones_bf"])
        mset("pool", rmask[:], 1.0, ["rmask"])
        mset("pool", rmask[:].rearrange("p (c t) -> p c t", t=64)[:, :, 0:1], 0.0, ["rmask"])

        def pl(dst, src, w, **kw):
            S.dma("pool", "init", dst, src, (), w, **kw)

        def fm(v, k):
            return v.rearrange("(k p) -> p k", p=128)

        pl(flg[:], flags[:, :], ["flg"])
        pl(c_fm[:], fm(cvec, 8), ["c_fm"], allow_slow_non_contiguous=True)
        pl(n1g[:], fm(norm1_g, 8), ["n1g"], allow_slow_non_contiguous=True)
        pl(n2g[:], fm(norm2_g, 8), ["n2g"], allow_slow_non_contiguous=True)
        pl(mu[:], mu_sel[:, :], ["mu"])
        pl(pscale[:], fm(pool_scale, 4), ["pscale"], allow_slow_non_contiguous=True)
        pl(pv[:, 0:5], pv_sel[:, :], ["pv"])
        for h in range(2):
            pl(lnw_bc[h * 64:(h + 1) * 64, :], ln2[0, h:h + 1, :].broadcast_to([64, 64]), ["lnw_bc"])
            pl(lnb_bc[h * 64:(h + 1) * 64, :], ln2[1, h:h + 1, :].broadcast_to([64, 64]), ["lnb_bc"])
        pl(Wd[0:64, :], w_decay_up[:, :], ["Wd"])
        pl(Wa[64:128, :], w_iclr_up[:, :], ["Wa"])
        pl(Wg[:], w_gate_up[:, :], ["Wg"])
        pl(pw_f[:], pool_w.rearrange("g c d -> c g d"), ["pw_f"])
        pl(fgbc[:], final_g.partition_broadcast(128), ["fgbc"])
        S.close("init")
        cp("pool", pw_bf[:], pw_f[:], ["pw_f"], ["pw_bf"])
        tsc("pool", omm[:], mu[:], -1.0, 1.0, ALU.mult, ALU.add, ["mu"], ["omm"])

        if STOP == "setup":
            S.barrier()
            return nc
        _st = [0]
        _ce = [0]

        def load_cast(dst_ap_fn, src_ap_fn, nparts, K, N, w, scale_bc=None, part0=0):
            ncol = max(1, 2048 // K)
            for n0 in range(0, N, ncol):
                n1 = min(N, n0 + ncol)
                i = _st[0] % 2
                _st[0] += 1
                sv = stage[i][part0:part0 + nparts, 0:K * (n1 - n0)].rearrange("p (k n) -> p k n", k=K)
                S.dma(("sp", "act")[i], f"stg{i}", sv, src_ap_fn(n0, n1), (), [f"stage{i}"])
                e = ("act", "dve")[_ce[0] % 2]
                _ce[0] += 1
                if scale_bc is not None:
                    tt("dve" if e == "act" else e, dst_ap_fn(n0, n1), sv, scale_bc(n0, n1), ALU.mult, [f"stage{i}"] + w[1:], [w[0]])
                else:
                    cp(e, dst_ap_fn(n0, n1), sv, [f"stage{i}"], [w[0]])

        _dc = [0]

        def dump(name, ap, n, e="dve"):
            if not DEBUG:
                return
            c0 = _dc[0]
            _dc[0] += n
            DBG_MAP[name] = (c0, n)
            S.wait_all(e)
            npart = ap.shape[0]
            cp(e, dbgt[0:npart, 0:n], ap, [], ["dbgt"])
            S.dma("sp", "dbgs", dbg[0:npart, c0:c0 + n], dbgt[0:npart, 0:n], ["dbgt"], ["dbgd"])

        with contextlib.ExitStack() as p0:
            csil = sb([128, 8, 1], stack=p0)
            crep = sb([128, 8, 128], stack=p0)
            adab = [sb([128, 512], stack=p0) for _ in range(2)]
            mblk = sb([128, 512], stack=p0)
            tmp4 = sb([128, 4, 128], stack=p0)
            act(csil[:, :, 0], c_fm[:], AF.Silu, ["c_fm"], ["csil"])
            cp("dve", crep[:], csil[:].broadcast_to([128, 8, 128]), ["csil"], ["crep"])
            for cb in range(12):
                bank, bk = gbank()
                for k2 in range(4):
                    j = _st[0] % 2
                    _st[0] += 1
                    sv = stage[j][:, 0:1024].rearrange("p (k n) -> p k n", k=2)
                    S.dma(("sp", "act")[j], f"stg{j}", sv,
                          ada_w[k2 * 256:(k2 + 1) * 256, cb * 512:(cb + 1) * 512].rearrange("(k p) n -> p k n", p=128),
                          (), [f"stage{j}"])
                    for kk in range(2):
                        k = k2 * 2 + kk
                        mm(bank[:], crep[:, k, :], sv[:, kk, :], ["crep", f"stage{j}"], [bk], start=(k == 0), stop=(k == 7))
                a = cb % 2
                S.dma("sp", f"adab{a}", adab[a][:], ada_b[cb * 512:(cb + 1) * 512].partition_broadcast(128), (), [f"adab{a}"])
                sec = cb // 2
                if sec == 2:
                    dst, dk = g1bc[:, (cb % 2) * 512:(cb % 2 + 1) * 512], "g1bc"
                elif sec == 5:
                    dst, dk = g2bc[:, (cb % 2) * 512:(cb % 2 + 1) * 512], "g2bc"
                else:
                    dst, dk = mblk[:], "mblk"
                tt("dve", dst, bank[:], adab[a][:], ALU.add, [bk, f"adab{a}"], [dk])
                tt("dve", tmp4[:], dst.rearrange("p (a b) -> p a b", b=128),
                   ident_f[:].rearrange("p (a b) -> p a b", a=1).broadcast_to([128, 4, 128]), ALU.mult, [dk, "ident_f"], ["tmp4"])
                S.op("dve", lambda: nc.vector.tensor_reduce(out=modfm[:, cb * 4:(cb + 1) * 4], in_=tmp4[:], axis=AX.X, op=ALU.add),
                     ["tmp4"], ["modfm"])
            S.barrier()
        if STOP == "ada":
            return nc
        stt(gsc1[:], modfm[:, 8:16], 1.0, n1g[:], ALU.add, ALU.mult, ["modfm", "n1g"], ["gsc1"])
        stt(gsc2[:], modfm[:, 32:40], 1.0, n2g[:], ALU.add, ALU.mult, ["modfm", "n2g"], ["gsc2"])
        dump("modfm", modfm[:], 48)
        sh1 = modfm[:, 0:8]
        sh2 = modfm[:, 24:32]

        _xt = [0]

        def norm_to_fm(x_ap, nrow, gsc, sh, shk, dst_fn, dst_key, dq="sp", src_key=None, keep=None):
            if src_key is None:
                i = _xt[0] % 2
                _xt[0] += 1
                xs, xk = xt[i], f"xt{i}"
                S.dma(dq, f"xld{i}", xs[0:nrow, :], x_ap, (), [xk])
                xa = xs[0:nrow, :]
            else:
                xa, xk = x_ap, src_key
            act(junk[0:nrow, :], xa, AF.Square, [xk], ["junk", "ss"], accum_out=ss[0:nrow, 0:1])
            act(ss[0:nrow, 1:2], ss[0:nrow, 0:1], AF.Sqrt, ["ss"], ["ss1"], bias=RMS_EPS, scale=1.0 / D)
            S.op("dve", lambda: nc.vector.reciprocal(out=ss[0:nrow, 2:3], in_=ss[0:nrow, 1:2]), ["ss1"], ["ss2"])
            act(xn[0:nrow, :], xa, AF.Copy, [xk, "ss2"], ["xn"], scale=ss[0:nrow, 2:3])
            for half in range(2):
                bank, bk = gbank()
                for kk_ in range(4):
                    k = half * 4 + kk_
                    tr(bank[:, kk_ * 128:kk_ * 128 + nrow], xn[0:nrow, k * 128:(k + 1) * 128], ident_f[0:nrow, 0:nrow], ["xn", "ident_f"], [bk])
                for kk_ in range(4):
                    k = half * 4 + kk_
                    tsc("dve", dst_fn(k), bank[:, kk_ * 128:kk_ * 128 + nrow], gsc[:, k:k + 1], sh[:, k:k + 1], ALU.mult, ALU.add,
                        [bk, "gsc1", "gsc2", "modfm"], [dst_key])
            return xa, xk

        def norm_gen(x_ap, nrow, gsc, sh, shk, dst_fn, dst_key, dq="sp", src_key=None, keep=None):
            if src_key is None:
                i = _xt[0] % 2
                _xt[0] += 1
                xs, xk = xt[i], f"xt{i}"
                S.dma(dq, f"xld{i}", xs[0:nrow, :], x_ap, (), [xk])
                xa = xs[0:nrow, :]
            else:
                xa, xk = x_ap, src_key
            act(junk[0:nrow, :], xa, AF.Square, [xk], ["junk", "ss"], accum_out=ss[0:nrow, 0:1])
            yield
            act(ss[0:nrow, 1:2], ss[0:nrow, 0:1], AF.Sqrt, ["ss"], ["ss1"], bias=RMS_EPS, scale=1.0 / D)
            yield
            S.op("dve", lambda: nc.vector.reciprocal(out=ss[0:nrow, 2:3], in_=ss[0:nrow, 1:2]), ["ss1"], ["ss2"])
            yield
            act(xn[0:nrow, :], xa, AF.Copy, [xk, "ss2"], ["xn"], scale=ss[0:nrow, 2:3])
            yield
            for half in range(2):
                bank, bk = gbank()
                for kk_ in range(4):
                    k = half * 4 + kk_
                    tr(bank[:, kk_ * 128:kk_ * 128 + nrow], xn[0:nrow, k * 128:(k + 1) * 128], ident_f[0:nrow, 0:nrow], ["xn", "ident_f"], [bk])
                yield
                for kk_ in range(4):
                    k = half * 4 + kk_
                    tsc("dve", dst_fn(k), bank[:, kk_ * 128:kk_ * 128 + nrow], gsc[:, k:k + 1], sh[:, k:k + 1], ALU.mult, ALU.add,
                        [bk, "gsc1", "gsc2", "modfm"], [dst_key])
                yield

        with contextlib.ExitStack() as p1:
            w_in_bf = sb([128, 8, 9 * 128], BF16, stack=p1)
            w3 = w_in.rearrange("(k p) n -> p k n", p=128)
            load_cast(lambda a, b: w_in_bf[:, :, a:b], lambda a, b: w3[:, :, a:b], 128, 8, 1152, ["w_in_bf"])

            if STOP == "w_in":
                S.barrier()
                return nc
            hT = sb([128, 8, 528], BF16, stack=p1)
            dn = [sb([128, 528], stack=p1) for _ in range(12)]
            zT = [sb([128, 5, 513], stack=p1) for _ in range(2)]
            gT = [sb([128, 2, 512], stack=p1) for _ in range(2)]
            yTb = [sb([128, 2, 512], BF16, stack=p1) for _ in range(2)]
            gamC = [sb([128, 8], stack=p1) for _ in range(2)]
            Hs = [sb([128, 64], stack=p1) for _ in range(2)]
            AR_bd = [sb([128, 8, 256], BF16, stack=p1) for _ in range(2)]
            B_bd = [sb([128, 8, 128], BF16, stack=p1) for _ in range(2)]
            K_bd = [sb([128, 8, 128], BF16, stack=p1) for _ in range(2)]
            BH_bd = [sb([128, 8, 128], BF16, stack=p1) for _ in range(2)]
            KH_bd = [sb([128, 8, 128], BF16, stack=p1) for _ in range(2)]
            V_bd = [sb([128, 8, 128], BF16, stack=p1) for _ in range(2)]
            RRK_bd = [sb([128, 8, 128], BF16, stack=p1) for _ in range(2)]
            Xi = [sb([128, 3, 128], stack=p1) for _ in range(NW)]
            Mb = [sb([128, 128], BF16, stack=p1) for _ in range(NW)]
            A3 = [sb([128, 3, 128], BF16, stack=p1) for _ in range(NW)]
            R3 = [sb([128, 192], BF16, stack=p1) for _ in range(NW)]
            BK = [sb([128, 2, 128], BF16, stack=p1) for _ in range(NW)]
            Vst = [sb([128, 64], BF16, stack=p1) for _ in range(NW)]
            WU = [sb([128, 192], BF16, stack=p1) for _ in range(NW)]
            PT = [sb([128, 128], stack=p1) for _ in range(NW)]
            RH = [sb([128, 128], stack=p1) for _ in range(NW)]
            sm = [sb([128, 16], stack=p1) for _ in range(NW)]
            yfin = [sb([128, 64], stack=p1) for _ in range(NW)]
            yh = [sb([128, 64], stack=p1) for _ in range(NW)]
            for p in range(2):
                for tl, nm in ((AR_bd, "AR"), (B_bd, "B"), (K_bd, "K"), (BH_bd, "BH"), (KH_bd, "KH"), (V_bd, "V"), (RRK_bd, "RRK")):
                    mset("pool", tl[p][:], 0.0, [f"{nm}{p}"])
                mset("pool", Hs[p][:], 0.0, [f"H{p}"])

            uT = dn[0]
            for ob in range(4):
                norm_to_fm(xo[ob * 512:ob * 512 + 16, :], 16, gsc1, sh1, "sh1",
                           lambda k: hT[:, k, 0:16], "hT")
                if STOP == "norm16":
                    S.barrier()
                    return nc
                for ti in range(4):
                    norm_to_fm(xo[16 + ob * 512 + ti * 128:16 + ob * 512 + (ti + 1) * 128, :], 128, gsc1, sh1, "sh1",
                               lambda k, ti=ti: hT[:, k, 16 + ti * 128:16 + (ti + 1) * 128], "hT")
                if STOP == "norm":
                    S.barrier()
                    return nc
                if ob == 0:
                    for k_ in range(8):
                        dump(f"hT{k_}", hT[:, k_, 0:144], 144)
                for g in range(4):
                    if STOP == "g1" and g == 1:
                        S.barrier()
                        return nc
                    w = (2, 4, 8, 16)[g]
                    bank, bk = gbank()
                    for k in range(8):
                        mm(bank[:], w_in_bf[:, k, g * 128:(g + 1) * 128], hT[:, k, 16:528], ["w_in_bf", "hT"], [bk], start=(k == 0), stop=(k == 7))
                    cp("act", uT[:, 16:528], bank[:], [bk], ["dn0"])
                    bank2, bk2 = gbank()
                    for k in range(8):
                        mm(bank2[:, 0:16], w_in_bf[:, k, g * 128:(g + 1) * 128], hT[:, k, 0:16], ["w_in_bf", "hT"], [bk2], start=(k == 0), stop=(k == 7))
                    if ob == 0:
                        tsc("dve", uT[:, 0:16], bank2[:, 0:16], flg[:, 0:1], None, ALU.mult, None, [bk2, "flg"], ["dn0"])
                    else:
                        cp("dve", uT[:, 0:16], bank2[:, 0:16], [bk2], ["dn0"])
                    src, sk = uT, "dn0"
                    sh_ = 1
                    lvl = 0
                    while sh_ < w:
                        dst, dk = dn[1 + lvl % 2], f"dn{1 + lvl % 2}"
                        tt("pool", dst[:, sh_:528], src[:, sh_:528], src[:, 0:528 - sh_], ALU.add, [sk], [dk])
                        src, sk = dst, dk
                        sh_ *= 2
                        lvl += 1
                    dd = dn[3]
                    stt(dd[:, 16:528], src[:, 16:528], 1.0 / w, uT[:, 16:528], ALU.mult, ALU.subtract, [sk, "dn0"], ["dn3"])
                    if ob == 0:
                        tt("dve", dn[4][:, 0:16], src[:, 16:32], flg[:, 1 + g * 16:1 + (g + 1) * 16], ALU.mult, [sk, "flg"], ["dn4"])
                        tt("dve", dd[:, 16:32], dn[4][:, 0:16], uT[:, 16:32], ALU.subtract, ["dn4", "dn0", "dn3"], ["dn3"])
                    dbfT = dn[5][:, 0:256].bitcast(BF16)
                    cp("act", dbfT, dd[:, 16:528], ["dn3"], ["dn5"])
                    bank3, bk3 = gbank()
                    mm(bank3[:], pw_bf[:, g, :], dbfT, ["pw_bf", "dn5"], [bk3])
                    tsc("dve", mixp[:, g, ob * 512:(ob + 1) * 512], bank3[:], pscale[:, g:g + 1], None, ALU.mult, None, [bk3, "pscale"], ["mixp"])

            for g_ in range(4):
                dump(f"mixp{g_}", mixp[:, g_, 0:128], 128)
            if STOP == "pool":
                S.barrier()
                return nc
            S.barrier()
            GBL[:] = [7]
            ysv = [ysrc[j].ap().rearrange("(h i) t -> i h t", i=64) for j in range(4)]
            S.stream("cc", 1)

            ymark = {}

            def exchange(j):
                for st_, v_ in ymark[j]:
                    S._wait("pool", st_, v_)
                nc.gpsimd.collective_compute("AllGather", ALU.bypass, replica_groups=[[0, 1, 2, 3], [4, 5, 6, 7]],
                                             ins=[ysrc[j].ap().opt()], outs=[ydst.ap()[j].opt()]).then_inc(S.sem["cc"], 1)
                S.cnt["cc"] += 1
            state = {"pk": 0}

            def prep(blk):
                bp = blk % 2
                zt, zk = zT[bp], f"zT{bp}"
                for ti in range(4):
                    yield from norm_gen(xb[16 + blk * 512 + ti * 128:16 + blk * 512 + (ti + 1) * 128, :], 128, gsc1, sh1, "sh1",
                                        lambda k, ti=ti: hT[:, k, 16 + ti * 128:16 + (ti + 1) * 128], "hT")
                if blk == 0:
                    mset("pool", zt[:, :, 0:1], 0.0, [zk])
                else:
                    cp("pool", zt[:, :, 0:1], zT[1 - bp][:, :, 512:513], [f"zT{1 - bp}"], [zk])
                    yield
                for m in range(5):
                    bank, bk = gbank()
                    for k in range(8):
                        mm(bank[:], w_in_bf[:, k, 512 + m * 128:512 + (m + 1) * 128], hT[:, k, 16:528], ["w_in_bf", "hT"], [bk], start=(k == 0), stop=(k == 7))
                    yield
                    cp("act", zt[:, m, 1:513], bank[:], [bk], [zk])
                    yield
                zs = []
                for m in range(5):
                    tmp = dn[11]
                    act(tmp[:, 0:512], zt[:, m, 0:512], AF.Copy, [zk, "mu"], ["dn11"], scale=mu[:, m:m + 1])
                    yield
                    dst = dn[m]
                    stt(dst[:, 0:512], zt[:, m, 1:513], omm[:, m:m + 1], tmp[:, 0:512], ALU.mult, ALU.add, [zk, "omm", "dn11"], [f"dn{m}"])
                    yield
                    zs.append(dst)
                rT, kT, vT, xwa, xg = [z[:, 0:512] for z in zs]
                thx = dn[5]
                act(thx[0:64, 0:512], xwa[0:64, :], AF.Tanh, ["dn3"], ["dn5"])
                yield
                bank, bk = gbank()
                mm(bank[:], Wd[0:64, :], thx[0:64, 0:512], ["Wd", "dn5"], [bk])
                yield
                sg = dn[6]
                act(sg[:, 0:512], bank[:], AF.Sigmoid, [bk, "pv"], ["dn6"], bias=pv[:, 0:1])
                yield
                bank, bk = gbank()
                mm(bank[:], Wa[64:128, :], xwa[64:128, :], ["Wa", "dn3"], [bk])
                yield
                aT = dn[7]
                act(aT[:, 0:512], bank[:], AF.Sigmoid, [bk, "pv"], ["dn7"], bias=pv[:, 1:2])
                yield
                sgx = dn[5]
                act(sgx[:, 0:512], xg, AF.Sigmoid, ["dn4"], ["dn5"])
                yield
                for h in range(2):
                    bank, bk = gbank()
                    mm(bank[0:64, :], Wg[:, h * 64:(h + 1) * 64], sgx[:, 0:512], ["Wg", "dn5"], [bk])
                    yield
                    cp("act", gT[bp][0:64, h, :], bank[0:64, :], [bk], [f"gT{bp}"])
                    yield
                cs = dn[8]
                S.op("dve", lambda: nc.vector.tensor_tensor_scan(out=cs[:, 0:512], data0=rmask[:], data1=sg[:, 0:512], initial=0.0,
                                                                 op0=ALU.mult, op1=ALU.add), ["rmask", "dn6"], ["dn8"])
                yield
                epos = dn[9]
                act(epos[:, 0:512], cs[:, 0:512], AF.Exp, ["dn8"], ["dn9"], scale=-DEC)
                yield
                cp("pool", gamC[bp][:], epos[:, 0:512].rearrange("p (c t) -> p c t", t=64)[:, :, 63], ["dn9"], [f"gamC{bp}"])
                yield
                eneg = dn[10]
                act(eneg[:, 0:512], cs[:, 0:512], AF.Exp, ["dn8"], ["dn10"], scale=DEC)
                yield
                tt("dve", cs[:, 0:512], cs[:, 0:512], sg[:, 0:512], ALU.subtract, ["dn8", "dn6"], ["dn8"])
                yield
                eprev = dn[6]
                act(eprev[:, 0:512], cs[:, 0:512], AF.Exp, ["dn8"], ["dn6"], scale=-DEC)
                yield
                kkr = dn[3]
                tsc("pool", kkr[:, 0:512], kT, pv[:, 2:3], None, ALU.mult, None, ["dn1", "pv"], ["dn3"])
                yield
                sq = dn[4]
                tt("pool", sq[:, 0:512], kkr[:, 0:512], kkr[:, 0:512], ALU.mult, ["dn3"], ["dn4"])
                yield
                bank, bk = gbank()
                mm(bank[:], Blk[:], sq[:, 0:512], ["Blk", "dn4"], [bk])
                yield
                act(sq[:, 0:512], bank[:], AF.Sqrt, [bk], ["dn4"])
                yield
                tsc("dve", sq[:, 0:512], sq[:, 0:512], 1e-12, None, ALU.max, None, ["dn4"], ["dn4"])
                yield
                S.op("dve", lambda: nc.vector.reciprocal(out=sq[:, 0:512], in_=sq[:, 0:512]), ["dn4"], ["dn4"])
                yield
                kk = dn[3]
                tt("dve", kk[:, 0:512], kkr[:, 0:512], sq[:, 0:512], ALU.mult, ["dn3", "dn4"], ["dn3"])
                yield
                t1 = dn[4]
                tsc("dve", t1[:, 0:512], aT[:, 0:512], -1.0, pv[:, 3:4], ALU.add, ALU.mult, ["dn7", "pv"], ["dn4"])
                yield
                kp = dn[8]
                stt(kp[:, 0:512], t1[:, 0:512], 1.0, kT, ALU.add, ALU.mult, ["dn4", "dn1"], ["dn8"])
                yield
                tt("pool", aT[:, 0:512], aT[:, 0:512], kk[:, 0:512], ALU.mult, ["dn7", "dn3"], ["dn7"])
                yield
                stt(t1[:, 0:512], rT, pv[:, 4:5], kp[:, 0:512], ALU.mult, ALU.mult, ["dn0", "pv", "dn8"], ["dn4"])
                yield
                stt(kk[:, 0:512], kk[:, 0:512], -1.0, eprev[:, 0:512], ALU.mult, ALU.mult, ["dn3", "dn6"], ["dn3"])
                yield
                tt("pool", aT[:, 0:512], aT[:, 0:512], eneg[:, 0:512], ALU.mult, ["dn7", "dn10"], ["dn7"])
                yield
                tt("dve", kp[:, 0:512], kp[:, 0:512], eneg[:, 0:512], ALU.mult, ["dn8", "dn10"], ["dn8"])
                yield

                if blk == 0:
                    for nm_, t__ in (("rT", rT), ("vT", vT), ("atil", kk[:, 0:512]), ("btil", aT[:, 0:512]), ("ktil", kp[:, 0:512]),
                                     ("epos", epos[:, 0:512]), ("rrk", t1[:, 0:512])):
                        dump(nm_, t__[:, 0:128], 128)
                    for h_ in range(2):
                        dump(f"gT{h_}", gT[bp][0:64, h_, 0:64], 64)

                def c3(t_):
                    return t_.rearrange("p (c t) -> p c t", t=64)

                gam3 = gamC[bp][:].rearrange("p (c o) -> p c o", o=1)
                for h in range(2):
                    hs = slice(h * 64, (h + 1) * 64)
                    cs_ = slice(h * 64, (h + 1) * 64)
                    e1 = "dve" if h == 0 else "pool"
                    cp(e1, AR_bd[bp][hs, :, cs_], c3(kk[hs, 0:512]), ["dn3"], [f"AR{bp}"])
                    yield
                    tt(e1, AR_bd[bp][hs, :, 128 + h * 64:128 + (h + 1) * 64], c3(rT[hs, :]), c3(epos[hs, 0:512]), ALU.mult, ["dn0", "dn9"], [f"AR{bp}"])
                    yield
                    cp(e1, B_bd[bp][hs, :, cs_], c3(aT[hs, 0:512]), ["dn7"], [f"B{bp}"])
                    yield
                    cp(e1, K_bd[bp][hs, :, cs_], c3(kp[hs, 0:512]), ["dn8"], [f"K{bp}"])
                    yield
                    tt(e1, BH_bd[bp][hs, :, cs_], c3(aT[hs, 0:512]), gam3[hs].broadcast_to([64, 8, 64]), ALU.mult, ["dn7", f"gamC{bp}"], [f"BH{bp}"])
                    yield
                    tt(e1, KH_bd[bp][hs, :, cs_], c3(kp[hs, 0:512]), gam3[hs].broadcast_to([64, 8, 64]), ALU.mult, ["dn8", f"gamC{bp}"], [f"KH{bp}"])
                    yield
                    cp(e1, V_bd[bp][hs, :, cs_], c3(vT[hs, :]), ["dn2"], [f"V{bp}"])
                    yield
                    cp(e1, RRK_bd[bp][hs, :, cs_], c3(t1[hs, 0:512]), ["dn4"], [f"RRK{bp}"])
                    yield

            def pack(blk, c):
                bp = blk % 2
                g = state["pk"]
                state["pk"] += 1
                import os
                pp = (g + int(os.environ.get("PPX", "0"))) % NW
                hc, hn = g % 2, (g + 1) % 2
                s0, s1 = SB0[pp], SB1[pp]
                I0, I1, I2, KAV, KVS = [f"s0_{pp}_{i}" for i in range(5)]
                k1 = [f"s1_{pp}_{i}" for i in range(5)]
                tb = [("TB", j) for j in range(3)]
                X, a3, r3, bk_, vs, wu, pt, rh, smm = Xi[pp], A3[pp], R3[pp], BK[pp], Vst[pp], WU[pp], PT[pp], RH[pp], sm[pp]
                kX, kA3, kR3, kBK, kV, kWU, kPT, kRH = f"X{pp}", f"A3{pp}", f"R3{pp}", f"BK{pp}", f"Vst{pp}", f"WU{pp}", f"PT{pp}", f"RH{pp}"
                ar, bb, kb, bh, kh, vb, rrk = AR_bd[bp], B_bd[bp], K_bd[bp], BH_bd[bp], KH_bd[bp], V_bd[bp], RRK_bd[bp]
                s0v = s0[:, 0:384].rearrange("p (a b) -> p a b", b=128)
                mm(s0[:, 0:128], bb[:, c, :], ar[:, c, 0:128], [f"B{bp}", f"AR{bp}"], [I0])
                mm(s0[:, 256:384], ar[:, c, 0:128], bb[:, c, :], [f"B{bp}", f"AR{bp}"], [I2])
                mm(s1[:, 0:128], bb[:, c, :], ar[:, c, 128:256], [f"B{bp}", f"AR{bp}"], [k1[0]])
                mm(s1[:, 128:384], kb[:, c, :], ar[:, c, :], [f"K{bp}", f"AR{bp}"], [k1[1], k1[2], k1[3]])
                o = 0
                mm(TB[:, o:o + 128], ar[:, c, 0:128], ident_bf[:], [f"AR{bp}", "ident_bf"], [tb[0]])
                mm(TB[:, o + 128:o + 256], bh[:, c, :], ident_bf[:], [f"BH{bp}", "ident_bf"], [tb[1]])
                mm(TB[:, o + 256:o + 384], kh[:, c, :], ident_bf[:], [f"KH{bp}", "ident_bf"], [tb[2]])
                mm(s0[:, 448:512], vb[:, c, :], E_bf[:], [f"V{bp}", "E_bf"], [KVS])
                import os
                if os.environ.get("RKTB", "0") == "1":
                    mm(TB[:, 384:386], rrk[:, c, :], ones_bf[:], [f"RRK{bp}", "ones_bf"], [k1[4]])
                else:
                    mm(s1[:, 384:386], rrk[:, c, :], ones_bf[:], [f"RRK{bp}", "ones_bf"], [k1[4]])
                yield
                import os
                OPS = os.environ.get("OPS", "abcdefg")
                if "a" in OPS:
                    tt("dve", X[:, 0:3:2, :], s0v[:, 0:3:2, :], M_TS[:], ALU.mult, [I0, I2, "M_TS"], [kX])
                if "b" in OPS:
                    tt("dve", a3[:], s1[:, 0:384].rearrange("p (a b) -> p a b", b=128), M_A3[:], ALU.mult, [k1[0], k1[1], k1[2], k1[3], "M_A3"], [kA3])
                if "c" in OPS:
                    tt("pool", X[:, 1, :], X[:, 0, :], ident_f[:], ALU.add, [kX, "ident_f"], [kX])
                if "d" in OPS:
                    cp("act", r3[:, 0:128], TB[:, o:o + 128], [tb[0]], [kR3])
                if "e" in OPS:
                    cp("act", bk_[:], TB[:, o + 128:o + 384].rearrange("p (a b) -> p a b", b=128), [tb[1], tb[2]], [kBK])
                if "f" in OPS:
                    cp("act", vs[:], s0[:, 448:512], [KVS], [kV])
                if "g" in OPS:
                    cp("act", smm[:, 0:1], s1[:, 384:385], [k1[4]], [f"sm{pp}rk"])
                yield
                mm(s0[:, 0:128], X[:, 2, :], X[:, 0, :], [kX], [I0])
                mm(s0[:, 256:384], X[:, 0, :], X[:, 2, :], [kX], [I2])
                yield
                cp("act", X[:, 0:3:2, :], s0v[:, 0:3:2, :], [I0, I2], [kX])
                yield
                for lv in range(1, 5):
                    mm(s0[:, 0:256], X[:, 2, :], X[:, 0:2, :].rearrange("p a b -> p (a b)"), [kX], [I0, I1])
                    mm(s0[:, 256:384], X[:, 0, :], X[:, 2, :], [kX], [I2])
                    yield
                    tt("dve", X[:, 1, :], s0[:, 128:256], X[:, 1, :], ALU.add, [I1, kX], [kX])
                    cp("act", X[:, 0:3:2, :], s0v[:, 0:3:2, :], [I0, I2], [kX])
                    yield
                mm(s0[:, 128:256], X[:, 2, :], X[:, 1, :], [kX], [I1])
                mm(s0[:, 384:448], a3[:, 1, :], vs[:], [kA3, kV], [KAV])
                yield
                tt("dve", Mb[pp][:], s0[:, 128:256], X[:, 1, :], ALU.add, [I1, kX], [f"Mb{pp}"])
                cp("act", r3[:, 128:192], s0[:, 384:448], [KAV], [kR3])
                yield
                mm(s0[:, 0:192], Mb[pp][:], r3[:], [f"Mb{pp}", kR3], [I0, I1])
                yield
                cp("act", wu[:], s0[:, 0:192], [I0, I1], [kWU])
                yield
                mm(s0[:, 192:320], wu[:, 0:128], bk_[:, 0, :], [kWU, kBK], [I1, I2])
                mm(s1[:, 0:128], wu[:, 0:128], a3[:, 0, :], [kWU, kA3], [k1[0]])
                yield
                cp("act", pt[:], s0[:, 192:320], [I1, I2], [kPT])
                tt("dve", rh[:], s1[:, 0:128], ar[:, c, 128:256], ALU.add, [k1[0], f"AR{bp}"], [kRH])
                yield
                mm(s1[:, 256:320], a3[:, 0, :], wu[:, 128:192], [kA3, kWU], [k1[2]], start=True, stop=False)
                mm(s1[:, 256:320], a3[:, 2, :], vs[:], [kA3, kV], [k1[2]], start=False, stop=False)
                mm(s1[:, 256:320], rh[:], Hs[hc][:], [kRH, f"H{hc}"], [k1[2]], start=False, stop=True)
                mm(s1[:, 320:384], bk_[:, 0, :], wu[:, 128:192], [kBK, kWU], [k1[3]], start=True, stop=False)
                mm(s1[:, 320:384], bk_[:, 1, :], vs[:], [kBK, kV], [k1[3]], start=False, stop=False)
                mm(s1[:, 320:384], pt[:], Hs[hc][:], [kPT, f"H{hc}"], [k1[3]], start=False, stop=True)
                yield
                stt(Hs[hn][:], Hs[hc][:], gamC[bp][:, c:c + 1], s1[:, 320:384], ALU.mult, ALU.add, [f"H{hc}", f"gamC{bp}", k1[3]], [f"H{hn}"])
                S.op("dve", lambda: nc.vector.bn_stats(out=smm[:, 2:8], in_=s1[:, 256:320]), [k1[2]], [f"sm{pp}st"])
                S.op("dve", lambda: nc.vector.bn_aggr(out=smm[:, 8:10], in_=smm[:, 2:8]), [f"sm{pp}st"], [f"sm{pp}mv"])
                act(smm[:, 10:11], smm[:, 9:10], AF.Sqrt, [f"sm{pp}mv"], [f"sm{pp}sd"], bias=GN_EPS, scale=1.0)
                S.op("dve", lambda: nc.vector.reciprocal(out=smm[:, 11:12], in_=smm[:, 10:11]), [f"sm{pp}sd"], [f"sm{pp}rs"])
                tsc("dve", yh[pp][:], s1[:, 256:320], smm[:, 8:9], smm[:, 11:12], ALU.subtract, ALU.mult, [k1[2], f"sm{pp}mv", f"sm{pp}rs"], [f"yh{pp}"])
                tt("pool", yh[pp][:], yh[pp][:], lnw_bc[:], ALU.mult, [f"yh{pp}", "lnw_bc"], [f"yh{pp}"])
                tt("pool", yh[pp][:], yh[pp][:], lnb_bc[:], ALU.add, [f"yh{pp}", "lnb_bc"], [f"yh{pp}"])
                stt(yfin[pp][:], vs[:], smm[:, 0:1], yh[pp][:], ALU.mult, ALU.add, [kV, f"sm{pp}rk", f"yh{pp}"], [f"yfin{pp}"])
                yield
                tr(s1[0:64, 128:256], yfin[pp][:], ident_f[:], [f"yfin{pp}", "ident_f"], [k1[1]])
                yield
                tt("dve", yTb[bp][0:64, :, c * 64:(c + 1) * 64], s1[0:64, 128:256].rearrange("p (h t) -> p h t", t=64),
                   gT[bp][0:64, :, c * 64:(c + 1) * 64], ALU.mult, [k1[1], f"gT{bp}"], [f"yTb{bp}"])

            def run_packs(blk, extra=None):
                import os
                gens = [pack(blk, c) for c in range(int(os.environ.get("PKN", "8")))]
                maxs = int(STOP[2:]) if (STOP or "").startswith("pk") else 10 ** 9
                adv = {}
                active = []
                if extra is not None and maxs > 10 ** 8:
                    active.append(extra)
                nxt = 0
                stepc = 0
                NG = len(gens)
                while nxt < NG or active:
                    npk = len([a_ for a_ in active if a_ is not extra])
                    if nxt < NG and npk < NW and (npk == 0 or stepc % 3 == 0):
                        active.append(gens[nxt])
                        nxt += 1
                    for gkk in list(active):
                        try:
                            adv[id(gkk)] = adv.get(id(gkk), 0) + 1
                            if adv[id(gkk)] > maxs:
                                raise StopIteration
                            next(gkk)
                            if gkk is extra:
                                next(gkk)
                                next(gkk)
                        except StopIteration:
                            active.remove(gkk)
                    stepc += 1

            for _ in prep(0):
                pass
            for blk in range(NBLK):
                if STOP == "prep":
                    S.barrier()
                    return nc
                run_packs(blk, prep(blk + 1) if blk + 1 < NBLK else None)
                if STOP == "blk1" or (STOP or "").startswith("pk"):
                    S.barrier()
                    return nc
                bp = blk % 2
                if blk == 0:
                    for h_ in range(2):
                        dump(f"yT{h_}", yTb[bp][0:64, h_, 0:128], 128)
                    dump("H1", Hs[0][:], 64)
                S.dma("sp", f"yst{bp}", ysv[blk // 4][:, :, (blk % 4) * 512:(blk % 4 + 1) * 512], yTb[bp][0:64, :, :], [f"yTb{bp}"], ["ysrc"])
                if blk % 4 == 3:
                    ymark[blk // 4] = [(st_, S.cnt[st_]) for st_ in ("yst0", "yst1")]
                if blk % 4 == 0 and blk > 0:
                    exchange(blk // 4 - 1)
            S.barrier()
            exchange(3)
            GBL[:] = [0, 1, 2, 3, 4, 5, 7]

        if STOP == "p1":
            return nc
        S.barrier()

        if STOP == "cc":
            return nc
        ydv = ydst.ap().rearrange("j (h i) t -> i j h t", i=64)
        with contextlib.ExitStack() as p2:
            wo_p = sb([128, 4, D], BF16, stack=p2)
            wo_r = sb([128, 8, D], BF16, stack=p2)
            yall = [sb([128, 8, 512], BF16, stack=p2) for _ in range(2)]
            x1t = [sb([128, D], stack=p2) for _ in range(2)]
            g1b3 = g1bc[:].rearrange("p (k n) -> p k n", k=1)
            load_cast(lambda a, b: wo_p[:, :, a:b], lambda a, b: w_out[0:512, a:b].rearrange("(k p) n -> p k n", p=128), 128, 4, D,
                      ["wo_p", "g1bc"], scale_bc=lambda a, b: g1b3[:, :, a:b].broadcast_to([128, 4, b - a]))
            load_cast(lambda a, b: wo_r[0:64, :, a:b], lambda a, b: w_out[512:1024, a:b].rearrange("(h i) n -> i h n", i=64), 64, 8, D,
                      ["wo_r", "g1bc"], scale_bc=lambda a, b: g1b3[0:64, :, a:b].broadcast_to([64, 8, b - a]))
            for ob in range(4):
                ya = yall[ob % 2]
                S.dma("pool", f"yld{ob % 2}", ya[0:64, :, :], ydv[:, bass.ds(q, 1), :, ob * 512:(ob + 1) * 512].rearrange("i j h t -> i (j h) t"),
                      ["wo_r", "wo_p"], [f"yall{ob % 2}"])
                for ti in range(4):
                    i = _xt[0] % 2
                    _xt[0] += 1
                    S.dma("sp", f"xld{i}", xt[i][:], xo[16 + ob * 512 + ti * 128:16 + ob * 512 + (ti + 1) * 128, :], (), [f"xt{i}"])
                    j = (ob * 4 + ti) % 2
                    for hf in range(2):
                        bank, bk = gbank()
                        for m in range(4):
                            mm(bank[:], mixp[:, m, ob * 512 + ti * 128:ob * 512 + (ti + 1) * 128], wo_p[:, m, hf * 512:(hf + 1) * 512],
                               ["mixp", "wo_p"], [bk], start=(m == 0), stop=False)
                        for h in range(8):
                            mm(bank[:], ya[0:64, h, ti * 128:(ti + 1) * 128], wo_r[0:64, h, hf * 512:(hf + 1) * 512],
                               [f"yall{ob % 2}", "wo_r"], [bk], start=False, stop=(h == 7))
                        tt("dve", x1t[j][:, hf * 512:(hf + 1) * 512], bank[:], xt[i][:, hf * 512:(hf + 1) * 512], ALU.add, [bk, f"xt{i}"], [f"x1t{j}"])
                    r0 = ob * 512 + ti * 128
                    if ob == 0 and ti == 0:
                        dump("x1", x1t[j][:, 0:256], 256)
                        for h_ in range(8):
                            dump(f"yall{h_}", ya[0:64, h_, 0:32], 32)
                    S.dma("sp", f"x1st{j}", x1d[r0:r0 + 128, :], x1t[j][:], [f"x1t{j}"], ["x1d"])
            S.barrier()
        pA.close()
        if STOP == "p2a":
            return nc

        with contextlib.ExitStack() as p3:
            wgu = sb([128, 8, 2 * DFF], BF16, stack=p3)
            wdn = sb([128, NFF, D], BF16, stack=p3)
            x1b = sb([128, 2, D], stack=p3)
            h2T = sb([128, 8, 256], BF16, stack=p3)
            actT = sb([128, NFF, 256], BF16, stack=p3)
            sgt = [sb([128, 256], stack=p3) for _ in range(2)]
            ot = sb([128, D], stack=p3)
            g2b3 = g2bc[:].rearrange("p (k n) -> p k n", k=1)
            for kh in range(2):
                load_cast(lambda a, b, kh=kh: wgu[:, kh * 4:(kh + 1) * 4, a:b],
                          lambda a, b, kh=kh: w_gu[kh * 512:(kh + 1) * 512, a:b].rearrange("(k p) n -> p k n", p=128), 128, 4, 2 * DFF, ["wgu"])
            wd3 = w_down.rearrange("(f p) n -> p f n", p=128)
            for f0 in range(0, NFF, 2):
                load_cast(lambda a, b, f0=f0: wdn[:, f0:f0 + 2, a:b], lambda a, b, f0=f0: wd3[:, f0:f0 + 2, a:b], 128, 2, D,
                          ["wdn", "g2bc"], scale_bc=lambda a, b: g2b3[:, :, a:b].broadcast_to([128, 2, b - a]))
            for ob in range(8):
                for ti in range(2):
                    r0 = ob * 256 + ti * 128
                    S.dma("sp", f"x1ld{ti}", x1b[:, ti, :], x1d[r0:r0 + 128, :], ["x1d"], [("x1b", ti)])
                    norm_to_fm(x1b[:, ti, :], 128, gsc2, sh2, "sh2", lambda k, ti=ti: h2T[:, k, ti * 128:(ti + 1) * 128], "h2T", src_key=("x1b", ti))
                for f in range(NFF):
                    bg, kg = gbank()
                    for k in range(8):
                        mm(bg[:, 0:256], wgu[:, k, f * 128:(f + 1) * 128], h2T[:, k, :], ["wgu", "h2T"], [kg], start=(k == 0), stop=(k == 7))
                    bu, ku = gbank()
                    for k in range(8):
                        mm(bu[:, 0:256], wgu[:, k, DFF + f * 128:DFF + (f + 1) * 128], h2T[:, k, :], ["wgu", "h2T"], [ku], start=(k == 0), stop=(k == 7))
                    sj = f % 2
                    act(sgt[sj][:], bg[:, 0:256], AF.Silu, [kg], [f"sgt{sj}"])
                    tt("dve", actT[:, f, :], bu[:, 0:256], sgt[sj][:], ALU.mult, [ku, f"sgt{sj}"], ["actT"])
                for ti in range(2):
                    for hf in range(2):
                        bank, bk = gbank()
                        for f in range(NFF):
                            mm(bank[:], actT[:, f, ti * 128:(ti + 1) * 128], wdn[:, f, hf * 512:(hf + 1) * 512], ["actT", "wdn"], [bk],
                               start=(f == 0), stop=(f == NFF - 1))
                        tt("dve", x1b[:, ti, hf * 512:(hf + 1) * 512], bank[:], x1b[:, ti, hf * 512:(hf + 1) * 512], ALU.add,
                           [bk, ("x1b", ti)], [("x1b", ti)])
                    xa = x1b[:, ti, :]
                    act(junk[:], xa, AF.Square, [("x1b", ti)], ["junk", "ss"], accum_out=ss[:, 0:1])
                    act(ss[:, 1:2], ss[:, 0:1], AF.Sqrt, ["ss"], ["ss1"], bias=RMS_EPS, scale=1.0 / D)
                    S.op("dve", lambda: nc.vector.reciprocal(out=ss[:, 2:3], in_=ss[:, 1:2]), ["ss1"], ["ss2"])
                    stt(ot[:], xa, ss[:, 2:3], fgbc[:], ALU.mult, ALU.mult, [("x1b", ti), "ss2", "fgbc"], ["ot"])
                    r0 = ob * 256 + ti * 128
                    S.dma("sp", "ost", out[r0:r0 + 128, :], ot[:], ["ot"], ["outd"])
            S.barrier()
    return nc


_NC = None


def kernel(**inputs):
    global _NC
    x = np.asarray(inputs["x"], np.float32)
    c = np.asarray(inputs["c"], np.float32)
    if _NC is None:
        _NC = build()
    g = lambda n: np.asarray(inputs[n], np.float32)[0]
    names = ["ada_w", "ada_b", "norm1_g", "pool_w", "pool_scale", "w_out", "norm2_g", "w_ffn_gu", "w_ffn_down"]
    shared = {n: np.ascontiguousarray(g(n)) for n in names}
    shared["final_g"] = np.ascontiguousarray(np.asarray(inputs["final_g"], np.float32))
    w_in, mu_shift = g("w_in"), g("mu_shift")
    in_maps = []
    wins = (2, 4, 8, 16)
    for core in range(8):
        b, qq = core // 4, core % 4
        xpad = np.zeros((T + 16, D), np.float32)
        xpad[16:] = x[b]
        fl = np.zeros((128, 65), np.float32)
        fl[:, 0] = 0.0 if qq == 0 else 1.0
        for gi, w in enumerate(wins):
            for t in range(16):
                fl[:, 1 + gi * 16 + t] = 1.0 / min(qq * TQ + t + 1, w)
        ps_ = slice(qq * 128, (qq + 1) * 128)
        cols = np.concatenate([np.arange(0, 512)] + [np.arange(512 + i * 512 + qq * 128, 512 + i * 512 + (qq + 1) * 128) for i in range(3)]
                              + [np.arange(2048, 2304)])
        mcols = np.stack([mu_shift[i * 512 + qq * 128:i * 512 + (qq + 1) * 128] for i in range(3)]
                         + [mu_shift[1536:1664], mu_shift[1664:1792]], axis=1)
        m = dict(shared)
        m["xb"] = xpad
        m["xo"] = np.ascontiguousarray(xpad[qq * TQ:qq * TQ + TQ + 16])
        m["cvec"] = np.ascontiguousarray(c[b])
        m["flags"] = fl
        m["w_in_sel"] = np.ascontiguousarray(w_in[:, cols])
        m["mu_sel"] = np.ascontiguousarray(mcols)
        m["pv_sel"] = np.ascontiguousarray(np.stack([g(n)[ps_] for n in ("w0", "a0", "k_k", "k_a", "r_k")], axis=1))
        m["ln2"] = np.ascontiguousarray(np.stack([g("lnx_w")[ps_].reshape(2, 64), g("lnx_b")[ps_].reshape(2, 64)], axis=0))
        m["Wd_sel"] = np.ascontiguousarray(g("w_decay_up")[:, ps_])
        m["Wa_sel"] = np.ascontiguousarray(g("w_iclr_up")[:, ps_])
        m["Wg_sel"] = np.ascontiguousarray(g("w_gate_up")[:, ps_])
        in_maps.append(m)
    res = run_bass_kernel_spmd(_NC, in_maps, core_ids=list(range(8)))
    if DEBUG:
        global DBG_OUT
        DBG_OUT = [res.results[core]["dbg"] for core in range(8)]
    outp = np.zeros((2, T, D), np.float32)
    for core in range(8):
        b, qq = core // 4, core % 4
        outp[b, qq * TQ:(qq + 1) * TQ] = res.results[core]["out"]
    return outp
```

```python
import numpy as np
import ml_dtypes
import concourse.bass as bass
import concourse.mybir as mybir
from concourse.bass_utils import run_bass_kernel_spmd

F32 = mybir.dt.float32
BF16 = mybir.dt.bfloat16
ALU = mybir.AluOpType
AF = mybir.ActivationFunctionType
AX = mybir.AxisListType

D = 1024
T = 8192
TQ = 2048
NBLK = T // 512
DFF = 2816
NFF = DFF // 128
GN_EPS = 64e-5
RMS_EPS = 1e-6
DEC = 0.6065306597126334


class Sched:
    def __init__(self, nc, es):
        self.nc = nc
        self.es = es
        self.eng = {"pe": nc.tensor, "act": nc.scalar, "dve": nc.vector, "pool": nc.gpsimd, "sp": nc.sync}
        self.sem = {}
        self.cnt = {}
        self.inc = {}
        self.waited = {}
        self.lastw = {}
        self.readers = {}
        for e in ("pe", "act", "dve", "pool"):
            self.stream(e, 1)

    def stream(self, name, inc):
        if name not in self.sem:
            self.sem[name] = self.es.enter_context(self.nc.semaphore("s_" + name))
            self.cnt[name] = 0
            self.inc[name] = inc
        return name

    def _wait(self, e, s, v):
        if s == e and e == "pe":
            return
        if self.waited.get((e, s), 0) >= v:
            return
        self.eng[e].wait_ge(self.sem[s], v)
        self.waited[(e, s)] = v

    @staticmethod
    def _bank(k):
        if isinstance(k, tuple) and k and k[0] == "TB":
            return "BANK_TB"
        if isinstance(k, str) and (k.startswith("s0_") or k.startswith("s1_")):
            return "BANK_" + k[:4]
        return None

    def _aug(self, keys):
        out = list(keys)
        for k in keys:
            b = self._bank(k)
            if b is not None and b not in out:
                out.append(b)
        return out

    def _deps(self, reads, writes):
        deps = set()
        for r in reads:
            if r in self.lastw:
                deps.add(self.lastw[r])
        for r in writes:
            if r in self.lastw:
                deps.add(self.lastw[r])
            for d in self.readers.get(r, ()):
                deps.add(d)
        return deps

    def _commit(self, s, reads, writes):
        v = self.cnt[s]
        for r in writes:
            self.lastw[r] = (s, v)
            self.readers[r] = []
        for r in reads:
            self.readers.setdefault(r, []).append((s, v))

    def op(self, e, fn, reads=(), writes=()):
        bk = [k for k in self._aug(list(reads) + list(writes)) if isinstance(k, str) and k.startswith("BANK_")]
        reads, writes = list(reads), list(writes) + bk
        for (s, v) in self._deps(reads, writes):
            self._wait(e, s, v)
        ins = fn()
        self.cnt[e] += 1
        ins.then_inc(self.sem[e], 1)
        self._commit(e, reads, writes)
        return ins

    def dma(self, q, s, out, in_, reads=(), writes=(), **kw):
        self.stream(s, 16)
        for (ss, v) in self._deps(reads, writes):
            self._wait(q, ss, v)
        ins = self.eng[q].dma_start(out=out, in_=in_, **kw)
        self.cnt[s] += 16
        ins.then_inc(self.sem[s], 16)
        self._commit(s, reads, writes)
        return ins

    def close(self, s):
        for r, (ss, v) in list(self.lastw.items()):
            if ss == s:
                self.lastw[r] = (s, self.cnt[s])

    def wait_all(self, e):
        for s in self.cnt:
            if self.cnt[s] > 0:
                self._wait(e, s, self.cnt[s])

    def barrier(self):
        for e in ("pe", "act", "dve", "pool", "sp"):
            self.wait_all(e)
        self.lastw.clear()
        self.readers.clear()


STOP = None
DEBUG = False
DBG_MAP = {}
DBG_OUT = None
_LAST_S = None


def build():
    import contextlib

    nc = bass.Bass("TRN2", target_bir_lowering=False)

    def din(name, shape, dt=F32):
        return nc.dram_tensor(name, list(shape), dt, kind="ExternalInput").ap()

    xb = din("xb", [T + 16, D])
    cvec = din("cvec", [D])
    flags = din("flags", [128, 65])
    ada_w = din("ada_w", [D, 6 * D])
    ada_b = din("ada_b", [6 * D])
    norm1_g = din("norm1_g", [D])
    xo = din("xo", [TQ + 16, D])
    w_in = din("w_in_sel", [D, 1152])
    mu_sel = din("mu_sel", [128, 5])
    pv_sel = din("pv_sel", [128, 5])
    ln2 = din("ln2", [2, 2, 64])
    pool_w = din("pool_w", [4, 128, 128])
    pool_scale = din("pool_scale", [512])
    w_decay_up = din("Wd_sel", [64, 128])
    w_iclr_up = din("Wa_sel", [64, 128])
    w_gate_up = din("Wg_sel", [128, 128])
    w_out = din("w_out", [D, D])
    norm2_g = din("norm2_g", [D])
    w_gu = din("w_ffn_gu", [D, 2 * DFF])
    w_down = din("w_ffn_down", [DFF, D])
    final_g = din("final_g", [D])
    out = nc.dram_tensor("out", [TQ, D], F32, kind="ExternalOutput").ap()
    ysrc = [nc.dram_tensor(f"ysrc{j}", [128, TQ], BF16) for j in range(4)]
    ydst = nc.dram_tensor("ydst", [4, 512, TQ], BF16)
    x1d = nc.dram_tensor("x1d", [TQ, D], F32)
    dbg = nc.dram_tensor("dbg", [128, 8192], F32, kind="ExternalOutput").ap() if DEBUG else None

    es = contextlib.ExitStack()
    with es:
        S = Sched(nc, es)
        global _LAST_S
        _LAST_S = S
        pid = nc.gpsimd.partition_id()
        q = pid % 4
        _n = [0]

        def sb(shape, dt=F32, stack=es, name=None):
            _n[0] += 1
            return stack.enter_context(nc.sbuf_tensor(name or f"t{_n[0]}", list(shape), dt))

        def ps(shape, dt=F32, name=None):
            _n[0] += 1
            return es.enter_context(nc.psum_tensor(name or f"p{_n[0]}", list(shape), dt))

        def mm(out, lhsT, rhs, r, w, start=True, stop=True):
            return S.op("pe", lambda: nc.tensor.matmul(out, lhsT, rhs, start=start, stop=stop), r, w)

        def tr(out, in_, ident, r, w):
            return S.op("pe", lambda: nc.tensor.transpose(out, in_, ident), r, w)

        def act(out, in_, func, r, w, bias=None, scale=None, accum_out=None):
            kw = {}
            if bias is not None:
                kw["bias"] = bias
            if scale is not None:
                kw["scale"] = scale
            if accum_out is not None:
                kw["accum_out"] = accum_out
            return S.op("act", lambda: nc.scalar.activation(out=out, in_=in_, func=func, **kw), r, w)

        def tt(e, out, in0, in1, op, r, w):
            eng = nc.vector if e == "dve" else nc.gpsimd
            return S.op(e, lambda: eng.tensor_tensor(out=out, in0=in0, in1=in1, op=op), r, w)

        def tsc(e, out, in0, s1, s2, op0, op1, r, w):
            eng = nc.vector if e == "dve" else nc.gpsimd
            if op1 is None:
                return S.op(e, lambda: eng.tensor_scalar(out=out, in0=in0, scalar1=s1, scalar2=None, op0=op0), r, w)
            return S.op(e, lambda: eng.tensor_scalar(out=out, in0=in0, scalar1=s1, scalar2=s2, op0=op0, op1=op1), r, w)

        def stt(out, in0, scalar, in1, op0, op1, r, w):
            return S.op("dve", lambda: nc.vector.scalar_tensor_tensor(out=out, in0=in0, scalar=scalar, in1=in1, op0=op0, op1=op1), r, w)

        def cp(e, out, in_, r, w):
            if e == "act":
                return S.op("act", lambda: nc.scalar.copy(out=out, in_=in_), r, w)
            eng = nc.vector if e == "dve" else nc.gpsimd
            return S.op(e, lambda: eng.tensor_copy(out=out, in_=in_), r, w)

        def mset(e, ap, val, w):
            eng = nc.vector if e == "dve" else nc.gpsimd
            return S.op(e, lambda: eng.memset(ap, val), (), w)

        NW = 3
        PS = [ps([128, 512], name=f"PS{i}") for i in range(8)]
        SB0 = [PS[p] for p in range(NW)]
        SB1 = [PS[NW + p] for p in range(NW)]
        TB = PS[6]
        GBL = [0, 1, 2, 3, 4, 5, 7]
        _gb = [0]

        def gbank():
            i = GBL[_gb[0] % len(GBL)]
            _gb[0] += 1
            return PS[i], f"GB{i}"

        ones_f = sb([128, 128])
        ident_f = sb([128, 128])
        ident_bf = sb([128, 128], BF16)
        flg = sb([128, 65])
        c_fm = sb([128, 8])
        n1g = sb([128, 8])
        n2g = sb([128, 8])
        mu = sb([128, 5])
        pscale = sb([128, 4])
        pv = sb([128, 8])
        fgbc = sb([128, D])
        omm = sb([128, 5])
        stage = [sb([128, 2048]) for _ in range(2)]
        g2bc = sb([128, D])
        modfm = sb([128, 48])
        gsc1 = sb([128, 8])
        gsc2 = sb([128, 8])
        xn = sb([128, D])
        junk = sb([128, D], BF16)
        ss = sb([128, 4])
        dbgt = sb([128, 512]) if DEBUG else None
        pA = contextlib.ExitStack()
        es.enter_context(pA)
        M_TS = sb([128, 2, 128], stack=pA)
        M_A3 = sb([128, 3, 128], stack=pA)
        Blk = sb([128, 128], stack=pA)
        E_bf = sb([128, 64], BF16, stack=pA)
        ones_bf = sb([128, 2], BF16, stack=pA)
        rmask = sb([128, 512], stack=pA)
        lnw_bc = sb([128, 64], stack=pA)
        lnb_bc = sb([128, 64], stack=pA)
        Wd = sb([128, 128], stack=pA)
        Wa = sb([128, 128], stack=pA)
        Wg = sb([128, 128], stack=pA)
        pw_f = sb([128, 4, 128], stack=pA)
        pw_bf = sb([128, 4, 128], BF16, stack=pA)
        g1bc = sb([128, D], stack=pA)
        xt = [sb([128, D], stack=pA) for _ in range(2)]
        mixp = sb([128, 4, TQ], BF16, stack=pA)
        mset("pool", ones_f[:], 1.0, ["ones_f"])

        def asel(out, pattern, cmul, cmp, w):
            return S.op("pool", lambda: nc.gpsimd.affine_select(out=out, in_=ones_f[:], pattern=pattern, compare_op=cmp,
                                                                fill=0.0, base=0, channel_multiplier=cmul), ["ones_f"], w)

        asel(ident_f[:], [[1, 128]], -1, ALU.is_equal, ["ident_f"])
        asel(M_TS[:, 0, :], [[1, 128]], -1, ALU.is_gt, ["M_TS"])
        asel(M_TS[:, 1, :], [[-1, 128]], 1, ALU.is_gt, ["M_TS"])
        asel(M_A3[:, 0, :], [[1, 128]], -1, ALU.is_ge, ["M_A3"])
        asel(M_A3[:, 1, :], [[1, 128]], -1, ALU.is_gt, ["M_A3"])
        asel(M_A3[:, 2, :], [[1, 128]], -1, ALU.is_ge, ["M_A3"])
        cp("pool", ident_bf[:], ident_f[:], ["ident_f"], ["ident_bf"])
        mset("pool", Blk[:], 0.0, ["Blk"])
        mset("pool", Blk[0:64, 0:64], 1.0, ["Blk"])
        mset("pool", Blk[64:128, 64:128], 1.0, ["Blk"])
        tt("pool", E_bf[:], ident_f[:, 0:64], ident_f[:, 64:128], ALU.add, ["ident_f"], ["E_bf"])
        mset("pool", ones_bf[:], 1.0, ["ones_bf"])
        mset("pool", rmask[:], 1.0, ["rmask"])
        mset("pool", rmask[:].rearrange("p (c t) -> p c t", t=64)[:, :, 0:1], 0.0, ["rmask"])

        def pl(dst, src, w, **kw):
            S.dma("pool", "init", dst, src, (), w, **kw)

        def fm(v, k):
            return v.rearrange("(k p) -> p k", p=128)

        pl(flg[:], flags[:, :], ["flg"])
        pl(c_fm[:], fm(cvec, 8), ["c_fm"], allow_slow_non_contiguous=True)
        pl(n1g[:], fm(norm1_g, 8), ["n1g"], allow_slow_non_contiguous=True)
        pl(n2g[:], fm(norm2_g, 8), ["n2g"], allow_slow_non_contiguous=True)
        pl(mu[:], mu_sel[:, :], ["mu"])
        pl(pscale[:], fm(pool_scale, 4), ["pscale"], allow_slow_non_contiguous=True)
        pl(pv[:, 0:5], pv_sel[:, :], ["pv"])
        for h in range(2):
            pl(lnw_bc[h * 64:(h + 1) * 64, :], ln2[0, h:h + 1, :].broadcast_to([64, 64]), ["lnw_bc"])
            pl(lnb_bc[h * 64:(h + 1) * 64, :], ln2[1, h:h + 1, :].broadcast_to([64, 64]), ["lnb_bc"])
        pl(Wd[0:64, :], w_decay_up[:, :], ["Wd"])
        pl(Wa[64:128, :], w_iclr_up[:, :], ["Wa"])
        pl(Wg[:], w_gate_up[:, :], ["Wg"])
        pl(pw_f[:], pool_w.rearrange("g c d -> c g d"), ["pw_f"])
        pl(fgbc[:], final_g.partition_broadcast(128), ["fgbc"])
        S.close("init")
        cp("pool", pw_bf[:], pw_f[:], ["pw_f"], ["pw_bf"])
        tsc("pool", omm[:], mu[:], -1.0, 1.0, ALU.mult, ALU.add, ["mu"], ["omm"])

        if STOP == "setup":
            S.barrier()
            return nc
        _st = [0]
        _ce = [0]

        def load_cast(dst_ap_fn, src_ap_fn, nparts, K, N, w, scale_bc=None, part0=0):
            ncol = max(1, 2048 // K)
            for n0 in range(0, N, ncol):
                n1 = min(N, n0 + ncol)
                i = _st[0] % 2
                _st[0] += 1
                sv = stage[i][part0:part0 + nparts, 0:K * (n1 - n0)].rearrange("p (k n) -> p k n", k=K)
                S.dma(("sp", "act")[i], f"stg{i}", sv, src_ap_fn(n0, n1), (), [f"stage{i}"])
                e = ("act", "dve")[_ce[0] % 2]
                _ce[0] += 1
                if scale_bc is not None:
                    tt("dve" if e == "act" else e, dst_ap_fn(n0, n1), sv, scale_bc(n0, n1), ALU.mult, [f"stage{i}"] + w[1:], [w[0]])
                else:
                    cp(e, dst_ap_fn(n0, n1), sv, [f"stage{i}"], [w[0]])

        _dc = [0]

        def dump(name, ap, n, e="dve"):
            if not DEBUG:
                return
            c0 = _dc[0]
            _dc[0] += n
            DBG_MAP[name] = (c0, n)
            S.wait_all(e)
            npart = ap.shape[0]
            cp(e, dbgt[0:npart, 0:n], ap, [], ["dbgt"])
            S.dma("sp", "dbgs", dbg[0:npart, c0:c0 + n], dbgt[0:npart, 0:n], ["dbgt"], ["dbgd"])

        with contextlib.ExitStack() as p0:
            csil = sb([128, 8, 1], stack=p0)
            crep = sb([128, 8, 128], stack=p0)
            adab = [sb([128, 512], stack=p0) for _ in range(2)]
            mblk = sb([128, 512], stack=p0)
            tmp4 = sb([128, 4, 128], stack=p0)
            act(csil[:, :, 0], c_fm[:], AF.Silu, ["c_fm"], ["csil"])
            cp("dve", crep[:], csil[:].broadcast_to([128, 8, 128]), ["csil"], ["crep"])
            for cb in range(12):
                bank, bk = gbank()
                for k2 in range(4):
                    j = _st[0] % 2
                    _st[0] += 1
                    sv = stage[j][:, 0:1024].rearrange("p (k n) -> p k n", k=2)
                    S.dma(("sp", "act")[j], f"stg{j}", sv,
                          ada_w[k2 * 256:(k2 + 1) * 256, cb * 512:(cb + 1) * 512].rearrange("(k p) n -> p k n", p=128),
                          (), [f"stage{j}"])
                    for kk in range(2):
                        k = k2 * 2 + kk
                        mm(bank[:], crep[:, k, :], sv[:, kk, :], ["crep", f"stage{j}"], [bk], start=(k == 0), stop=(k == 7))
                a = cb % 2
                S.dma("sp", f"adab{a}", adab[a][:], ada_b[cb * 512:(cb + 1) * 512].partition_broadcast(128), (), [f"adab{a}"])
                sec = cb // 2
                if sec == 2:
                    dst, dk = g1bc[:, (cb % 2) * 512:(cb % 2 + 1) * 512], "g1bc"
                elif sec == 5:
                    dst, dk = g2bc[:, (cb % 2) * 512:(cb % 2 + 1) * 512], "g2bc"
                else:
                    dst, dk = mblk[:], "mblk"
                tt("dve", dst, bank[:], adab[a][:], ALU.add, [bk, f"adab{a}"], [dk])
                tt("dve", tmp4[:], dst.rearrange("p (a b) -> p a b", b=128),
                   ident_f[:].rearrange("p (a b) -> p a b", a=1).broadcast_to([128, 4, 128]), ALU.mult, [dk, "ident_f"], ["tmp4"])
                S.op("dve", lambda: nc.vector.tensor_reduce(out=modfm[:, cb * 4:(cb + 1) * 4], in_=tmp4[:], axis=AX.X, op=ALU.add),
                     ["tmp4"], ["modfm"])
            S.barrier()
        if STOP == "ada":
            return nc
        stt(gsc1[:], modfm[:, 8:16], 1.0, n1g[:], ALU.add, ALU.mult, ["modfm", "n1g"], ["gsc1"])
        stt(gsc2[:], modfm[:, 32:40], 1.0, n2g[:], ALU.add, ALU.mult, ["modfm", "n2g"], ["gsc2"])
        dump("modfm", modfm[:], 48)
        sh1 = modfm[:, 0:8]
        sh2 = modfm[:, 24:32]

        _xt = [0]

        def norm_to_fm(x_ap, nrow, gsc, sh, shk, dst_fn, dst_key, dq="sp", src_key=None, keep=None):
            if src_key is None:
                i = _xt[0] % 2
                _xt[0] += 1
                xs, xk = xt[i], f"xt{i}"
                S.dma(dq, f"xld{i}", xs[0:nrow, :], x_ap, (), [xk])
                xa = xs[0:nrow, :]
            else:
                xa, xk = x_ap, src_key
            act(junk[0:nrow, :], xa, AF.Square, [xk], ["junk", "ss"], accum_out=ss[0:nrow, 0:1])
            act(ss[0:nrow, 1:2], ss[0:nrow, 0:1], AF.Sqrt, ["ss"], ["ss1"], bias=RMS_EPS, scale=1.0 / D)
            S.op("dve", lambda: nc.vector.reciprocal(out=ss[0:nrow, 2:3], in_=ss[0:nrow, 1:2]), ["ss1"], ["ss2"])
            act(xn[0:nrow, :], xa, AF.Copy, [xk, "ss2"], ["xn"], scale=ss[0:nrow, 2:3])
            for half in range(2):
                bank, bk = gbank()
                for kk_ in range(4):
                    k = half * 4 + kk_
                    tr(bank[:, kk_ * 128:kk_ * 128 + nrow], xn[0:nrow, k * 128:(k + 1) * 128], ident_f[0:nrow, 0:nrow], ["xn", "ident_f"], [bk])
                for kk_ in range(4):
                    k = half * 4 + kk_
                    tsc("dve", dst_fn(k), bank[:, kk_ * 128:kk_ * 128 + nrow], gsc[:, k:k + 1], sh[:, k:k + 1], ALU.mult, ALU.add,
                        [bk, "gsc1", "gsc2", "modfm"], [dst_key])
            return xa, xk

        def norm_gen(x_ap, nrow, gsc, sh, shk, dst_fn, dst_key, dq="sp", src_key=None, keep=None):
            if src_key is None:
                i = _xt[0] % 2
                _xt[0] += 1
                xs, xk = xt[i], f"xt{i}"
                S.dma(dq, f"xld{i}", xs[0:nrow, :], x_ap, (), [xk])
                xa = xs[0:nrow, :]
            else:
                xa, xk = x_ap, src_key
            act(junk[0:nrow, :], xa, AF.Square, [xk], ["junk", "ss"], accum_out=ss[0:nrow, 0:1])
            yield
            act(ss[0:nrow, 1:2], ss[0:nrow, 0:1], AF.Sqrt, ["ss"], ["ss1"], bias=RMS_EPS, scale=1.0 / D)
            yield
            S.op("dve", lambda: nc.vector.reciprocal(out=ss[0:nrow, 2:3], in_=ss[0:nrow, 1:2]), ["ss1"], ["ss2"])
            yield
            act(xn[0:nrow, :], xa, AF.Copy, [xk, "ss2"], ["xn"], scale=ss[0:nrow, 2:3])
            yield
            for half in range(2):
                bank, bk = gbank()
                for kk_ in range(4):
                    k = half * 4 + kk_
                    tr(bank[:, kk_ * 128:kk_ * 128 + nrow], xn[0:nrow, k * 128:(k + 1) * 128], ident_f[0:nrow, 0:nrow], ["xn", "ident_f"], [bk])
                yield
                for kk_ in range(4):
                    k = half * 4 + kk_
                    tsc("dve", dst_fn(k), bank[:, kk_ * 128:kk_ * 128 + nrow], gsc[:, k:k + 1], sh[:, k:k + 1], ALU.mult, ALU.add,
                        [bk, "gsc1", "gsc2", "modfm"], [dst_key])
                yield

        with contextlib.ExitStack() as p1:
            w_in_bf = sb([128, 8, 9 * 128], BF16, stack=p1)
            w3 = w_in.rearrange("(k p) n -> p k n", p=128)
            load_cast(lambda a, b: w_in_bf[:, :, a:b], lambda a, b: w3[:, :, a:b], 128, 8, 1152, ["w_in_bf"])

            if STOP == "w_in":
                S.barrier()
                return nc
            hT = sb([128, 8, 528], BF16, stack=p1)
            dn = [sb([128, 528], stack=p1) for _ in range(12)]
            zT = [sb([128, 5, 513], stack=p1) for _ in range(2)]
            gT = [sb([128, 2, 512], stack=p1) for _ in range(2)]
            yTb = [sb([128, 2, 512], BF16, stack=p1) for _ in range(2)]
            gamC = [sb([128, 8], stack=p1) for _ in range(2)]
            Hs = [sb([128, 64], stack=p1) for _ in range(2)]
            AR_bd = [sb([128, 8, 256], BF16, stack=p1) for _ in range(2)]
            B_bd = [sb([128, 8, 128], BF16, stack=p1) for _ in range(2)]
            K_bd = [sb([128, 8, 128], BF16, stack=p1) for _ in range(2)]
            BH_bd = [sb([128, 8, 128], BF16, stack=p1) for _ in range(2)]
            KH_bd = [sb([128, 8, 128], BF16, stack=p1) for _ in range(2)]
            V_bd = [sb([128, 8, 128], BF16, stack=p1) for _ in range(2)]
            RRK_bd = [sb([128, 8, 128], BF16, stack=p1) for _ in range(2)]
            Xi = [sb([128, 3, 128], stack=p1) for _ in range(NW)]
            Mb = [sb([128, 128], BF16, stack=p1) for _ in range(NW)]
            A3 = [sb([128, 3, 128], BF16, stack=p1) for _ in range(NW)]
            R3 = [sb([128, 192], BF16, stack=p1) for _ in range(NW)]
            BK = [sb([128, 2, 128], BF16, stack=p1) for _ in range(NW)]
            Vst = [sb([128, 64], BF16, stack=p1) for _ in range(NW)]
            WU = [sb([128, 192], BF16, stack=p1) for _ in range(NW)]
            PT = [sb([128, 128], stack=p1) for _ in range(NW)]
            RH = [sb([128, 128], stack=p1) for _ in range(NW)]
            sm = [sb([128, 16], stack=p1) for _ in range(NW)]
            yfin = [sb([128, 64], stack=p1) for _ in range(NW)]
            yh = [sb([128, 64], stack=p1) for _ in range(NW)]
            for p in range(2):
                for tl, nm in ((AR_bd, "AR"), (B_bd, "B"), (K_bd, "K"), (BH_bd, "BH"), (KH_bd, "KH"), (V_bd, "V"), (RRK_bd, "RRK")):
                    mset("pool", tl[p][:], 0.0, [f"{nm}{p}"])
                mset("pool", Hs[p][:], 0.0, [f"H{p}"])

            uT = dn[0]
            for ob in range(4):
                norm_to_fm(xo[ob * 512:ob * 512 + 16, :], 16, gsc1, sh1, "sh1",
                           lambda k: hT[:, k, 0:16], "hT")
                if STOP == "norm16":
                    S.barrier()
                    return nc
                for ti in range(4):
                    norm_to_fm(xo[16 + ob * 512 + ti * 128:16 + ob * 512 + (ti + 1) * 128, :], 128, gsc1, sh1, "sh1",
                               lambda k, ti=ti: hT[:, k, 16 + ti * 128:16 + (ti + 1) * 128], "hT")
                if STOP == "norm":
                    S.barrier()
                    return nc
                if ob == 0:
                    for k_ in range(8):
                        dump(f"hT{k_}", hT[:, k_, 0:144], 144)
                for g in range(4):
                    if STOP == "g1" and g == 1:
                        S.barrier()
                        return nc
                    w = (2, 4, 8, 16)[g]
                    bank, bk = gbank()
                    for k in range(8):
                        mm(bank[:], w_in_bf[:, k, g * 128:(g + 1) * 128], hT[:, k, 16:528], ["w_in_bf", "hT"], [bk], start=(k == 0), stop=(k == 7))
                    cp("act", uT[:, 16:528], bank[:], [bk], ["dn0"])
                    bank2, bk2 = gbank()
                    for k in range(8):
                        mm(bank2[:, 0:16], w_in_bf[:, k, g * 128:(g + 1) * 128], hT[:, k, 0:16], ["w_in_bf", "hT"], [bk2], start=(k == 0), stop=(k == 7))
                    if ob == 0:
                        tsc("dve", uT[:, 0:16], bank2[:, 0:16], flg[:, 0:1], None, ALU.mult, None, [bk2, "flg"], ["dn0"])
                    else:
                        cp("dve", uT[:, 0:16], bank2[:, 0:16], [bk2], ["dn0"])
                    src, sk = uT, "dn0"
                    sh_ = 1
                    lvl = 0
                    while sh_ < w:
                        dst, dk = dn[1 + lvl % 2], f"dn{1 + lvl % 2}"
                        tt("pool", dst[:, sh_:528], src[:, sh_:528], src[:, 0:528 - sh_], ALU.add, [sk], [dk])
                        src, sk = dst, dk
                        sh_ *= 2
                        lvl += 1
                    dd = dn[3]
                    stt(dd[:, 16:528], src[:, 16:528], 1.0 / w, uT[:, 16:528], ALU.mult, ALU.subtract, [sk, "dn0"], ["dn3"])
                    if ob == 0:
                        tt("dve", dn[4][:, 0:16], src[:, 16:32], flg[:, 1 + g * 16:1 + (g + 1) * 16], ALU.mult, [sk, "flg"], ["dn4"])
                        tt("dve", dd[:, 16:32], dn[4][:, 0:16], uT[:, 16:32], ALU.subtract, ["dn4", "dn0", "dn3"], ["dn3"])
                    dbfT = dn[5][:, 0:256].bitcast(BF16)
                    cp("act", dbfT, dd[:, 16:528], ["dn3"], ["dn5"])
                    bank3, bk3 = gbank()
                    mm(bank3[:], pw_bf[:, g, :], dbfT, ["pw_bf", "dn5"], [bk3])
                    tsc("dve", mixp[:, g, ob * 512:(ob + 1) * 512], bank3[:], pscale[:, g:g + 1], None, ALU.mult, None, [bk3, "pscale"], ["mixp"])

            for g_ in range(4):
                dump(f"mixp{g_}", mixp[:, g_, 0:128], 128)
            if STOP == "pool":
                S.barrier()
                return nc
            S.barrier()
            GBL[:] = [7]
            ysv = [ysrc[j].ap().rearrange("(h i) t -> i h t", i=64) for j in range(4)]
            S.stream("cc", 1)

            ymark = {}

            def exchange(j):
                for st_, v_ in ymark[j]:
                    S._wait("pool", st_, v_)
                nc.gpsimd.collective_compute("AllGather", ALU.bypass, replica_groups=[[0, 1, 2, 3], [4, 5, 6, 7]],
                                             ins=[ysrc[j].ap().opt()], outs=[ydst.ap()[j].opt()]).then_inc(S.sem["cc"], 1)
                S.cnt["cc"] += 1
            state = {"pk": 0}

            def prep(blk):
                bp = blk % 2
                zt, zk = zT[bp], f"zT{bp}"
                for ti in range(4):
                    yield from norm_gen(xb[16 + blk * 512 + ti * 128:16 + blk * 512 + (ti + 1) * 128, :], 128, gsc1, sh1, "sh1",
                                        lambda k, ti=ti: hT[:, k, 16 + ti * 128:16 + (ti + 1) * 128], "hT")
                if blk == 0:
                    mset("pool", zt[:, :, 0:1], 0.0, [zk])
                else:
                    cp("pool", zt[:, :, 0:1], zT[1 - bp][:, :, 512:513], [f"zT{1 - bp}"], [zk])
                    yield
                for m in range(5):
                    bank, bk = gbank()
                    for k in range(8):
                        mm(bank[:], w_in_bf[:, k, 512 + m * 128:512 + (m + 1) * 128], hT[:, k, 16:528], ["w_in_bf", "hT"], [bk], start=(k == 0), stop=(k == 7))
                    yield
                    cp("act", zt[:, m, 1:513], bank[:], [bk], [zk])
                    yield
                zs = []
                for m in range(5):
                    tmp = dn[11]
                    act(tmp[:, 0:512], zt[:, m, 0:512], AF.Copy, [zk, "mu"], ["dn11"], scale=mu[:, m:m + 1])
                    yield
                    dst = dn[m]
                    stt(dst[:, 0:512], zt[:, m, 1:513], omm[:, m:m + 1], tmp[:, 0:512], ALU.mult, ALU.add, [zk, "omm", "dn11"], [f"dn{m}"])
                    yield
                    zs.append(dst)
                rT, kT, vT, xwa, xg = [z[:, 0:512] for z in zs]
                thx = dn[5]
                act(thx[0:64, 0:512], xwa[0:64, :], AF.Tanh, ["dn3"], ["dn5"])
                yield
                bank, bk = gbank()
                mm(bank[:], Wd[0:64, :], thx[0:64, 0:512], ["Wd", "dn5"], [bk])
                yield
                sg = dn[6]
                act(sg[:, 0:512], bank[:], AF.Sigmoid, [bk, "pv"], ["dn6"], bias=pv[:, 0:1])
                yield
                bank, bk = gbank()
                mm(bank[:], Wa[64:128, :], xwa[64:128, :], ["Wa", "dn3"], [bk])
                yield
                aT = dn[7]
                act(aT[:, 0:512], bank[:], AF.Sigmoid, [bk, "pv"], ["dn7"], bias=pv[:, 1:2])
                yield
                sgx = dn[5]
                act(sgx[:, 0:512], xg, AF.Sigmoid, ["dn4"], ["dn5"])
                yield
                for h in range(2):
                    bank, bk = gbank()
                    mm(bank[0:64, :], Wg[:, h * 64:(h + 1) * 64], sgx[:, 0:512], ["Wg", "dn5"], [bk])
                    yield
                    cp("act", gT[bp][0:64, h, :], bank[0:64, :], [bk], [f"gT{bp}"])
                    yield
                cs = dn[8]
                S.op("dve", lambda: nc.vector.tensor_tensor_scan(out=cs[:, 0:512], data0=rmask[:], data1=sg[:, 0:512], initial=0.0,
                                                                 op0=ALU.mult, op1=ALU.add), ["rmask", "dn6"], ["dn8"])
                yield
                epos = dn[9]
                act(epos[:, 0:512], cs[:, 0:512], AF.Exp, ["dn8"], ["dn9"], scale=-DEC)
                yield
                cp("pool", gamC[bp][:], epos[:, 0:512].rearrange("p (c t) -> p c t", t=64)[:, :, 63], ["dn9"], [f"gamC{bp}"])
                yield
                eneg = dn[10]
                act(eneg[:, 0:512], cs[:, 0:512], AF.Exp, ["dn8"], ["dn10"], scale=DEC)
                yield
                tt("dve", cs[:, 0:512], cs[:, 0:512], sg[:, 0:512], ALU.subtract, ["dn8", "dn6"], ["dn8"])
                yield
                eprev = dn[6]
                act(eprev[:, 0:512], cs[:, 0:512], AF.Exp, ["dn8"], ["dn6"], scale=-DEC)
                yield
                kkr = dn[3]
                tsc("pool", kkr[:, 0:512], kT, pv[:, 2:3], None, ALU.mult, None, ["dn1", "pv"], ["dn3"])
                yield
                sq = dn[4]
                tt("pool", sq[:, 0:512], kkr[:, 0:512], kkr[:, 0:512], ALU.mult, ["dn3"], ["dn4"])
                yield
                bank, bk = gbank()
                mm(bank[:], Blk[:], sq[:, 0:512], ["Blk", "dn4"], [bk])
                yield
                act(sq[:, 0:512], bank[:], AF.Sqrt, [bk], ["dn4"])
                yield
                tsc("dve", sq[:, 0:512], sq[:, 0:512], 1e-12, None, ALU.max, None, ["dn4"], ["dn4"])
                yield
                S.op("dve", lambda: nc.vector.reciprocal(out=sq[:, 0:512], in_=sq[:, 0:512]), ["dn4"], ["dn4"])
                yield
                kk = dn[3]
                tt("dve", kk[:, 0:512], kkr[:, 0:512], sq[:, 0:512], ALU.mult, ["dn3", "dn4"], ["dn3"])
                yield
                t1 = dn[4]
                tsc("dve", t1[:, 0:512], aT[:, 0:512], -1.0, pv[:, 3:4], ALU.add, ALU.mult, ["dn7", "pv"], ["dn4"])
                yield
                kp = dn[8]
                stt(kp[:, 0:512], t1[:, 0:512], 1.0, kT, ALU.add, ALU.mult, ["dn4", "dn1"], ["dn8"])
                yield
                tt("pool", aT[:, 0:512], aT[:, 0:512], kk[:, 0:512], ALU.mult, ["dn7", "dn3"], ["dn7"])
                yield
                stt(t1[:, 0:512], rT, pv[:, 4:5], kp[:, 0:512], ALU.mult, ALU.mult, ["dn0", "pv", "dn8"], ["dn4"])
                yield
                stt(kk[:, 0:512], kk[:, 0:512], -1.0, eprev[:, 0:512], ALU.mult, ALU.mult, ["dn3", "dn6"], ["dn3"])
                yield
                tt("pool", aT[:, 0:512], aT[:, 0:512], eneg[:, 0:512], ALU.mult, ["dn7", "dn10"], ["dn7"])
                yield
                tt("dve", kp[:, 0:512], kp[:, 0:512], eneg[:, 0:512], ALU.mult, ["dn8", "dn10"], ["dn8"])
                yield

                if blk == 0:
                    for nm_, t__ in (("rT", rT), ("vT", vT), ("atil", kk[:, 0:512]), ("btil", aT[:, 0:512]), ("ktil", kp[:, 0:512]),
                                     ("epos", epos[:, 0:512]), ("rrk", t1[:, 0:512])):
                        dump(nm_, t__[:, 0:128], 128)
                    for h_ in range(2):
                        dump(f"gT{h_}", gT[bp][0:64, h_, 0:64], 64)

                def c3(t_):
                    return t_.rearrange("p (c t) -> p c t", t=64)

                gam3 = gamC[bp][:].rearrange("p (c o) -> p c o", o=1)
                for h in range(2):
                    hs = slice(h * 64, (h + 1) * 64)
                    cs_ = slice(h * 64, (h + 1) * 64)
                    e1 = "dve" if h == 0 else "pool"
                    cp(e1, AR_bd[bp][hs, :, cs_], c3(kk[hs, 0:512]), ["dn3"], [f"AR{bp}"])
                    yield
                    tt(e1, AR_bd[bp][hs, :, 128 + h * 64:128 + (h + 1) * 64], c3(rT[hs, :]), c3(epos[hs, 0:512]), ALU.mult, ["dn0", "dn9"], [f"AR{bp}"])
                    yield
                    cp(e1, B_bd[bp][hs, :, cs_], c3(aT[hs, 0:512]), ["dn7"], [f"B{bp}"])
                    yield
                    cp(e1, K_bd[bp][hs, :, cs_], c3(kp[hs, 0:512]), ["dn8"], [f"K{bp}"])
                    yield
                    tt(e1, BH_bd[bp][hs, :, cs_], c3(aT[hs, 0:512]), gam3[hs].broadcast_to([64, 8, 64]), ALU.mult, ["dn7", f"gamC{bp}"], [f"BH{bp}"])
                    yield
                    tt(e1, KH_bd[bp][hs, :, cs_], c3(kp[hs, 0:512]), gam3[hs].broadcast_to([64, 8, 64]), ALU.mult, ["dn8", f"gamC{bp}"], [f"KH{bp}"])
                    yield
                    cp(e1, V_bd[bp][hs, :, cs_], c3(vT[hs, :]), ["dn2"], [f"V{bp}"])
                    yield
                    cp(e1, RRK_bd[bp][hs, :, cs_], c3(t1[hs, 0:512]), ["dn4"], [f"RRK{bp}"])
                    yield

            def pack(blk, c):
                bp = blk % 2
                g = state["pk"]
                state["pk"] += 1
                import os
                pp = (g + int(os.environ.get("PPX", "0"))) % NW
                hc, hn = g % 2, (g + 1) % 2
                s0, s1 = SB0[pp], SB1[pp]
                I0, I1, I2, KAV, KVS = [f"s0_{pp}_{i}" for i in range(5)]
                k1 = [f"s1_{pp}_{i}" for i in range(5)]
                tb = [("TB", j) for j in range(3)]
                X, a3, r3, bk_, vs, wu, pt, rh, smm = Xi[pp], A3[pp], R3[pp], BK[pp], Vst[pp], WU[pp], PT[pp], RH[pp], sm[pp]
                kX, kA3, kR3, kBK, kV, kWU, kPT, kRH = f"X{pp}", f"A3{pp}", f"R3{pp}", f"BK{pp}", f"Vst{pp}", f"WU{pp}", f"PT{pp}", f"RH{pp}"
                ar, bb, kb, bh, kh, vb, rrk = AR_bd[bp], B_bd[bp], K_bd[bp], BH_bd[bp], KH_bd[bp], V_bd[bp], RRK_bd[bp]
                s0v = s0[:, 0:384].rearrange("p (a b) -> p a b", b=128)
                mm(s0[:, 0:128], bb[:, c, :], ar[:, c, 0:128], [f"B{bp}", f"AR{bp}"], [I0])
                mm(s0[:, 256:384], ar[:, c, 0:128], bb[:, c, :], [f"B{bp}", f"AR{bp}"], [I2])
                mm(s1[:, 0:128], bb[:, c, :], ar[:, c, 128:256], [f"B{bp}", f"AR{bp}"], [k1[0]])
                mm(s1[:, 128:384], kb[:, c, :], ar[:, c, :], [f"K{bp}", f"AR{bp}"], [k1[1], k1[2], k1[3]])
                o = 0
                mm(TB[:, o:o + 128], ar[:, c, 0:128], ident_bf[:], [f"AR{bp}", "ident_bf"], [tb[0]])
                mm(TB[:, o + 128:o + 256], bh[:, c, :], ident_bf[:], [f"BH{bp}", "ident_bf"], [tb[1]])
                mm(TB[:, o + 256:o + 384], kh[:, c, :], ident_bf[:], [f"KH{bp}", "ident_bf"], [tb[2]])
                mm(s0[:, 448:512], vb[:, c, :], E_bf[:], [f"V{bp}", "E_bf"], [KVS])
                import os
                if os.environ.get("RKTB", "0") == "1":
                    mm(TB[:, 384:386], rrk[:, c, :], ones_bf[:], [f"RRK{bp}", "ones_bf"], [k1[4]])
                else:
                    mm(s1[:, 384:386], rrk[:, c, :], ones_bf[:], [f"RRK{bp}", "ones_bf"], [k1[4]])
                yield
                import os
                OPS = os.environ.get("OPS", "abcdefg")
                if "a" in OPS:
                    tt("dve", X[:, 0:3:2, :], s0v[:, 0:3:2, :], M_TS[:], ALU.mult, [I0, I2, "M_TS"], [kX])
                if "b" in OPS:
                    tt("dve", a3[:], s1[:, 0:384].rearrange("p (a b) -> p a b", b=128), M_A3[:], ALU.mult, [k1[0], k1[1], k1[2], k1[3], "M_A3"], [kA3])
                if "c" in OPS:
                    tt("pool", X[:, 1, :], X[:, 0, :], ident_f[:], ALU.add, [kX, "ident_f"], [kX])
                if "d" in OPS:
                    cp("act", r3[:, 0:128], TB[:, o:o + 128], [tb[0]], [kR3])
                if "e" in OPS:
                    cp("act", bk_[:], TB[:, o + 128:o + 384].rearrange("p (a b) -> p a b", b=128), [tb[1], tb[2]], [kBK])
                if "f" in OPS:
                    cp("act", vs[:], s0[:, 448:512], [KVS], [kV])
                if "g" in OPS:
                    cp("act", smm[:, 0:1], s1[:, 384:385], [k1[4]], [f"sm{pp}rk"])
                yield
                mm(s0[:, 0:128], X[:, 2, :], X[:, 0, :], [kX], [I0])
                mm(s0[:, 256:384], X[:, 0, :], X[:, 2, :], [kX], [I2])
                yield
                cp("act", X[:, 0:3:2, :], s0v[:, 0:3:2, :], [I0, I2], [kX])
                yield
                for lv in range(1, 5):
                    mm(s0[:, 0:256], X[:, 2, :], X[:, 0:2, :].rearrange("p a b -> p (a b)"), [kX], [I0, I1])
                    mm(s0[:, 256:384], X[:, 0, :], X[:, 2, :], [kX], [I2])
                    yield
                    tt("dve", X[:, 1, :], s0[:, 128:256], X[:, 1, :], ALU.add, [I1, kX], [kX])
                    cp("act", X[:, 0:3:2, :], s0v[:, 0:3:2, :], [I0, I2], [kX])
                    yield
                mm(s0[:, 128:256], X[:, 2, :], X[:, 1, :], [kX], [I1])
                mm(s0[:, 384:448], a3[:, 1, :], vs[:], [kA3, kV], [KAV])
                yield
                tt("dve", Mb[pp][:], s0[:, 128:256], X[:, 1, :], ALU.add, [I1, kX], [f"Mb{pp}"])
                cp("act", r3[:, 128:192], s0[:, 384:448], [KAV], [kR3])
                yield
                mm(s0[:, 0:192], Mb[pp][:], r3[:], [f"Mb{pp}", kR3], [I0, I1])
                yield
                cp("act", wu[:], s0[:, 0:192], [I0, I1], [kWU])
                yield
                mm(s0[:, 192:320], wu[:, 0:128], bk_[:, 0, :], [kWU, kBK], [I1, I2])
                mm(s1[:, 0:128], wu[:, 0:128], a3[:, 0, :], [kWU, kA3], [k1[0]])
                yield
                cp("act", pt[:], s0[:, 192:320], [I1, I2], [kPT])
                tt("dve", rh[:], s1[:, 0:128], ar[:, c, 128:256], ALU.add, [k1[0], f"AR{bp}"], [kRH])
                yield
                mm(s1[:, 256:320], a3[:, 0, :], wu[:, 128:192], [kA3, kWU], [k1[2]], start=True, stop=False)
                mm(s1[:, 256:320], a3[:, 2, :], vs[:], [kA3, kV], [k1[2]], start=False, stop=False)
                mm(s1[:, 256:320], rh[:], Hs[hc][:], [kRH, f"H{hc}"], [k1[2]], start=False, stop=True)
                mm(s1[:, 320:384], bk_[:, 0, :], wu[:, 128:192], [kBK, kWU], [k1[3]], start=True, stop=False)
                mm(s1[:, 320:384], bk_[:, 1, :], vs[:], [kBK, kV], [k1[3]], start=False, stop=False)
                mm(s1[:, 320:384], pt[:], Hs[hc][:], [kPT, f"H{hc}"], [k1[3]], start=False, stop=True)
                yield
                stt(Hs[hn][:], Hs[hc][:], gamC[bp][:, c:c + 1], s1[:, 320:384], ALU.mult, ALU.add, [f"H{hc}", f"gamC{bp}", k1[3]], [f"H{hn}"])
                S.op("dve", lambda: nc.vector.bn_stats(out=smm[:, 2:8], in_=s1[:, 256:320]), [k1[2]], [f"sm{pp}st"])
                S.op("dve", lambda: nc.vector.bn_aggr(out=smm[:, 8:10], in_=smm[:, 2:8]), [f"sm{pp}st"], [f"sm{pp}mv"])
                act(smm[:, 10:11], smm[:, 9:10], AF.Sqrt, [f"sm{pp}mv"], [f"sm{pp}sd"], bias=GN_EPS, scale=1.0)
                S.op("dve", lambda: nc.vector.reciprocal(out=smm[:, 11:12], in_=smm[:, 10:11]), [f"sm{pp}sd"], [f"sm{pp}rs"])
                tsc("dve", yh[pp][:], s1[:, 256:320], smm[:, 8:9], smm[:, 11:12], ALU.subtract, ALU.mult, [k1[2], f"sm{pp}mv", f"sm{pp}rs"], [f"yh{pp}"])
                tt("pool", yh[pp][:], yh[pp][:], lnw_bc[:], ALU.mult, [f"yh{pp}", "lnw_bc"], [f"yh{pp}"])
                tt("pool", yh[pp][:], yh[pp][:], lnb_bc[:], ALU.add, [f"yh{pp}", "lnb_bc"], [f"yh{pp}"])
                stt(yfin[pp][:], vs[:], smm[:, 0:1], yh[pp][:], ALU.mult, ALU.add, [kV, f"sm{pp}rk", f"yh{pp}"], [f"yfin{pp}"])
                yield
                tr(s1[0:64, 128:256], yfin[pp][:], ident_f[:], [f"yfin{pp}", "ident_f"], [k1[1]])
                yield
                tt("dve", yTb[bp][0:64, :, c * 64:(c + 1) * 64], s1[0:64, 128:256].rearrange("p (h t) -> p h t", t=64),
                   gT[bp][0:64, :, c * 64:(c + 1) * 64], ALU.mult, [k1[1], f"gT{bp}"], [f"yTb{bp}"])

            def run_packs(blk, extra=None):
                import os
                gens = [pack(blk, c) for c in range(int(os.environ.get("PKN", "8")))]
                maxs = int(STOP[2:]) if (STOP or "").startswith("pk") else 10 ** 9
                adv = {}
                active = []
                if extra is not None and maxs > 10 ** 8:
                    active.append(extra)
                nxt = 0
                stepc = 0
                NG = len(gens)
                while nxt < NG or active:
                    npk = len([a_ for a_ in active if a_ is not extra])
                    if nxt < NG and npk < NW and (npk == 0 or stepc % 5 == 0):
                        active.append(gens[nxt])
                        nxt += 1
                    for gkk in list(active):
                        try:
                            adv[id(gkk)] = adv.get(id(gkk), 0) + 1
                            if adv[id(gkk)] > maxs:
                                raise StopIteration
                            next(gkk)
                            if gkk is extra:
                                next(gkk)
                        except StopIteration:
                            active.remove(gkk)
                    stepc += 1

            for _ in prep(0):
                pass
            for blk in range(NBLK):
                if STOP == "prep":
                    S.barrier()
                    return nc
                run_packs(blk, prep(blk + 1) if blk + 1 < NBLK else None)
                if STOP == "blk1" or (STOP or "").startswith("pk"):
                    S.barrier()
                    return nc
                bp = blk % 2
                if blk == 0:
                    for h_ in range(2):
                        dump(f"yT{h_}", yTb[bp][0:64, h_, 0:128], 128)
                    dump("H1", Hs[0][:], 64)
                S.dma("sp", f"yst{bp}", ysv[blk // 4][:, :, (blk % 4) * 512:(blk % 4 + 1) * 512], yTb[bp][0:64, :, :], [f"yTb{bp}"], ["ysrc"])
                if blk % 4 == 3:
                    ymark[blk // 4] = [(st_, S.cnt[st_]) for st_ in ("yst0", "yst1")]
                if blk % 4 == 0 and blk > 0:
                    exchange(blk // 4 - 1)
            S.barrier()
            exchange(3)
            GBL[:] = [0, 1, 2, 3, 4, 5, 7]

        if STOP == "p1":
            return nc
        S.barrier()

        if STOP == "cc":
            return nc
        ydv = ydst.ap().rearrange("j (h i) t -> i j h t", i=64)
        with contextlib.ExitStack() as p2:
            wo_p = sb([128, 4, D], BF16, stack=p2)
            wo_r = sb([128, 8, D], BF16, stack=p2)
            yall = [sb([128, 8, 512], BF16, stack=p2) for _ in range(2)]
            x1t = [sb([128, D], stack=p2) for _ in range(2)]
            g1b3 = g1bc[:].rearrange("p (k n) -> p k n", k=1)
            load_cast(lambda a, b: wo_p[:, :, a:b], lambda a, b: w_out[0:512, a:b].rearrange("(k p) n -> p k n", p=128), 128, 4, D,
                      ["wo_p", "g1bc"], scale_bc=lambda a, b: g1b3[:, :, a:b].broadcast_to([128, 4, b - a]))
            load_cast(lambda a, b: wo_r[0:64, :, a:b], lambda a, b: w_out[512:1024, a:b].rearrange("(h i) n -> i h n", i=64), 64, 8, D,
                      ["wo_r", "g1bc"], scale_bc=lambda a, b: g1b3[0:64, :, a:b].broadcast_to([64, 8, b - a]))
            for ob in range(4):
                ya = yall[ob % 2]
                S.dma("pool", f"yld{ob % 2}", ya[0:64, :, :], ydv[:, bass.ds(q, 1), :, ob * 512:(ob + 1) * 512].rearrange("i j h t -> i (j h) t"),
                      ["wo_r", "wo_p"], [f"yall{ob % 2}"])
                for ti in range(4):
                    i = _xt[0] % 2
                    _xt[0] += 1
                    S.dma("sp", f"xld{i}", xt[i][:], xo[16 + ob * 512 + ti * 128:16 + ob * 512 + (ti + 1) * 128, :], (), [f"xt{i}"])
                    j = (ob * 4 + ti) % 2
                    for hf in range(2):
                        bank, bk = gbank()
                        for m in range(4):
                            mm(bank[:], mixp[:, m, ob * 512 + ti * 128:ob * 512 + (ti + 1) * 128], wo_p[:, m, hf * 512:(hf + 1) * 512],
                               ["mixp", "wo_p"], [bk], start=(m == 0), stop=False)
                        for h in range(8):
                            mm(bank[:], ya[0:64, h, ti * 128:(ti + 1) * 128], wo_r[0:64, h, hf * 512:(hf + 1) * 512],
                               [f"yall{ob % 2}", "wo_r"], [bk], start=False, stop=(h == 7))
                        tt("dve", x1t[j][:, hf * 512:(hf + 1) * 512], bank[:], xt[i][:, hf * 512:(hf + 1) * 512], ALU.add, [bk, f"xt{i}"], [f"x1t{j}"])
                    r0 = ob * 512 + ti * 128
                    if ob == 0 and ti == 0:
                        dump("x1", x1t[j][:, 0:256], 256)
                        for h_ in range(8):
                            dump(f"yall{h_}", ya[0:64, h_, 0:32], 32)
                    S.dma("sp", f"x1st{j}", x1d[r0:r0 + 128, :], x1t[j][:], [f"x1t{j}"], ["x1d"])
            S.barrier()
        pA.close()
        if STOP == "p2a":
            return nc

        with contextlib.ExitStack() as p3:
            wgu = sb([128, 8, 2 * DFF], BF16, stack=p3)
            wdn = sb([128, NFF, D], BF16, stack=p3)
            x1b = sb([128, 2, D], stack=p3)
            h2T = sb([128, 8, 256], BF16, stack=p3)
            actT = sb([128, NFF, 256], BF16, stack=p3)
            sgt = [sb([128, 256], stack=p3) for _ in range(2)]
            ot = sb([128, D], stack=p3)
            g2b3 = g2bc[:].rearrange("p (k n) -> p k n", k=1)
            for kh in range(2):
                load_cast(lambda a, b, kh=kh: wgu[:, kh * 4:(kh + 1) * 4, a:b],
                          lambda a, b, kh=kh: w_gu[kh * 512:(kh + 1) * 512, a:b].rearrange("(k p) n -> p k n", p=128), 128, 4, 2 * DFF, ["wgu"])
            wd3 = w_down.rearrange("(f p) n -> p f n", p=128)
            for f0 in range(0, NFF, 2):
                load_cast(lambda a, b, f0=f0: wdn[:, f0:f0 + 2, a:b], lambda a, b, f0=f0: wd3[:, f0:f0 + 2, a:b], 128, 2, D,
                          ["wdn", "g2bc"], scale_bc=lambda a, b: g2b3[:, :, a:b].broadcast_to([128, 2, b - a]))
            for ob in range(8):
                for ti in range(2):
                    r0 = ob * 256 + ti * 128
                    S.dma("sp", f"x1ld{ti}", x1b[:, ti, :], x1d[r0:r0 + 128, :], ["x1d"], [("x1b", ti)])
                    norm_to_fm(x1b[:, ti, :], 128, gsc2, sh2, "sh2", lambda k, ti=ti: h2T[:, k, ti * 128:(ti + 1) * 128], "h2T", src_key=("x1b", ti))
                for f in range(NFF):
                    bg, kg = gbank()
                    for k in range(8):
                        mm(bg[:, 0:256], wgu[:, k, f * 128:(f + 1) * 128], h2T[:, k, :], ["wgu", "h2T"], [kg], start=(k == 0), stop=(k == 7))
                    bu, ku = gbank()
                    for k in range(8):
                        mm(bu[:, 0:256], wgu[:, k, DFF + f * 128:DFF + (f + 1) * 128], h2T[:, k, :], ["wgu", "h2T"], [ku], start=(k == 0), stop=(k == 7))
                    sj = f % 2
                    act(sgt[sj][:], bg[:, 0:256], AF.Silu, [kg], [f"sgt{sj}"])
                    tt("dve", actT[:, f, :], bu[:, 0:256], sgt[sj][:], ALU.mult, [ku, f"sgt{sj}"], ["actT"])
                for ti in range(2):
                    for hf in range(2):
                        bank, bk = gbank()
                        for f in range(NFF):
                            mm(bank[:], actT[:, f, ti * 128:(ti + 1) * 128], wdn[:, f, hf * 512:(hf + 1) * 512], ["actT", "wdn"], [bk],
                               start=(f == 0), stop=(f == NFF - 1))
                        tt("dve", x1b[:, ti, hf * 512:(hf + 1) * 512], bank[:], x1b[:, ti, hf * 512:(hf + 1) * 512], ALU.add,
                           [bk, ("x1b", ti)], [("x1b", ti)])
                    xa = x1b[:, ti, :]
                    act(junk[:], xa, AF.Square, [("x1b", ti)], ["junk", "ss"], accum_out=ss[:, 0:1])
                    act(ss[:, 1:2], ss[:, 0:1], AF.Sqrt, ["ss"], ["ss1"], bias=RMS_EPS, scale=1.0 / D)
                    S.op("dve", lambda: nc.vector.reciprocal(out=ss[:, 2:3], in_=ss[:, 1:2]), ["ss1"], ["ss2"])
                    stt(ot[:], xa, ss[:, 2:3], fgbc[:], ALU.mult, ALU.mult, [("x1b", ti), "ss2", "fgbc"], ["ot"])
                    r0 = ob * 256 + ti * 128
                    S.dma("sp", "ost", out[r0:r0 + 128, :], ot[:], ["ot"], ["outd"])
            S.barrier()
    return nc


_NC = None


def kernel(**inputs):
    global _NC
    x = np.asarray(inputs["x"], np.float32)
    c = np.asarray(inputs["c"], np.float32)
    if _NC is None:
        _NC = build()
    g = lambda n: np.asarray(inputs[n], np.float32)[0]
    names = ["ada_w", "ada_b", "norm1_g", "pool_w", "pool_scale", "w_out", "norm2_g", "w_ffn_gu", "w_ffn_down"]
    shared = {n: np.ascontiguousarray(g(n)) for n in names}
    shared["final_g"] = np.ascontiguousarray(np.asarray(inputs["final_g"], np.float32))
    w_in, mu_shift = g("w_in"), g("mu_shift")
    in_maps = []
    wins = (2, 4, 8, 16)
    for core in range(8):
        b, qq = core // 4, core % 4
        xpad = np.zeros((T + 16, D), np.float32)
        xpad[16:] = x[b]
        fl = np.zeros((128, 65), np.float32)
        fl[:, 0] = 0.0 if qq == 0 else 1.0
        for gi, w in enumerate(wins):
            for t in range(16):
                fl[:, 1 + gi * 16 + t] = 1.0 / min(qq * TQ + t + 1, w)
        ps_ = slice(qq * 128, (qq + 1) * 128)
        cols = np.concatenate([np.arange(0, 512)] + [np.arange(512 + i * 512 + qq * 128, 512 + i * 512 + (qq + 1) * 128) for i in range(3)]
                              + [np.arange(2048, 2304)])
        mcols = np.stack([mu_shift[i * 512 + qq * 128:i * 512 + (qq + 1) * 128] for i in range(3)]
                         + [mu_shift[1536:1664], mu_shift[1664:1792]], axis=1)
        m = dict(shared)
        m["xb"] = xpad
        m["xo"] = np.ascontiguousarray(xpad[qq * TQ:qq * TQ + TQ + 16])
        m["cvec"] = np.ascontiguousarray(c[b])
        m["flags"] = fl
        m["w_in_sel"] = np.ascontiguousarray(w_in[:, cols])
        m["mu_sel"] = np.ascontiguousarray(mcols)
        m["pv_sel"] = np.ascontiguousarray(np.stack([g(n)[ps_] for n in ("w0", "a0", "k_k", "k_a", "r_k")], axis=1))
        m["ln2"] = np.ascontiguousarray(np.stack([g("lnx_w")[ps_].reshape(2, 64), g("lnx_b")[ps_].reshape(2, 64)], axis=0))
        m["Wd_sel"] = np.ascontiguousarray(g("w_decay_up")[:, ps_])
        m["Wa_sel"] = np.ascontiguousarray(g("w_iclr_up")[:, ps_])
        m["Wg_sel"] = np.ascontiguousarray(g("w_gate_up")[:, ps_])
        in_maps.append(m)
    res = run_bass_kernel_spmd(_NC, in_maps, core_ids=list(range(8)))
    if DEBUG:
        global DBG_OUT
        DBG_OUT = [res.results[core]["dbg"] for core in range(8)]
    outp = np.zeros((2, T, D), np.float32)
    for core in range(8):
        b, qq = core // 4, core % 4
        outp[b, qq * TQ:(qq + 1) * TQ] = res.results[core]["out"]
    return outp
```

```python
import numpy as np
import ml_dtypes
import concourse.bass as bass
import concourse.mybir as mybir
from concourse.bass_utils import run_bass_kernel_spmd

F32 = mybir.dt.float32
BF16 = mybir.dt.bfloat16
ALU = mybir.AluOpType
AF = mybir.ActivationFunctionType
AX = mybir.AxisListType

D = 1024
T = 8192
TQ = 2048
NBLK = T // 512
DFF = 2816
NFF = DFF // 128
GN_EPS = 64e-5
RMS_EPS = 1e-6
DEC = 0.6065306597126334


class Sched:
    def __init__(self, nc, es):
        self.nc = nc
        self.es = es
        self.eng = {"pe": nc.tensor, "act": nc.scalar, "dve": nc.vector, "pool": nc.gpsimd, "sp": nc.sync}
        self.sem = {}
        self.cnt = {}
        self.inc = {}
        self.waited = {}
        self.lastw = {}
        self.readers = {}
        for e in ("pe", "act", "dve", "pool"):
            self.stream(e, 1)

    def stream(self, name, inc):
        if name not in self.sem:
            self.sem[name] = self.es.enter_context(self.nc.semaphore("s_" + name))
            self.cnt[name] = 0
            self.inc[name] = inc
        return name

    def _wait(self, e, s, v):
        if s == e and e == "pe":
            return
        if self.waited.get((e, s), 0) >= v:
            return
        self.eng[e].wait_ge(self.sem[s], v)
        self.waited[(e, s)] = v

    @staticmethod
    def _bank(k):
        if isinstance(k, tuple) and k and k[0] == "TB":
            return "BANK_TB"
        if isinstance(k, str) and (k.startswith("s0_") or k.startswith("s1_")):
            return "BANK_" + k[:4]
        return None

    def _aug(self, keys):
        out = list(keys)
        for k in keys:
            b = self._bank(k)
            if b is not None and b not in out:
                out.append(b)
        return out

    def _deps(self, reads, writes):
        deps = set()
        for r in reads:
            if r in self.lastw:
                deps.add(self.lastw[r])
        for r in writes:
            if r in self.lastw:
                deps.add(self.lastw[r])
            for d in self.readers.get(r, ()):
                deps.add(d)
        return deps

    def _commit(self, s, reads, writes):
        v = self.cnt[s]
        for r in writes:
            self.lastw[r] = (s, v)
            self.readers[r] = []
        for r in reads:
            self.readers.setdefault(r, []).append((s, v))

    def op(self, e, fn, reads=(), writes=()):
        bk = [k for k in self._aug(list(reads) + list(writes)) if isinstance(k, str) and k.startswith("BANK_")]
        reads, writes = list(reads), list(writes) + bk
        for (s, v) in self._deps(reads, writes):
            self._wait(e, s, v)
        ins = fn()
        self.cnt[e] += 1
        ins.then_inc(self.sem[e], 1)
        self._commit(e, reads, writes)
        return ins

    def dma(self, q, s, out, in_, reads=(), writes=(), **kw):
        self.stream(s, 16)
        for (ss, v) in self._deps(reads, writes):
            self._wait(q, ss, v)
        ins = self.eng[q].dma_start(out=out, in_=in_, **kw)
        self.cnt[s] += 16
        ins.then_inc(self.sem[s], 16)
        self._commit(s, reads, writes)
        return ins

    def close(self, s):
        for r, (ss, v) in list(self.lastw.items()):
            if ss == s:
                self.lastw[r] = (s, self.cnt[s])

    def wait_all(self, e):
        for s in self.cnt:
            if self.cnt[s] > 0:
                self._wait(e, s, self.cnt[s])

    def barrier(self):
        for e in ("pe", "act", "dve", "pool", "sp"):
            self.wait_all(e)
        self.lastw.clear()
        self.readers.clear()


STOP = None
DEBUG = False
DBG_MAP = {}
DBG_OUT = None
_LAST_S = None


def build():
    import contextlib

    nc = bass.Bass("TRN2", target_bir_lowering=False)

    def din(name, shape, dt=F32):
        return nc.dram_tensor(name, list(shape), dt, kind="ExternalInput").ap()

    xb = din("xb", [T + 16, D])
    cvec = din("cvec", [D])
    flags = din("flags", [128, 65])
    ada_w = din("ada_w", [D, 6 * D])
    ada_b = din("ada_b", [6 * D])
    norm1_g = din("norm1_g", [D])
    xo = din("xo", [TQ + 16, D])
    w_in = din("w_in_sel", [D, 1152])
    mu_sel = din("mu_sel", [128, 5])
    pv_sel = din("pv_sel", [128, 5])
    ln2 = din("ln2", [2, 2, 64])
    pool_w = din("pool_w", [4, 128, 128])
    pool_scale = din("pool_scale", [512])
    w_decay_up = din("Wd_sel", [64, 128])
    w_iclr_up = din("Wa_sel", [64, 128])
    w_gate_up = din("Wg_sel", [128, 128])
    w_out = din("w_out", [D, D])
    norm2_g = din("norm2_g", [D])
    w_gu = din("w_ffn_gu", [D, 2 * DFF])
    w_down = din("w_ffn_down", [DFF, D])
    final_g = din("final_g", [D])
    out = nc.dram_tensor("out", [TQ, D], F32, kind="ExternalOutput").ap()
    ysrc = [nc.dram_tensor(f"ysrc{j}", [128, TQ], BF16) for j in range(4)]
    ydst = nc.dram_tensor("ydst", [4, 512, TQ], BF16)
    x1d = nc.dram_tensor("x1d", [TQ, D], F32)
    dbg = nc.dram_tensor("dbg", [128, 8192], F32, kind="ExternalOutput").ap() if DEBUG else None

    es = contextlib.ExitStack()
    with es:
        S = Sched(nc, es)
        global _LAST_S
        _LAST_S = S
        pid = nc.gpsimd.partition_id()
        q = pid % 4
        _n = [0]

        def sb(shape, dt=F32, stack=es, name=None):
            _n[0] += 1
            return stack.enter_context(nc.sbuf_tensor(name or f"t{_n[0]}", list(shape), dt))

        def ps(shape, dt=F32, name=None):
            _n[0] += 1
            return es.enter_context(nc.psum_tensor(name or f"p{_n[0]}", list(shape), dt))

        def mm(out, lhsT, rhs, r, w, start=True, stop=True):
            return S.op("pe", lambda: nc.tensor.matmul(out, lhsT, rhs, start=start, stop=stop), r, w)

        def tr(out, in_, ident, r, w):
            return S.op("pe", lambda: nc.tensor.transpose(out, in_, ident), r, w)

        def act(out, in_, func, r, w, bias=None, scale=None, accum_out=None):
            kw = {}
            if bias is not None:
                kw["bias"] = bias
            if scale is not None:
                kw["scale"] = scale
            if accum_out is not None:
                kw["accum_out"] = accum_out
            return S.op("act", lambda: nc.scalar.activation(out=out, in_=in_, func=func, **kw), r, w)

        def tt(e, out, in0, in1, op, r, w):
            eng = nc.vector if e == "dve" else nc.gpsimd
            return S.op(e, lambda: eng.tensor_tensor(out=out, in0=in0, in1=in1, op=op), r, w)

        def tsc(e, out, in0, s1, s2, op0, op1, r, w):
            eng = nc.vector if e == "dve" else nc.gpsimd
            if op1 is None:
                return S.op(e, lambda: eng.tensor_scalar(out=out, in0=in0, scalar1=s1, scalar2=None, op0=op0), r, w)
            return S.op(e, lambda: eng.tensor_scalar(out=out, in0=in0, scalar1=s1, scalar2=s2, op0=op0, op1=op1), r, w)

        def stt(out, in0, scalar, in1, op0, op1, r, w):
            return S.op("dve", lambda: nc.vector.scalar_tensor_tensor(out=out, in0=in0, scalar=scalar, in1=in1, op0=op0, op1=op1), r, w)

        def cp(e, out, in_, r, w):
            if e == "act":
                return S.op("act", lambda: nc.scalar.copy(out=out, in_=in_), r, w)
            eng = nc.vector if e == "dve" else nc.gpsimd
            return S.op(e, lambda: eng.tensor_copy(out=out, in_=in_), r, w)

        def mset(e, ap, val, w):
            eng = nc.vector if e == "dve" else nc.gpsimd
            return S.op(e, lambda: eng.memset(ap, val), (), w)

        NW = 3
        PS = [ps([128, 512], name=f"PS{i}") for i in range(8)]
        SB0 = [PS[p] for p in range(NW)]
        SB1 = [PS[NW + p] for p in range(NW)]
        TB = PS[6]
        GBL = [0, 1, 2, 3, 4, 5, 7]
        _gb = [0]

        def gbank():
            i = GBL[_gb[0] % len(GBL)]
            _gb[0] += 1
            return PS[i], f"GB{i}"

        ones_f = sb([128, 128])
        ident_f = sb([128, 128])
        ident_bf = sb([128, 128], BF16)
        flg = sb([128, 65])
        c_fm = sb([128, 8])
        n1g = sb([128, 8])
        n2g = sb([128, 8])
        mu = sb([128, 5])
        pscale = sb([128, 4])
        pv = sb([128, 8])
        fgbc = sb([128, D])
        omm = sb([128, 5])
        stage = [sb([128, 2048]) for _ in range(2)]
        g2bc = sb([128, D])
        modfm = sb([128, 48])
        gsc1 = sb([128, 8])
        gsc2 = sb([128, 8])
        xn = sb([128, D])
        junk = sb([128, D], BF16)
        ss = sb([128, 4])
        dbgt = sb([128, 512]) if DEBUG else None
        pA = contextlib.ExitStack()
        es.enter_context(pA)
        M_TS = sb([128, 2, 128], stack=pA)
        M_A3 = sb([128, 3, 128], stack=pA)
        Blk = sb([128, 128], stack=pA)
        E_bf = sb([128, 64], BF16, stack=pA)
        ones_bf = sb([128, 2], BF16, stack=pA)
        rmask = sb([128, 512], stack=pA)
        lnw_bc = sb([128, 64], stack=pA)
        lnb_bc = sb([128, 64], stack=pA)
        Wd = sb([128, 128], stack=pA)
        Wa = sb([128, 128], stack=pA)
        Wg = sb([128, 128], stack=pA)
        pw_f = sb([128, 4, 128], stack=pA)
        pw_bf = sb([128, 4, 128], BF16, stack=pA)
        g1bc = sb([128, D], stack=pA)
        xt = [sb([128, D], stack=pA) for _ in range(2)]
        mixp = sb([128, 4, TQ], BF16, stack=pA)
        mset("pool", ones_f[:], 1.0, ["ones_f"])

        def asel(out, pattern, cmul, cmp, w):
            return S.op("pool", lambda: nc.gpsimd.affine_select(out=out, in_=ones_f[:], pattern=pattern, compare_op=cmp,
                                                                fill=0.0, base=0, channel_multiplier=cmul), ["ones_f"], w)

        asel(ident_f[:], [[1, 128]], -1, ALU.is_equal, ["ident_f"])
        asel(M_TS[:, 0, :], [[1, 128]], -1, ALU.is_gt, ["M_TS"])
        asel(M_TS[:, 1, :], [[-1, 128]], 1, ALU.is_gt, ["M_TS"])
        asel(M_A3[:, 0, :], [[1, 128]], -1, ALU.is_ge, ["M_A3"])
        asel(M_A3[:, 1, :], [[1, 128]], -1, ALU.is_gt, ["M_A3"])
        asel(M_A3[:, 2, :], [[1, 128]], -1, ALU.is_ge, ["M_A3"])
        cp("pool", ident_bf[:], ident_f[:], ["ident_f"], ["ident_bf"])
        mset("pool", Blk[:], 0.0, ["Blk"])
        mset("pool", Blk[0:64, 0:64], 1.0, ["Blk"])
        mset("pool", Blk[64:128, 64:128], 1.0, ["Blk"])
        tt("pool", E_bf[:], ident_f[:, 0:64], ident_f[:, 64:128], ALU.add, ["ident_f"], ["E_bf"])
        mset("pool", ones_bf[:], 1.0, ["ones_bf"])
        mset("pool", rmask[:], 1.0, ["rmask"])
        mset("pool", rmask[:].rearrange("p (c t) -> p c t", t=64)[:, :, 0:1], 0.0, ["rmask"])

        def pl(dst, src, w, **kw):
            S.dma("pool", "init", dst, src, (), w, **kw)

        def fm(v, k):
            return v.rearrange("(k p) -> p k", p=128)

        pl(flg[:], flags[:, :], ["flg"])
        pl(c_fm[:], fm(cvec, 8), ["c_fm"], allow_slow_non_contiguous=True)
        pl(n1g[:], fm(norm1_g, 8), ["n1g"], allow_slow_non_contiguous=True)
        pl(n2g[:], fm(norm2_g, 8), ["n2g"], allow_slow_non_contiguous=True)
        pl(mu[:], mu_sel[:, :], ["mu"])
        pl(pscale[:], fm(pool_scale, 4), ["pscale"], allow_slow_non_contiguous=True)
        pl(pv[:, 0:5], pv_sel[:, :], ["pv"])
        for h in range(2):
            pl(lnw_bc[h * 64:(h + 1) * 64, :], ln2[0, h:h + 1, :].broadcast_to([64, 64]), ["lnw_bc"])
            pl(lnb_bc[h * 64:(h + 1) * 64, :], ln2[1, h:h + 1, :].broadcast_to([64, 64]), ["lnb_bc"])
        pl(Wd[0:64, :], w_decay_up[:, :], ["Wd"])
        pl(Wa[64:128, :], w_iclr_up[:, :], ["Wa"])
        pl(Wg[:], w_gate_up[:, :], ["Wg"])
        pl(pw_f[:], pool_w.rearrange("g c d -> c g d"), ["pw_f"])
        pl(fgbc[:], final_g.partition_broadcast(128), ["fgbc"])
        S.close("init")
        cp("pool", pw_bf[:], pw_f[:], ["pw_f"], ["pw_bf"])
        tsc("pool", omm[:], mu[:], -1.0, 1.0, ALU.mult, ALU.add, ["mu"], ["omm"])

        if STOP == "setup":
            S.barrier()
            return nc
        _st = [0]
        _ce = [0]

        def load_cast(dst_ap_fn, src_ap_fn, nparts, K, N, w, scale_bc=None, part0=0):
            ncol = max(1, 2048 // K)
            for n0 in range(0, N, ncol):
                n1 = min(N, n0 + ncol)
                i = _st[0] % 2
                _st[0] += 1
                sv = stage[i][part0:part0 + nparts, 0:K * (n1 - n0)].rearrange("p (k n) -> p k n", k=K)
                S.dma(("sp", "act")[i], f"stg{i}", sv, src_ap_fn(n0, n1), (), [f"stage{i}"])
                e = ("act", "dve")[_ce[0] % 2]
                _ce[0] += 1
                wk = w[0](n0, n1) if callable(w[0]) else [w[0]]
                if scale_bc is not None:
                    tt("dve" if e == "act" else e, dst_ap_fn(n0, n1), sv, scale_bc(n0, n1), ALU.mult, [f"stage{i}"] + w[1:], wk)
                else:
                    cp(e, dst_ap_fn(n0, n1), sv, [f"stage{i}"], wk)

        _dc = [0]

        def dump(name, ap, n, e="dve"):
            if not DEBUG:
                return
            c0 = _dc[0]
            _dc[0] += n
            DBG_MAP[name] = (c0, n)
            S.wait_all(e)
            npart = ap.shape[0]
            cp(e, dbgt[0:npart, 0:n], ap, [], ["dbgt"])
            S.dma("sp", "dbgs", dbg[0:npart, c0:c0 + n], dbgt[0:npart, 0:n], ["dbgt"], ["dbgd"])

        with contextlib.ExitStack() as p0:
            csil = sb([128, 8, 1], stack=p0)
            crep = sb([128, 8, 128], stack=p0)
            adab = [sb([128, 512], stack=p0) for _ in range(2)]
            mblk = sb([128, 512], stack=p0)
            tmp4 = sb([128, 4, 128], stack=p0)
            act(csil[:, :, 0], c_fm[:], AF.Silu, ["c_fm"], ["csil"])
            cp("dve", crep[:], csil[:].broadcast_to([128, 8, 128]), ["csil"], ["crep"])
            for cb in range(12):
                bank, bk = gbank()
                for k2 in range(4):
                    j = _st[0] % 2
                    _st[0] += 1
                    sv = stage[j][:, 0:1024].rearrange("p (k n) -> p k n", k=2)
                    S.dma(("sp", "act")[j], f"stg{j}", sv,
                          ada_w[k2 * 256:(k2 + 1) * 256, cb * 512:(cb + 1) * 512].rearrange("(k p) n -> p k n", p=128),
                          (), [f"stage{j}"])
                    for kk in range(2):
                        k = k2 * 2 + kk
                        mm(bank[:], crep[:, k, :], sv[:, kk, :], ["crep", f"stage{j}"], [bk], start=(k == 0), stop=(k == 7))
                a = cb % 2
                S.dma("sp", f"adab{a}", adab[a][:], ada_b[cb * 512:(cb + 1) * 512].partition_broadcast(128), (), [f"adab{a}"])
                sec = cb // 2
                if sec == 2:
                    dst, dk = g1bc[:, (cb % 2) * 512:(cb % 2 + 1) * 512], "g1bc"
                elif sec == 5:
                    dst, dk = g2bc[:, (cb % 2) * 512:(cb % 2 + 1) * 512], "g2bc"
                else:
                    dst, dk = mblk[:], "mblk"
                tt("dve", dst, bank[:], adab[a][:], ALU.add, [bk, f"adab{a}"], [dk])
                tt("dve", tmp4[:], dst.rearrange("p (a b) -> p a b", b=128),
                   ident_f[:].rearrange("p (a b) -> p a b", a=1).broadcast_to([128, 4, 128]), ALU.mult, [dk, "ident_f"], ["tmp4"])
                S.op("dve", lambda: nc.vector.tensor_reduce(out=modfm[:, cb * 4:(cb + 1) * 4], in_=tmp4[:], axis=AX.X, op=ALU.add),
                     ["tmp4"], ["modfm"])
            S.barrier()
        if STOP == "ada":
            return nc
        stt(gsc1[:], modfm[:, 8:16], 1.0, n1g[:], ALU.add, ALU.mult, ["modfm", "n1g"], ["gsc1"])
        stt(gsc2[:], modfm[:, 32:40], 1.0, n2g[:], ALU.add, ALU.mult, ["modfm", "n2g"], ["gsc2"])
        dump("modfm", modfm[:], 48)
        sh1 = modfm[:, 0:8]
        sh2 = modfm[:, 24:32]

        _xt = [0]

        def norm_to_fm(x_ap, nrow, gsc, sh, shk, dst_fn, dst_key, dq="sp", src_key=None, keep=None):
            if src_key is None:
                i = _xt[0] % 2
                _xt[0] += 1
                xs, xk = xt[i], f"xt{i}"
                S.dma(dq, f"xld{i}", xs[0:nrow, :], x_ap, (), [xk])
                xa = xs[0:nrow, :]
            else:
                xa, xk = x_ap, src_key
            act(junk[0:nrow, :], xa, AF.Square, [xk], ["junk", "ss"], accum_out=ss[0:nrow, 0:1])
            act(ss[0:nrow, 1:2], ss[0:nrow, 0:1], AF.Sqrt, ["ss"], ["ss1"], bias=RMS_EPS, scale=1.0 / D)
            S.op("dve", lambda: nc.vector.reciprocal(out=ss[0:nrow, 2:3], in_=ss[0:nrow, 1:2]), ["ss1"], ["ss2"])
            act(xn[0:nrow, :], xa, AF.Copy, [xk, "ss2"], ["xn"], scale=ss[0:nrow, 2:3])
            for half in range(2):
                bank, bk = gbank()
                for kk_ in range(4):
                    k = half * 4 + kk_
                    tr(bank[:, kk_ * 128:kk_ * 128 + nrow], xn[0:nrow, k * 128:(k + 1) * 128], ident_f[0:nrow, 0:nrow], ["xn", "ident_f"], [bk])
                for kk_ in range(4):
                    k = half * 4 + kk_
                    tsc("dve", dst_fn(k), bank[:, kk_ * 128:kk_ * 128 + nrow], gsc[:, k:k + 1], sh[:, k:k + 1], ALU.mult, ALU.add,
                        [bk, "gsc1", "gsc2", "modfm"], [dst_key])
            return xa, xk

        def norm_gen(x_ap, nrow, gsc, sh, shk, dst_fn, dst_key, dq="sp", src_key=None, keep=None):
            if src_key is None:
                i = _xt[0] % 2
                _xt[0] += 1
                xs, xk = xt[i], f"xt{i}"
                S.dma(dq, f"xld{i}", xs[0:nrow, :], x_ap, (), [xk])
                xa = xs[0:nrow, :]
            else:
                xa, xk = x_ap, src_key
            act(junk[0:nrow, :], xa, AF.Square, [xk], ["junk", "ss"], accum_out=ss[0:nrow, 0:1])
            yield
            act(ss[0:nrow, 1:2], ss[0:nrow, 0:1], AF.Sqrt, ["ss"], ["ss1"], bias=RMS_EPS, scale=1.0 / D)
            yield
            S.op("dve", lambda: nc.vector.reciprocal(out=ss[0:nrow, 2:3], in_=ss[0:nrow, 1:2]), ["ss1"], ["ss2"])
            yield
            act(xn[0:nrow, :], xa, AF.Copy, [xk, "ss2"], ["xn"], scale=ss[0:nrow, 2:3])
            yield
            for half in range(2):
                bank, bk = gbank()
                for kk_ in range(4):
                    k = half * 4 + kk_
                    tr(bank[:, kk_ * 128:kk_ * 128 + nrow], xn[0:nrow, k * 128:(k + 1) * 128], ident_f[0:nrow, 0:nrow], ["xn", "ident_f"], [bk])
                yield
                for kk_ in range(4):
                    k = half * 4 + kk_
                    tsc("dve", dst_fn(k), bank[:, kk_ * 128:kk_ * 128 + nrow], gsc[:, k:k + 1], sh[:, k:k + 1], ALU.mult, ALU.add,
                        [bk, "gsc1", "gsc2", "modfm"], [dst_key])
                yield

        with contextlib.ExitStack() as p1:
            w_in_bf = sb([128, 8, 9 * 128], BF16, stack=p1)
            w3 = w_in.rearrange("(k p) n -> p k n", p=128)
            load_cast(lambda a, b: w_in_bf[:, :, a:b], lambda a, b: w3[:, :, a:b], 128, 8, 1152, ["w_in_bf"])

            if STOP == "w_in":
                S.barrier()
                return nc
            hT = sb([128, 8, 528], BF16, stack=p1)
            dn = [sb([128, 528], stack=p1) for _ in range(12)]
            zT = [sb([128, 5, 513], stack=p1) for _ in range(2)]
            gT = [sb([128, 2, 512], stack=p1) for _ in range(2)]
            yTb = [sb([128, 2, 512], BF16, stack=p1) for _ in range(2)]
            gamC = [sb([128, 8], stack=p1) for _ in range(2)]
            Hs = [sb([128, 64], stack=p1) for _ in range(2)]
            AR_bd = [sb([128, 8, 256], BF16, stack=p1) for _ in range(2)]
            B_bd = [sb([128, 8, 128], BF16, stack=p1) for _ in range(2)]
            K_bd = [sb([128, 8, 128], BF16, stack=p1) for _ in range(2)]
            BH_bd = [sb([128, 8, 128], BF16, stack=p1) for _ in range(2)]
            KH_bd = [sb([128, 8, 128], BF16, stack=p1) for _ in range(2)]
            V_bd = [sb([128, 8, 128], BF16, stack=p1) for _ in range(2)]
            RRK_bd = [sb([128, 8, 128], BF16, stack=p1) for _ in range(2)]
            Xi = [sb([128, 3, 128], stack=p1) for _ in range(NW)]
            Mb = [sb([128, 128], BF16, stack=p1) for _ in range(NW)]
            A3 = [sb([128, 3, 128], BF16, stack=p1) for _ in range(NW)]
            R3 = [sb([128, 192], BF16, stack=p1) for _ in range(NW)]
            BK = [sb([128, 2, 128], BF16, stack=p1) for _ in range(NW)]
            Vst = [sb([128, 64], BF16, stack=p1) for _ in range(NW)]
            WU = [sb([128, 192], BF16, stack=p1) for _ in range(NW)]
            PT = [sb([128, 128], stack=p1) for _ in range(NW)]
            RH = [sb([128, 128], stack=p1) for _ in range(NW)]
            sm = [sb([128, 16], stack=p1) for _ in range(NW)]
            yfin = [sb([128, 64], stack=p1) for _ in range(NW)]
            yh = [sb([128, 64], stack=p1) for _ in range(NW)]
            for p in range(2):
                for tl, nm in ((AR_bd, "AR"), (B_bd, "B"), (K_bd, "K"), (BH_bd, "BH"), (KH_bd, "KH"), (V_bd, "V"), (RRK_bd, "RRK")):
                    mset("pool", tl[p][:], 0.0, [f"{nm}{p}"])
                mset("pool", Hs[p][:], 0.0, [f"H{p}"])

            uT = dn[0]
            for ob in range(4):
                norm_to_fm(xo[ob * 512:ob * 512 + 16, :], 16, gsc1, sh1, "sh1",
                           lambda k: hT[:, k, 0:16], "hT")
                if STOP == "norm16":
                    S.barrier()
                    return nc
                for ti in range(4):
                    norm_to_fm(xo[16 + ob * 512 + ti * 128:16 + ob * 512 + (ti + 1) * 128, :], 128, gsc1, sh1, "sh1",
                               lambda k, ti=ti: hT[:, k, 16 + ti * 128:16 + (ti + 1) * 128], "hT")
                if STOP == "norm":
                    S.barrier()
                    return nc
                if ob == 0:
                    for k_ in range(8):
                        dump(f"hT{k_}", hT[:, k_, 0:144], 144)
                for g in range(4):
                    if STOP == "g1" and g == 1:
                        S.barrier()
                        return nc
                    w = (2, 4, 8, 16)[g]
                    bank, bk = gbank()
                    for k in range(8):
                        mm(bank[:], w_in_bf[:, k, g * 128:(g + 1) * 128], hT[:, k, 16:528], ["w_in_bf", "hT"], [bk], start=(k == 0), stop=(k == 7))
                    cp("act", uT[:, 16:528], bank[:], [bk], ["dn0"])
                    bank2, bk2 = gbank()
                    for k in range(8):
                        mm(bank2[:, 0:16], w_in_bf[:, k, g * 128:(g + 1) * 128], hT[:, k, 0:16], ["w_in_bf", "hT"], [bk2], start=(k == 0), stop=(k == 7))
                    if ob == 0:
                        tsc("dve", uT[:, 0:16], bank2[:, 0:16], flg[:, 0:1], None, ALU.mult, None, [bk2, "flg"], ["dn0"])
                    else:
                        cp("dve", uT[:, 0:16], bank2[:, 0:16], [bk2], ["dn0"])
                    src, sk = uT, "dn0"
                    sh_ = 1
                    lvl = 0
                    while sh_ < w:
                        dst, dk = dn[1 + lvl % 2], f"dn{1 + lvl % 2}"
                        tt("pool", dst[:, sh_:528], src[:, sh_:528], src[:, 0:528 - sh_], ALU.add, [sk], [dk])
                        src, sk = dst, dk
                        sh_ *= 2
                        lvl += 1
                    dd = dn[3]
                    stt(dd[:, 16:528], src[:, 16:528], 1.0 / w, uT[:, 16:528], ALU.mult, ALU.subtract, [sk, "dn0"], ["dn3"])
                    if ob == 0:
                        tt("dve", dn[4][:, 0:16], src[:, 16:32], flg[:, 1 + g * 16:1 + (g + 1) * 16], ALU.mult, [sk, "flg"], ["dn4"])
                        tt("dve", dd[:, 16:32], dn[4][:, 0:16], uT[:, 16:32], ALU.subtract, ["dn4", "dn0", "dn3"], ["dn3"])
                    dbfT = dn[5][:, 0:256].bitcast(BF16)
                    cp("act", dbfT, dd[:, 16:528], ["dn3"], ["dn5"])
                    bank3, bk3 = gbank()
                    mm(bank3[:], pw_bf[:, g, :], dbfT, ["pw_bf", "dn5"], [bk3])
                    tsc("dve", mixp[:, g, ob * 512:(ob + 1) * 512], bank3[:], pscale[:, g:g + 1], None, ALU.mult, None, [bk3, "pscale"], ["mixp"])

            for g_ in range(4):
                dump(f"mixp{g_}", mixp[:, g_, 0:128], 128)
            if STOP == "pool":
                S.barrier()
                return nc
            S.barrier()
            GBL[:] = [7]
            ysv = [ysrc[j].ap().rearrange("(h i) t -> i h t", i=64) for j in range(4)]
            S.stream("cc", 1)

            ymark = {}

            def exchange(j):
                for st_, v_ in ymark[j]:
                    S._wait("pool", st_, v_)
                nc.gpsimd.collective_compute("AllGather", ALU.bypass, replica_groups=[[0, 1, 2, 3], [4, 5, 6, 7]],
                                             ins=[ysrc[j].ap().opt()], outs=[ydst.ap()[j].opt()]).then_inc(S.sem["cc"], 1)
                S.cnt["cc"] += 1
            state = {"pk": 0}

            def prep(blk):
                bp = blk % 2
                zt, zk = zT[bp], f"zT{bp}"
                for ti in range(4):
                    yield from norm_gen(xb[16 + blk * 512 + ti * 128:16 + blk * 512 + (ti + 1) * 128, :], 128, gsc1, sh1, "sh1",
                                        lambda k, ti=ti: hT[:, k, 16 + ti * 128:16 + (ti + 1) * 128], "hT")
                if blk == 0:
                    mset("pool", zt[:, :, 0:1], 0.0, [zk])
                else:
                    cp("pool", zt[:, :, 0:1], zT[1 - bp][:, :, 512:513], [f"zT{1 - bp}"], [zk])
                    yield
                for m in range(5):
                    bank, bk = gbank()
                    for k in range(8):
                        mm(bank[:], w_in_bf[:, k, 512 + m * 128:512 + (m + 1) * 128], hT[:, k, 16:528], ["w_in_bf", "hT"], [bk], start=(k == 0), stop=(k == 7))
                    yield
                    cp("act", zt[:, m, 1:513], bank[:], [bk], [zk])
                    yield
                zs = []
                for m in range(5):
                    tmp = dn[11]
                    act(tmp[:, 0:512], zt[:, m, 0:512], AF.Copy, [zk, "mu"], ["dn11"], scale=mu[:, m:m + 1])
                    yield
                    dst = dn[m]
                    stt(dst[:, 0:512], zt[:, m, 1:513], omm[:, m:m + 1], tmp[:, 0:512], ALU.mult, ALU.add, [zk, "omm", "dn11"], [f"dn{m}"])
                    yield
                    zs.append(dst)
                rT, kT, vT, xwa, xg = [z[:, 0:512] for z in zs]
                thx = dn[5]
                act(thx[0:64, 0:512], xwa[0:64, :], AF.Tanh, ["dn3"], ["dn5"])
                yield
                bank, bk = gbank()
                mm(bank[:], Wd[0:64, :], thx[0:64, 0:512], ["Wd", "dn5"], [bk])
                yield
                sg = dn[6]
                act(sg[:, 0:512], bank[:], AF.Sigmoid, [bk, "pv"], ["dn6"], bias=pv[:, 0:1])
                yield
                bank, bk = gbank()
                mm(bank[:], Wa[64:128, :], xwa[64:128, :], ["Wa", "dn3"], [bk])
                yield
                aT = dn[7]
                act(aT[:, 0:512], bank[:], AF.Sigmoid, [bk, "pv"], ["dn7"], bias=pv[:, 1:2])
                yield
                sgx = dn[5]
                act(sgx[:, 0:512], xg, AF.Sigmoid, ["dn4"], ["dn5"])
                yield
                for h in range(2):
                    bank, bk = gbank()
                    mm(bank[0:64, :], Wg[:, h * 64:(h + 1) * 64], sgx[:, 0:512], ["Wg", "dn5"], [bk])
                    yield
                    cp("act", gT[bp][0:64, h, :], bank[0:64, :], [bk], [f"gT{bp}"])
                    yield
                cs = dn[8]
                S.op("dve", lambda: nc.vector.tensor_tensor_scan(out=cs[:, 0:512], data0=rmask[:], data1=sg[:, 0:512], initial=0.0,
                                                                 op0=ALU.mult, op1=ALU.add), ["rmask", "dn6"], ["dn8"])
                yield
                epos = dn[9]
                act(epos[:, 0:512], cs[:, 0:512], AF.Exp, ["dn8"], ["dn9"], scale=-DEC)
                yield
                cp("pool", gamC[bp][:], epos[:, 0:512].rearrange("p (c t) -> p c t", t=64)[:, :, 63], ["dn9"], [f"gamC{bp}"])
                yield
                eneg = dn[10]
                act(eneg[:, 0:512], cs[:, 0:512], AF.Exp, ["dn8"], ["dn10"], scale=DEC)
                yield
                tt("dve", cs[:, 0:512], cs[:, 0:512], sg[:, 0:512], ALU.subtract, ["dn8", "dn6"], ["dn8"])
                yield
                eprev = dn[6]
                act(eprev[:, 0:512], cs[:, 0:512], AF.Exp, ["dn8"], ["dn6"], scale=-DEC)
                yield
                kkr = dn[3]
                tsc("pool", kkr[:, 0:512], kT, pv[:, 2:3], None, ALU.mult, None, ["dn1", "pv"], ["dn3"])
                yield
                sq = dn[4]
                tt("pool", sq[:, 0:512], kkr[:, 0:512], kkr[:, 0:512], ALU.mult, ["dn3"], ["dn4"])
                yield
                bank, bk = gbank()
                mm(bank[:], Blk[:], sq[:, 0:512], ["Blk", "dn4"], [bk])
                yield
                act(sq[:, 0:512], bank[:], AF.Sqrt, [bk], ["dn4"])
                yield
                tsc("dve", sq[:, 0:512], sq[:, 0:512], 1e-12, None, ALU.max, None, ["dn4"], ["dn4"])
                yield
                S.op("dve", lambda: nc.vector.reciprocal(out=sq[:, 0:512], in_=sq[:, 0:512]), ["dn4"], ["dn4"])
                yield
                kk = dn[3]
                tt("dve", kk[:, 0:512], kkr[:, 0:512], sq[:, 0:512], ALU.mult, ["dn3", "dn4"], ["dn3"])
                yield
                t1 = dn[4]
                tsc("dve", t1[:, 0:512], aT[:, 0:512], -1.0, pv[:, 3:4], ALU.add, ALU.mult, ["dn7", "pv"], ["dn4"])
                yield
                kp = dn[8]
                stt(kp[:, 0:512], t1[:, 0:512], 1.0, kT, ALU.add, ALU.mult, ["dn4", "dn1"], ["dn8"])
                yield
                tt("pool", aT[:, 0:512], aT[:, 0:512], kk[:, 0:512], ALU.mult, ["dn7", "dn3"], ["dn7"])
                yield
                stt(t1[:, 0:512], rT, pv[:, 4:5], kp[:, 0:512], ALU.mult, ALU.mult, ["dn0", "pv", "dn8"], ["dn4"])
                yield
                stt(kk[:, 0:512], kk[:, 0:512], -1.0, eprev[:, 0:512], ALU.mult, ALU.mult, ["dn3", "dn6"], ["dn3"])
                yield
                tt("pool", aT[:, 0:512], aT[:, 0:512], eneg[:, 0:512], ALU.mult, ["dn7", "dn10"], ["dn7"])
                yield
                tt("dve", kp[:, 0:512], kp[:, 0:512], eneg[:, 0:512], ALU.mult, ["dn8", "dn10"], ["dn8"])
                yield

                if blk == 0:
                    for nm_, t__ in (("rT", rT), ("vT", vT), ("atil", kk[:, 0:512]), ("btil", aT[:, 0:512]), ("ktil", kp[:, 0:512]),
                                     ("epos", epos[:, 0:512]), ("rrk", t1[:, 0:512])):
                        dump(nm_, t__[:, 0:128], 128)
                    for h_ in range(2):
                        dump(f"gT{h_}", gT[bp][0:64, h_, 0:64], 64)

                def c3(t_):
                    return t_.rearrange("p (c t) -> p c t", t=64)

                gam3 = gamC[bp][:].rearrange("p (c o) -> p c o", o=1)
                for h in range(2):
                    hs = slice(h * 64, (h + 1) * 64)
                    cs_ = slice(h * 64, (h + 1) * 64)
                    e1 = "dve" if h == 0 else "pool"
                    cp(e1, AR_bd[bp][hs, :, cs_], c3(kk[hs, 0:512]), ["dn3"], [f"AR{bp}"])
                    yield
                    tt(e1, AR_bd[bp][hs, :, 128 + h * 64:128 + (h + 1) * 64], c3(rT[hs, :]), c3(epos[hs, 0:512]), ALU.mult, ["dn0", "dn9"], [f"AR{bp}"])
                    yield
                    cp(e1, B_bd[bp][hs, :, cs_], c3(aT[hs, 0:512]), ["dn7"], [f"B{bp}"])
                    yield
                    cp(e1, K_bd[bp][hs, :, cs_], c3(kp[hs, 0:512]), ["dn8"], [f"K{bp}"])
                    yield
                    tt(e1, BH_bd[bp][hs, :, cs_], c3(aT[hs, 0:512]), gam3[hs].broadcast_to([64, 8, 64]), ALU.mult, ["dn7", f"gamC{bp}"], [f"BH{bp}"])
                    yield
                    tt(e1, KH_bd[bp][hs, :, cs_], c3(kp[hs, 0:512]), gam3[hs].broadcast_to([64, 8, 64]), ALU.mult, ["dn8", f"gamC{bp}"], [f"KH{bp}"])
                    yield
                    cp(e1, V_bd[bp][hs, :, cs_], c3(vT[hs, :]), ["dn2"], [f"V{bp}"])
                    yield
                    cp(e1, RRK_bd[bp][hs, :, cs_], c3(t1[hs, 0:512]), ["dn4"], [f"RRK{bp}"])
                    yield

            def pack(blk, c):
                bp = blk % 2
                g = state["pk"]
                state["pk"] += 1
                import os
                pp = (g + int(os.environ.get("PPX", "0"))) % NW
                hc, hn = g % 2, (g + 1) % 2
                s0, s1 = SB0[pp], SB1[pp]
                I0, I1, I2, KAV, KVS = [f"s0_{pp}_{i}" for i in range(5)]
                k1 = [f"s1_{pp}_{i}" for i in range(5)]
                tb = [("TB", j) for j in range(3)]
                X, a3, r3, bk_, vs, wu, pt, rh, smm = Xi[pp], A3[pp], R3[pp], BK[pp], Vst[pp], WU[pp], PT[pp], RH[pp], sm[pp]
                kX, kA3, kR3, kBK, kV, kWU, kPT, kRH = f"X{pp}", f"A3{pp}", f"R3{pp}", f"BK{pp}", f"Vst{pp}", f"WU{pp}", f"PT{pp}", f"RH{pp}"
                ar, bb, kb, bh, kh, vb, rrk = AR_bd[bp], B_bd[bp], K_bd[bp], BH_bd[bp], KH_bd[bp], V_bd[bp], RRK_bd[bp]
                s0v = s0[:, 0:384].rearrange("p (a b) -> p a b", b=128)
                mm(s0[:, 0:128], bb[:, c, :], ar[:, c, 0:128], [f"B{bp}", f"AR{bp}"], [I0])
                mm(s0[:, 256:384], ar[:, c, 0:128], bb[:, c, :], [f"B{bp}", f"AR{bp}"], [I2])
                mm(s1[:, 0:128], bb[:, c, :], ar[:, c, 128:256], [f"B{bp}", f"AR{bp}"], [k1[0]])
                mm(s1[:, 128:384], kb[:, c, :], ar[:, c, :], [f"K{bp}", f"AR{bp}"], [k1[1], k1[2], k1[3]])
                o = 0
                mm(TB[:, o:o + 128], ar[:, c, 0:128], ident_bf[:], [f"AR{bp}", "ident_bf"], [tb[0]])
                mm(TB[:, o + 128:o + 256], bh[:, c, :], ident_bf[:], [f"BH{bp}", "ident_bf"], [tb[1]])
                mm(TB[:, o + 256:o + 384], kh[:, c, :], ident_bf[:], [f"KH{bp}", "ident_bf"], [tb[2]])
                mm(s0[:, 448:512], vb[:, c, :], E_bf[:], [f"V{bp}", "E_bf"], [KVS])
                import os
                if os.environ.get("RKTB", "0") == "1":
                    mm(TB[:, 384:386], rrk[:, c, :], ones_bf[:], [f"RRK{bp}", "ones_bf"], [k1[4]])
                else:
                    mm(s1[:, 384:386], rrk[:, c, :], ones_bf[:], [f"RRK{bp}", "ones_bf"], [k1[4]])
                yield
                import os
                OPS = os.environ.get("OPS", "abcdefg")
                if "a" in OPS:
                    tt("dve", X[:, 0:3:2, :], s0v[:, 0:3:2, :], M_TS[:], ALU.mult, [I0, I2, "M_TS"], [kX])
                if "b" in OPS:
                    tt("dve", a3[:], s1[:, 0:384].rearrange("p (a b) -> p a b", b=128), M_A3[:], ALU.mult, [k1[0], k1[1], k1[2], k1[3], "M_A3"], [kA3])
                if "c" in OPS:
                    tt("pool", X[:, 1, :], X[:, 0, :], ident_f[:], ALU.add, [kX, "ident_f"], [kX])
                if "d" in OPS:
                    cp("act", r3[:, 0:128], TB[:, o:o + 128], [tb[0]], [kR3])
                if "e" in OPS:
                    cp("act", bk_[:], TB[:, o + 128:o + 384].rearrange("p (a b) -> p a b", b=128), [tb[1], tb[2]], [kBK])
                if "f" in OPS:
                    cp("act", vs[:], s0[:, 448:512], [KVS], [kV])
                if "g" in OPS:
                    cp("act", smm[:, 0:1], s1[:, 384:385], [k1[4]], [f"sm{pp}rk"])
                yield
                mm(s0[:, 0:128], X[:, 2, :], X[:, 0, :], [kX], [I0])
                mm(s0[:, 256:384], X[:, 0, :], X[:, 2, :], [kX], [I2])
                yield
                cp("act", X[:, 0:3:2, :], s0v[:, 0:3:2, :], [I0, I2], [kX])
                yield
                for lv in range(1, 5):
                    mm(s0[:, 0:256], X[:, 2, :], X[:, 0:2, :].rearrange("p a b -> p (a b)"), [kX], [I0, I1])
                    mm(s0[:, 256:384], X[:, 0, :], X[:, 2, :], [kX], [I2])
                    yield
                    tt("dve", X[:, 1, :], s0[:, 128:256], X[:, 1, :], ALU.add, [I1, kX], [kX])
                    cp("act", X[:, 0:3:2, :], s0v[:, 0:3:2, :], [I0, I2], [kX])
                    yield
                mm(s0[:, 128:256], X[:, 2, :], X[:, 1, :], [kX], [I1])
                mm(s0[:, 384:448], a3[:, 1, :], vs[:], [kA3, kV], [KAV])
                yield
                tt("dve", Mb[pp][:], s0[:, 128:256], X[:, 1, :], ALU.add, [I1, kX], [f"Mb{pp}"])
                cp("act", r3[:, 128:192], s0[:, 384:448], [KAV], [kR3])
                yield
                mm(s0[:, 0:192], Mb[pp][:], r3[:], [f"Mb{pp}", kR3], [I0, I1])
                yield
                cp("act", wu[:], s0[:, 0:192], [I0, I1], [kWU])
                yield
                mm(s0[:, 192:320], wu[:, 0:128], bk_[:, 0, :], [kWU, kBK], [I1, I2])
                mm(s1[:, 0:128], wu[:, 0:128], a3[:, 0, :], [kWU, kA3], [k1[0]])
                yield
                cp("act", pt[:], s0[:, 192:320], [I1, I2], [kPT])
                tt("dve", rh[:], s1[:, 0:128], ar[:, c, 128:256], ALU.add, [k1[0], f"AR{bp}"], [kRH])
                yield
                mm(s1[:, 256:320], a3[:, 0, :], wu[:, 128:192], [kA3, kWU], [k1[2]], start=True, stop=False)
                mm(s1[:, 256:320], a3[:, 2, :], vs[:], [kA3, kV], [k1[2]], start=False, stop=False)
                mm(s1[:, 256:320], rh[:], Hs[hc][:], [kRH, f"H{hc}"], [k1[2]], start=False, stop=True)
                mm(s1[:, 320:384], bk_[:, 0, :], wu[:, 128:192], [kBK, kWU], [k1[3]], start=True, stop=False)
                mm(s1[:, 320:384], bk_[:, 1, :], vs[:], [kBK, kV], [k1[3]], start=False, stop=False)
                mm(s1[:, 320:384], pt[:], Hs[hc][:], [kPT, f"H{hc}"], [k1[3]], start=False, stop=True)
                yield
                stt(Hs[hn][:], Hs[hc][:], gamC[bp][:, c:c + 1], s1[:, 320:384], ALU.mult, ALU.add, [f"H{hc}", f"gamC{bp}", k1[3]], [f"H{hn}"])
                S.op("dve", lambda: nc.vector.bn_stats(out=smm[:, 2:8], in_=s1[:, 256:320]), [k1[2]], [f"sm{pp}st"])
                S.op("dve", lambda: nc.vector.bn_aggr(out=smm[:, 8:10], in_=smm[:, 2:8]), [f"sm{pp}st"], [f"sm{pp}mv"])
                act(smm[:, 10:11], smm[:, 9:10], AF.Sqrt, [f"sm{pp}mv"], [f"sm{pp}sd"], bias=GN_EPS, scale=1.0)
                S.op("dve", lambda: nc.vector.reciprocal(out=smm[:, 11:12], in_=smm[:, 10:11]), [f"sm{pp}sd"], [f"sm{pp}rs"])
                tsc("dve", yh[pp][:], s1[:, 256:320], smm[:, 8:9], smm[:, 11:12], ALU.subtract, ALU.mult, [k1[2], f"sm{pp}mv", f"sm{pp}rs"], [f"yh{pp}"])
                tt("pool", yh[pp][:], yh[pp][:], lnw_bc[:], ALU.mult, [f"yh{pp}", "lnw_bc"], [f"yh{pp}"])
                tt("pool", yh[pp][:], yh[pp][:], lnb_bc[:], ALU.add, [f"yh{pp}", "lnb_bc"], [f"yh{pp}"])
                stt(yfin[pp][:], vs[:], smm[:, 0:1], yh[pp][:], ALU.mult, ALU.add, [kV, f"sm{pp}rk", f"yh{pp}"], [f"yfin{pp}"])
                yield
                tr(s1[0:64, 128:256], yfin[pp][:], ident_f[:], [f"yfin{pp}", "ident_f"], [k1[1]])
                yield
                tt("dve", yTb[bp][0:64, :, c * 64:(c + 1) * 64], s1[0:64, 128:256].rearrange("p (h t) -> p h t", t=64),
                   gT[bp][0:64, :, c * 64:(c + 1) * 64], ALU.mult, [k1[1], f"gT{bp}"], [f"yTb{bp}"])

            def run_packs(blk, extra=None):
                import os
                gens = [pack(blk, c) for c in range(int(os.environ.get("PKN", "8")))]
                maxs = int(STOP[2:]) if (STOP or "").startswith("pk") else 10 ** 9
                adv = {}
                active = []
                if extra is not None and maxs > 10 ** 8:
                    active.append(extra)
                nxt = 0
                stepc = 0
                NG = len(gens)
                while nxt < NG or active:
                    npk = len([a_ for a_ in active if a_ is not extra])
                    if nxt < NG and npk < NW and (npk == 0 or stepc % 5 == 0):
                        active.append(gens[nxt])
                        nxt += 1
                    for gkk in list(active):
                        try:
                            adv[id(gkk)] = adv.get(id(gkk), 0) + 1
                            if adv[id(gkk)] > maxs:
                                raise StopIteration
                            next(gkk)
                            if gkk is extra:
                                next(gkk)
                        except StopIteration:
                            active.remove(gkk)
                    stepc += 1

            for _ in prep(0):
                pass
            for blk in range(NBLK):
                if STOP == "prep":
                    S.barrier()
                    return nc
                run_packs(blk, prep(blk + 1) if blk + 1 < NBLK else None)
                if STOP == "blk1" or (STOP or "").startswith("pk"):
                    S.barrier()
                    return nc
                bp = blk % 2
                if blk == 0:
                    for h_ in range(2):
                        dump(f"yT{h_}", yTb[bp][0:64, h_, 0:128], 128)
                    dump("H1", Hs[0][:], 64)
                S.dma("sp", f"yst{bp}", ysv[blk // 4][:, :, (blk % 4) * 512:(blk % 4 + 1) * 512], yTb[bp][0:64, :, :], [f"yTb{bp}"], ["ysrc"])
                if blk % 4 == 3:
                    ymark[blk // 4] = [(st_, S.cnt[st_]) for st_ in ("yst0", "yst1")]
                if blk % 4 == 0 and blk > 0:
                    exchange(blk // 4 - 1)
            S.barrier()
            exchange(3)
            GBL[:] = [0, 1, 2, 3, 4, 5, 7]

        if STOP == "p1":
            return nc

        if STOP == "cc":
            return nc
        ydv = ydst.ap().rearrange("j (h i) t -> i j h t", i=64)
        with contextlib.ExitStack() as p2:
            wo_p = sb([128, 4, D], BF16, stack=p2)
            wo_r = sb([128, 8, D], BF16, stack=p2)
            yall = [sb([128, 8, 512], BF16, stack=p2) for _ in range(2)]
            x1t = [sb([128, D], stack=p2) for _ in range(2)]
            g1b3 = g1bc[:].rearrange("p (k n) -> p k n", k=1)
            load_cast(lambda a, b: wo_p[:, :, a:b], lambda a, b: w_out[0:512, a:b].rearrange("(k p) n -> p k n", p=128), 128, 4, D,
                      ["wo_p", "g1bc"], scale_bc=lambda a, b: g1b3[:, :, a:b].broadcast_to([128, 4, b - a]))
            load_cast(lambda a, b: wo_r[0:64, :, a:b], lambda a, b: w_out[512:1024, a:b].rearrange("(h i) n -> i h n", i=64), 64, 8, D,
                      ["wo_r", "g1bc"], scale_bc=lambda a, b: g1b3[0:64, :, a:b].broadcast_to([64, 8, b - a]))
            S.barrier()
            for ob in range(4):
                ya = yall[ob % 2]
                S.dma("pool", f"yld{ob % 2}", ya[0:64, :, :], ydv[:, bass.ds(q, 1), :, ob * 512:(ob + 1) * 512].rearrange("i j h t -> i (j h) t"),
                      ["wo_r", "wo_p"], [f"yall{ob % 2}"])
                for ti in range(4):
                    i = _xt[0] % 2
                    _xt[0] += 1
                    S.dma("sp", f"xld{i}", xt[i][:], xo[16 + ob * 512 + ti * 128:16 + ob * 512 + (ti + 1) * 128, :], (), [f"xt{i}"])
                    j = (ob * 4 + ti) % 2
                    for hf in range(2):
                        bank, bk = gbank()
                        for m in range(4):
                            mm(bank[:], mixp[:, m, ob * 512 + ti * 128:ob * 512 + (ti + 1) * 128], wo_p[:, m, hf * 512:(hf + 1) * 512],
                               ["mixp", "wo_p"], [bk], start=(m == 0), stop=False)
                        for h in range(8):
                            mm(bank[:], ya[0:64, h, ti * 128:(ti + 1) * 128], wo_r[0:64, h, hf * 512:(hf + 1) * 512],
                               [f"yall{ob % 2}", "wo_r"], [bk], start=False, stop=(h == 7))
                        tt("dve", x1t[j][:, hf * 512:(hf + 1) * 512], bank[:], xt[i][:, hf * 512:(hf + 1) * 512], ALU.add, [bk, f"xt{i}"], [f"x1t{j}"])
                    r0 = ob * 512 + ti * 128
                    if ob == 0 and ti == 0:
                        dump("x1", x1t[j][:, 0:256], 256)
                        for h_ in range(8):
                            dump(f"yall{h_}", ya[0:64, h_, 0:32], 32)
                    S.dma("sp", f"x1st{j}", x1d[r0:r0 + 128, :], x1t[j][:], [f"x1t{j}"], ["x1d"])
            S.barrier()
        pA.close()
        if STOP == "p2a":
            return nc

        with contextlib.ExitStack() as p3:
            wgu = sb([128, 8, 2 * DFF], BF16, stack=p3)
            wdn = sb([128, NFF, D], BF16, stack=p3)
            x1b = sb([128, 2, D], stack=p3)
            h2T = sb([128, 8, 256], BF16, stack=p3)
            actT = sb([128, NFF, 256], BF16, stack=p3)
            sgt = [sb([128, 256], stack=p3) for _ in range(2)]
            ot = sb([128, D], stack=p3)
            g2b3 = g2bc[:].rearrange("p (k n) -> p k n", k=1)
            rngs = []
            for i_ in range(6):
                rngs.append((i_ * 512, min((i_ + 1) * 512, DFF)))
                rngs.append((DFF + i_ * 512, min(DFF + (i_ + 1) * 512, 2 * DFF)))
            for (a0, a1) in rngs:
                for kh in range(2):
                    load_cast(lambda a, b, kh=kh, a0=a0: wgu[:, kh * 4:(kh + 1) * 4, a0 + a:a0 + b],
                              lambda a, b, kh=kh, a0=a0: w_gu[kh * 512:(kh + 1) * 512, a0 + a:a0 + b].rearrange("(k p) n -> p k n", p=128),
                              128, 4, a1 - a0,
                              [lambda a, b, kh=kh, a0=a0: [("wgu", kh, c_) for c_ in range((a0 + a) // 256, (a0 + b + 255) // 256)]])
            wd3 = w_down.rearrange("(f p) n -> p f n", p=128)
            for f0 in range(0, NFF, 2):
                load_cast(lambda a, b, f0=f0: wdn[:, f0:f0 + 2, a:b], lambda a, b, f0=f0: wd3[:, f0:f0 + 2, a:b], 128, 2, D,
                          [lambda a, b, f0=f0: [("wdn", f0 // 2)], "g2bc"], scale_bc=lambda a, b: g2b3[:, :, a:b].broadcast_to([128, 2, b - a]))
            for ob in range(8):
                for ti in range(2):
                    r0 = ob * 256 + ti * 128
                    S.dma("sp", f"x1ld{ti}", x1b[:, ti, :], x1d[r0:r0 + 128, :], ["x1d"], [("x1b", ti)])
                    norm_to_fm(x1b[:, ti, :], 128, gsc2, sh2, "sh2", lambda k, ti=ti: h2T[:, k, ti * 128:(ti + 1) * 128], "h2T", src_key=("x1b", ti))
                for f in range(NFF):
                    bg, kg = gbank()
                    for k in range(8):
                        mm(bg[:, 0:256], wgu[:, k, f * 128:(f + 1) * 128], h2T[:, k, :], [("wgu", k // 4, (f * 128) // 256), "h2T"], [kg], start=(k == 0), stop=(k == 7))
                    bu, ku = gbank()
                    for k in range(8):
                        mm(bu[:, 0:256], wgu[:, k, DFF + f * 128:DFF + (f + 1) * 128], h2T[:, k, :], [("wgu", k // 4, (DFF + f * 128) // 256), "h2T"], [ku], start=(k == 0), stop=(k == 7))
                    sj = f % 2
                    act(sgt[sj][:], bg[:, 0:256], AF.Silu, [kg], [f"sgt{sj}"])
                    tt("dve", actT[:, f, :], bu[:, 0:256], sgt[sj][:], ALU.mult, [ku, f"sgt{sj}"], ["actT"])
                for ti in range(2):
                    for hf in range(2):
                        bank, bk = gbank()
                        for f in range(NFF):
                            mm(bank[:], actT[:, f, ti * 128:(ti + 1) * 128], wdn[:, f, hf * 512:(hf + 1) * 512], ["actT", ("wdn", f // 2)], [bk],
                               start=(f == 0), stop=(f == NFF - 1))
                        tt("dve", x1b[:, ti, hf * 512:(hf + 1) * 512], bank[:], x1b[:, ti, hf * 512:(hf + 1) * 512], ALU.add,
                           [bk, ("x1b", ti)], [("x1b", ti)])
                    xa = x1b[:, ti, :]
                    act(junk[:], xa, AF.Square, [("x1b", ti)], ["junk", "ss"], accum_out=ss[:, 0:1])
                    act(ss[:, 1:2], ss[:, 0:1], AF.Sqrt, ["ss"], ["ss1"], bias=RMS_EPS, scale=1.0 / D)
                    S.op("dve", lambda: nc.vector.reciprocal(out=ss[:, 2:3], in_=ss[:, 1:2]), ["ss1"], ["ss2"])
                    stt(ot[:], xa, ss[:, 2:3], fgbc[:], ALU.mult, ALU.mult, [("x1b", ti), "ss2", "fgbc"], ["ot"])
                    r0 = ob * 256 + ti * 128
                    S.dma("sp", "ost", out[r0:r0 + 128, :], ot[:], ["ot"], ["outd"])
            S.barrier()
    return nc


_NC = None


def kernel(**inputs):
    global _NC
    x = np.asarray(inputs["x"], np.float32)
    c = np.asarray(inputs["c"], np.float32)
    if _NC is None:
        _NC = build()
    g = lambda n: np.asarray(inputs[n], np.float32)[0]
    names = ["ada_w", "ada_b", "norm1_g", "pool_w", "pool_scale", "w_out", "norm2_g", "w_ffn_gu", "w_ffn_down"]
    shared = {n: np.ascontiguousarray(g(n)) for n in names}
    shared["final_g"] = np.ascontiguousarray(np.asarray(inputs["final_g"], np.float32))
    w_in, mu_shift = g("w_in"), g("mu_shift")
    in_maps = []
    wins = (2, 4, 8, 16)
    for core in range(8):
        b, qq = core // 4, core % 4
        xpad = np.zeros((T + 16, D), np.float32)
        xpad[16:] = x[b]
        fl = np.zeros((128, 65), np.float32)
        fl[:, 0] = 0.0 if qq == 0 else 1.0
        for gi, w in enumerate(wins):
            for t in range(16):
                fl[:, 1 + gi * 16 + t] = 1.0 / min(qq * TQ + t + 1, w)
        ps_ = slice(qq * 128, (qq + 1) * 128)
        cols = np.concatenate([np.arange(0, 512)] + [np.arange(512 + i * 512 + qq * 128, 512 + i * 512 + (qq + 1) * 128) for i in range(3)]
                              + [np.arange(2048, 2304)])
        mcols = np.stack([mu_shift[i * 512 + qq * 128:i * 512 + (qq + 1) * 128] for i in range(3)]
                         + [mu_shift[1536:1664], mu_shift[1664:1792]], axis=1)
        m = dict(shared)
        m["xb"] = xpad
        m["xo"] = np.ascontiguousarray(xpad[qq * TQ:qq * TQ + TQ + 16])
        m["cvec"] = np.ascontiguousarray(c[b])
        m["flags"] = fl
        m["w_in_sel"] = np.ascontiguousarray(w_in[:, cols])
        m["mu_sel"] = np.ascontiguousarray(mcols)
        m["pv_sel"] = np.ascontiguousarray(np.stack([g(n)[ps_] for n in ("w0", "a0", "k_k", "k_a", "r_k")], axis=1))
        m["ln2"] = np.ascontiguousarray(np.stack([g("lnx_w")[ps_].reshape(2, 64), g("lnx_b")[ps_].reshape(2, 64)], axis=0))
        m["Wd_sel"] = np.ascontiguousarray(g("w_decay_up")[:, ps_])
        m["Wa_sel"] = np.ascontiguousarray(g("w_iclr_up")[:, ps_])
        m["Wg_sel"] = np.ascontiguousarray(g("w_gate_up")[:, ps_])
        in_maps.append(m)
    res = run_bass_kernel_spmd(_NC, in_maps, core_ids=list(range(8)))
    if DEBUG:
        global DBG_OUT
        DBG_OUT = [res.results[core]["dbg"] for core in range(8)]
    outp = np.zeros((2, T, D), np.float32)
    for core in range(8):
        b, qq = core // 4, core % 4
        outp[b, qq * TQ:(qq + 1) * TQ] = res.results[core]["out"]
    return outp
```

```python
import numpy as np
import ml_dtypes
import concourse.bass as bass
import concourse.mybir as mybir
from concourse.bass_utils import run_bass_kernel_spmd

F32 = mybir.dt.float32
BF16 = mybir.dt.bfloat16
ALU = mybir.AluOpType
AF = mybir.ActivationFunctionType
AX = mybir.AxisListType

D = 1024
T = 8192
TQ = 2048
NBLK = T // 512
DFF = 2816
NFF = DFF // 128
GN_EPS = 64e-5
RMS_EPS = 1e-6
DEC = 0.6065306597126334


class Sched:
    def __init__(self, nc, es):
        self.nc = nc
        self.es = es
        self.eng = {"pe": nc.tensor, "act": nc.scalar, "dve": nc.vector, "pool": nc.gpsimd, "sp": nc.sync}
        self.sem = {}
        self.cnt = {}
        self.inc = {}
        self.waited = {}
        self.lastw = {}
        self.readers = {}
        for e in ("pe", "act", "dve", "pool"):
            self.stream(e, 1)

    def stream(self, name, inc):
        if name not in self.sem:
            self.sem[name] = self.es.enter_context(self.nc.semaphore("s_" + name))
            self.cnt[name] = 0
            self.inc[name] = inc
        return name

    def _wait(self, e, s, v):
        if s == e and e == "pe":
            return
        if self.waited.get((e, s), 0) >= v:
            return
        self.eng[e].wait_ge(self.sem[s], v)
        self.waited[(e, s)] = v

    @staticmethod
    def _bank(k):
        if isinstance(k, tuple) and k and k[0] == "TB":
            return "BANK_TB"
        if isinstance(k, str) and (k.startswith("s0_") or k.startswith("s1_")):
            return "BANK_" + k[:4]
        return None

    def _aug(self, keys):
        out = list(keys)
        for k in keys:
            b = self._bank(k)
            if b is not None and b not in out:
                out.append(b)
        return out

    def _deps(self, reads, writes):
        deps = set()
        for r in reads:
            if r in self.lastw:
                deps.add(self.lastw[r])
        for r in writes:
            if r in self.lastw:
                deps.add(self.lastw[r])
            for d in self.readers.get(r, ()):
                deps.add(d)
        return deps

    def _commit(self, s, reads, writes):
        v = self.cnt[s]
        for r in writes:
            self.lastw[r] = (s, v)
            self.readers[r] = []
        for r in reads:
            self.readers.setdefault(r, []).append((s, v))

    def op(self, e, fn, reads=(), writes=()):
        bk = [k for k in self._aug(list(reads) + list(writes)) if isinstance(k, str) and k.startswith("BANK_")]
        reads, writes = list(reads), list(writes) + bk
        for (s, v) in self._deps(reads, writes):
            self._wait(e, s, v)
        ins = fn()
        self.cnt[e] += 1
        ins.then_inc(self.sem[e], 1)
        self._commit(e, reads, writes)
        return ins

    def dma(self, q, s, out, in_, reads=(), writes=(), **kw):
        self.stream(s, 16)
        for (ss, v) in self._deps(reads, writes):
            self._wait(q, ss, v)
        ins = self.eng[q].dma_start(out=out, in_=in_, **kw)
        self.cnt[s] += 16
        ins.then_inc(self.sem[s], 16)
        self._commit(s, reads, writes)
        return ins

    def close(self, s):
        for r, (ss, v) in list(self.lastw.items()):
            if ss == s:
                self.lastw[r] = (s, self.cnt[s])

    def wait_all(self, e):
        for s in self.cnt:
            if self.cnt[s] > 0:
                self._wait(e, s, self.cnt[s])

    def barrier(self):
        for e in ("pe", "act", "dve", "pool", "sp"):
            self.wait_all(e)
        self.lastw.clear()
        self.readers.clear()


STOP = None
DEBUG = False
DBG_MAP = {}
DBG_OUT = None
_LAST_S = None


def build():
    import contextlib

    nc = bass.Bass("TRN2", target_bir_lowering=False)

    def din(name, shape, dt=F32):
        return nc.dram_tensor(name, list(shape), dt, kind="ExternalInput").ap()

    xb = din("xb", [T + 16, D])
    cvec = din("cvec", [D])
    flags = din("flags", [128, 65])
    ada_w = din("ada_w", [D, 6 * D])
    ada_b = din("ada_b", [6 * D])
    norm1_g = din("norm1_g", [D])
    xo = din("xo", [TQ + 16, D])
    w_in = din("w_in_sel", [D, 1152])
    mu_sel = din("mu_sel", [128, 5])
    pv_sel = din("pv_sel", [128, 5])
    ln2 = din("ln2", [2, 2, 64])
    pool_w = din("pool_w", [4, 128, 128])
    pool_scale = din("pool_scale", [512])
    w_decay_up = din("Wd_sel", [64, 128])
    w_iclr_up = din("Wa_sel", [64, 128])
    w_gate_up = din("Wg_sel", [128, 128])
    w_out = din("w_out", [D, D])
    norm2_g = din("norm2_g", [D])
    w_gu = din("w_ffn_gu", [D, 2 * DFF])
    w_down = din("w_ffn_down", [DFF, D])
    final_g = din("final_g", [D])
    out = nc.dram_tensor("out", [TQ, D], F32, kind="ExternalOutput").ap()
    ysrc = [nc.dram_tensor(f"ysrc{j}", [128, TQ], BF16) for j in range(4)]
    ydst = nc.dram_tensor("ydst", [4, 512, TQ], BF16)
    x1d = nc.dram_tensor("x1d", [TQ, D], F32)
    dbg = nc.dram_tensor("dbg", [128, 8192], F32, kind="ExternalOutput").ap() if DEBUG else None

    es = contextlib.ExitStack()
    with es:
        S = Sched(nc, es)
        global _LAST_S
        _LAST_S = S
        pid = nc.gpsimd.partition_id()
        q = pid % 4
        _n = [0]

        def sb(shape, dt=F32, stack=es, name=None):
            _n[0] += 1
            return stack.enter_context(nc.sbuf_tensor(name or f"t{_n[0]}", list(shape), dt))

        def ps(shape, dt=F32, name=None):
            _n[0] += 1
            return es.enter_context(nc.psum_tensor(name or f"p{_n[0]}", list(shape), dt))

        def mm(out, lhsT, rhs, r, w, start=True, stop=True):
            return S.op("pe", lambda: nc.tensor.matmul(out, lhsT, rhs, start=start, stop=stop), r, w)

        def tr(out, in_, ident, r, w):
            return S.op("pe", lambda: nc.tensor.transpose(out, in_, ident), r, w)

        def act(out, in_, func, r, w, bias=None, scale=None, accum_out=None):
            kw = {}
            if bias is not None:
                kw["bias"] = bias
            if scale is not None:
                kw["scale"] = scale
            if accum_out is not None:
                kw["accum_out"] = accum_out
            return S.op("act", lambda: nc.scalar.activation(out=out, in_=in_, func=func, **kw), r, w)

        def tt(e, out, in0, in1, op, r, w):
            eng = nc.vector if e == "dve" else nc.gpsimd
            return S.op(e, lambda: eng.tensor_tensor(out=out, in0=in0, in1=in1, op=op), r, w)

        def tsc(e, out, in0, s1, s2, op0, op1, r, w):
            eng = nc.vector if e == "dve" else nc.gpsimd
            if op1 is None:
                return S.op(e, lambda: eng.tensor_scalar(out=out, in0=in0, scalar1=s1, scalar2=None, op0=op0), r, w)
            return S.op(e, lambda: eng.tensor_scalar(out=out, in0=in0, scalar1=s1, scalar2=s2, op0=op0, op1=op1), r, w)

        def stt(out, in0, scalar, in1, op0, op1, r, w):
            return S.op("dve", lambda: nc.vector.scalar_tensor_tensor(out=out, in0=in0, scalar=scalar, in1=in1, op0=op0, op1=op1), r, w)

        def cp(e, out, in_, r, w):
            if e == "act":
                return S.op("act", lambda: nc.scalar.copy(out=out, in_=in_), r, w)
            eng = nc.vector if e == "dve" else nc.gpsimd
            return S.op(e, lambda: eng.tensor_copy(out=out, in_=in_), r, w)

        def mset(e, ap, val, w):
            eng = nc.vector if e == "dve" else nc.gpsimd
            return S.op(e, lambda: eng.memset(ap, val), (), w)

        NW = 3
        PS = [ps([128, 512], name=f"PS{i}") for i in range(8)]
        SB0 = [PS[p] for p in range(NW)]
        SB1 = [PS[NW + p] for p in range(NW)]
        TB = PS[6]
        GBL = [0, 1, 2, 3, 4, 5, 7]
        _gb = [0]

        def gbank():
            i = GBL[_gb[0] % len(GBL)]
            _gb[0] += 1
            return PS[i], f"GB{i}"

        ones_f = sb([128, 128])
        ident_f = sb([128, 128])
        ident_bf = sb([128, 128], BF16)
        flg = sb([128, 65])
        c_fm = sb([128, 8])
        n1g = sb([128, 8])
        n2g = sb([128, 8])
        mu = sb([128, 5])
        pscale = sb([128, 4])
        pv = sb([128, 8])
        fgbc = sb([128, D])
        omm = sb([128, 5])
        stage = [sb([128, 2048]) for _ in range(2)]
        g2bc = sb([128, D])
        modfm = sb([128, 48])
        gsc1 = sb([128, 8])
        gsc2 = sb([128, 8])
        xn = sb([128, D])
        junk = sb([128, D], BF16)
        ss = sb([128, 4])
        dbgt = sb([128, 512]) if DEBUG else None
        pA = contextlib.ExitStack()
        es.enter_context(pA)
        M_TS = sb([128, 2, 128], stack=pA)
        M_A3 = sb([128, 3, 128], stack=pA)
        Blk = sb([128, 128], stack=pA)
        E_bf = sb([128, 64], BF16, stack=pA)
        ones_bf = sb([128, 2], BF16, stack=pA)
        rmask = sb([128, 512], stack=pA)
        lnw_bc = sb([128, 64], stack=pA)
        lnb_bc = sb([128, 64], stack=pA)
        Wd = sb([128, 128], stack=pA)
        Wa = sb([128, 128], stack=pA)
        Wg = sb([128, 128], stack=pA)
        pw_f = sb([128, 4, 128], stack=pA)
        pw_bf = sb([128, 4, 128], BF16, stack=pA)
        g1bc = sb([128, D], stack=pA)
        xt = [sb([128, D], stack=pA) for _ in range(2)]
        mixp = sb([128, 4, TQ], BF16, stack=pA)
        mset("pool", ones_f[:], 1.0, ["ones_f"])

        def asel(out, pattern, cmul, cmp, w):
            return S.op("pool", lambda: nc.gpsimd.affine_select(out=out, in_=ones_f[:], pattern=pattern, compare_op=cmp,
                                                                fill=0.0, base=0, channel_multiplier=cmul), ["ones_f"], w)

        asel(ident_f[:], [[1, 128]], -1, ALU.is_equal, ["ident_f"])
        asel(M_TS[:, 0, :], [[1, 128]], -1, ALU.is_gt, ["M_TS"])
        asel(M_TS[:, 1, :], [[-1, 128]], 1, ALU.is_gt, ["M_TS"])
        asel(M_A3[:, 0, :], [[1, 128]], -1, ALU.is_ge, ["M_A3"])
        asel(M_A3[:, 1, :], [[1, 128]], -1, ALU.is_gt, ["M_A3"])
        asel(M_A3[:, 2, :], [[1, 128]], -1, ALU.is_ge, ["M_A3"])
        cp("pool", ident_bf[:], ident_f[:], ["ident_f"], ["ident_bf"])
        mset("pool", Blk[:], 0.0, ["Blk"])
        mset("pool", Blk[0:64, 0:64], 1.0, ["Blk"])
        mset("pool", Blk[64:128, 64:128], 1.0, ["Blk"])
        tt("pool", E_bf[:], ident_f[:, 0:64], ident_f[:, 64:128], ALU.add, ["ident_f"], ["E_bf"])
        mset("pool", ones_bf[:], 1.0, ["ones_bf"])
        mset("pool", rmask[:], 1.0, ["rmask"])
        mset("pool", rmask[:].rearrange("p (c t) -> p c t", t=64)[:, :, 0:1], 0.0, ["rmask"])

        def pl(dst, src, w, **kw):
            S.dma("pool", "init", dst, src, (), w, **kw)

        def fm(v, k):
            return v.rearrange("(k p) -> p k", p=128)

        pl(flg[:], flags[:, :], ["flg"])
        pl(c_fm[:], fm(cvec, 8), ["c_fm"], allow_slow_non_contiguous=True)
        pl(n1g[:], fm(norm1_g, 8), ["n1g"], allow_slow_non_contiguous=True)
        pl(n2g[:], fm(norm2_g, 8), ["n2g"], allow_slow_non_contiguous=True)
        pl(mu[:], mu_sel[:, :], ["mu"])
        pl(pscale[:], fm(pool_scale, 4), ["pscale"], allow_slow_non_contiguous=True)
        pl(pv[:, 0:5], pv_sel[:, :], ["pv"])
        for h in range(2):
            pl(lnw_bc[h * 64:(h + 1) * 64, :], ln2[0, h:h + 1, :].broadcast_to([64, 64]), ["lnw_bc"])
            pl(lnb_bc[h * 64:(h + 1) * 64, :], ln2[1, h:h + 1, :].broadcast_to([64, 64]), ["lnb_bc"])
        pl(Wd[0:64, :], w_decay_up[:, :], ["Wd"])
        pl(Wa[64:128, :], w_iclr_up[:, :], ["Wa"])
        pl(Wg[:], w_gate_up[:, :], ["Wg"])
        pl(pw_f[:], pool_w.rearrange("g c d -> c g d"), ["pw_f"])
        pl(fgbc[:], final_g.partition_broadcast(128), ["fgbc"])
        S.close("init")
        cp("pool", pw_bf[:], pw_f[:], ["pw_f"], ["pw_bf"])
        tsc("pool", omm[:], mu[:], -1.0, 1.0, ALU.mult, ALU.add, ["mu"], ["omm"])

        if STOP == "setup":
            S.barrier()
            return nc
        _st = [0]
        _ce = [0]

        def load_cast(dst_ap_fn, src_ap_fn, nparts, K, N, w, scale_bc=None, part0=0):
            ncol = max(1, 2048 // K)
            for n0 in range(0, N, ncol):
                n1 = min(N, n0 + ncol)
                i = _st[0] % 2
                _st[0] += 1
                sv = stage[i][part0:part0 + nparts, 0:K * (n1 - n0)].rearrange("p (k n) -> p k n", k=K)
                S.dma(("sp", "act")[i], f"stg{i}", sv, src_ap_fn(n0, n1), (), [f"stage{i}"])
                e = ("act", "dve")[_ce[0] % 2]
                _ce[0] += 1
                wk = w[0](n0, n1) if callable(w[0]) else [w[0]]
                if scale_bc is not None:
                    tt("dve" if e == "act" else e, dst_ap_fn(n0, n1), sv, scale_bc(n0, n1), ALU.mult, [f"stage{i}"] + w[1:], wk)
                else:
                    cp(e, dst_ap_fn(n0, n1), sv, [f"stage{i}"], wk)

        _dc = [0]

        def dump(name, ap, n, e="dve"):
            if not DEBUG:
                return
            c0 = _dc[0]
            _dc[0] += n
            DBG_MAP[name] = (c0, n)
            S.wait_all(e)
            npart = ap.shape[0]
            cp(e, dbgt[0:npart, 0:n], ap, [], ["dbgt"])
            S.dma("sp", "dbgs", dbg[0:npart, c0:c0 + n], dbgt[0:npart, 0:n], ["dbgt"], ["dbgd"])

        with contextlib.ExitStack() as p0:
            csil = sb([128, 8, 1], stack=p0)
            crep = sb([128, 8, 128], stack=p0)
            adab = [sb([128, 512], stack=p0) for _ in range(2)]
            mblk = sb([128, 512], stack=p0)
            tmp4 = sb([128, 4, 128], stack=p0)
            act(csil[:, :, 0], c_fm[:], AF.Silu, ["c_fm"], ["csil"])
            cp("dve", crep[:], csil[:].broadcast_to([128, 8, 128]), ["csil"], ["crep"])
            for cb in range(12):
                bank, bk = gbank()
                for k2 in range(4):
                    j = _st[0] % 2
                    _st[0] += 1
                    sv = stage[j][:, 0:1024].rearrange("p (k n) -> p k n", k=2)
                    S.dma(("sp", "act")[j], f"stg{j}", sv,
                          ada_w[k2 * 256:(k2 + 1) * 256, cb * 512:(cb + 1) * 512].rearrange("(k p) n -> p k n", p=128),
                          (), [f"stage{j}"])
                    for kk in range(2):
                        k = k2 * 2 + kk
                        mm(bank[:], crep[:, k, :], sv[:, kk, :], ["crep", f"stage{j}"], [bk], start=(k == 0), stop=(k == 7))
                a = cb % 2
                S.dma("sp", f"adab{a}", adab[a][:], ada_b[cb * 512:(cb + 1) * 512].partition_broadcast(128), (), [f"adab{a}"])
                sec = cb // 2
                if sec == 2:
                    dst, dk = g1bc[:, (cb % 2) * 512:(cb % 2 + 1) * 512], "g1bc"
                elif sec == 5:
                    dst, dk = g2bc[:, (cb % 2) * 512:(cb % 2 + 1) * 512], "g2bc"
                else:
                    dst, dk = mblk[:], "mblk"
                tt("dve", dst, bank[:], adab[a][:], ALU.add, [bk, f"adab{a}"], [dk])
                tt("dve", tmp4[:], dst.rearrange("p (a b) -> p a b", b=128),
                   ident_f[:].rearrange("p (a b) -> p a b", a=1).broadcast_to([128, 4, 128]), ALU.mult, [dk, "ident_f"], ["tmp4"])
                S.op("dve", lambda: nc.vector.tensor_reduce(out=modfm[:, cb * 4:(cb + 1) * 4], in_=tmp4[:], axis=AX.X, op=ALU.add),
                     ["tmp4"], ["modfm"])
            S.barrier()
        if STOP == "ada":
            return nc
        stt(gsc1[:], modfm[:, 8:16], 1.0, n1g[:], ALU.add, ALU.mult, ["modfm", "n1g"], ["gsc1"])
        stt(gsc2[:], modfm[:, 32:40], 1.0, n2g[:], ALU.add, ALU.mult, ["modfm", "n2g"], ["gsc2"])
        dump("modfm", modfm[:], 48)
        sh1 = modfm[:, 0:8]
        sh2 = modfm[:, 24:32]

        _xt = [0]

        def norm_to_fm(x_ap, nrow, gsc, sh, shk, dst_fn, dst_key, dq="sp", src_key=None, keep=None):
            if src_key is None:
                i = _xt[0] % 2
                _xt[0] += 1
                xs, xk = xt[i], f"xt{i}"
                S.dma(dq, f"xld{i}", xs[0:nrow, :], x_ap, (), [xk])
                xa = xs[0:nrow, :]
            else:
                xa, xk = x_ap, src_key
            act(junk[0:nrow, :], xa, AF.Square, [xk], ["junk", "ss"], accum_out=ss[0:nrow, 0:1])
            act(ss[0:nrow, 1:2], ss[0:nrow, 0:1], AF.Sqrt, ["ss"], ["ss1"], bias=RMS_EPS, scale=1.0 / D)
            S.op("dve", lambda: nc.vector.reciprocal(out=ss[0:nrow, 2:3], in_=ss[0:nrow, 1:2]), ["ss1"], ["ss2"])
            act(xn[0:nrow, :], xa, AF.Copy, [xk, "ss2"], ["xn"], scale=ss[0:nrow, 2:3])
            for half in range(2):
                bank, bk = gbank()
                for kk_ in range(4):
                    k = half * 4 + kk_
                    tr(bank[:, kk_ * 128:kk_ * 128 + nrow], xn[0:nrow, k * 128:(k + 1) * 128], ident_f[0:nrow, 0:nrow], ["xn", "ident_f"], [bk])
                for kk_ in range(4):
                    k = half * 4 + kk_
                    tsc("dve", dst_fn(k), bank[:, kk_ * 128:kk_ * 128 + nrow], gsc[:, k:k + 1], sh[:, k:k + 1], ALU.mult, ALU.add,
                        [bk, "gsc1", "gsc2", "modfm"], [dst_key])
            return xa, xk

        def norm_gen(x_ap, nrow, gsc, sh, shk, dst_fn, dst_key, dq="sp", src_key=None, keep=None):
            if src_key is None:
                i = _xt[0] % 2
                _xt[0] += 1
                xs, xk = xt[i], f"xt{i}"
                S.dma(dq, f"xld{i}", xs[0:nrow, :], x_ap, (), [xk])
                xa = xs[0:nrow, :]
            else:
                xa, xk = x_ap, src_key
            act(junk[0:nrow, :], xa, AF.Square, [xk], ["junk", "ss"], accum_out=ss[0:nrow, 0:1])
            yield
            act(ss[0:nrow, 1:2], ss[0:nrow, 0:1], AF.Sqrt, ["ss"], ["ss1"], bias=RMS_EPS, scale=1.0 / D)
            yield
            S.op("dve", lambda: nc.vector.reciprocal(out=ss[0:nrow, 2:3], in_=ss[0:nrow, 1:2]), ["ss1"], ["ss2"])
            yield
            act(xn[0:nrow, :], xa, AF.Copy, [xk, "ss2"], ["xn"], scale=ss[0:nrow, 2:3])
            yield
            for half in range(2):
                bank, bk = gbank()
                for kk_ in range(4):
                    k = half * 4 + kk_
                    tr(bank[:, kk_ * 128:kk_ * 128 + nrow], xn[0:nrow, k * 128:(k + 1) * 128], ident_f[0:nrow, 0:nrow], ["xn", "ident_f"], [bk])
                yield
                for kk_ in range(4):
                    k = half * 4 + kk_
                    tsc("dve", dst_fn(k), bank[:, kk_ * 128:kk_ * 128 + nrow], gsc[:, k:k + 1], sh[:, k:k + 1], ALU.mult, ALU.add,
                        [bk, "gsc1", "gsc2", "modfm"], [dst_key])
                yield

        with contextlib.ExitStack() as p1:
            w_in_bf = sb([128, 8, 9 * 128], BF16, stack=p1)
            w3 = w_in.rearrange("(k p) n -> p k n", p=128)
            load_cast(lambda a, b: w_in_bf[:, :, a:b], lambda a, b: w3[:, :, a:b], 128, 8, 1152, ["w_in_bf"])

            if STOP == "w_in":
                S.barrier()
                return nc
            hT = sb([128, 8, 528], BF16, stack=p1)
            dn = [sb([128, 528], stack=p1) for _ in range(12)]
            zT = [sb([128, 5, 513], stack=p1) for _ in range(2)]
            gT = [sb([128, 2, 512], stack=p1) for _ in range(2)]
            yTb = [sb([128, 2, 512], BF16, stack=p1) for _ in range(2)]
            gamC = [sb([128, 8], stack=p1) for _ in range(2)]
            Hs = [sb([128, 64], stack=p1) for _ in range(2)]
            AR_bd = [sb([128, 8, 256], BF16, stack=p1) for _ in range(2)]
            B_bd = [sb([128, 8, 128], BF16, stack=p1) for _ in range(2)]
            K_bd = [sb([128, 8, 128], BF16, stack=p1) for _ in range(2)]
            BH_bd = [sb([128, 8, 128], BF16, stack=p1) for _ in range(2)]
            KH_bd = [sb([128, 8, 128], BF16, stack=p1) for _ in range(2)]
            V_bd = [sb([128, 8, 128], BF16, stack=p1) for _ in range(2)]
            RRK_bd = [sb([128, 8, 128], BF16, stack=p1) for _ in range(2)]
            Xi = [sb([128, 3, 128], stack=p1) for _ in range(NW)]
            Mb = [sb([128, 128], BF16, stack=p1) for _ in range(NW)]
            A3 = [sb([128, 3, 128], BF16, stack=p1) for _ in range(NW)]
            R3 = [sb([128, 192], BF16, stack=p1) for _ in range(NW)]
            BK = [sb([128, 2, 128], BF16, stack=p1) for _ in range(NW)]
            Vst = [sb([128, 64], BF16, stack=p1) for _ in range(NW)]
            WU = [sb([128, 192], BF16, stack=p1) for _ in range(NW)]
            PT = [sb([128, 128], stack=p1) for _ in range(NW)]
            RH = [sb([128, 128], stack=p1) for _ in range(NW)]
            sm = [sb([128, 16], stack=p1) for _ in range(NW)]
            yfin = [sb([128, 64], stack=p1) for _ in range(NW)]
            yh = [sb([128, 64], stack=p1) for _ in range(NW)]
            for p in range(2):
                for tl, nm in ((AR_bd, "AR"), (B_bd, "B"), (K_bd, "K"), (BH_bd, "BH"), (KH_bd, "KH"), (V_bd, "V"), (RRK_bd, "RRK")):
                    mset("pool", tl[p][:], 0.0, [f"{nm}{p}"])
                mset("pool", Hs[p][:], 0.0, [f"H{p}"])

            uT = dn[0]
            for ob in range(4):
                norm_to_fm(xo[ob * 512:ob * 512 + 16, :], 16, gsc1, sh1, "sh1",
                           lambda k: hT[:, k, 0:16], "hT")
                if STOP == "norm16":
                    S.barrier()
                    return nc
                for ti in range(4):
                    norm_to_fm(xo[16 + ob * 512 + ti * 128:16 + ob * 512 + (ti + 1) * 128, :], 128, gsc1, sh1, "sh1",
                               lambda k, ti=ti: hT[:, k, 16 + ti * 128:16 + (ti + 1) * 128], "hT")
                if STOP == "norm":
                    S.barrier()
                    return nc
                if ob == 0:
                    for k_ in range(8):
                        dump(f"hT{k_}", hT[:, k_, 0:144], 144)
                for g in range(4):
                    if STOP == "g1" and g == 1:
                        S.barrier()
                        return nc
                    w = (2, 4, 8, 16)[g]
                    bank, bk = gbank()
                    for k in range(8):
                        mm(bank[:], w_in_bf[:, k, g * 128:(g + 1) * 128], hT[:, k, 16:528], ["w_in_bf", "hT"], [bk], start=(k == 0), stop=(k == 7))
                    cp("act", uT[:, 16:528], bank[:], [bk], ["dn0"])
                    bank2, bk2 = gbank()
                    for k in range(8):
                        mm(bank2[:, 0:16], w_in_bf[:, k, g * 128:(g + 1) * 128], hT[:, k, 0:16], ["w_in_bf", "hT"], [bk2], start=(k == 0), stop=(k == 7))
                    if ob == 0:
                        tsc("dve", uT[:, 0:16], bank2[:, 0:16], flg[:, 0:1], None, ALU.mult, None, [bk2, "flg"], ["dn0"])
                    else:
                        cp("dve", uT[:, 0:16], bank2[:, 0:16], [bk2], ["dn0"])
                    src, sk = uT, "dn0"
                    sh_ = 1
                    lvl = 0
                    while sh_ < w:
                        dst, dk = dn[1 + lvl % 2], f"dn{1 + lvl % 2}"
                        tt("pool", dst[:, sh_:528], src[:, sh_:528], src[:, 0:528 - sh_], ALU.add, [sk], [dk])
                        src, sk = dst, dk
                        sh_ *= 2
                        lvl += 1
                    dd = dn[3]
                    stt(dd[:, 16:528], src[:, 16:528], 1.0 / w, uT[:, 16:528], ALU.mult, ALU.subtract, [sk, "dn0"], ["dn3"])
                    if ob == 0:
                        tt("dve", dn[4][:, 0:16], src[:, 16:32], flg[:, 1 + g * 16:1 + (g + 1) * 16], ALU.mult, [sk, "flg"], ["dn4"])
                        tt("dve", dd[:, 16:32], dn[4][:, 0:16], uT[:, 16:32], ALU.subtract, ["dn4", "dn0", "dn3"], ["dn3"])
                    dbfT = dn[5][:, 0:256].bitcast(BF16)
                    cp("act", dbfT, dd[:, 16:528], ["dn3"], ["dn5"])
                    bank3, bk3 = gbank()
                    mm(bank3[:], pw_bf[:, g, :], dbfT, ["pw_bf", "dn5"], [bk3])
                    tsc("dve", mixp[:, g, ob * 512:(ob + 1) * 512], bank3[:], pscale[:, g:g + 1], None, ALU.mult, None, [bk3, "pscale"], ["mixp"])

            for g_ in range(4):
                dump(f"mixp{g_}", mixp[:, g_, 0:128], 128)
            if STOP == "pool":
                S.barrier()
                return nc
            S.barrier()
            GBL[:] = [7]
            ysv = [ysrc[j].ap().rearrange("(h i) t -> i h t", i=64) for j in range(4)]
            S.stream("cc", 1)

            ymark = {}

            def exchange(j):
                for st_, v_ in ymark[j]:
                    S._wait("pool", st_, v_)
                nc.gpsimd.collective_compute("AllGather", ALU.bypass, replica_groups=[[0, 1, 2, 3], [4, 5, 6, 7]],
                                             ins=[ysrc[j].ap().opt()], outs=[ydst.ap()[j].opt()]).then_inc(S.sem["cc"], 1)
                S.cnt["cc"] += 1
            state = {"pk": 0}

            def prep(blk):
                bp = blk % 2
                zt, zk = zT[bp], f"zT{bp}"
                for ti in range(4):
                    yield from norm_gen(xb[16 + blk * 512 + ti * 128:16 + blk * 512 + (ti + 1) * 128, :], 128, gsc1, sh1, "sh1",
                                        lambda k, ti=ti: hT[:, k, 16 + ti * 128:16 + (ti + 1) * 128], "hT")
                if blk == 0:
                    mset("pool", zt[:, :, 0:1], 0.0, [zk])
                else:
                    cp("pool", zt[:, :, 0:1], zT[1 - bp][:, :, 512:513], [f"zT{1 - bp}"], [zk])
                    yield
                for m in range(5):
                    bank, bk = gbank()
                    for k in range(8):
                        mm(bank[:], w_in_bf[:, k, 512 + m * 128:512 + (m + 1) * 128], hT[:, k, 16:528], ["w_in_bf", "hT"], [bk], start=(k == 0), stop=(k == 7))
                    yield
                    cp("act", zt[:, m, 1:513], bank[:], [bk], [zk])
                    yield
                zs = []
                for m in range(5):
                    tmp = dn[11]
                    act(tmp[:, 0:512], zt[:, m, 0:512], AF.Copy, [zk, "mu"], ["dn11"], scale=mu[:, m:m + 1])
                    yield
                    dst = dn[m]
                    stt(dst[:, 0:512], zt[:, m, 1:513], omm[:, m:m + 1], tmp[:, 0:512], ALU.mult, ALU.add, [zk, "omm", "dn11"], [f"dn{m}"])
                    yield
                    zs.append(dst)
                rT, kT, vT, xwa, xg = [z[:, 0:512] for z in zs]
                thx = dn[5]
                act(thx[0:64, 0:512], xwa[0:64, :], AF.Tanh, ["dn3"], ["dn5"])
                yield
                bank, bk = gbank()
                mm(bank[:], Wd[0:64, :], thx[0:64, 0:512], ["Wd", "dn5"], [bk])
                yield
                sg = dn[6]
                act(sg[:, 0:512], bank[:], AF.Sigmoid, [bk, "pv"], ["dn6"], bias=pv[:, 0:1])
                yield
                bank, bk = gbank()
                mm(bank[:], Wa[64:128, :], xwa[64:128, :], ["Wa", "dn3"], [bk])
                yield
                aT = dn[7]
                act(aT[:, 0:512], bank[:], AF.Sigmoid, [bk, "pv"], ["dn7"], bias=pv[:, 1:2])
                yield
                sgx = dn[5]
                act(sgx[:, 0:512], xg, AF.Sigmoid, ["dn4"], ["dn5"])
                yield
                for h in range(2):
                    bank, bk = gbank()
                    mm(bank[0:64, :], Wg[:, h * 64:(h + 1) * 64], sgx[:, 0:512], ["Wg", "dn5"], [bk])
                    yield
                    cp("act", gT[bp][0:64, h, :], bank[0:64, :], [bk], [f"gT{bp}"])
                    yield
                cs = dn[8]
                S.op("dve", lambda: nc.vector.tensor_tensor_scan(out=cs[:, 0:512], data0=rmask[:], data1=sg[:, 0:512], initial=0.0,
                                                                 op0=ALU.mult, op1=ALU.add), ["rmask", "dn6"], ["dn8"])
                yield
                epos = dn[9]
                act(epos[:, 0:512], cs[:, 0:512], AF.Exp, ["dn8"], ["dn9"], scale=-DEC)
                yield
                cp("pool", gamC[bp][:], epos[:, 0:512].rearrange("p (c t) -> p c t", t=64)[:, :, 63], ["dn9"], [f"gamC{bp}"])
                yield
                eneg = dn[10]
                act(eneg[:, 0:512], cs[:, 0:512], AF.Exp, ["dn8"], ["dn10"], scale=DEC)
                yield
                tt("dve", cs[:, 0:512], cs[:, 0:512], sg[:, 0:512], ALU.subtract, ["dn8", "dn6"], ["dn8"])
                yield
                eprev = dn[6]
                act(eprev[:, 0:512], cs[:, 0:512], AF.Exp, ["dn8"], ["dn6"], scale=-DEC)
                yield
                kkr = dn[3]
                tsc("pool", kkr[:, 0:512], kT, pv[:, 2:3], None, ALU.mult, None, ["dn1", "pv"], ["dn3"])
                yield
                sq = dn[4]
                tt("pool", sq[:, 0:512], kkr[:, 0:512], kkr[:, 0:512], ALU.mult, ["dn3"], ["dn4"])
                yield
                bank, bk = gbank()
                mm(bank[:], Blk[:], sq[:, 0:512], ["Blk", "dn4"], [bk])
                yield
                act(sq[:, 0:512], bank[:], AF.Sqrt, [bk], ["dn4"])
                yield
                tsc("dve", sq[:, 0:512], sq[:, 0:512], 1e-12, None, ALU.max, None, ["dn4"], ["dn4"])
                yield
                S.op("dve", lambda: nc.vector.reciprocal(out=sq[:, 0:512], in_=sq[:, 0:512]), ["dn4"], ["dn4"])
                yield
                kk = dn[3]
                tt("dve", kk[:, 0:512], kkr[:, 0:512], sq[:, 0:512], ALU.mult, ["dn3", "dn4"], ["dn3"])
                yield
                t1 = dn[4]
                tsc("dve", t1[:, 0:512], aT[:, 0:512], -1.0, pv[:, 3:4], ALU.add, ALU.mult, ["dn7", "pv"], ["dn4"])
                yield
                kp = dn[8]
                stt(kp[:, 0:512], t1[:, 0:512], 1.0, kT, ALU.add, ALU.mult, ["dn4", "dn1"], ["dn8"])
                yield
                tt("pool", aT[:, 0:512], aT[:, 0:512], kk[:, 0:512], ALU.mult, ["dn7", "dn3"], ["dn7"])
                yield
                stt(t1[:, 0:512], rT, pv[:, 4:5], kp[:, 0:512], ALU.mult, ALU.mult, ["dn0", "pv", "dn8"], ["dn4"])
                yield
                stt(kk[:, 0:512], kk[:, 0:512], -1.0, eprev[:, 0:512], ALU.mult, ALU.mult, ["dn3", "dn6"], ["dn3"])
                yield
                tt("pool", aT[:, 0:512], aT[:, 0:512], eneg[:, 0:512], ALU.mult, ["dn7", "dn10"], ["dn7"])
                yield
                tt("dve", kp[:, 0:512], kp[:, 0:512], eneg[:, 0:512], ALU.mult, ["dn8", "dn10"], ["dn8"])
                yield

                if blk == 0:
                    for nm_, t__ in (("rT", rT), ("vT", vT), ("atil", kk[:, 0:512]), ("btil", aT[:, 0:512]), ("ktil", kp[:, 0:512]),
                                     ("epos", epos[:, 0:512]), ("rrk", t1[:, 0:512])):
                        dump(nm_, t__[:, 0:128], 128)
                    for h_ in range(2):
                        dump(f"gT{h_}", gT[bp][0:64, h_, 0:64], 64)

                def c3(t_):
                    return t_.rearrange("p (c t) -> p c t", t=64)

                gam3 = gamC[bp][:].rearrange("p (c o) -> p c o", o=1)
                for h in range(2):
                    hs = slice(h * 64, (h + 1) * 64)
                    cs_ = slice(h * 64, (h + 1) * 64)
                    e1 = "dve" if h == 0 else "pool"
                    cp(e1, AR_bd[bp][hs, :, cs_], c3(kk[hs, 0:512]), ["dn3"], [f"AR{bp}"])
                    yield
                    tt(e1, AR_bd[bp][hs, :, 128 + h * 64:128 + (h + 1) * 64], c3(rT[hs, :]), c3(epos[hs, 0:512]), ALU.mult, ["dn0", "dn9"], [f"AR{bp}"])
                    yield
                    cp(e1, B_bd[bp][hs, :, cs_], c3(aT[hs, 0:512]), ["dn7"], [f"B{bp}"])
                    yield
                    cp(e1, K_bd[bp][hs, :, cs_], c3(kp[hs, 0:512]), ["dn8"], [f"K{bp}"])
                    yield
                    tt(e1, BH_bd[bp][hs, :, cs_], c3(aT[hs, 0:512]), gam3[hs].broadcast_to([64, 8, 64]), ALU.mult, ["dn7", f"gamC{bp}"], [f"BH{bp}"])
                    yield
                    tt(e1, KH_bd[bp][hs, :, cs_], c3(kp[hs, 0:512]), gam3[hs].broadcast_to([64, 8, 64]), ALU.mult, ["dn8", f"gamC{bp}"], [f"KH{bp}"])
                    yield
                    cp(e1, V_bd[bp][hs, :, cs_], c3(vT[hs, :]), ["dn2"], [f"V{bp}"])
                    yield
                    cp(e1, RRK_bd[bp][hs, :, cs_], c3(t1[hs, 0:512]), ["dn4"], [f"RRK{bp}"])
                    yield

            def pack(blk, c):
                bp = blk % 2
                g = state["pk"]
                state["pk"] += 1
                import os
                pp = (g + int(os.environ.get("PPX", "0"))) % NW
                hc, hn = g % 2, (g + 1) % 2
                s0, s1 = SB0[pp], SB1[pp]
                I0, I1, I2, KAV, KVS = [f"s0_{pp}_{i}" for i in range(5)]
                k1 = [f"s1_{pp}_{i}" for i in range(5)]
                tb = [("TB", j) for j in range(3)]
                X, a3, r3, bk_, vs, wu, pt, rh, smm = Xi[pp], A3[pp], R3[pp], BK[pp], Vst[pp], WU[pp], PT[pp], RH[pp], sm[pp]
                kX, kA3, kR3, kBK, kV, kWU, kPT, kRH = f"X{pp}", f"A3{pp}", f"R3{pp}", f"BK{pp}", f"Vst{pp}", f"WU{pp}", f"PT{pp}", f"RH{pp}"
                ar, bb, kb, bh, kh, vb, rrk = AR_bd[bp], B_bd[bp], K_bd[bp], BH_bd[bp], KH_bd[bp], V_bd[bp], RRK_bd[bp]
                s0v = s0[:, 0:384].rearrange("p (a b) -> p a b", b=128)
                mm(s0[:, 0:128], bb[:, c, :], ar[:, c, 0:128], [f"B{bp}", f"AR{bp}"], [I0])
                mm(s0[:, 256:384], ar[:, c, 0:128], bb[:, c, :], [f"B{bp}", f"AR{bp}"], [I2])
                mm(s1[:, 0:128], bb[:, c, :], ar[:, c, 128:256], [f"B{bp}", f"AR{bp}"], [k1[0]])
                mm(s1[:, 128:384], kb[:, c, :], ar[:, c, :], [f"K{bp}", f"AR{bp}"], [k1[1], k1[2], k1[3]])
                o = 0
                mm(TB[:, o:o + 128], ar[:, c, 0:128], ident_bf[:], [f"AR{bp}", "ident_bf"], [tb[0]])
                mm(TB[:, o + 128:o + 256], bh[:, c, :], ident_bf[:], [f"BH{bp}", "ident_bf"], [tb[1]])
                mm(TB[:, o + 256:o + 384], kh[:, c, :], ident_bf[:], [f"KH{bp}", "ident_bf"], [tb[2]])
                mm(s0[:, 448:512], vb[:, c, :], E_bf[:], [f"V{bp}", "E_bf"], [KVS])
                import os
                if os.environ.get("RKTB", "0") == "1":
                    mm(TB[:, 384:386], rrk[:, c, :], ones_bf[:], [f"RRK{bp}", "ones_bf"], [k1[4]])
                else:
                    mm(s1[:, 384:386], rrk[:, c, :], ones_bf[:], [f"RRK{bp}", "ones_bf"], [k1[4]])
                yield
                import os
                OPS = os.environ.get("OPS", "abcdefg")
                if "a" in OPS:
                    tt("dve", X[:, 0:3:2, :], s0v[:, 0:3:2, :], M_TS[:], ALU.mult, [I0, I2, "M_TS"], [kX])
                if "b" in OPS:
                    tt("dve", a3[:], s1[:, 0:384].rearrange("p (a b) -> p a b", b=128), M_A3[:], ALU.mult, [k1[0], k1[1], k1[2], k1[3], "M_A3"], [kA3])
                if "c" in OPS:
                    tt("pool", X[:, 1, :], X[:, 0, :], ident_f[:], ALU.add, [kX, "ident_f"], [kX])
                if "d" in OPS:
                    cp("act", r3[:, 0:128], TB[:, o:o + 128], [tb[0]], [kR3])
                if "e" in OPS:
                    cp("act", bk_[:], TB[:, o + 128:o + 384].rearrange("p (a b) -> p a b", b=128), [tb[1], tb[2]], [kBK])
                if "f" in OPS:
                    cp("act", vs[:], s0[:, 448:512], [KVS], [kV])
                if "g" in OPS:
                    cp("act", smm[:, 0:1], s1[:, 384:385], [k1[4]], [f"sm{pp}rk"])
                yield
                mm(s0[:, 0:128], X[:, 2, :], X[:, 0, :], [kX], [I0])
                mm(s0[:, 256:384], X[:, 0, :], X[:, 2, :], [kX], [I2])
                yield
                cp("act", X[:, 0:3:2, :], s0v[:, 0:3:2, :], [I0, I2], [kX])
                yield
                for lv in range(1, 5):
                    mm(s0[:, 0:256], X[:, 2, :], X[:, 0:2, :].rearrange("p a b -> p (a b)"), [kX], [I0, I1])
                    mm(s0[:, 256:384], X[:, 0, :], X[:, 2, :], [kX], [I2])
                    yield
                    tt("dve", X[:, 1, :], s0[:, 128:256], X[:, 1, :], ALU.add, [I1, kX], [kX])
                    cp("act", X[:, 0:3:2, :], s0v[:, 0:3:2, :], [I0, I2], [kX])
                    yield
                mm(s0[:, 128:256], X[:, 2, :], X[:, 1, :], [kX], [I1])
                mm(s0[:, 384:448], a3[:, 1, :], vs[:], [kA3, kV], [KAV])
                yield
                tt("dve", Mb[pp][:], s0[:, 128:256], X[:, 1, :], ALU.add, [I1, kX], [f"Mb{pp}"])
                cp("act", r3[:, 128:192], s0[:, 384:448], [KAV], [kR3])
                yield
                mm(s0[:, 0:192], Mb[pp][:], r3[:], [f"Mb{pp}", kR3], [I0, I1])
                yield
                cp("act", wu[:], s0[:, 0:192], [I0, I1], [kWU])
                yield
                mm(s0[:, 192:320], wu[:, 0:128], bk_[:, 0, :], [kWU, kBK], [I1, I2])
                mm(s1[:, 0:128], wu[:, 0:128], a3[:, 0, :], [kWU, kA3], [k1[0]])
                yield
                cp("act", pt[:], s0[:, 192:320], [I1, I2], [kPT])
                tt("dve", rh[:], s1[:, 0:128], ar[:, c, 128:256], ALU.add, [k1[0], f"AR{bp}"], [kRH])
                yield
                mm(s1[:, 256:320], a3[:, 0, :], wu[:, 128:192], [kA3, kWU], [k1[2]], start=True, stop=False)
                mm(s1[:, 256:320], a3[:, 2, :], vs[:], [kA3, kV], [k1[2]], start=False, stop=False)
                mm(s1[:, 256:320], rh[:], Hs[hc][:], [kRH, f"H{hc}"], [k1[2]], start=False, stop=True)
                mm(s1[:, 320:384], bk_[:, 0, :], wu[:, 128:192], [kBK, kWU], [k1[3]], start=True, stop=False)
                mm(s1[:, 320:384], bk_[:, 1, :], vs[:], [kBK, kV], [k1[3]], start=False, stop=False)
                mm(s1[:, 320:384], pt[:], Hs[hc][:], [kPT, f"H{hc}"], [k1[3]], start=False, stop=True)
                yield
                stt(Hs[hn][:], Hs[hc][:], gamC[bp][:, c:c + 1], s1[:, 320:384], ALU.mult, ALU.add, [f"H{hc}", f"gamC{bp}", k1[3]], [f"H{hn}"])
                S.op("dve", lambda: nc.vector.bn_stats(out=smm[:, 2:8], in_=s1[:, 256:320]), [k1[2]], [f"sm{pp}st"])
                S.op("dve", lambda: nc.vector.bn_aggr(out=smm[:, 8:10], in_=smm[:, 2:8]), [f"sm{pp}st"], [f"sm{pp}mv"])
                act(smm[:, 10:11], smm[:, 9:10], AF.Sqrt, [f"sm{pp}mv"], [f"sm{pp}sd"], bias=GN_EPS, scale=1.0)
                S.op("dve", lambda: nc.vector.reciprocal(out=smm[:, 11:12], in_=smm[:, 10:11]), [f"sm{pp}sd"], [f"sm{pp}rs"])
                tsc("dve", yh[pp][:], s1[:, 256:320], smm[:, 8:9], smm[:, 11:12], ALU.subtract, ALU.mult, [k1[2], f"sm{pp}mv", f"sm{pp}rs"], [f"yh{pp}"])
                tt("pool", yh[pp][:], yh[pp][:], lnw_bc[:], ALU.mult, [f"yh{pp}", "lnw_bc"], [f"yh{pp}"])
                tt("pool", yh[pp][:], yh[pp][:], lnb_bc[:], ALU.add, [f"yh{pp}", "lnb_bc"], [f"yh{pp}"])
                stt(yfin[pp][:], vs[:], smm[:, 0:1], yh[pp][:], ALU.mult, ALU.add, [kV, f"sm{pp}rk", f"yh{pp}"], [f"yfin{pp}"])
                yield
                tr(s1[0:64, 128:256], yfin[pp][:], ident_f[:], [f"yfin{pp}", "ident_f"], [k1[1]])
                yield
                tt("dve", yTb[bp][0:64, :, c * 64:(c + 1) * 64], s1[0:64, 128:256].rearrange("p (h t) -> p h t", t=64),
                   gT[bp][0:64, :, c * 64:(c + 1) * 64], ALU.mult, [k1[1], f"gT{bp}"], [f"yTb{bp}"])

            def run_packs(blk, extra=None):
                import os
                gens = [pack(blk, c) for c in range(int(os.environ.get("PKN", "8")))]
                maxs = int(STOP[2:]) if (STOP or "").startswith("pk") else 10 ** 9
                adv = {}
                active = []
                if extra is not None and maxs > 10 ** 8:
                    active.append(extra)
                nxt = 0
                stepc = 0
                NG = len(gens)
                while nxt < NG or active:
                    npk = len([a_ for a_ in active if a_ is not extra])
                    if nxt < NG and npk < NW and (npk == 0 or stepc % 5 == 0):
                        active.append(gens[nxt])
                        nxt += 1
                    for gkk in list(active):
                        try:
                            adv[id(gkk)] = adv.get(id(gkk), 0) + 1
                            if adv[id(gkk)] > maxs:
                                raise StopIteration
                            next(gkk)
                            if gkk is extra:
                                next(gkk)
                        except StopIteration:
                            active.remove(gkk)
                    stepc += 1

            for _ in prep(0):
                pass
            for blk in range(NBLK):
                if STOP == "prep":
                    S.barrier()
                    return nc
                run_packs(blk, prep(blk + 1) if blk + 1 < NBLK else None)
                if STOP == "blk1" or (STOP or "").startswith("pk"):
                    S.barrier()
                    return nc
                bp = blk % 2
                if blk == 0:
                    for h_ in range(2):
                        dump(f"yT{h_}", yTb[bp][0:64, h_, 0:128], 128)
                    dump("H1", Hs[0][:], 64)
                S.dma("sp", f"yst{bp}", ysv[blk // 4][:, :, (blk % 4) * 512:(blk % 4 + 1) * 512], yTb[bp][0:64, :, :], [f"yTb{bp}"], ["ysrc"])
                if blk % 4 == 3:
                    ymark[blk // 4] = [(st_, S.cnt[st_]) for st_ in ("yst0", "yst1")]
                if blk % 4 == 0 and blk > 0:
                    exchange(blk // 4 - 1)
            S.barrier()
            exchange(3)
            GBL[:] = [0, 1, 2, 3, 4, 5, 7]

        if STOP == "p1":
            return nc

        if STOP == "cc":
            return nc
        ydv = ydst.ap().rearrange("j (h i) t -> i j h t", i=64)
        with contextlib.ExitStack() as p2:
            wo_p = sb([128, 4, D], BF16, stack=p2)
            wo_r = sb([128, 8, D], BF16, stack=p2)
            yall = [sb([128, 8, 512], BF16, stack=p2) for _ in range(2)]
            x1t = [sb([128, D], stack=p2) for _ in range(2)]
            g1b3 = g1bc[:].rearrange("p (k n) -> p k n", k=1)
            load_cast(lambda a, b: wo_p[:, :, a:b], lambda a, b: w_out[0:512, a:b].rearrange("(k p) n -> p k n", p=128), 128, 4, D,
                      ["wo_p", "g1bc"], scale_bc=lambda a, b: g1b3[:, :, a:b].broadcast_to([128, 4, b - a]))
            load_cast(lambda a, b: wo_r[0:64, :, a:b], lambda a, b: w_out[512:1024, a:b].rearrange("(h i) n -> i h n", i=64), 64, 8, D,
                      ["wo_r", "g1bc"], scale_bc=lambda a, b: g1b3[0:64, :, a:b].broadcast_to([64, 8, b - a]))
            S.barrier()
            for ob in range(4):
                ya = yall[ob % 2]
                S.dma("pool", f"yld{ob % 2}", ya[0:64, :, :], ydv[:, bass.ds(q, 1), :, ob * 512:(ob + 1) * 512].rearrange("i j h t -> i (j h) t"),
                      ["wo_r", "wo_p"], [f"yall{ob % 2}"])
                for ti in range(4):
                    i = _xt[0] % 2
                    _xt[0] += 1
                    S.dma("sp", f"xld{i}", xt[i][:], xo[16 + ob * 512 + ti * 128:16 + ob * 512 + (ti + 1) * 128, :], (), [f"xt{i}"])
                    j = (ob * 4 + ti) % 2
                    for hf in range(2):
                        bank, bk = gbank()
                        for m in range(4):
                            mm(bank[:], mixp[:, m, ob * 512 + ti * 128:ob * 512 + (ti + 1) * 128], wo_p[:, m, hf * 512:(hf + 1) * 512],
                               ["mixp", "wo_p"], [bk], start=(m == 0), stop=False)
                        for h in range(8):
                            mm(bank[:], ya[0:64, h, ti * 128:(ti + 1) * 128], wo_r[0:64, h, hf * 512:(hf + 1) * 512],
                               [f"yall{ob % 2}", "wo_r"], [bk], start=False, stop=(h == 7))
                        tt("dve", x1t[j][:, hf * 512:(hf + 1) * 512], bank[:], xt[i][:, hf * 512:(hf + 1) * 512], ALU.add, [bk, f"xt{i}"], [f"x1t{j}"])
                    r0 = ob * 512 + ti * 128
                    if ob == 0 and ti == 0:
                        dump("x1", x1t[j][:, 0:256], 256)
                        for h_ in range(8):
                            dump(f"yall{h_}", ya[0:64, h_, 0:32], 32)
                    S.dma("sp", f"x1st{j}", x1d[r0:r0 + 128, :], x1t[j][:], [f"x1t{j}"], ["x1d"])
            S.barrier()
        pA.close()
        if STOP == "p2a":
            return nc

        with contextlib.ExitStack() as p3:
            wgu = sb([128, 8, 2 * DFF], BF16, stack=p3)
            wdn = sb([128, NFF, D], BF16, stack=p3)
            x1b = [sb([128, 2, D], stack=p3) for _ in range(2)]
            h2T = [sb([128, 8, 256], BF16, stack=p3) for _ in range(2)]
            actT = sb([128, NFF, 256], BF16, stack=p3)
            sgt = [sb([128, 256], stack=p3) for _ in range(2)]
            ot = sb([128, D], stack=p3)
            g2b3 = g2bc[:].rearrange("p (k n) -> p k n", k=1)

            def load_norm(ob):
                p = ob % 2
                for ti in range(2):
                    r0 = ob * 256 + ti * 128
                    S.dma("pool", f"x1ld{p}{ti}", x1b[p][:, ti, :], x1d[r0:r0 + 128, :], ["x1d"], [("x1b", p, ti)])
                    norm_to_fm(x1b[p][:, ti, :], 128, gsc2, sh2, "sh2", lambda k, ti=ti, p=p: h2T[p][:, k, ti * 128:(ti + 1) * 128], f"h2T{p}",
                               src_key=("x1b", p, ti))

            load_norm(0)
            rngs = []
            for i_ in range(6):
                rngs.append((i_ * 512, min((i_ + 1) * 512, DFF)))
                rngs.append((DFF + i_ * 512, min(DFF + (i_ + 1) * 512, 2 * DFF)))
            for (a0, a1) in rngs:
                for kh in range(2):
                    load_cast(lambda a, b, kh=kh, a0=a0: wgu[:, kh * 4:(kh + 1) * 4, a0 + a:a0 + b],
                              lambda a, b, kh=kh, a0=a0: w_gu[kh * 512:(kh + 1) * 512, a0 + a:a0 + b].rearrange("(k p) n -> p k n", p=128),
                              128, 4, a1 - a0,
                              [lambda a, b, kh=kh, a0=a0: [("wgu", kh, c_) for c_ in range((a0 + a) // 256, (a0 + b + 255) // 256)]])
            wd3 = w_down.rearrange("(f p) n -> p f n", p=128)
            for f0 in range(0, NFF, 2):
                load_cast(lambda a, b, f0=f0: wdn[:, f0:f0 + 2, a:b], lambda a, b, f0=f0: wd3[:, f0:f0 + 2, a:b], 128, 2, D,
                          [lambda a, b, f0=f0: [("wdn", f0 // 2)], "g2bc"], scale_bc=lambda a, b: g2b3[:, :, a:b].broadcast_to([128, 2, b - a]))
            for ob in range(8):
                p = ob % 2
                for f in range(NFF):
                    bg, kg = gbank()
                    for k in range(8):
                        mm(bg[:, 0:256], wgu[:, k, f * 128:(f + 1) * 128], h2T[p][:, k, :], [("wgu", k // 4, (f * 128) // 256), f"h2T{p}"], [kg], start=(k == 0), stop=(k == 7))
                    bu, ku = gbank()
                    for k in range(8):
                        mm(bu[:, 0:256], wgu[:, k, DFF + f * 128:DFF + (f + 1) * 128], h2T[p][:, k, :], [("wgu", k // 4, (DFF + f * 128) // 256), f"h2T{p}"], [ku], start=(k == 0), stop=(k == 7))
                    sj = f % 2
                    act(sgt[sj][:], bg[:, 0:256], AF.Silu, [kg], [f"sgt{sj}"])
                    tt("dve", actT[:, f, :], bu[:, 0:256], sgt[sj][:], ALU.mult, [ku, f"sgt{sj}"], ["actT"])
                    if f == NFF // 2 and ob + 1 < 8:
                        load_norm(ob + 1)
                for ti in range(2):
                    for hf in range(2):
                        bank, bk = gbank()
                        for f in range(NFF):
                            mm(bank[:], actT[:, f, ti * 128:(ti + 1) * 128], wdn[:, f, hf * 512:(hf + 1) * 512], ["actT", ("wdn", f // 2)], [bk],
                               start=(f == 0), stop=(f == NFF - 1))
                        tt("dve", x1b[p][:, ti, hf * 512:(hf + 1) * 512], bank[:], x1b[p][:, ti, hf * 512:(hf + 1) * 512], ALU.add,
                           [bk, ("x1b", p, ti)], [("x1b", p, ti)])
                    xa = x1b[p][:, ti, :]
                    act(junk[:], xa, AF.Square, [("x1b", p, ti)], ["junk", "ss"], accum_out=ss[:, 0:1])
                    act(ss[:, 1:2], ss[:, 0:1], AF.Sqrt, ["ss"], ["ss1"], bias=RMS_EPS, scale=1.0 / D)
                    S.op("dve", lambda: nc.vector.reciprocal(out=ss[:, 2:3], in_=ss[:, 1:2]), ["ss1"], ["ss2"])
                    stt(ot[:], xa, ss[:, 2:3], fgbc[:], ALU.mult, ALU.mult, [("x1b", p, ti), "ss2", "fgbc"], ["ot"])
                    r0 = ob * 256 + ti * 128
                    S.dma("sp", "ost", out[r0:r0 + 128, :], ot[:], ["ot"], ["outd"])
            S.barrier()
    return nc


_NC = None


def kernel(**inputs):
    global _NC
    x = np.asarray(inputs["x"], np.float32)
    c = np.asarray(inputs["c"], np.float32)
    if _NC is None:
        _NC = build()
    g = lambda n: np.asarray(inputs[n], np.float32)[0]
    names = ["ada_w", "ada_b", "norm1_g", "pool_w", "pool_scale", "w_out", "norm2_g", "w_ffn_gu", "w_ffn_down"]
    shared = {n: np.ascontiguousarray(g(n)) for n in names}
    shared["final_g"] = np.ascontiguousarray(np.asarray(inputs["final_g"], np.float32))
    w_in, mu_shift = g("w_in"), g("mu_shift")
    in_maps = []
    wins = (2, 4, 8, 16)
    for core in range(8):
        b, qq = core // 4, core % 4
        xpad = np.zeros((T + 16, D), np.float32)
        xpad[16:] = x[b]
        fl = np.zeros((128, 65), np.float32)
        fl[:, 0] = 0.0 if qq == 0 else 1.0
        for gi, w in enumerate(wins):
            for t in range(16):
                fl[:, 1 + gi * 16 + t] = 1.0 / min(qq * TQ + t + 1, w)
        ps_ = slice(qq * 128, (qq + 1) * 128)
        cols = np.concatenate([np.arange(0, 512)] + [np.arange(512 + i * 512 + qq * 128, 512 + i * 512 + (qq + 1) * 128) for i in range(3)]
                              + [np.arange(2048, 2304)])
        mcols = np.stack([mu_shift[i * 512 + qq * 128:i * 512 + (qq + 1) * 128] for i in range(3)]
                         + [mu_shift[1536:1664], mu_shift[1664:1792]], axis=1)
        m = dict(shared)
        m["xb"] = xpad
        m["xo"] = np.ascontiguousarray(xpad[qq * TQ:qq * TQ + TQ + 16])
        m["cvec"] = np.ascontiguousarray(c[b])
        m["flags"] = fl
        m["w_in_sel"] = np.ascontiguousarray(w_in[:, cols])
        m["mu_sel"] = np.ascontiguousarray(mcols)
        m["pv_sel"] = np.ascontiguousarray(np.stack([g(n)[ps_] for n in ("w0", "a0", "k_k", "k_a", "r_k")], axis=1))
        m["ln2"] = np.ascontiguousarray(np.stack([g("lnx_w")[ps_].reshape(2, 64), g("lnx_b")[ps_].reshape(2, 64)], axis=0))
        m["Wd_sel"] = np.ascontiguousarray(g("w_decay_up")[:, ps_])
        m["Wa_sel"] = np.ascontiguousarray(g("w_iclr_up")[:, ps_])
        m["Wg_sel"] = np.ascontiguousarray(g("w_gate_up")[:, ps_])
        in_maps.append(m)
    res = run_bass_kernel_spmd(_NC, in_maps, core_ids=list(range(8)))
    if DEBUG:
        global DBG_OUT
        DBG_OUT = [res.results[core]["dbg"] for core in range(8)]
    outp = np.zeros((2, T, D), np.float32)
    for core in range(8):
        b, qq = core // 4, core % 4
        outp[b, qq * TQ:(qq + 1) * TQ] = res.results[core]["out"]
    return outp
```

```python
import numpy as np
import ml_dtypes
import concourse.bass as bass
import concourse.mybir as mybir
from concourse.bass_utils import run_bass_kernel_spmd

F32 = mybir.dt.float32
BF16 = mybir.dt.bfloat16
ALU = mybir.AluOpType
AF = mybir.ActivationFunctionType
AX = mybir.AxisListType

D = 1024
T = 8192
TQ = 2048
NBLK = T // 512
DFF = 2816
NFF = DFF // 128
GN_EPS = 64e-5
RMS_EPS = 1e-6
DEC = 0.6065306597126334


class Sched:
    def __init__(self, nc, es):
        self.nc = nc
        self.es = es
        self.eng = {"pe": nc.tensor, "act": nc.scalar, "dve": nc.vector, "pool": nc.gpsimd, "sp": nc.sync}
        self.sem = {}
        self.cnt = {}
        self.inc = {}
        self.waited = {}
        self.lastw = {}
        self.readers = {}
        for e in ("pe", "act", "dve", "pool"):
            self.stream(e, 1)

    def stream(self, name, inc):
        if name not in self.sem:
            self.sem[name] = self.es.enter_context(self.nc.semaphore("s_" + name))
            self.cnt[name] = 0
            self.inc[name] = inc
        return name

    def _wait(self, e, s, v):
        if s == e and e == "pe":
            return
        if self.waited.get((e, s), 0) >= v:
            return
        self.eng[e].wait_ge(self.sem[s], v)
        self.waited[(e, s)] = v

    @staticmethod
    def _bank(k):
        if isinstance(k, tuple) and k and k[0] == "TB":
            return "BANK_TB"
        if isinstance(k, str) and (k.startswith("s0_") or k.startswith("s1_")):
            return "BANK_" + k[:4]
        return None

    def _aug(self, keys):
        out = list(keys)
        for k in keys:
            b = self._bank(k)
            if b is not None and b not in out:
                out.append(b)
        return out

    def _deps(self, reads, writes):
        deps = set()
        for r in reads:
            if r in self.lastw:
                deps.add(self.lastw[r])
        for r in writes:
            if r in self.lastw:
                deps.add(self.lastw[r])
            for d in self.readers.get(r, ()):
                deps.add(d)
        return deps

    def _commit(self, s, reads, writes):
        v = self.cnt[s]
        for r in writes:
            self.lastw[r] = (s, v)
            self.readers[r] = []
        for r in reads:
            self.readers.setdefault(r, []).append((s, v))

    def op(self, e, fn, reads=(), writes=()):
        bk = [k for k in self._aug(list(reads) + list(writes)) if isinstance(k, str) and k.startswith("BANK_")]
        reads, writes = list(reads), list(writes) + bk
        for (s, v) in self._deps(reads, writes):
            self._wait(e, s, v)
        ins = fn()
        self.cnt[e] += 1
        ins.then_inc(self.sem[e], 1)
        self._commit(e, reads, writes)
        return ins

    def dma(self, q, s, out, in_, reads=(), writes=(), **kw):
        self.stream(s, 16)
        for (ss, v) in self._deps(reads, writes):
            self._wait(q, ss, v)
        ins = self.eng[q].dma_start(out=out, in_=in_, **kw)
        self.cnt[s] += 16
        ins.then_inc(self.sem[s], 16)
        self._commit(s, reads, writes)
        return ins

    def close(self, s):
        for r, (ss, v) in list(self.lastw.items()):
            if ss == s:
                self.lastw[r] = (s, self.cnt[s])

    def wait_all(self, e):
        for s in self.cnt:
            if self.cnt[s] > 0:
                self._wait(e, s, self.cnt[s])

    def barrier(self):
        for e in ("pe", "act", "dve", "pool", "sp"):
            self.wait_all(e)
        self.lastw.clear()
        self.readers.clear()


STOP = None
DEBUG = False
DBG_MAP = {}
DBG_OUT = None
_LAST_S = None


def build():
    import contextlib

    nc = bass.Bass("TRN2", target_bir_lowering=False)

    def din(name, shape, dt=F32):
        return nc.dram_tensor(name, list(shape), dt, kind="ExternalInput").ap()

    xb = din("xb", [T + 16, D])
    cvec = din("cvec", [D])
    flags = din("flags", [128, 65])
    ada_w = din("ada_w", [D, 6 * D])
    ada_b = din("ada_b", [6 * D])
    norm1_g = din("norm1_g", [D])
    xo = din("xo", [TQ + 16, D])
    w_in = din("w_in_sel", [D, 1152])
    mu_sel = din("mu_sel", [128, 5])
    pv_sel = din("pv_sel", [128, 5])
    ln2 = din("ln2", [2, 2, 64])
    pool_w = din("pool_w", [4, 128, 128])
    pool_scale = din("pool_scale", [512])
    w_decay_up = din("Wd_sel", [64, 128])
    w_iclr_up = din("Wa_sel", [64, 128])
    w_gate_up = din("Wg_sel", [128, 128])
    w_out = din("w_out", [D, D])
    norm2_g = din("norm2_g", [D])
    w_gu = din("w_ffn_gu", [D, 2 * DFF])
    w_down = din("w_ffn_down", [DFF, D])
    final_g = din("final_g", [D])
    out = nc.dram_tensor("out", [TQ, D], F32, kind="ExternalOutput").ap()
    ysrc = [nc.dram_tensor(f"ysrc{j}", [128, TQ], BF16) for j in range(4)]
    ydst = nc.dram_tensor("ydst", [4, 512, TQ], BF16)
    x1d = nc.dram_tensor("x1d", [TQ, D], F32)
    dbg = nc.dram_tensor("dbg", [128, 8192], F32, kind="ExternalOutput").ap() if DEBUG else None

    es = contextlib.ExitStack()
    with es:
        S = Sched(nc, es)
        global _LAST_S
        _LAST_S = S
        pid = nc.gpsimd.partition_id()
        q = pid % 4
        _n = [0]

        def sb(shape, dt=F32, stack=es, name=None):
            _n[0] += 1
            return stack.enter_context(nc.sbuf_tensor(name or f"t{_n[0]}", list(shape), dt))

        def ps(shape, dt=F32, name=None):
            _n[0] += 1
            return es.enter_context(nc.psum_tensor(name or f"p{_n[0]}", list(shape), dt))

        def mm(out, lhsT, rhs, r, w, start=True, stop=True):
            return S.op("pe", lambda: nc.tensor.matmul(out, lhsT, rhs, start=start, stop=stop), r, w)

        def tr(out, in_, ident, r, w):
            return S.op("pe", lambda: nc.tensor.transpose(out, in_, ident), r, w)

        def act(out, in_, func, r, w, bias=None, scale=None, accum_out=None):
            kw = {}
            if bias is not None:
                kw["bias"] = bias
            if scale is not None:
                kw["scale"] = scale
            if accum_out is not None:
                kw["accum_out"] = accum_out
            return S.op("act", lambda: nc.scalar.activation(out=out, in_=in_, func=func, **kw), r, w)

        def tt(e, out, in0, in1, op, r, w):
            eng = nc.vector if e == "dve" else nc.gpsimd
            return S.op(e, lambda: eng.tensor_tensor(out=out, in0=in0, in1=in1, op=op), r, w)

        def tsc(e, out, in0, s1, s2, op0, op1, r, w):
            eng = nc.vector if e == "dve" else nc.gpsimd
            if op1 is None:
                return S.op(e, lambda: eng.tensor_scalar(out=out, in0=in0, scalar1=s1, scalar2=None, op0=op0), r, w)
            return S.op(e, lambda: eng.tensor_scalar(out=out, in0=in0, scalar1=s1, scalar2=s2, op0=op0, op1=op1), r, w)

        def stt(out, in0, scalar, in1, op0, op1, r, w):
            return S.op("dve", lambda: nc.vector.scalar_tensor_tensor(out=out, in0=in0, scalar=scalar, in1=in1, op0=op0, op1=op1), r, w)

        def cp(e, out, in_, r, w):
            if e == "act":
                return S.op("act", lambda: nc.scalar.copy(out=out, in_=in_), r, w)
            eng = nc.vector if e == "dve" else nc.gpsimd
            return S.op(e, lambda: eng.tensor_copy(out=out, in_=in_), r, w)

        def mset(e, ap, val, w):
            eng = nc.vector if e == "dve" else nc.gpsimd
            return S.op(e, lambda: eng.memset(ap, val), (), w)

        NW = 3
        PS = [ps([128, 512], name=f"PS{i}") for i in range(8)]
        SB0 = [PS[p] for p in range(NW)]
        SB1 = [PS[NW + p] for p in range(NW)]
        TB = PS[6]
        GBL = [0, 1, 2, 3, 4, 5, 7]
        _gb = [0]

        def gbank():
            i = GBL[_gb[0] % len(GBL)]
            _gb[0] += 1
            return PS[i], f"GB{i}"

        ones_f = sb([128, 128])
        ident_f = sb([128, 128])
        ident_bf = sb([128, 128], BF16)
        flg = sb([128, 65])
        c_fm = sb([128, 8])
        n1g = sb([128, 8])
        n2g = sb([128, 8])
        mu = sb([128, 5])
        pscale = sb([128, 4])
        pv = sb([128, 8])
        fgbc = sb([128, D])
        omm = sb([128, 5])
        stage = [sb([128, 2048]) for _ in range(2)]
        g2bc = sb([128, D])
        modfm = sb([128, 48])
        gsc1 = sb([128, 8])
        gsc2 = sb([128, 8])
        xn = sb([128, D])
        junk = sb([128, D], BF16)
        ss = sb([128, 4])
        dbgt = sb([128, 512]) if DEBUG else None
        pA = contextlib.ExitStack()
        es.enter_context(pA)
        M_TS = sb([128, 2, 128], stack=pA)
        M_A3 = sb([128, 3, 128], stack=pA)
        Blk = sb([128, 128], stack=pA)
        E_bf = sb([128, 64], BF16, stack=pA)
        ones_bf = sb([128, 2], BF16, stack=pA)
        rmask = sb([128, 512], stack=pA)
        lnw_bc = sb([128, 64], stack=pA)
        lnb_bc = sb([128, 64], stack=pA)
        Wd = sb([128, 128], stack=pA)
        Wa = sb([128, 128], stack=pA)
        Wg = sb([128, 128], stack=pA)
        pw_f = sb([128, 4, 128], stack=pA)
        pw_bf = sb([128, 4, 128], BF16, stack=pA)
        g1bc = sb([128, D], stack=pA)
        xt = [sb([128, D], stack=pA) for _ in range(2)]
        mixp = sb([128, 4, TQ], BF16, stack=pA)
        mset("pool", ones_f[:], 1.0, ["ones_f"])

        def asel(out, pattern, cmul, cmp, w):
            return S.op("pool", lambda: nc.gpsimd.affine_select(out=out, in_=ones_f[:], pattern=pattern, compare_op=cmp,
                                                                fill=0.0, base=0, channel_multiplier=cmul), ["ones_f"], w)

        asel(ident_f[:], [[1, 128]], -1, ALU.is_equal, ["ident_f"])
        asel(M_TS[:, 0, :], [[1, 128]], -1, ALU.is_gt, ["M_TS"])
        asel(M_TS[:, 1, :], [[-1, 128]], 1, ALU.is_gt, ["M_TS"])
        asel(M_A3[:, 0, :], [[1, 128]], -1, ALU.is_ge, ["M_A3"])
        asel(M_A3[:, 1, :], [[1, 128]], -1, ALU.is_gt, ["M_A3"])
        asel(M_A3[:, 2, :], [[1, 128]], -1, ALU.is_ge, ["M_A3"])
        cp("pool", ident_bf[:], ident_f[:], ["ident_f"], ["ident_bf"])
        mset("pool", Blk[:], 0.0, ["Blk"])
        mset("pool", Blk[0:64, 0:64], 1.0, ["Blk"])
        mset("pool", Blk[64:128, 64:128], 1.0, ["Blk"])
        tt("pool", E_bf[:], ident_f[:, 0:64], ident_f[:, 64:128], ALU.add, ["ident_f"], ["E_bf"])
        mset("pool", ones_bf[:], 1.0, ["ones_bf"])
        mset("pool", rmask[:], 1.0, ["rmask"])
        mset("pool", rmask[:].rearrange("p (c t) -> p c t", t=64)[:, :, 0:1], 0.0, ["rmask"])

        def pl(dst, src, w, **kw):
            S.dma("pool", "init", dst, src, (), w, **kw)

        def fm(v, k):
            return v.rearrange("(k p) -> p k", p=128)

        pl(flg[:], flags[:, :], ["flg"])
        pl(c_fm[:], fm(cvec, 8), ["c_fm"], allow_slow_non_contiguous=True)
        pl(n1g[:], fm(norm1_g, 8), ["n1g"], allow_slow_non_contiguous=True)
        pl(n2g[:], fm(norm2_g, 8), ["n2g"], allow_slow_non_contiguous=True)
        pl(mu[:], mu_sel[:, :], ["mu"])
        pl(pscale[:], fm(pool_scale, 4), ["pscale"], allow_slow_non_contiguous=True)
        pl(pv[:, 0:5], pv_sel[:, :], ["pv"])
        for h in range(2):
            pl(lnw_bc[h * 64:(h + 1) * 64, :], ln2[0, h:h + 1, :].broadcast_to([64, 64]), ["lnw_bc"])
            pl(lnb_bc[h * 64:(h + 1) * 64, :], ln2[1, h:h + 1, :].broadcast_to([64, 64]), ["lnb_bc"])
        pl(Wd[0:64, :], w_decay_up[:, :], ["Wd"])
        pl(Wa[64:128, :], w_iclr_up[:, :], ["Wa"])
        pl(Wg[:], w_gate_up[:, :], ["Wg"])
        pl(pw_f[:], pool_w.rearrange("g c d -> c g d"), ["pw_f"])
        pl(fgbc[:], final_g.partition_broadcast(128), ["fgbc"])
        S.close("init")
        cp("pool", pw_bf[:], pw_f[:], ["pw_f"], ["pw_bf"])
        tsc("pool", omm[:], mu[:], -1.0, 1.0, ALU.mult, ALU.add, ["mu"], ["omm"])

        if STOP == "setup":
            S.barrier()
            return nc
        _st = [0]
        _ce = [0]

        def load_cast(dst_ap_fn, src_ap_fn, nparts, K, N, w, scale_bc=None, part0=0):
            ncol = max(1, 2048 // K)
            for n0 in range(0, N, ncol):
                n1 = min(N, n0 + ncol)
                i = _st[0] % 2
                _st[0] += 1
                sv = stage[i][part0:part0 + nparts, 0:K * (n1 - n0)].rearrange("p (k n) -> p k n", k=K)
                S.dma(("sp", "act")[i], f"stg{i}", sv, src_ap_fn(n0, n1), (), [f"stage{i}"])
                e = ("act", "dve")[_ce[0] % 2]
                _ce[0] += 1
                wk = w[0](n0, n1) if callable(w[0]) else [w[0]]
                if scale_bc is not None:
                    tt("dve" if e == "act" else e, dst_ap_fn(n0, n1), sv, scale_bc(n0, n1), ALU.mult, [f"stage{i}"] + w[1:], wk)
                else:
                    cp(e, dst_ap_fn(n0, n1), sv, [f"stage{i}"], wk)

        _dc = [0]

        def dump(name, ap, n, e="dve"):
            if not DEBUG:
                return
            c0 = _dc[0]
            _dc[0] += n
            DBG_MAP[name] = (c0, n)
            S.wait_all(e)
            npart = ap.shape[0]
            cp(e, dbgt[0:npart, 0:n], ap, [], ["dbgt"])
            S.dma("sp", "dbgs", dbg[0:npart, c0:c0 + n], dbgt[0:npart, 0:n], ["dbgt"], ["dbgd"])

        with contextlib.ExitStack() as p0:
            csil = sb([128, 8, 1], stack=p0)
            crep = sb([128, 8, 128], stack=p0)
            adab = [sb([128, 512], stack=p0) for _ in range(2)]
            mblk = sb([128, 512], stack=p0)
            tmp4 = sb([128, 4, 128], stack=p0)
            act(csil[:, :, 0], c_fm[:], AF.Silu, ["c_fm"], ["csil"])
            cp("dve", crep[:], csil[:].broadcast_to([128, 8, 128]), ["csil"], ["crep"])
            for cb in range(12):
                bank, bk = gbank()
                for k2 in range(4):
                    j = _st[0] % 2
                    _st[0] += 1
                    sv = stage[j][:, 0:1024].rearrange("p (k n) -> p k n", k=2)
                    S.dma(("sp", "act")[j], f"stg{j}", sv,
                          ada_w[k2 * 256:(k2 + 1) * 256, cb * 512:(cb + 1) * 512].rearrange("(k p) n -> p k n", p=128),
                          (), [f"stage{j}"])
                    for kk in range(2):
                        k = k2 * 2 + kk
                        mm(bank[:], crep[:, k, :], sv[:, kk, :], ["crep", f"stage{j}"], [bk], start=(k == 0), stop=(k == 7))
                a = cb % 2
                S.dma("sp", f"adab{a}", adab[a][:], ada_b[cb * 512:(cb + 1) * 512].partition_broadcast(128), (), [f"adab{a}"])
                sec = cb // 2
                if sec == 2:
                    dst, dk = g1bc[:, (cb % 2) * 512:(cb % 2 + 1) * 512], "g1bc"
                elif sec == 5:
                    dst, dk = g2bc[:, (cb % 2) * 512:(cb % 2 + 1) * 512], "g2bc"
                else:
                    dst, dk = mblk[:], "mblk"
                tt("dve", dst, bank[:], adab[a][:], ALU.add, [bk, f"adab{a}"], [dk])
                tt("dve", tmp4[:], dst.rearrange("p (a b) -> p a b", b=128),
                   ident_f[:].rearrange("p (a b) -> p a b", a=1).broadcast_to([128, 4, 128]), ALU.mult, [dk, "ident_f"], ["tmp4"])
                S.op("dve", lambda: nc.vector.tensor_reduce(out=modfm[:, cb * 4:(cb + 1) * 4], in_=tmp4[:], axis=AX.X, op=ALU.add),
                     ["tmp4"], ["modfm"])
            S.barrier()
        if STOP == "ada":
            return nc
        stt(gsc1[:], modfm[:, 8:16], 1.0, n1g[:], ALU.add, ALU.mult, ["modfm", "n1g"], ["gsc1"])
        stt(gsc2[:], modfm[:, 32:40], 1.0, n2g[:], ALU.add, ALU.mult, ["modfm", "n2g"], ["gsc2"])
        dump("modfm", modfm[:], 48)
        sh1 = modfm[:, 0:8]
        sh2 = modfm[:, 24:32]

        _xt = [0]

        def norm_to_fm(x_ap, nrow, gsc, sh, shk, dst_fn, dst_key, dq="sp", src_key=None, keep=None):
            if src_key is None:
                i = _xt[0] % 2
                _xt[0] += 1
                xs, xk = xt[i], f"xt{i}"
                S.dma(dq, f"xld{i}", xs[0:nrow, :], x_ap, (), [xk])
                xa = xs[0:nrow, :]
            else:
                xa, xk = x_ap, src_key
            act(junk[0:nrow, :], xa, AF.Square, [xk], ["junk", "ss"], accum_out=ss[0:nrow, 0:1])
            act(ss[0:nrow, 1:2], ss[0:nrow, 0:1], AF.Sqrt, ["ss"], ["ss1"], bias=RMS_EPS, scale=1.0 / D)
            S.op("dve", lambda: nc.vector.reciprocal(out=ss[0:nrow, 2:3], in_=ss[0:nrow, 1:2]), ["ss1"], ["ss2"])
            act(xn[0:nrow, :], xa, AF.Copy, [xk, "ss2"], ["xn"], scale=ss[0:nrow, 2:3])
            for half in range(2):
                bank, bk = gbank()
                for kk_ in range(4):
                    k = half * 4 + kk_
                    tr(bank[:, kk_ * 128:kk_ * 128 + nrow], xn[0:nrow, k * 128:(k + 1) * 128], ident_f[0:nrow, 0:nrow], ["xn", "ident_f"], [bk])
                for kk_ in range(4):
                    k = half * 4 + kk_
                    tsc("dve", dst_fn(k), bank[:, kk_ * 128:kk_ * 128 + nrow], gsc[:, k:k + 1], sh[:, k:k + 1], ALU.mult, ALU.add,
                        [bk, "gsc1", "gsc2", "modfm"], [dst_key])
            return xa, xk

        def norm_gen(x_ap, nrow, gsc, sh, shk, dst_fn, dst_key, dq="sp", src_key=None, keep=None):
            if src_key is None:
                i = _xt[0] % 2
                _xt[0] += 1
                xs, xk = xt[i], f"xt{i}"
                S.dma(dq, f"xld{i}", xs[0:nrow, :], x_ap, (), [xk])
                xa = xs[0:nrow, :]
            else:
                xa, xk = x_ap, src_key
            act(junk[0:nrow, :], xa, AF.Square, [xk], ["junk", "ss"], accum_out=ss[0:nrow, 0:1])
            yield
            act(ss[0:nrow, 1:2], ss[0:nrow, 0:1], AF.Sqrt, ["ss"], ["ss1"], bias=RMS_EPS, scale=1.0 / D)
            yield
            S.op("dve", lambda: nc.vector.reciprocal(out=ss[0:nrow, 2:3], in_=ss[0:nrow, 1:2]), ["ss1"], ["ss2"])
            yield
            act(xn[0:nrow, :], xa, AF.Copy, [xk, "ss2"], ["xn"], scale=ss[0:nrow, 2:3])
            yield
            for half in range(2):
                bank, bk = gbank()
                for kk_ in range(4):
                    k = half * 4 + kk_
                    tr(bank[:, kk_ * 128:kk_ * 128 + nrow], xn[0:nrow, k * 128:(k + 1) * 128], ident_f[0:nrow, 0:nrow], ["xn", "ident_f"], [bk])
                yield
                for kk_ in range(4):
                    k = half * 4 + kk_
                    tsc("dve", dst_fn(k), bank[:, kk_ * 128:kk_ * 128 + nrow], gsc[:, k:k + 1], sh[:, k:k + 1], ALU.mult, ALU.add,
                        [bk, "gsc1", "gsc2", "modfm"], [dst_key])
                yield

        with contextlib.ExitStack() as p1:
            w_in_bf = sb([128, 8, 9 * 128], BF16, stack=p1)
            w3 = w_in.rearrange("(k p) n -> p k n", p=128)
            load_cast(lambda a, b: w_in_bf[:, :, a:b], lambda a, b: w3[:, :, a:b], 128, 8, 1152, ["w_in_bf"])

            if STOP == "w_in":
                S.barrier()
                return nc
            hT = sb([128, 8, 528], BF16, stack=p1)
            dn = [sb([128, 528], stack=p1) for _ in range(12)]
            zT = [sb([128, 5, 513], stack=p1) for _ in range(2)]
            gT = [sb([128, 2, 512], stack=p1) for _ in range(2)]
            yTb = [sb([128, 2, 512], BF16, stack=p1) for _ in range(2)]
            gamC = [sb([128, 8], stack=p1) for _ in range(2)]
            Hs = [sb([128, 64], stack=p1) for _ in range(2)]
            AR_bd = [sb([128, 8, 256], BF16, stack=p1) for _ in range(2)]
            B_bd = [sb([128, 8, 128], BF16, stack=p1) for _ in range(2)]
            K_bd = [sb([128, 8, 128], BF16, stack=p1) for _ in range(2)]
            BH_bd = [sb([128, 8, 128], BF16, stack=p1) for _ in range(2)]
            KH_bd = [sb([128, 8, 128], BF16, stack=p1) for _ in range(2)]
            V_bd = [sb([128, 8, 128], BF16, stack=p1) for _ in range(2)]
            RRK_bd = [sb([128, 8, 128], BF16, stack=p1) for _ in range(2)]
            Xi = [sb([128, 3, 128], stack=p1) for _ in range(NW)]
            Mb = [sb([128, 128], BF16, stack=p1) for _ in range(NW)]
            A3 = [sb([128, 3, 128], BF16, stack=p1) for _ in range(NW)]
            R3 = [sb([128, 192], BF16, stack=p1) for _ in range(NW)]
            BK = [sb([128, 2, 128], BF16, stack=p1) for _ in range(NW)]
            Vst = [sb([128, 64], BF16, stack=p1) for _ in range(NW)]
            WU = [sb([128, 192], BF16, stack=p1) for _ in range(NW)]
            PT = [sb([128, 128], stack=p1) for _ in range(NW)]
            RH = [sb([128, 128], stack=p1) for _ in range(NW)]
            sm = [sb([128, 16], stack=p1) for _ in range(NW)]
            yfin = [sb([128, 64], stack=p1) for _ in range(NW)]
            yh = [sb([128, 64], stack=p1) for _ in range(NW)]
            for p in range(2):
                for tl, nm in ((AR_bd, "AR"), (B_bd, "B"), (K_bd, "K"), (BH_bd, "BH"), (KH_bd, "KH"), (V_bd, "V"), (RRK_bd, "RRK")):
                    mset("pool", tl[p][:], 0.0, [f"{nm}{p}"])
                mset("pool", Hs[p][:], 0.0, [f"H{p}"])

            uT = dn[0]
            for ob in range(4):
                norm_to_fm(xo[ob * 512:ob * 512 + 16, :], 16, gsc1, sh1, "sh1",
                           lambda k: hT[:, k, 0:16], "hT")
                if STOP == "norm16":
                    S.barrier()
                    return nc
                for ti in range(4):
                    norm_to_fm(xo[16 + ob * 512 + ti * 128:16 + ob * 512 + (ti + 1) * 128, :], 128, gsc1, sh1, "sh1",
                               lambda k, ti=ti: hT[:, k, 16 + ti * 128:16 + (ti + 1) * 128], "hT")
                if STOP == "norm":
                    S.barrier()
                    return nc
                if ob == 0:
                    for k_ in range(8):
                        dump(f"hT{k_}", hT[:, k_, 0:144], 144)
                for g in range(4):
                    if STOP == "g1" and g == 1:
                        S.barrier()
                        return nc
                    w = (2, 4, 8, 16)[g]
                    bank, bk = gbank()
                    for k in range(8):
                        mm(bank[:], w_in_bf[:, k, g * 128:(g + 1) * 128], hT[:, k, 16:528], ["w_in_bf", "hT"], [bk], start=(k == 0), stop=(k == 7))
                    cp("act", uT[:, 16:528], bank[:], [bk], ["dn0"])
                    bank2, bk2 = gbank()
                    for k in range(8):
                        mm(bank2[:, 0:16], w_in_bf[:, k, g * 128:(g + 1) * 128], hT[:, k, 0:16], ["w_in_bf", "hT"], [bk2], start=(k == 0), stop=(k == 7))
                    if ob == 0:
                        tsc("dve", uT[:, 0:16], bank2[:, 0:16], flg[:, 0:1], None, ALU.mult, None, [bk2, "flg"], ["dn0"])
                    else:
                        cp("dve", uT[:, 0:16], bank2[:, 0:16], [bk2], ["dn0"])
                    src, sk = uT, "dn0"
                    sh_ = 1
                    lvl = 0
                    while sh_ < w:
                        dst, dk = dn[1 + lvl % 2], f"dn{1 + lvl % 2}"
                        tt("pool", dst[:, sh_:528], src[:, sh_:528], src[:, 0:528 - sh_], ALU.add, [sk], [dk])
                        src, sk = dst, dk
                        sh_ *= 2
                        lvl += 1
                    dd = dn[3]
                    stt(dd[:, 16:528], src[:, 16:528], 1.0 / w, uT[:, 16:528], ALU.mult, ALU.subtract, [sk, "dn0"], ["dn3"])
                    if ob == 0:
                        tt("dve", dn[4][:, 0:16], src[:, 16:32], flg[:, 1 + g * 16:1 + (g + 1) * 16], ALU.mult, [sk, "flg"], ["dn4"])
                        tt("dve", dd[:, 16:32], dn[4][:, 0:16], uT[:, 16:32], ALU.subtract, ["dn4", "dn0", "dn3"], ["dn3"])
                    dbfT = dn[5][:, 0:256].bitcast(BF16)
                    cp("act", dbfT, dd[:, 16:528], ["dn3"], ["dn5"])
                    bank3, bk3 = gbank()
                    mm(bank3[:], pw_bf[:, g, :], dbfT, ["pw_bf", "dn5"], [bk3])
                    tsc("dve", mixp[:, g, ob * 512:(ob + 1) * 512], bank3[:], pscale[:, g:g + 1], None, ALU.mult, None, [bk3, "pscale"], ["mixp"])

            for g_ in range(4):
                dump(f"mixp{g_}", mixp[:, g_, 0:128], 128)
            if STOP == "pool":
                S.barrier()
                return nc
            S.barrier()
            GBL[:] = [7]
            ysv = [ysrc[j].ap().rearrange("(h i) t -> i h t", i=64) for j in range(4)]
            S.stream("cc", 1)

            ymark = {}

            def exchange(j):
                for st_, v_ in ymark[j]:
                    S._wait("pool", st_, v_)
                nc.gpsimd.collective_compute("AllGather", ALU.bypass, replica_groups=[[0, 1, 2, 3], [4, 5, 6, 7]],
                                             ins=[ysrc[j].ap().opt()], outs=[ydst.ap()[j].opt()]).then_inc(S.sem["cc"], 1)
                S.cnt["cc"] += 1
            state = {"pk": 0}

            def prep(blk):
                bp = blk % 2
                zt, zk = zT[bp], f"zT{bp}"
                for ti in range(4):
                    yield from norm_gen(xb[16 + blk * 512 + ti * 128:16 + blk * 512 + (ti + 1) * 128, :], 128, gsc1, sh1, "sh1",
                                        lambda k, ti=ti: hT[:, k, 16 + ti * 128:16 + (ti + 1) * 128], "hT")
                if blk == 0:
                    mset("pool", zt[:, :, 0:1], 0.0, [zk])
                else:
                    cp("pool", zt[:, :, 0:1], zT[1 - bp][:, :, 512:513], [f"zT{1 - bp}"], [zk])
                    yield
                for m in range(5):
                    bank, bk = gbank()
                    for k in range(8):
                        mm(bank[:], w_in_bf[:, k, 512 + m * 128:512 + (m + 1) * 128], hT[:, k, 16:528], ["w_in_bf", "hT"], [bk], start=(k == 0), stop=(k == 7))
                    yield
                    cp("act", zt[:, m, 1:513], bank[:], [bk], [zk])
                    yield
                zs = []
                for m in range(5):
                    tmp = dn[11]
                    act(tmp[:, 0:512], zt[:, m, 0:512], AF.Copy, [zk, "mu"], ["dn11"], scale=mu[:, m:m + 1])
                    yield
                    dst = dn[m]
                    stt(dst[:, 0:512], zt[:, m, 1:513], omm[:, m:m + 1], tmp[:, 0:512], ALU.mult, ALU.add, [zk, "omm", "dn11"], [f"dn{m}"])
                    yield
                    zs.append(dst)
                rT, kT, vT, xwa, xg = [z[:, 0:512] for z in zs]
                thx = dn[5]
                act(thx[0:64, 0:512], xwa[0:64, :], AF.Tanh, ["dn3"], ["dn5"])
                yield
                bank, bk = gbank()
                mm(bank[:], Wd[0:64, :], thx[0:64, 0:512], ["Wd", "dn5"], [bk])
                yield
                sg = dn[6]
                act(sg[:, 0:512], bank[:], AF.Sigmoid, [bk, "pv"], ["dn6"], bias=pv[:, 0:1])
                yield
                bank, bk = gbank()
                mm(bank[:], Wa[64:128, :], xwa[64:128, :], ["Wa", "dn3"], [bk])
                yield
                aT = dn[7]
                act(aT[:, 0:512], bank[:], AF.Sigmoid, [bk, "pv"], ["dn7"], bias=pv[:, 1:2])
                yield
                sgx = dn[5]
                act(sgx[:, 0:512], xg, AF.Sigmoid, ["dn4"], ["dn5"])
                yield
                for h in range(2):
                    bank, bk = gbank()
                    mm(bank[0:64, :], Wg[:, h * 64:(h + 1) * 64], sgx[:, 0:512], ["Wg", "dn5"], [bk])
                    yield
                    cp("act", gT[bp][0:64, h, :], bank[0:64, :], [bk], [f"gT{bp}"])
                    yield
                cs = dn[8]
                S.op("dve", lambda: nc.vector.tensor_tensor_scan(out=cs[:, 0:512], data0=rmask[:], data1=sg[:, 0:512], initial=0.0,
                                                                 op0=ALU.mult, op1=ALU.add), ["rmask", "dn6"], ["dn8"])
                yield
                epos = dn[9]
                act(epos[:, 0:512], cs[:, 0:512], AF.Exp, ["dn8"], ["dn9"], scale=-DEC)
                yield
                cp("pool", gamC[bp][:], epos[:, 0:512].rearrange("p (c t) -> p c t", t=64)[:, :, 63], ["dn9"], [f"gamC{bp}"])
                yield
                eneg = dn[10]
                act(eneg[:, 0:512], cs[:, 0:512], AF.Exp, ["dn8"], ["dn10"], scale=DEC)
                yield
                tt("dve", cs[:, 0:512], cs[:, 0:512], sg[:, 0:512], ALU.subtract, ["dn8", "dn6"], ["dn8"])
                yield
                eprev = dn[6]
                act(eprev[:, 0:512], cs[:, 0:512], AF.Exp, ["dn8"], ["dn6"], scale=-DEC)
                yield
                kkr = dn[3]
                tsc("pool", kkr[:, 0:512], kT, pv[:, 2:3], None, ALU.mult, None, ["dn1", "pv"], ["dn3"])
                yield
                sq = dn[4]
                tt("pool", sq[:, 0:512], kkr[:, 0:512], kkr[:, 0:512], ALU.mult, ["dn3"], ["dn4"])
                yield
                bank, bk = gbank()
                mm(bank[:], Blk[:], sq[:, 0:512], ["Blk", "dn4"], [bk])
                yield
                act(sq[:, 0:512], bank[:], AF.Sqrt, [bk], ["dn4"])
                yield
                tsc("dve", sq[:, 0:512], sq[:, 0:512], 1e-12, None, ALU.max, None, ["dn4"], ["dn4"])
                yield
                S.op("dve", lambda: nc.vector.reciprocal(out=sq[:, 0:512], in_=sq[:, 0:512]), ["dn4"], ["dn4"])
                yield
                kk = dn[3]
                tt("dve", kk[:, 0:512], kkr[:, 0:512], sq[:, 0:512], ALU.mult, ["dn3", "dn4"], ["dn3"])
                yield
                t1 = dn[4]
                tsc("dve", t1[:, 0:512], aT[:, 0:512], -1.0, pv[:, 3:4], ALU.add, ALU.mult, ["dn7", "pv"], ["dn4"])
                yield
                kp = dn[8]
                stt(kp[:, 0:512], t1[:, 0:512], 1.0, kT, ALU.add, ALU.mult, ["dn4", "dn1"], ["dn8"])
                yield
                tt("pool", aT[:, 0:512], aT[:, 0:512], kk[:, 0:512], ALU.mult, ["dn7", "dn3"], ["dn7"])
                yield
                stt(t1[:, 0:512], rT, pv[:, 4:5], kp[:, 0:512], ALU.mult, ALU.mult, ["dn0", "pv", "dn8"], ["dn4"])
                yield
                stt(kk[:, 0:512], kk[:, 0:512], -1.0, eprev[:, 0:512], ALU.mult, ALU.mult, ["dn3", "dn6"], ["dn3"])
                yield
                tt("pool", aT[:, 0:512], aT[:, 0:512], eneg[:, 0:512], ALU.mult, ["dn7", "dn10"], ["dn7"])
                yield
                tt("dve", kp[:, 0:512], kp[:, 0:512], eneg[:, 0:512], ALU.mult, ["dn8", "dn10"], ["dn8"])
                yield

                if blk == 0:
                    for nm_, t__ in (("rT", rT), ("vT", vT), ("atil", kk[:, 0:512]), ("btil", aT[:, 0:512]), ("ktil", kp[:, 0:512]),
                                     ("epos", epos[:, 0:512]), ("rrk", t1[:, 0:512])):
                        dump(nm_, t__[:, 0:128], 128)
                    for h_ in range(2):
                        dump(f"gT{h_}", gT[bp][0:64, h_, 0:64], 64)

                def c3(t_):
                    return t_.rearrange("p (c t) -> p c t", t=64)

                gam3 = gamC[bp][:].rearrange("p (c o) -> p c o", o=1)
                for h in range(2):
                    hs = slice(h * 64, (h + 1) * 64)
                    cs_ = slice(h * 64, (h + 1) * 64)
                    e1 = "dve" if h == 0 else "pool"
                    cp(e1, AR_bd[bp][hs, :, cs_], c3(kk[hs, 0:512]), ["dn3"], [f"AR{bp}"])
                    yield
                    tt(e1, AR_bd[bp][hs, :, 128 + h * 64:128 + (h + 1) * 64], c3(rT[hs, :]), c3(epos[hs, 0:512]), ALU.mult, ["dn0", "dn9"], [f"AR{bp}"])
                    yield
                    cp(e1, B_bd[bp][hs, :, cs_], c3(aT[hs, 0:512]), ["dn7"], [f"B{bp}"])
                    yield
                    cp(e1, K_bd[bp][hs, :, cs_], c3(kp[hs, 0:512]), ["dn8"], [f"K{bp}"])
                    yield
                    tt(e1, BH_bd[bp][hs, :, cs_], c3(aT[hs, 0:512]), gam3[hs].broadcast_to([64, 8, 64]), ALU.mult, ["dn7", f"gamC{bp}"], [f"BH{bp}"])
                    yield
                    tt(e1, KH_bd[bp][hs, :, cs_], c3(kp[hs, 0:512]), gam3[hs].broadcast_to([64, 8, 64]), ALU.mult, ["dn8", f"gamC{bp}"], [f"KH{bp}"])
                    yield
                    cp(e1, V_bd[bp][hs, :, cs_], c3(vT[hs, :]), ["dn2"], [f"V{bp}"])
                    yield
                    cp(e1, RRK_bd[bp][hs, :, cs_], c3(t1[hs, 0:512]), ["dn4"], [f"RRK{bp}"])
                    yield

            def pack(blk, c):
                bp = blk % 2
                g = state["pk"]
                state["pk"] += 1
                import os
                pp = (g + int(os.environ.get("PPX", "0"))) % NW
                hc, hn = g % 2, (g + 1) % 2
                s0, s1 = SB0[pp], SB1[pp]
                I0, I1, I2, KAV, KVS = [f"s0_{pp}_{i}" for i in range(5)]
                k1 = [f"s1_{pp}_{i}" for i in range(5)]
                tb = [("TB", j) for j in range(3)]
                X, a3, r3, bk_, vs, wu, pt, rh, smm = Xi[pp], A3[pp], R3[pp], BK[pp], Vst[pp], WU[pp], PT[pp], RH[pp], sm[pp]
                kX, kA3, kR3, kBK, kV, kWU, kPT, kRH = f"X{pp}", f"A3{pp}", f"R3{pp}", f"BK{pp}", f"Vst{pp}", f"WU{pp}", f"PT{pp}", f"RH{pp}"
                ar, bb, kb, bh, kh, vb, rrk = AR_bd[bp], B_bd[bp], K_bd[bp], BH_bd[bp], KH_bd[bp], V_bd[bp], RRK_bd[bp]
                s0v = s0[:, 0:384].rearrange("p (a b) -> p a b", b=128)
                mm(s0[:, 0:128], bb[:, c, :], ar[:, c, 0:128], [f"B{bp}", f"AR{bp}"], [I0])
                mm(s0[:, 256:384], ar[:, c, 0:128], bb[:, c, :], [f"B{bp}", f"AR{bp}"], [I2])
                mm(s1[:, 0:128], bb[:, c, :], ar[:, c, 128:256], [f"B{bp}", f"AR{bp}"], [k1[0]])
                mm(s1[:, 128:384], kb[:, c, :], ar[:, c, :], [f"K{bp}", f"AR{bp}"], [k1[1], k1[2], k1[3]])
                o = 0
                mm(TB[:, o:o + 128], ar[:, c, 0:128], ident_bf[:], [f"AR{bp}", "ident_bf"], [tb[0]])
                mm(TB[:, o + 128:o + 256], bh[:, c, :], ident_bf[:], [f"BH{bp}", "ident_bf"], [tb[1]])
                mm(TB[:, o + 256:o + 384], kh[:, c, :], ident_bf[:], [f"KH{bp}", "ident_bf"], [tb[2]])
                mm(s0[:, 448:512], vb[:, c, :], E_bf[:], [f"V{bp}", "E_bf"], [KVS])
                import os
                if os.environ.get("RKTB", "0") == "1":
                    mm(TB[:, 384:386], rrk[:, c, :], ones_bf[:], [f"RRK{bp}", "ones_bf"], [k1[4]])
                else:
                    mm(s1[:, 384:386], rrk[:, c, :], ones_bf[:], [f"RRK{bp}", "ones_bf"], [k1[4]])
                yield
                import os
                OPS = os.environ.get("OPS", "abcdefg")
                if "a" in OPS:
                    tt("dve", X[:, 0:3:2, :], s0v[:, 0:3:2, :], M_TS[:], ALU.mult, [I0, I2, "M_TS"], [kX])
                if "b" in OPS:
                    tt("dve", a3[:], s1[:, 0:384].rearrange("p (a b) -> p a b", b=128), M_A3[:], ALU.mult, [k1[0], k1[1], k1[2], k1[3], "M_A3"], [kA3])
                if "c" in OPS:
                    tt("pool", X[:, 1, :], X[:, 0, :], ident_f[:], ALU.add, [kX, "ident_f"], [kX])
                if "d" in OPS:
                    cp("act", r3[:, 0:128], TB[:, o:o + 128], [tb[0]], [kR3])
                if "e" in OPS:
                    cp("act", bk_[:], TB[:, o + 128:o + 384].rearrange("p (a b) -> p a b", b=128), [tb[1], tb[2]], [kBK])
                if "f" in OPS:
                    cp("act", vs[:], s0[:, 448:512], [KVS], [kV])
                if "g" in OPS:
                    cp("act", smm[:, 0:1], s1[:, 384:385], [k1[4]], [f"sm{pp}rk"])
                yield
                mm(s0[:, 0:128], X[:, 2, :], X[:, 0, :], [kX], [I0])
                mm(s0[:, 256:384], X[:, 0, :], X[:, 2, :], [kX], [I2])
                yield
                cp("act", X[:, 0:3:2, :], s0v[:, 0:3:2, :], [I0, I2], [kX])
                yield
                for lv in range(1, 5):
                    mm(s0[:, 0:256], X[:, 2, :], X[:, 0:2, :].rearrange("p a b -> p (a b)"), [kX], [I0, I1])
                    mm(s0[:, 256:384], X[:, 0, :], X[:, 2, :], [kX], [I2])
                    yield
                    tt("dve", X[:, 1, :], s0[:, 128:256], X[:, 1, :], ALU.add, [I1, kX], [kX])
                    cp("act", X[:, 0:3:2, :], s0v[:, 0:3:2, :], [I0, I2], [kX])
                    yield
                mm(s0[:, 128:256], X[:, 2, :], X[:, 1, :], [kX], [I1])
                mm(s0[:, 384:448], a3[:, 1, :], vs[:], [kA3, kV], [KAV])
                yield
                tt("dve", Mb[pp][:], s0[:, 128:256], X[:, 1, :], ALU.add, [I1, kX], [f"Mb{pp}"])
                cp("act", r3[:, 128:192], s0[:, 384:448], [KAV], [kR3])
                yield
                mm(s0[:, 0:192], Mb[pp][:], r3[:], [f"Mb{pp}", kR3], [I0, I1])
                yield
                cp("act", wu[:], s0[:, 0:192], [I0, I1], [kWU])
                yield
                mm(s0[:, 192:320], wu[:, 0:128], bk_[:, 0, :], [kWU, kBK], [I1, I2])
                mm(s1[:, 0:128], wu[:, 0:128], a3[:, 0, :], [kWU, kA3], [k1[0]])
                yield
                cp("act", pt[:], s0[:, 192:320], [I1, I2], [kPT])
                tt("dve", rh[:], s1[:, 0:128], ar[:, c, 128:256], ALU.add, [k1[0], f"AR{bp}"], [kRH])
                yield
                mm(s1[:, 256:320], a3[:, 0, :], wu[:, 128:192], [kA3, kWU], [k1[2]], start=True, stop=False)
                mm(s1[:, 256:320], a3[:, 2, :], vs[:], [kA3, kV], [k1[2]], start=False, stop=False)
                mm(s1[:, 256:320], rh[:], Hs[hc][:], [kRH, f"H{hc}"], [k1[2]], start=False, stop=True)
                mm(s1[:, 320:384], bk_[:, 0, :], wu[:, 128:192], [kBK, kWU], [k1[3]], start=True, stop=False)
                mm(s1[:, 320:384], bk_[:, 1, :], vs[:], [kBK, kV], [k1[3]], start=False, stop=False)
                mm(s1[:, 320:384], pt[:], Hs[hc][:], [kPT, f"H{hc}"], [k1[3]], start=False, stop=True)
                yield
                stt(Hs[hn][:], Hs[hc][:], gamC[bp][:, c:c + 1], s1[:, 320:384], ALU.mult, ALU.add, [f"H{hc}", f"gamC{bp}", k1[3]], [f"H{hn}"])
                S.op("dve", lambda: nc.vector.bn_stats(out=smm[:, 2:8], in_=s1[:, 256:320]), [k1[2]], [f"sm{pp}st"])
                S.op("dve", lambda: nc.vector.bn_aggr(out=smm[:, 8:10], in_=smm[:, 2:8]), [f"sm{pp}st"], [f"sm{pp}mv"])
                act(smm[:, 10:11], smm[:, 9:10], AF.Sqrt, [f"sm{pp}mv"], [f"sm{pp}sd"], bias=GN_EPS, scale=1.0)
                S.op("dve", lambda: nc.vector.reciprocal(out=smm[:, 11:12], in_=smm[:, 10:11]), [f"sm{pp}sd"], [f"sm{pp}rs"])
                tsc("dve", yh[pp][:], s1[:, 256:320], smm[:, 8:9], smm[:, 11:12], ALU.subtract, ALU.mult, [k1[2], f"sm{pp}mv", f"sm{pp}rs"], [f"yh{pp}"])
                tt("pool", yh[pp][:], yh[pp][:], lnw_bc[:], ALU.mult, [f"yh{pp}", "lnw_bc"], [f"yh{pp}"])
                tt("pool", yh[pp][:], yh[pp][:], lnb_bc[:], ALU.add, [f"yh{pp}", "lnb_bc"], [f"yh{pp}"])
                stt(yfin[pp][:], vs[:], smm[:, 0:1], yh[pp][:], ALU.mult, ALU.add, [kV, f"sm{pp}rk", f"yh{pp}"], [f"yfin{pp}"])
                yield
                tr(s1[0:64, 128:256], yfin[pp][:], ident_f[:], [f"yfin{pp}", "ident_f"], [k1[1]])
                yield
                tt("dve", yTb[bp][0:64, :, c * 64:(c + 1) * 64], s1[0:64, 128:256].rearrange("p (h t) -> p h t", t=64),
                   gT[bp][0:64, :, c * 64:(c + 1) * 64], ALU.mult, [k1[1], f"gT{bp}"], [f"yTb{bp}"])

            def run_packs(blk, extra=None):
                import os
                gens = [pack(blk, c) for c in range(int(os.environ.get("PKN", "8")))]
                maxs = int(STOP[2:]) if (STOP or "").startswith("pk") else 10 ** 9
                adv = {}
                active = []
                if extra is not None and maxs > 10 ** 8:
                    active.append(extra)
                nxt = 0
                stepc = 0
                NG = len(gens)
                while nxt < NG or active:
                    npk = len([a_ for a_ in active if a_ is not extra])
                    if nxt < NG and npk < NW and (npk == 0 or stepc % 5 == 0):
                        active.append(gens[nxt])
                        nxt += 1
                    for gkk in list(active):
                        try:
                            adv[id(gkk)] = adv.get(id(gkk), 0) + 1
                            if adv[id(gkk)] > maxs:
                                raise StopIteration
                            next(gkk)
                            if gkk is extra:
                                next(gkk)
                        except StopIteration:
                            active.remove(gkk)
                    stepc += 1

            for _ in prep(0):
                pass
            for blk in range(NBLK):
                if STOP == "prep":
                    S.barrier()
                    return nc
                run_packs(blk, prep(blk + 1) if blk + 1 < NBLK else None)
                if STOP == "blk1" or (STOP or "").startswith("pk"):
                    S.barrier()
                    return nc
                bp = blk % 2
                if blk == 0:
                    for h_ in range(2):
                        dump(f"yT{h_}", yTb[bp][0:64, h_, 0:128], 128)
                    dump("H1", Hs[0][:], 64)
                S.dma("sp", f"yst{bp}", ysv[blk // 4][:, :, (blk % 4) * 512:(blk % 4 + 1) * 512], yTb[bp][0:64, :, :], [f"yTb{bp}"], ["ysrc"])
                if blk % 4 == 3:
                    ymark[blk // 4] = [(st_, S.cnt[st_]) for st_ in ("yst0", "yst1")]
                if blk % 4 == 0 and blk > 0:
                    exchange(blk // 4 - 1)
            S.barrier()
            exchange(3)
            GBL[:] = [0, 1, 2, 3, 4, 5, 7]

        if STOP == "p1":
            return nc

        if STOP == "cc":
            return nc
        ydv = ydst.ap().rearrange("j (h i) t -> i j h t", i=64)
        with contextlib.ExitStack() as p2:
            wo_p = sb([128, 4, D], BF16, stack=p2)
            wo_r = sb([128, 8, D], BF16, stack=p2)
            yall = [sb([128, 8, 512], BF16, stack=p2) for _ in range(2)]
            x1t = [sb([128, D], stack=p2) for _ in range(2)]
            g1b3 = g1bc[:].rearrange("p (k n) -> p k n", k=1)
            load_cast(lambda a, b: wo_p[:, :, a:b], lambda a, b: w_out[0:512, a:b].rearrange("(k p) n -> p k n", p=128), 128, 4, D,
                      ["wo_p", "g1bc"], scale_bc=lambda a, b: g1b3[:, :, a:b].broadcast_to([128, 4, b - a]))
            load_cast(lambda a, b: wo_r[0:64, :, a:b], lambda a, b: w_out[512:1024, a:b].rearrange("(h i) n -> i h n", i=64), 64, 8, D,
                      ["wo_r", "g1bc"], scale_bc=lambda a, b: g1b3[0:64, :, a:b].broadcast_to([64, 8, b - a]))
            S.barrier()
            for ob in range(4):
                ya = yall[ob % 2]
                S.dma("pool", f"yld{ob % 2}", ya[0:64, :, :], ydv[:, bass.ds(q, 1), :, ob * 512:(ob + 1) * 512].rearrange("i j h t -> i (j h) t"),
                      ["wo_r", "wo_p"], [f"yall{ob % 2}"])
                for ti in range(4):
                    i = _xt[0] % 2
                    _xt[0] += 1
                    S.dma("sp", f"xld{i}", xt[i][:], xo[16 + ob * 512 + ti * 128:16 + ob * 512 + (ti + 1) * 128, :], (), [f"xt{i}"])
                    j = (ob * 4 + ti) % 2
                    for hf in range(2):
                        bank, bk = gbank()
                        for m in range(4):
                            mm(bank[:], mixp[:, m, ob * 512 + ti * 128:ob * 512 + (ti + 1) * 128], wo_p[:, m, hf * 512:(hf + 1) * 512],
                               ["mixp", "wo_p"], [bk], start=(m == 0), stop=False)
                        for h in range(8):
                            mm(bank[:], ya[0:64, h, ti * 128:(ti + 1) * 128], wo_r[0:64, h, hf * 512:(hf + 1) * 512],
                               [f"yall{ob % 2}", "wo_r"], [bk], start=False, stop=(h == 7))
                        tt("dve", x1t[j][:, hf * 512:(hf + 1) * 512], bank[:], xt[i][:, hf * 512:(hf + 1) * 512], ALU.add, [bk, f"xt{i}"], [f"x1t{j}"])
                    r0 = ob * 512 + ti * 128
                    if ob == 0 and ti == 0:
                        dump("x1", x1t[j][:, 0:256], 256)
                        for h_ in range(8):
                            dump(f"yall{h_}", ya[0:64, h_, 0:32], 32)
                    S.dma("sp", f"x1st{j}", x1d[r0:r0 + 128, :], x1t[j][:], [f"x1t{j}"], ["x1d"])
            S.barrier()
        pA.close()
        if STOP == "p2a":
            return nc

        with contextlib.ExitStack() as p3:
            wgu = sb([128, 8, 2 * DFF], BF16, stack=p3)
            wdn = sb([128, NFF, D], BF16, stack=p3)
            x1b = [sb([128, 2, D], stack=p3) for _ in range(2)]
            h2T = [sb([128, 8, 256], BF16, stack=p3) for _ in range(2)]
            actT = sb([128, NFF, 256], BF16, stack=p3)
            sgt = [sb([128, 256], stack=p3) for _ in range(2)]
            ot = sb([128, D], stack=p3)
            g2b3 = g2bc[:].rearrange("p (k n) -> p k n", k=1)

            def load_norm(ob):
                p = ob % 2
                for ti in range(2):
                    r0 = ob * 256 + ti * 128
                    S.dma("pool", f"x1ld{p}{ti}", x1b[p][:, ti, :], x1d[r0:r0 + 128, :], ["x1d"], [("x1b", p, ti)])
                    norm_to_fm(x1b[p][:, ti, :], 128, gsc2, sh2, "sh2", lambda k, ti=ti, p=p: h2T[p][:, k, ti * 128:(ti + 1) * 128], f"h2T{p}",
                               src_key=("x1b", p, ti))

            load_norm(0)
            rngs = []
            for i_ in range(6):
                rngs.append((i_ * 512, min((i_ + 1) * 512, DFF)))
                rngs.append((DFF + i_ * 512, min(DFF + (i_ + 1) * 512, 2 * DFF)))
            pieces = []
            for (a0, a1) in rngs:
                for kh in range(2):
                    pieces.append(lambda kh=kh, a0=a0, a1=a1: load_cast(
                        lambda a, b: wgu[:, kh * 4:(kh + 1) * 4, a0 + a:a0 + b],
                        lambda a, b: w_gu[kh * 512:(kh + 1) * 512, a0 + a:a0 + b].rearrange("(k p) n -> p k n", p=128),
                        128, 4, a1 - a0,
                        [lambda a, b: [("wgu", kh, c_) for c_ in range((a0 + a) // 256, (a0 + b + 255) // 256)]]))
            wd3 = w_down.rearrange("(f p) n -> p f n", p=128)
            for f0 in range(0, NFF, 2):
                pieces.append(lambda f0=f0: load_cast(
                    lambda a, b: wdn[:, f0:f0 + 2, a:b], lambda a, b: wd3[:, f0:f0 + 2, a:b], 128, 2, D,
                    [lambda a, b: [("wdn", f0 // 2)], "g2bc"], scale_bc=lambda a, b: g2b3[:, :, a:b].broadcast_to([128, 2, b - a])))
            _pe = [0]

            def emit_pieces(upto):
                while _pe[0] < min(upto, len(pieces)):
                    pieces[_pe[0]]()
                    _pe[0] += 1
            for ob in range(8):
                p = ob % 2
                for f in range(NFF):
                    emit_pieces(max(4 * (f // 4) + 4, 4 + 2 * f) if ob == 0 else len(pieces))
                    bg, kg = gbank()
                    for k in range(8):
                        mm(bg[:, 0:256], wgu[:, k, f * 128:(f + 1) * 128], h2T[p][:, k, :], [("wgu", k // 4, (f * 128) // 256), f"h2T{p}"], [kg], start=(k == 0), stop=(k == 7))
                    bu, ku = gbank()
                    for k in range(8):
                        mm(bu[:, 0:256], wgu[:, k, DFF + f * 128:DFF + (f + 1) * 128], h2T[p][:, k, :], [("wgu", k // 4, (DFF + f * 128) // 256), f"h2T{p}"], [ku], start=(k == 0), stop=(k == 7))
                    sj = f % 2
                    act(sgt[sj][:], bg[:, 0:256], AF.Silu, [kg], [f"sgt{sj}"])
                    tt("dve", actT[:, f, :], bu[:, 0:256], sgt[sj][:], ALU.mult, [ku, f"sgt{sj}"], ["actT"])
                    if f == NFF // 2 and ob + 1 < 8:
                        load_norm(ob + 1)
                emit_pieces(len(pieces))
                for ti in range(2):
                    for hf in range(2):
                        bank, bk = gbank()
                        for f in range(NFF):
                            mm(bank[:], actT[:, f, ti * 128:(ti + 1) * 128], wdn[:, f, hf * 512:(hf + 1) * 512], ["actT", ("wdn", f // 2)], [bk],
                               start=(f == 0), stop=(f == NFF - 1))
                        tt("dve", x1b[p][:, ti, hf * 512:(hf + 1) * 512], bank[:], x1b[p][:, ti, hf * 512:(hf + 1) * 512], ALU.add,
                           [bk, ("x1b", p, ti)], [("x1b", p, ti)])
                    xa = x1b[p][:, ti, :]
                    act(junk[:], xa, AF.Square, [("x1b", p, ti)], ["junk", "ss"], accum_out=ss[:, 0:1])
                    act(ss[:, 1:2], ss[:, 0:1], AF.Sqrt, ["ss"], ["ss1"], bias=RMS_EPS, scale=1.0 / D)
                    S.op("dve", lambda: nc.vector.reciprocal(out=ss[:, 2:3], in_=ss[:, 1:2]), ["ss1"], ["ss2"])
                    stt(ot[:], xa, ss[:, 2:3], fgbc[:], ALU.mult, ALU.mult, [("x1b", p, ti), "ss2", "fgbc"], ["ot"])
                    r0 = ob * 256 + ti * 128
                    S.dma("sp", "ost", out[r0:r0 + 128, :], ot[:], ["ot"], ["outd"])
            S.barrier()
    return nc


_NC = None


def kernel(**inputs):
    global _NC
    x = np.asarray(inputs["x"], np.float32)
    c = np.asarray(inputs["c"], np.float32)
    if _NC is None:
        _NC = build()
    g = lambda n: np.asarray(inputs[n], np.float32)[0]
    names = ["ada_w", "ada_b", "norm1_g", "pool_w", "pool_scale", "w_out", "norm2_g", "w_ffn_gu", "w_ffn_down"]
    shared = {n: np.ascontiguousarray(g(n)) for n in names}
    shared["final_g"] = np.ascontiguousarray(np.asarray(inputs["final_g"], np.float32))
    w_in, mu_shift = g("w_in"), g("mu_shift")
    in_maps = []
    wins = (2, 4, 8, 16)
    for core in range(8):
        b, qq = core // 4, core % 4
        xpad = np.zeros((T + 16, D), np.float32)
        xpad[16:] = x[b]
        fl = np.zeros((128, 65), np.float32)
        fl[:, 0] = 0.0 if qq == 0 else 1.0
        for gi, w in enumerate(wins):
            for t in range(16):
                fl[:, 1 + gi * 16 + t] = 1.0 / min(qq * TQ + t + 1, w)
        ps_ = slice(qq * 128, (qq + 1) * 128)
        cols = np.concatenate([np.arange(0, 512)] + [np.arange(512 + i * 512 + qq * 128, 512 + i * 512 + (qq + 1) * 128) for i in range(3)]
                              + [np.arange(2048, 2304)])
        mcols = np.stack([mu_shift[i * 512 + qq * 128:i * 512 + (qq + 1) * 128] for i in range(3)]
                         + [mu_shift[1536:1664], mu_shift[1664:1792]], axis=1)
        m = dict(shared)
        m["xb"] = xpad
        m["xo"] = np.ascontiguousarray(xpad[qq * TQ:qq * TQ + TQ + 16])
        m["cvec"] = np.ascontiguousarray(c[b])
        m["flags"] = fl
        m["w_in_sel"] = np.ascontiguousarray(w_in[:, cols])
        m["mu_sel"] = np.ascontiguousarray(mcols)
        m["pv_sel"] = np.ascontiguousarray(np.stack([g(n)[ps_] for n in ("w0", "a0", "k_k", "k_a", "r_k")], axis=1))
        m["ln2"] = np.ascontiguousarray(np.stack([g("lnx_w")[ps_].reshape(2, 64), g("lnx_b")[ps_].reshape(2, 64)], axis=0))
        m["Wd_sel"] = np.ascontiguousarray(g("w_decay_up")[:, ps_])
        m["Wa_sel"] = np.ascontiguousarray(g("w_iclr_up")[:, ps_])
        m["Wg_sel"] = np.ascontiguousarray(g("w_gate_up")[:, ps_])
        in_maps.append(m)
    res = run_bass_kernel_spmd(_NC, in_maps, core_ids=list(range(8)))
    if DEBUG:
        global DBG_OUT
        DBG_OUT = [res.results[core]["dbg"] for core in range(8)]
    outp = np.zeros((2, T, D), np.float32)
    for core in range(8):
        b, qq = core // 4, core % 4
        outp[b, qq * TQ:(qq + 1) * TQ] = res.results[core]["out"]
    return outp
```

```python
import numpy as np
import ml_dtypes
import concourse.bass as bass
import concourse.mybir as mybir
from concourse.bass_utils import run_bass_kernel_spmd

F32 = mybir.dt.float32
BF16 = mybir.dt.bfloat16
ALU = mybir.AluOpType
AF = mybir.ActivationFunctionType
AX = mybir.AxisListType

D = 1024
T = 8192
TQ = 2048
NBLK = T // 512
DFF = 2816
NFF = DFF // 128
GN_EPS = 64e-5
RMS_EPS = 1e-6
DEC = 0.6065306597126334


class Sched:
    def __init__(self, nc, es):
        self.nc = nc
        self.es = es
        self.eng = {"pe": nc.tensor, "act": nc.scalar, "dve": nc.vector, "pool": nc.gpsimd, "sp": nc.sync}
        self.sem = {}
        self.cnt = {}
        self.inc = {}
        self.waited = {}
        self.lastw = {}
        self.readers = {}
        for e in ("pe", "act", "dve", "pool"):
            self.stream(e, 1)

    def stream(self, name, inc):
        if name not in self.sem:
            self.sem[name] = self.es.enter_context(self.nc.semaphore("s_" + name))
            self.cnt[name] = 0
            self.inc[name] = inc
        return name

    def _wait(self, e, s, v):
        if s == e and e == "pe":
            return
        if self.waited.get((e, s), 0) >= v:
            return
        self.eng[e].wait_ge(self.sem[s], v)
        self.waited[(e, s)] = v

    @staticmethod
    def _bank(k):
        if isinstance(k, tuple) and k and k[0] == "TB":
            return "BANK_TB"
        if isinstance(k, str) and (k.startswith("s0_") or k.startswith("s1_")):
            return "BANK_" + k[:4]
        return None

    def _aug(self, keys):
        out = list(keys)
        for k in keys:
            b = self._bank(k)
            if b is not None and b not in out:
                out.append(b)
        return out

    def _deps(self, reads, writes):
        deps = set()
        for r in reads:
            if r in self.lastw:
                deps.add(self.lastw[r])
        for r in writes:
            if r in self.lastw:
                deps.add(self.lastw[r])
            for d in self.readers.get(r, ()):
                deps.add(d)
        return deps

    def _commit(self, s, reads, writes):
        v = self.cnt[s]
        for r in writes:
            self.lastw[r] = (s, v)
            self.readers[r] = []
        for r in reads:
            self.readers.setdefault(r, []).append((s, v))

    def op(self, e, fn, reads=(), writes=()):
        bk = [k for k in self._aug(list(reads) + list(writes)) if isinstance(k, str) and k.startswith("BANK_")]
        reads, writes = list(reads), list(writes) + bk
        for (s, v) in self._deps(reads, writes):
            self._wait(e, s, v)
        ins = fn()
        self.cnt[e] += 1
        ins.then_inc(self.sem[e], 1)
        self._commit(e, reads, writes)
        return ins

    def dma(self, q, s, out, in_, reads=(), writes=(), **kw):
        self.stream(s, 16)
        for (ss, v) in self._deps(reads, writes):
            self._wait(q, ss, v)
        ins = self.eng[q].dma_start(out=out, in_=in_, **kw)
        self.cnt[s] += 16
        ins.then_inc(self.sem[s], 16)
        self._commit(s, reads, writes)
        return ins

    def close(self, s):
        for r, (ss, v) in list(self.lastw.items()):
            if ss == s:
                self.lastw[r] = (s, self.cnt[s])

    def wait_all(self, e):
        for s in self.cnt:
            if self.cnt[s] > 0:
                self._wait(e, s, self.cnt[s])

    def barrier(self):
        for e in ("pe", "act", "dve", "pool", "sp"):
            self.wait_all(e)
        self.lastw.clear()
        self.readers.clear()


STOP = None
DEBUG = False
DBG_MAP = {}
DBG_OUT = None
_LAST_S = None


def build():
    import contextlib

    nc = bass.Bass("TRN2", target_bir_lowering=False)

    def din(name, shape, dt=F32):
        return nc.dram_tensor(name, list(shape), dt, kind="ExternalInput").ap()

    xb = din("xb", [T + 16, D])
    cvec = din("cvec", [D])
    flags = din("flags", [128, 65])
    ada_w = din("ada_w", [D, 6 * D])
    ada_b = din("ada_b", [6 * D])
    norm1_g = din("norm1_g", [D])
    xo = din("xo", [TQ + 16, D])
    w_in = din("w_in_sel", [D, 1152])
    mu_sel = din("mu_sel", [128, 5])
    pv_sel = din("pv_sel", [128, 5])
    ln2 = din("ln2", [2, 2, 64])
    pool_w = din("pool_w", [4, 128, 128])
    pool_scale = din("pool_scale", [512])
    w_decay_up = din("Wd_sel", [64, 128])
    w_iclr_up = din("Wa_sel", [64, 128])
    w_gate_up = din("Wg_sel", [128, 128])
    w_out = din("w_out", [D, D])
    norm2_g = din("norm2_g", [D])
    w_gu = din("w_ffn_gu", [D, 2 * DFF])
    w_down = din("w_ffn_down", [DFF, D])
    final_g = din("final_g", [D])
    out = nc.dram_tensor("out", [TQ, D], F32, kind="ExternalOutput").ap()
    ysrc = [nc.dram_tensor(f"ysrc{j}", [128, TQ], BF16) for j in range(4)]
    ydst = nc.dram_tensor("ydst", [4, 512, TQ], BF16)
    x1d = nc.dram_tensor("x1d", [TQ, D], F32)
    dbg = nc.dram_tensor("dbg", [128, 8192], F32, kind="ExternalOutput").ap() if DEBUG else None

    es = contextlib.ExitStack()
    with es:
        S = Sched(nc, es)
        global _LAST_S
        _LAST_S = S
        pid = nc.gpsimd.partition_id()
        q = pid % 4
        _n = [0]

        def sb(shape, dt=F32, stack=es, name=None):
            _n[0] += 1
            return stack.enter_context(nc.sbuf_tensor(name or f"t{_n[0]}", list(shape), dt))

        def ps(shape, dt=F32, name=None):
            _n[0] += 1
            return es.enter_context(nc.psum_tensor(name or f"p{_n[0]}", list(shape), dt))

        def mm(out, lhsT, rhs, r, w, start=True, stop=True):
            return S.op("pe", lambda: nc.tensor.matmul(out, lhsT, rhs, start=start, stop=stop), r, w)

        def tr(out, in_, ident, r, w):
            return S.op("pe", lambda: nc.tensor.transpose(out, in_, ident), r, w)

        def act(out, in_, func, r, w, bias=None, scale=None, accum_out=None):
            kw = {}
            if bias is not None:
                kw["bias"] = bias
            if scale is not None:
                kw["scale"] = scale
            if accum_out is not None:
                kw["accum_out"] = accum_out
            return S.op("act", lambda: nc.scalar.activation(out=out, in_=in_, func=func, **kw), r, w)

        def tt(e, out, in0, in1, op, r, w):
            eng = nc.vector if e == "dve" else nc.gpsimd
            return S.op(e, lambda: eng.tensor_tensor(out=out, in0=in0, in1=in1, op=op), r, w)

        def tsc(e, out, in0, s1, s2, op0, op1, r, w):
            eng = nc.vector if e == "dve" else nc.gpsimd
            if op1 is None:
                return S.op(e, lambda: eng.tensor_scalar(out=out, in0=in0, scalar1=s1, scalar2=None, op0=op0), r, w)
            return S.op(e, lambda: eng.tensor_scalar(out=out, in0=in0, scalar1=s1, scalar2=s2, op0=op0, op1=op1), r, w)

        def stt(out, in0, scalar, in1, op0, op1, r, w):
            return S.op("dve", lambda: nc.vector.scalar_tensor_tensor(out=out, in0=in0, scalar=scalar, in1=in1, op0=op0, op1=op1), r, w)

        def cp(e, out, in_, r, w):
            if e == "act":
                return S.op("act", lambda: nc.scalar.copy(out=out, in_=in_), r, w)
            eng = nc.vector if e == "dve" else nc.gpsimd
            return S.op(e, lambda: eng.tensor_copy(out=out, in_=in_), r, w)

        def mset(e, ap, val, w):
            eng = nc.vector if e == "dve" else nc.gpsimd
            return S.op(e, lambda: eng.memset(ap, val), (), w)

        NW = 3
        PS = [ps([128, 512], name=f"PS{i}") for i in range(8)]
        SB0 = [PS[p] for p in range(NW)]
        SB1 = [PS[NW + p] for p in range(NW)]
        TB = PS[6]
        GBL = [0, 1, 2, 3, 4, 5, 7]
        _gb = [0]

        def gbank():
            i = GBL[_gb[0] % len(GBL)]
            _gb[0] += 1
            return PS[i], f"GB{i}"

        ones_f = sb([128, 128])
        ident_f = sb([128, 128])
        ident_bf = sb([128, 128], BF16)
        flg = sb([128, 65])
        c_fm = sb([128, 8])
        n1g = sb([128, 8])
        n2g = sb([128, 8])
        mu = sb([128, 5])
        pscale = sb([128, 4])
        pv = sb([128, 8])
        fgbc = sb([128, D])
        omm = sb([128, 5])
        stage = [sb([128, 2048]) for _ in range(2)]
        g2bc = sb([128, D])
        modfm = sb([128, 48])
        gsc1 = sb([128, 8])
        gsc2 = sb([128, 8])
        xn = sb([128, D])
        junk = sb([128, D], BF16)
        ss = sb([128, 4])
        dbgt = sb([128, 512]) if DEBUG else None
        pA = contextlib.ExitStack()
        es.enter_context(pA)
        M_TS = sb([128, 2, 128], stack=pA)
        M_A3 = sb([128, 3, 128], stack=pA)
        Blk = sb([128, 128], stack=pA)
        E_bf = sb([128, 64], BF16, stack=pA)
        ones_bf = sb([128, 2], BF16, stack=pA)
        rmask = sb([128, 512], stack=pA)
        lnw_bc = sb([128, 64], stack=pA)
        lnb_bc = sb([128, 64], stack=pA)
        Wd = sb([128, 128], stack=pA)
        Wa = sb([128, 128], stack=pA)
        Wg = sb([128, 128], stack=pA)
        pw_f = sb([128, 4, 128], stack=pA)
        pw_bf = sb([128, 4, 128], BF16, stack=pA)
        g1bc = sb([128, D], stack=pA)
        xt = [sb([128, D], stack=pA) for _ in range(2)]
        mixp = sb([128, 4, TQ], BF16, stack=pA)
        mset("pool", ones_f[:], 1.0, ["ones_f"])

        def asel(out, pattern, cmul, cmp, w):
            return S.op("pool", lambda: nc.gpsimd.affine_select(out=out, in_=ones_f[:], pattern=pattern, compare_op=cmp,
                                                                fill=0.0, base=0, channel_multiplier=cmul), ["ones_f"], w)

        asel(ident_f[:], [[1, 128]], -1, ALU.is_equal, ["ident_f"])
        asel(M_TS[:, 0, :], [[1, 128]], -1, ALU.is_gt, ["M_TS"])
        asel(M_TS[:, 1, :], [[-1, 128]], 1, ALU.is_gt, ["M_TS"])
        asel(M_A3[:, 0, :], [[1, 128]], -1, ALU.is_ge, ["M_A3"])
        asel(M_A3[:, 1, :], [[1, 128]], -1, ALU.is_gt, ["M_A3"])
        asel(M_A3[:, 2, :], [[1, 128]], -1, ALU.is_ge, ["M_A3"])
        cp("pool", ident_bf[:], ident_f[:], ["ident_f"], ["ident_bf"])
        mset("pool", Blk[:], 0.0, ["Blk"])
        mset("pool", Blk[0:64, 0:64], 1.0, ["Blk"])
        mset("pool", Blk[64:128, 64:128], 1.0, ["Blk"])
        tt("pool", E_bf[:], ident_f[:, 0:64], ident_f[:, 64:128], ALU.add, ["ident_f"], ["E_bf"])
        mset("pool", ones_bf[:], 1.0, ["ones_bf"])
        mset("pool", rmask[:], 1.0, ["rmask"])
        mset("pool", rmask[:].rearrange("p (c t) -> p c t", t=64)[:, :, 0:1], 0.0, ["rmask"])

        def pl(dst, src, w, **kw):
            S.dma("pool", "init", dst, src, (), w, **kw)

        def fm(v, k):
            return v.rearrange("(k p) -> p k", p=128)

        pl(flg[:], flags[:, :], ["flg"])
        pl(c_fm[:], fm(cvec, 8), ["c_fm"], allow_slow_non_contiguous=True)
        pl(n1g[:], fm(norm1_g, 8), ["n1g"], allow_slow_non_contiguous=True)
        pl(n2g[:], fm(norm2_g, 8), ["n2g"], allow_slow_non_contiguous=True)
        pl(mu[:], mu_sel[:, :], ["mu"])
        pl(pscale[:], fm(pool_scale, 4), ["pscale"], allow_slow_non_contiguous=True)
        pl(pv[:, 0:5], pv_sel[:, :], ["pv"])
        for h in range(2):
            pl(lnw_bc[h * 64:(h + 1) * 64, :], ln2[0, h:h + 1, :].broadcast_to([64, 64]), ["lnw_bc"])
            pl(lnb_bc[h * 64:(h + 1) * 64, :], ln2[1, h:h + 1, :].broadcast_to([64, 64]), ["lnb_bc"])
        pl(Wd[0:64, :], w_decay_up[:, :], ["Wd"])
        pl(Wa[64:128, :], w_iclr_up[:, :], ["Wa"])
        pl(Wg[:], w_gate_up[:, :], ["Wg"])
        pl(pw_f[:], pool_w.rearrange("g c d -> c g d"), ["pw_f"])
        pl(fgbc[:], final_g.partition_broadcast(128), ["fgbc"])
        S.close("init")
        cp("pool", pw_bf[:], pw_f[:], ["pw_f"], ["pw_bf"])
        tsc("pool", omm[:], mu[:], -1.0, 1.0, ALU.mult, ALU.add, ["mu"], ["omm"])

        if STOP == "setup":
            S.barrier()
            return nc
        _st = [0]
        _ce = [0]

        def load_cast(dst_ap_fn, src_ap_fn, nparts, K, N, w, scale_bc=None, part0=0):
            ncol = max(1, 2048 // K)
            for n0 in range(0, N, ncol):
                n1 = min(N, n0 + ncol)
                i = _st[0] % 2
                _st[0] += 1
                sv = stage[i][part0:part0 + nparts, 0:K * (n1 - n0)].rearrange("p (k n) -> p k n", k=K)
                S.dma(("sp", "act")[i], f"stg{i}", sv, src_ap_fn(n0, n1), (), [f"stage{i}"])
                e = ("act", "dve")[_ce[0] % 2]
                _ce[0] += 1
                wk = w[0](n0, n1) if callable(w[0]) else [w[0]]
                if scale_bc is not None:
                    tt("dve" if e == "act" else e, dst_ap_fn(n0, n1), sv, scale_bc(n0, n1), ALU.mult, [f"stage{i}"] + w[1:], wk)
                else:
                    cp(e, dst_ap_fn(n0, n1), sv, [f"stage{i}"], wk)

        _dc = [0]

        def dump(name, ap, n, e="dve"):
            if not DEBUG:
                return
            c0 = _dc[0]
            _dc[0] += n
            DBG_MAP[name] = (c0, n)
            S.wait_all(e)
            npart = ap.shape[0]
            cp(e, dbgt[0:npart, 0:n], ap, [], ["dbgt"])
            S.dma("sp", "dbgs", dbg[0:npart, c0:c0 + n], dbgt[0:npart, 0:n], ["dbgt"], ["dbgd"])

        with contextlib.ExitStack() as p0:
            csil = sb([128, 8, 1], stack=p0)
            crep = sb([128, 8, 128], stack=p0)
            adab = [sb([128, 512], stack=p0) for _ in range(2)]
            mblk = sb([128, 512], stack=p0)
            tmp4 = sb([128, 4, 128], stack=p0)
            act(csil[:, :, 0], c_fm[:], AF.Silu, ["c_fm"], ["csil"])
            cp("dve", crep[:], csil[:].broadcast_to([128, 8, 128]), ["csil"], ["crep"])
            for cb in range(12):
                bank, bk = gbank()
                for k2 in range(4):
                    j = _st[0] % 2
                    _st[0] += 1
                    sv = stage[j][:, 0:1024].rearrange("p (k n) -> p k n", k=2)
                    S.dma(("sp", "act")[j], f"stg{j}", sv,
                          ada_w[k2 * 256:(k2 + 1) * 256, cb * 512:(cb + 1) * 512].rearrange("(k p) n -> p k n", p=128),
                          (), [f"stage{j}"])
                    for kk in range(2):
                        k = k2 * 2 + kk
                        mm(bank[:], crep[:, k, :], sv[:, kk, :], ["crep", f"stage{j}"], [bk], start=(k == 0), stop=(k == 7))
                a = cb % 2
                S.dma("sp", f"adab{a}", adab[a][:], ada_b[cb * 512:(cb + 1) * 512].partition_broadcast(128), (), [f"adab{a}"])
                sec = cb // 2
                if sec == 2:
                    dst, dk = g1bc[:, (cb % 2) * 512:(cb % 2 + 1) * 512], "g1bc"
                elif sec == 5:
                    dst, dk = g2bc[:, (cb % 2) * 512:(cb % 2 + 1) * 512], "g2bc"
                else:
                    dst, dk = mblk[:], "mblk"
                tt("dve", dst, bank[:], adab[a][:], ALU.add, [bk, f"adab{a}"], [dk])
                tt("dve", tmp4[:], dst.rearrange("p (a b) -> p a b", b=128),
                   ident_f[:].rearrange("p (a b) -> p a b", a=1).broadcast_to([128, 4, 128]), ALU.mult, [dk, "ident_f"], ["tmp4"])
                S.op("dve", lambda: nc.vector.tensor_reduce(out=modfm[:, cb * 4:(cb + 1) * 4], in_=tmp4[:], axis=AX.X, op=ALU.add),
                     ["tmp4"], ["modfm"])
            S.barrier()
        if STOP == "ada":
            return nc
        stt(gsc1[:], modfm[:, 8:16], 1.0, n1g[:], ALU.add, ALU.mult, ["modfm", "n1g"], ["gsc1"])
        stt(gsc2[:], modfm[:, 32:40], 1.0, n2g[:], ALU.add, ALU.mult, ["modfm", "n2g"], ["gsc2"])
        dump("modfm", modfm[:], 48)
        sh1 = modfm[:, 0:8]
        sh2 = modfm[:, 24:32]

        _xt = [0]

        def norm_to_fm(x_ap, nrow, gsc, sh, shk, dst_fn, dst_key, dq="sp", src_key=None, keep=None):
            if src_key is None:
                i = _xt[0] % 2
                _xt[0] += 1
                xs, xk = xt[i], f"xt{i}"
                S.dma(dq, f"xld{i}", xs[0:nrow, :], x_ap, (), [xk])
                xa = xs[0:nrow, :]
            else:
                xa, xk = x_ap, src_key
            act(junk[0:nrow, :], xa, AF.Square, [xk], ["junk", "ss"], accum_out=ss[0:nrow, 0:1])
            act(ss[0:nrow, 1:2], ss[0:nrow, 0:1], AF.Sqrt, ["ss"], ["ss1"], bias=RMS_EPS, scale=1.0 / D)
            S.op("dve", lambda: nc.vector.reciprocal(out=ss[0:nrow, 2:3], in_=ss[0:nrow, 1:2]), ["ss1"], ["ss2"])
            act(xn[0:nrow, :], xa, AF.Copy, [xk, "ss2"], ["xn"], scale=ss[0:nrow, 2:3])
            for half in range(2):
                bank, bk = gbank()
                for kk_ in range(4):
                    k = half * 4 + kk_
                    tr(bank[:, kk_ * 128:kk_ * 128 + nrow], xn[0:nrow, k * 128:(k + 1) * 128], ident_f[0:nrow, 0:nrow], ["xn", "ident_f"], [bk])
                for kk_ in range(4):
                    k = half * 4 + kk_
                    tsc("dve", dst_fn(k), bank[:, kk_ * 128:kk_ * 128 + nrow], gsc[:, k:k + 1], sh[:, k:k + 1], ALU.mult, ALU.add,
                        [bk, "gsc1", "gsc2", "modfm"], [dst_key])
            return xa, xk

        def norm_gen(x_ap, nrow, gsc, sh, shk, dst_fn, dst_key, dq="sp", src_key=None, keep=None):
            if src_key is None:
                i = _xt[0] % 2
                _xt[0] += 1
                xs, xk = xt[i], f"xt{i}"
                S.dma(dq, f"xld{i}", xs[0:nrow, :], x_ap, (), [xk])
                xa = xs[0:nrow, :]
            else:
                xa, xk = x_ap, src_key
            act(junk[0:nrow, :], xa, AF.Square, [xk], ["junk", "ss"], accum_out=ss[0:nrow, 0:1])
            yield
            act(ss[0:nrow, 1:2], ss[0:nrow, 0:1], AF.Sqrt, ["ss"], ["ss1"], bias=RMS_EPS, scale=1.0 / D)
            yield
            S.op("dve", lambda: nc.vector.reciprocal(out=ss[0:nrow, 2:3], in_=ss[0:nrow, 1:2]), ["ss1"], ["ss2"])
            yield
            act(xn[0:nrow, :], xa, AF.Copy, [xk, "ss2"], ["xn"], scale=ss[0:nrow, 2:3])
            yield
            for half in range(2):
                bank, bk = gbank()
                for kk_ in range(4):
                    k = half * 4 + kk_
                    tr(bank[:, kk_ * 128:kk_ * 128 + nrow], xn[0:nrow, k * 128:(k + 1) * 128], ident_f[0:nrow, 0:nrow], ["xn", "ident_f"], [bk])
                yield
                for kk_ in range(4):
                    k = half * 4 + kk_
                    tsc("dve", dst_fn(k), bank[:, kk_ * 128:kk_ * 128 + nrow], gsc[:, k:k + 1], sh[:, k:k + 1], ALU.mult, ALU.add,
                        [bk, "gsc1", "gsc2", "modfm"], [dst_key])
                yield

        with contextlib.ExitStack() as p1:
            w_in_bf = sb([128, 8, 9 * 128], BF16, stack=p1)
            w3 = w_in.rearrange("(k p) n -> p k n", p=128)
            load_cast(lambda a, b: w_in_bf[:, :, a:b], lambda a, b: w3[:, :, a:b], 128, 8, 1152, ["w_in_bf"])

            if STOP == "w_in":
                S.barrier()
                return nc
            hT = sb([128, 8, 528], BF16, stack=p1)
            dn = [sb([128, 528], stack=p1) for _ in range(12)]
            zT = [sb([128, 5, 513], stack=p1) for _ in range(2)]
            gT = [sb([128, 2, 512], stack=p1) for _ in range(2)]
            yTb = [sb([128, 2, 512], BF16, stack=p1) for _ in range(2)]
            gamC = [sb([128, 8], stack=p1) for _ in range(2)]
            Hs = [sb([128, 64], stack=p1) for _ in range(2)]
            AR_bd = [sb([128, 8, 256], BF16, stack=p1) for _ in range(2)]
            B_bd = [sb([128, 8, 128], BF16, stack=p1) for _ in range(2)]
            K_bd = [sb([128, 8, 128], BF16, stack=p1) for _ in range(2)]
            BH_bd = [sb([128, 8, 128], BF16, stack=p1) for _ in range(2)]
            KH_bd = [sb([128, 8, 128], BF16, stack=p1) for _ in range(2)]
            V_bd = [sb([128, 8, 128], BF16, stack=p1) for _ in range(2)]
            RRK_bd = [sb([128, 8, 128], BF16, stack=p1) for _ in range(2)]
            Xi = [sb([128, 3, 128], stack=p1) for _ in range(NW)]
            Mb = [sb([128, 128], BF16, stack=p1) for _ in range(NW)]
            A3 = [sb([128, 3, 128], BF16, stack=p1) for _ in range(NW)]
            R3 = [sb([128, 192], BF16, stack=p1) for _ in range(NW)]
            BK = [sb([128, 2, 128], BF16, stack=p1) for _ in range(NW)]
            Vst = [sb([128, 64], BF16, stack=p1) for _ in range(NW)]
            WU = [sb([128, 192], BF16, stack=p1) for _ in range(NW)]
            PT = [sb([128, 128], stack=p1) for _ in range(NW)]
            RH = [sb([128, 128], stack=p1) for _ in range(NW)]
            sm = [sb([128, 16], stack=p1) for _ in range(NW)]
            yfin = [sb([128, 64], stack=p1) for _ in range(NW)]
            yh = [sb([128, 64], stack=p1) for _ in range(NW)]
            for p in range(2):
                for tl, nm in ((AR_bd, "AR"), (B_bd, "B"), (K_bd, "K"), (BH_bd, "BH"), (KH_bd, "KH"), (V_bd, "V"), (RRK_bd, "RRK")):
                    mset("pool", tl[p][:], 0.0, [f"{nm}{p}"])
                mset("pool", Hs[p][:], 0.0, [f"H{p}"])

            for ob in range(4):
                norm_to_fm(xo[ob * 512:ob * 512 + 16, :], 16, gsc1, sh1, "sh1",
                           lambda k: hT[:, k, 0:16], "hT")
                if STOP == "norm16":
                    S.barrier()
                    return nc
                for ti in range(4):
                    norm_to_fm(xo[16 + ob * 512 + ti * 128:16 + ob * 512 + (ti + 1) * 128, :], 128, gsc1, sh1, "sh1",
                               lambda k, ti=ti: hT[:, k, 16 + ti * 128:16 + (ti + 1) * 128], "hT")
                if STOP == "norm":
                    S.barrier()
                    return nc
                if ob == 0:
                    for k_ in range(8):
                        dump(f"hT{k_}", hT[:, k_, 0:144], 144)
                for g in range(4):
                    if STOP == "g1" and g == 1:
                        S.barrier()
                        return nc
                    w = (2, 4, 8, 16)[g]
                    o_ = 6 * (g % 2)
                    uT = dn[o_]
                    kU, kD1, kD3, kD4, kD5 = f"dn{o_}", o_ + 1, f"dn{o_ + 3}", f"dn{o_ + 4}", f"dn{o_ + 5}"
                    bank, bk = gbank()
                    for k in range(8):
                        mm(bank[:], w_in_bf[:, k, g * 128:(g + 1) * 128], hT[:, k, 16:528], ["w_in_bf", "hT"], [bk], start=(k == 0), stop=(k == 7))
                    cp("act", uT[:, 16:528], bank[:], [bk], [kU])
                    bank2, bk2 = gbank()
                    for k in range(8):
                        mm(bank2[:, 0:16], w_in_bf[:, k, g * 128:(g + 1) * 128], hT[:, k, 0:16], ["w_in_bf", "hT"], [bk2], start=(k == 0), stop=(k == 7))
                    if ob == 0:
                        tsc("dve", uT[:, 0:16], bank2[:, 0:16], flg[:, 0:1], None, ALU.mult, None, [bk2, "flg"], [kU])
                    else:
                        cp("dve", uT[:, 0:16], bank2[:, 0:16], [bk2], [kU])
                    src, sk = uT, kU
                    sh_ = 1
                    lvl = 0
                    while sh_ < w:
                        dst, dk = dn[kD1 + lvl % 2], f"dn{kD1 + lvl % 2}"
                        tt("pool", dst[:, sh_:528], src[:, sh_:528], src[:, 0:528 - sh_], ALU.add, [sk], [dk])
                        src, sk = dst, dk
                        sh_ *= 2
                        lvl += 1
                    dd = dn[o_ + 3]
                    stt(dd[:, 16:528], src[:, 16:528], 1.0 / w, uT[:, 16:528], ALU.mult, ALU.subtract, [sk, kU], [kD3])
                    if ob == 0:
                        tt("dve", dn[o_ + 4][:, 0:16], src[:, 16:32], flg[:, 1 + g * 16:1 + (g + 1) * 16], ALU.mult, [sk, "flg"], [kD4])
                        tt("dve", dd[:, 16:32], dn[o_ + 4][:, 0:16], uT[:, 16:32], ALU.subtract, [kD4, kU, kD3], [kD3])
                    dbfT = dn[o_ + 5][:, 0:256].bitcast(BF16)
                    cp("act", dbfT, dd[:, 16:528], [kD3], [kD5])
                    bank3, bk3 = gbank()
                    mm(bank3[:], pw_bf[:, g, :], dbfT, ["pw_bf", kD5], [bk3])
                    tsc("dve", mixp[:, g, ob * 512:(ob + 1) * 512], bank3[:], pscale[:, g:g + 1], None, ALU.mult, None, [bk3, "pscale"], ["mixp"])

            for g_ in range(4):
                dump(f"mixp{g_}", mixp[:, g_, 0:128], 128)
            if STOP == "pool":
                S.barrier()
                return nc
            S.barrier()
            GBL[:] = [7]
            ysv = [ysrc[j].ap().rearrange("(h i) t -> i h t", i=64) for j in range(4)]
            S.stream("cc", 1)

            ymark = {}

            def exchange(j):
                for st_, v_ in ymark[j]:
                    S._wait("pool", st_, v_)
                nc.gpsimd.collective_compute("AllGather", ALU.bypass, replica_groups=[[0, 1, 2, 3], [4, 5, 6, 7]],
                                             ins=[ysrc[j].ap().opt()], outs=[ydst.ap()[j].opt()]).then_inc(S.sem["cc"], 1)
                S.cnt["cc"] += 1
            state = {"pk": 0}

            def prep(blk):
                bp = blk % 2
                zt, zk = zT[bp], f"zT{bp}"
                for ti in range(4):
                    yield from norm_gen(xb[16 + blk * 512 + ti * 128:16 + blk * 512 + (ti + 1) * 128, :], 128, gsc1, sh1, "sh1",
                                        lambda k, ti=ti: hT[:, k, 16 + ti * 128:16 + (ti + 1) * 128], "hT")
                if blk == 0:
                    mset("pool", zt[:, :, 0:1], 0.0, [zk])
                else:
                    cp("pool", zt[:, :, 0:1], zT[1 - bp][:, :, 512:513], [f"zT{1 - bp}"], [zk])
                    yield
                for m in range(5):
                    bank, bk = gbank()
                    for k in range(8):
                        mm(bank[:], w_in_bf[:, k, 512 + m * 128:512 + (m + 1) * 128], hT[:, k, 16:528], ["w_in_bf", "hT"], [bk], start=(k == 0), stop=(k == 7))
                    yield
                    cp("act", zt[:, m, 1:513], bank[:], [bk], [zk])
                    yield
                zs = []
                for m in range(5):
                    tmp = dn[11]
                    act(tmp[:, 0:512], zt[:, m, 0:512], AF.Copy, [zk, "mu"], ["dn11"], scale=mu[:, m:m + 1])
                    yield
                    dst = dn[m]
                    stt(dst[:, 0:512], zt[:, m, 1:513], omm[:, m:m + 1], tmp[:, 0:512], ALU.mult, ALU.add, [zk, "omm", "dn11"], [f"dn{m}"])
                    yield
                    zs.append(dst)
                rT, kT, vT, xwa, xg = [z[:, 0:512] for z in zs]
                thx = dn[5]
                act(thx[0:64, 0:512], xwa[0:64, :], AF.Tanh, ["dn3"], ["dn5"])
                yield
                bank, bk = gbank()
                mm(bank[:], Wd[0:64, :], thx[0:64, 0:512], ["Wd", "dn5"], [bk])
                yield
                sg = dn[6]
                act(sg[:, 0:512], bank[:], AF.Sigmoid, [bk, "pv"], ["dn6"], bias=pv[:, 0:1])
                yield
                bank, bk = gbank()
                mm(bank[:], Wa[64:128, :], xwa[64:128, :], ["Wa", "dn3"], [bk])
                yield
                aT = dn[7]
                act(aT[:, 0:512], bank[:], AF.Sigmoid, [bk, "pv"], ["dn7"], bias=pv[:, 1:2])
                yield
                sgx = dn[5]
                act(sgx[:, 0:512], xg, AF.Sigmoid, ["dn4"], ["dn5"])
                yield
                for h in range(2):
                    bank, bk = gbank()
                    mm(bank[0:64, :], Wg[:, h * 64:(h + 1) * 64], sgx[:, 0:512], ["Wg", "dn5"], [bk])
                    yield
                    cp("act", gT[bp][0:64, h, :], bank[0:64, :], [bk], [f"gT{bp}"])
                    yield
                cs = dn[8]
                S.op("dve", lambda: nc.vector.tensor_tensor_scan(out=cs[:, 0:512], data0=rmask[:], data1=sg[:, 0:512], initial=0.0,
                                                                 op0=ALU.mult, op1=ALU.add), ["rmask", "dn6"], ["dn8"])
                yield
                epos = dn[9]
                act(epos[:, 0:512], cs[:, 0:512], AF.Exp, ["dn8"], ["dn9"], scale=-DEC)
                yield
                cp("pool", gamC[bp][:], epos[:, 0:512].rearrange("p (c t) -> p c t", t=64)[:, :, 63], ["dn9"], [f"gamC{bp}"])
                yield
                eneg = dn[10]
                act(eneg[:, 0:512], cs[:, 0:512], AF.Exp, ["dn8"], ["dn10"], scale=DEC)
                yield
                tt("dve", cs[:, 0:512], cs[:, 0:512], sg[:, 0:512], ALU.subtract, ["dn8", "dn6"], ["dn8"])
                yield
                eprev = dn[6]
                act(eprev[:, 0:512], cs[:, 0:512], AF.Exp, ["dn8"], ["dn6"], scale=-DEC)
                yield
                kkr = dn[3]
                tsc("pool", kkr[:, 0:512], kT, pv[:, 2:3], None, ALU.mult, None, ["dn1", "pv"], ["dn3"])
                yield
                sq = dn[4]
                tt("pool", sq[:, 0:512], kkr[:, 0:512], kkr[:, 0:512], ALU.mult, ["dn3"], ["dn4"])
                yield
                bank, bk = gbank()
                mm(bank[:], Blk[:], sq[:, 0:512], ["Blk", "dn4"], [bk])
                yield
                act(sq[:, 0:512], bank[:], AF.Sqrt, [bk], ["dn4"])
                yield
                tsc("dve", sq[:, 0:512], sq[:, 0:512], 1e-12, None, ALU.max, None, ["dn4"], ["dn4"])
                yield
                S.op("dve", lambda: nc.vector.reciprocal(out=sq[:, 0:512], in_=sq[:, 0:512]), ["dn4"], ["dn4"])
                yield
                kk = dn[3]
                tt("dve", kk[:, 0:512], kkr[:, 0:512], sq[:, 0:512], ALU.mult, ["dn3", "dn4"], ["dn3"])
                yield
                t1 = dn[4]
                tsc("dve", t1[:, 0:512], aT[:, 0:512], -1.0, pv[:, 3:4], ALU.add, ALU.mult, ["dn7", "pv"], ["dn4"])
                yield
                kp = dn[8]
                stt(kp[:, 0:512], t1[:, 0:512], 1.0, kT, ALU.add, ALU.mult, ["dn4", "dn1"], ["dn8"])
                yield
                tt("pool", aT[:, 0:512], aT[:, 0:512], kk[:, 0:512], ALU.mult, ["dn7", "dn3"], ["dn7"])
                yield
                stt(t1[:, 0:512], rT, pv[:, 4:5], kp[:, 0:512], ALU.mult, ALU.mult, ["dn0", "pv", "dn8"], ["dn4"])
                yield
                stt(kk[:, 0:512], kk[:, 0:512], -1.0, eprev[:, 0:512], ALU.mult, ALU.mult, ["dn3", "dn6"], ["dn3"])
                yield
                tt("pool", aT[:, 0:512], aT[:, 0:512], eneg[:, 0:512], ALU.mult, ["dn7", "dn10"], ["dn7"])
                yield
                tt("dve", kp[:, 0:512], kp[:, 0:512], eneg[:, 0:512], ALU.mult, ["dn8", "dn10"], ["dn8"])
                yield

                if blk == 0:
                    for nm_, t__ in (("rT", rT), ("vT", vT), ("atil", kk[:, 0:512]), ("btil", aT[:, 0:512]), ("ktil", kp[:, 0:512]),
                                     ("epos", epos[:, 0:512]), ("rrk", t1[:, 0:512])):
                        dump(nm_, t__[:, 0:128], 128)
                    for h_ in range(2):
                        dump(f"gT{h_}", gT[bp][0:64, h_, 0:64], 64)

                def c3(t_):
                    return t_.rearrange("p (c t) -> p c t", t=64)

                gam3 = gamC[bp][:].rearrange("p (c o) -> p c o", o=1)
                for h in range(2):
                    hs = slice(h * 64, (h + 1) * 64)
                    cs_ = slice(h * 64, (h + 1) * 64)
                    e1 = "dve" if h == 0 else "pool"
                    cp(e1, AR_bd[bp][hs, :, cs_], c3(kk[hs, 0:512]), ["dn3"], [f"AR{bp}"])
                    yield
                    tt(e1, AR_bd[bp][hs, :, 128 + h * 64:128 + (h + 1) * 64], c3(rT[hs, :]), c3(epos[hs, 0:512]), ALU.mult, ["dn0", "dn9"], [f"AR{bp}"])
                    yield
                    cp(e1, B_bd[bp][hs, :, cs_], c3(aT[hs, 0:512]), ["dn7"], [f"B{bp}"])
                    yield
                    cp(e1, K_bd[bp][hs, :, cs_], c3(kp[hs, 0:512]), ["dn8"], [f"K{bp}"])
                    yield
                    tt(e1, BH_bd[bp][hs, :, cs_], c3(aT[hs, 0:512]), gam3[hs].broadcast_to([64, 8, 64]), ALU.mult, ["dn7", f"gamC{bp}"], [f"BH{bp}"])
                    yield
                    tt(e1, KH_bd[bp][hs, :, cs_], c3(kp[hs, 0:512]), gam3[hs].broadcast_to([64, 8, 64]), ALU.mult, ["dn8", f"gamC{bp}"], [f"KH{bp}"])
                    yield
                    cp(e1, V_bd[bp][hs, :, cs_], c3(vT[hs, :]), ["dn2"], [f"V{bp}"])
                    yield
                    cp(e1, RRK_bd[bp][hs, :, cs_], c3(t1[hs, 0:512]), ["dn4"], [f"RRK{bp}"])
                    yield

            def pack(blk, c):
                bp = blk % 2
                g = state["pk"]
                state["pk"] += 1
                import os
                pp = (g + int(os.environ.get("PPX", "0"))) % NW
                hc, hn = g % 2, (g + 1) % 2
                s0, s1 = SB0[pp], SB1[pp]
                I0, I1, I2, KAV, KVS = [f"s0_{pp}_{i}" for i in range(5)]
                k1 = [f"s1_{pp}_{i}" for i in range(5)]
                tb = [("TB", j) for j in range(3)]
                X, a3, r3, bk_, vs, wu, pt, rh, smm = Xi[pp], A3[pp], R3[pp], BK[pp], Vst[pp], WU[pp], PT[pp], RH[pp], sm[pp]
                kX, kA3, kR3, kBK, kV, kWU, kPT, kRH = f"X{pp}", f"A3{pp}", f"R3{pp}", f"BK{pp}", f"Vst{pp}", f"WU{pp}", f"PT{pp}", f"RH{pp}"
                ar, bb, kb, bh, kh, vb, rrk = AR_bd[bp], B_bd[bp], K_bd[bp], BH_bd[bp], KH_bd[bp], V_bd[bp], RRK_bd[bp]
                s0v = s0[:, 0:384].rearrange("p (a b) -> p a b", b=128)
                mm(s0[:, 0:128], bb[:, c, :], ar[:, c, 0:128], [f"B{bp}", f"AR{bp}"], [I0])
                mm(s0[:, 256:384], ar[:, c, 0:128], bb[:, c, :], [f"B{bp}", f"AR{bp}"], [I2])
                mm(s1[:, 0:128], bb[:, c, :], ar[:, c, 128:256], [f"B{bp}", f"AR{bp}"], [k1[0]])
                mm(s1[:, 128:384], kb[:, c, :], ar[:, c, :], [f"K{bp}", f"AR{bp}"], [k1[1], k1[2], k1[3]])
                o = 0
                mm(TB[:, o:o + 128], ar[:, c, 0:128], ident_bf[:], [f"AR{bp}", "ident_bf"], [tb[0]])
                mm(TB[:, o + 128:o + 256], bh[:, c, :], ident_bf[:], [f"BH{bp}", "ident_bf"], [tb[1]])
                mm(TB[:, o + 256:o + 384], kh[:, c, :], ident_bf[:], [f"KH{bp}", "ident_bf"], [tb[2]])
                mm(s0[:, 448:512], vb[:, c, :], E_bf[:], [f"V{bp}", "E_bf"], [KVS])
                import os
                if os.environ.get("RKTB", "0") == "1":
                    mm(TB[:, 384:386], rrk[:, c, :], ones_bf[:], [f"RRK{bp}", "ones_bf"], [k1[4]])
                else:
                    mm(s1[:, 384:386], rrk[:, c, :], ones_bf[:], [f"RRK{bp}", "ones_bf"], [k1[4]])
                yield
                import os
                OPS = os.environ.get("OPS", "abcdefg")
                if "a" in OPS:
                    tt("dve", X[:, 0:3:2, :], s0v[:, 0:3:2, :], M_TS[:], ALU.mult, [I0, I2, "M_TS"], [kX])
                if "b" in OPS:
                    tt("dve", a3[:], s1[:, 0:384].rearrange("p (a b) -> p a b", b=128), M_A3[:], ALU.mult, [k1[0], k1[1], k1[2], k1[3], "M_A3"], [kA3])
                if "c" in OPS:
                    tt("pool", X[:, 1, :], X[:, 0, :], ident_f[:], ALU.add, [kX, "ident_f"], [kX])
                if "d" in OPS:
                    cp("act", r3[:, 0:128], TB[:, o:o + 128], [tb[0]], [kR3])
                if "e" in OPS:
                    cp("act", bk_[:], TB[:, o + 128:o + 384].rearrange("p (a b) -> p a b", b=128), [tb[1], tb[2]], [kBK])
                if "f" in OPS:
                    cp("act", vs[:], s0[:, 448:512], [KVS], [kV])
                if "g" in OPS:
                    cp("act", smm[:, 0:1], s1[:, 384:385], [k1[4]], [f"sm{pp}rk"])
                yield
                mm(s0[:, 0:128], X[:, 2, :], X[:, 0, :], [kX], [I0])
                mm(s0[:, 256:384], X[:, 0, :], X[:, 2, :], [kX], [I2])
                yield
                cp("act", X[:, 0:3:2, :], s0v[:, 0:3:2, :], [I0, I2], [kX])
                yield
                for lv in range(1, 5):
                    mm(s0[:, 0:256], X[:, 2, :], X[:, 0:2, :].rearrange("p a b -> p (a b)"), [kX], [I0, I1])
                    mm(s0[:, 256:384], X[:, 0, :], X[:, 2, :], [kX], [I2])
                    yield
                    tt("dve", X[:, 1, :], s0[:, 128:256], X[:, 1, :], ALU.add, [I1, kX], [kX])
                    cp("act", X[:, 0:3:2, :], s0v[:, 0:3:2, :], [I0, I2], [kX])
                    yield
                mm(s0[:, 128:256], X[:, 2, :], X[:, 1, :], [kX], [I1])
                mm(s0[:, 384:448], a3[:, 1, :], vs[:], [kA3, kV], [KAV])
                yield
                tt("dve", Mb[pp][:], s0[:, 128:256], X[:, 1, :], ALU.add, [I1, kX], [f"Mb{pp}"])
                cp("act", r3[:, 128:192], s0[:, 384:448], [KAV], [kR3])
                yield
                mm(s0[:, 0:192], Mb[pp][:], r3[:], [f"Mb{pp}", kR3], [I0, I1])
                yield
                cp("act", wu[:], s0[:, 0:192], [I0, I1], [kWU])
                yield
                mm(s0[:, 192:320], wu[:, 0:128], bk_[:, 0, :], [kWU, kBK], [I1, I2])
                mm(s1[:, 0:128], wu[:, 0:128], a3[:, 0, :], [kWU, kA3], [k1[0]])
                yield
                cp("act", pt[:], s0[:, 192:320], [I1, I2], [kPT])
                tt("dve", rh[:], s1[:, 0:128], ar[:, c, 128:256], ALU.add, [k1[0], f"AR{bp}"], [kRH])
                yield
                mm(s1[:, 256:320], a3[:, 0, :], wu[:, 128:192], [kA3, kWU], [k1[2]], start=True, stop=False)
                mm(s1[:, 256:320], a3[:, 2, :], vs[:], [kA3, kV], [k1[2]], start=False, stop=False)
                mm(s1[:, 256:320], rh[:], Hs[hc][:], [kRH, f"H{hc}"], [k1[2]], start=False, stop=True)
                mm(s1[:, 320:384], bk_[:, 0, :], wu[:, 128:192], [kBK, kWU], [k1[3]], start=True, stop=False)
                mm(s1[:, 320:384], bk_[:, 1, :], vs[:], [kBK, kV], [k1[3]], start=False, stop=False)
                mm(s1[:, 320:384], pt[:], Hs[hc][:], [kPT, f"H{hc}"], [k1[3]], start=False, stop=True)
                yield
                stt(Hs[hn][:], Hs[hc][:], gamC[bp][:, c:c + 1], s1[:, 320:384], ALU.mult, ALU.add, [f"H{hc}", f"gamC{bp}", k1[3]], [f"H{hn}"])
                S.op("dve", lambda: nc.vector.bn_stats(out=smm[:, 2:8], in_=s1[:, 256:320]), [k1[2]], [f"sm{pp}st"])
                S.op("dve", lambda: nc.vector.bn_aggr(out=smm[:, 8:10], in_=smm[:, 2:8]), [f"sm{pp}st"], [f"sm{pp}mv"])
                act(smm[:, 10:11], smm[:, 9:10], AF.Sqrt, [f"sm{pp}mv"], [f"sm{pp}sd"], bias=GN_EPS, scale=1.0)
                S.op("dve", lambda: nc.vector.reciprocal(out=smm[:, 11:12], in_=smm[:, 10:11]), [f"sm{pp}sd"], [f"sm{pp}rs"])
                tsc("dve", yh[pp][:], s1[:, 256:320], smm[:, 8:9], smm[:, 11:12], ALU.subtract, ALU.mult, [k1[2], f"sm{pp}mv", f"sm{pp}rs"], [f"yh{pp}"])
                tt("pool", yh[pp][:], yh[pp][:], lnw_bc[:], ALU.mult, [f"yh{pp}", "lnw_bc"], [f"yh{pp}"])
                tt("pool", yh[pp][:], yh[pp][:], lnb_bc[:], ALU.add, [f"yh{pp}", "lnb_bc"], [f"yh{pp}"])
                stt(yfin[pp][:], vs[:], smm[:, 0:1], yh[pp][:], ALU.mult, ALU.add, [kV, f"sm{pp}rk", f"yh{pp}"], [f"yfin{pp}"])
                yield
                tr(s1[0:64, 128:256], yfin[pp][:], ident_f[:], [f"yfin{pp}", "ident_f"], [k1[1]])
                yield
                tt("dve", yTb[bp][0:64, :, c * 64:(c + 1) * 64], s1[0:64, 128:256].rearrange("p (h t) -> p h t", t=64),
                   gT[bp][0:64, :, c * 64:(c + 1) * 64], ALU.mult, [k1[1], f"gT{bp}"], [f"yTb{bp}"])

            def run_packs(blk, extra=None):
                import os
                gens = [pack(blk, c) for c in range(int(os.environ.get("PKN", "8")))]
                maxs = int(STOP[2:]) if (STOP or "").startswith("pk") else 10 ** 9
                adv = {}
                active = []
                if extra is not None and maxs > 10 ** 8:
                    active.append(extra)
                nxt = 0
                stepc = 0
                NG = len(gens)
                while nxt < NG or active:
                    npk = len([a_ for a_ in active if a_ is not extra])
                    if nxt < NG and npk < NW and (npk == 0 or stepc % 5 == 0):
                        active.append(gens[nxt])
                        nxt += 1
                    for gkk in list(active):
                        try:
                            adv[id(gkk)] = adv.get(id(gkk), 0) + 1
                            if adv[id(gkk)] > maxs:
                                raise StopIteration
                            next(gkk)
                            if gkk is extra:
                                next(gkk)
                        except StopIteration:
                            active.remove(gkk)
                    stepc += 1

            for _ in prep(0):
                pass
            for blk in range(NBLK):
                if STOP == "prep":
                    S.barrier()
                    return nc
                run_packs(blk, prep(blk + 1) if blk + 1 < NBLK else None)
                if STOP == "blk1" or (STOP or "").startswith("pk"):
                    S.barrier()
                    return nc
                bp = blk % 2
                if blk == 0:
                    for h_ in range(2):
                        dump(f"yT{h_}", yTb[bp][0:64, h_, 0:128], 128)
                    dump("H1", Hs[0][:], 64)
                S.dma("sp", f"yst{bp}", ysv[blk // 4][:, :, (blk % 4) * 512:(blk % 4 + 1) * 512], yTb[bp][0:64, :, :], [f"yTb{bp}"], ["ysrc"])
                if blk % 4 == 3:
                    ymark[blk // 4] = [(st_, S.cnt[st_]) for st_ in ("yst0", "yst1")]
                if blk % 4 == 0 and blk > 0:
                    exchange(blk // 4 - 1)
            S.barrier()
            exchange(3)
            GBL[:] = [0, 1, 2, 3, 4, 5, 7]

        if STOP == "p1":
            return nc

        if STOP == "cc":
            return nc
        ydv = ydst.ap().rearrange("j (h i) t -> i j h t", i=64)
        with contextlib.ExitStack() as p2:
            wo_p = sb([128, 4, D], BF16, stack=p2)
            wo_r = sb([128, 8, D], BF16, stack=p2)
            yall = [sb([128, 8, 512], BF16, stack=p2) for _ in range(2)]
            x1t = [sb([128, D], stack=p2) for _ in range(2)]
            g1b3 = g1bc[:].rearrange("p (k n) -> p k n", k=1)
            load_cast(lambda a, b: wo_p[:, :, a:b], lambda a, b: w_out[0:512, a:b].rearrange("(k p) n -> p k n", p=128), 128, 4, D,
                      ["wo_p", "g1bc"], scale_bc=lambda a, b: g1b3[:, :, a:b].broadcast_to([128, 4, b - a]))
            load_cast(lambda a, b: wo_r[0:64, :, a:b], lambda a, b: w_out[512:1024, a:b].rearrange("(h i) n -> i h n", i=64), 64, 8, D,
                      ["wo_r", "g1bc"], scale_bc=lambda a, b: g1b3[0:64, :, a:b].broadcast_to([64, 8, b - a]))
            S.barrier()
            for ob in range(4):
                ya = yall[ob % 2]
                S.dma("pool", f"yld{ob % 2}", ya[0:64, :, :], ydv[:, bass.ds(q, 1), :, ob * 512:(ob + 1) * 512].rearrange("i j h t -> i (j h) t"),
                      ["wo_r", "wo_p"], [f"yall{ob % 2}"])
                for ti in range(4):
                    i = _xt[0] % 2
                    _xt[0] += 1
                    S.dma("sp", f"xld{i}", xt[i][:], xo[16 + ob * 512 + ti * 128:16 + ob * 512 + (ti + 1) * 128, :], (), [f"xt{i}"])
                    j = (ob * 4 + ti) % 2
                    for hf in range(2):
                        bank, bk = gbank()
                        for m in range(4):
                            mm(bank[:], mixp[:, m, ob * 512 + ti * 128:ob * 512 + (ti + 1) * 128], wo_p[:, m, hf * 512:(hf + 1) * 512],
                               ["mixp", "wo_p"], [bk], start=(m == 0), stop=False)
                        for h in range(8):
                            mm(bank[:], ya[0:64, h, ti * 128:(ti + 1) * 128], wo_r[0:64, h, hf * 512:(hf + 1) * 512],
                               [f"yall{ob % 2}", "wo_r"], [bk], start=False, stop=(h == 7))
                        tt("dve", x1t[j][:, hf * 512:(hf + 1) * 512], bank[:], xt[i][:, hf * 512:(hf + 1) * 512], ALU.add, [bk, f"xt{i}"], [f"x1t{j}"])
                    r0 = ob * 512 + ti * 128
                    if ob == 0 and ti == 0:
                        dump("x1", x1t[j][:, 0:256], 256)
                        for h_ in range(8):
                            dump(f"yall{h_}", ya[0:64, h_, 0:32], 32)
                    S.dma("sp", f"x1st{j}", x1d[r0:r0 + 128, :], x1t[j][:], [f"x1t{j}"], ["x1d"])
            S.barrier()
        pA.close()
        if STOP == "p2a":
            return nc

        with contextlib.ExitStack() as p3:
            wgu = sb([128, 8, 2 * DFF], BF16, stack=p3)
            wdn = sb([128, NFF, D], BF16, stack=p3)
            x1b = [sb([128, 2, D], stack=p3) for _ in range(2)]
            h2T = [sb([128, 8, 256], BF16, stack=p3) for _ in range(2)]
            actT = sb([128, NFF, 256], BF16, stack=p3)
            sgt = [sb([128, 256], stack=p3) for _ in range(2)]
            ot = sb([128, D], stack=p3)
            g2b3 = g2bc[:].rearrange("p (k n) -> p k n", k=1)

            def load_norm(ob):
                p = ob % 2
                for ti in range(2):
                    r0 = ob * 256 + ti * 128
                    S.dma("pool", f"x1ld{p}{ti}", x1b[p][:, ti, :], x1d[r0:r0 + 128, :], ["x1d"], [("x1b", p, ti)])
                    norm_to_fm(x1b[p][:, ti, :], 128, gsc2, sh2, "sh2", lambda k, ti=ti, p=p: h2T[p][:, k, ti * 128:(ti + 1) * 128], f"h2T{p}",
                               src_key=("x1b", p, ti))

            load_norm(0)
            rngs = []
            for i_ in range(6):
                rngs.append((i_ * 512, min((i_ + 1) * 512, DFF)))
                rngs.append((DFF + i_ * 512, min(DFF + (i_ + 1) * 512, 2 * DFF)))
            pieces = []
            for (a0, a1) in rngs:
                for kh in range(2):
                    pieces.append(lambda kh=kh, a0=a0, a1=a1: load_cast(
                        lambda a, b: wgu[:, kh * 4:(kh + 1) * 4, a0 + a:a0 + b],
                        lambda a, b: w_gu[kh * 512:(kh + 1) * 512, a0 + a:a0 + b].rearrange("(k p) n -> p k n", p=128),
                        128, 4, a1 - a0,
                        [lambda a, b: [("wgu", kh, c_) for c_ in range((a0 + a) // 256, (a0 + b + 255) // 256)]]))
            wd3 = w_down.rearrange("(f p) n -> p f n", p=128)
            for f0 in range(0, NFF, 2):
                pieces.append(lambda f0=f0: load_cast(
                    lambda a, b: wdn[:, f0:f0 + 2, a:b], lambda a, b: wd3[:, f0:f0 + 2, a:b], 128, 2, D,
                    [lambda a, b: [("wdn", f0 // 2)], "g2bc"], scale_bc=lambda a, b: g2b3[:, :, a:b].broadcast_to([128, 2, b - a])))
            _pe = [0]

            def emit_pieces(upto):
                while _pe[0] < min(upto, len(pieces)):
                    pieces[_pe[0]]()
                    _pe[0] += 1
            for ob in range(8):
                p = ob % 2
                for f in range(NFF):
                    emit_pieces(max(4 * (f // 4) + 4, 4 + 2 * f) if ob == 0 else len(pieces))
                    bg, kg = gbank()
                    for k in range(8):
                        mm(bg[:, 0:256], wgu[:, k, f * 128:(f + 1) * 128], h2T[p][:, k, :], [("wgu", k // 4, (f * 128) // 256), f"h2T{p}"], [kg], start=(k == 0), stop=(k == 7))
                    bu, ku = gbank()
                    for k in range(8):
                        mm(bu[:, 0:256], wgu[:, k, DFF + f * 128:DFF + (f + 1) * 128], h2T[p][:, k, :], [("wgu", k // 4, (DFF + f * 128) // 256), f"h2T{p}"], [ku], start=(k == 0), stop=(k == 7))
                    sj = f % 2
                    act(sgt[sj][:], bg[:, 0:256], AF.Silu, [kg], [f"sgt{sj}"])
                    tt("dve", actT[:, f, :], bu[:, 0:256], sgt[sj][:], ALU.mult, [ku, f"sgt{sj}"], ["actT"])
                    if f == NFF // 2 and ob + 1 < 8:
                        load_norm(ob + 1)
                emit_pieces(len(pieces))
                for ti in range(2):
                    for hf in range(2):
                        bank, bk = gbank()
                        for f in range(NFF):
                            mm(bank[:], actT[:, f, ti * 128:(ti + 1) * 128], wdn[:, f, hf * 512:(hf + 1) * 512], ["actT", ("wdn", f // 2)], [bk],
                               start=(f == 0), stop=(f == NFF - 1))
                        tt("dve", x1b[p][:, ti, hf * 512:(hf + 1) * 512], bank[:], x1b[p][:, ti, hf * 512:(hf + 1) * 512], ALU.add,
                           [bk, ("x1b", p, ti)], [("x1b", p, ti)])
                    xa = x1b[p][:, ti, :]
                    act(junk[:], xa, AF.Square, [("x1b", p, ti)], ["junk", "ss"], accum_out=ss[:, 0:1])
                    act(ss[:, 1:2], ss[:, 0:1], AF.Sqrt, ["ss"], ["ss1"], bias=RMS_EPS, scale=1.0 / D)
                    S.op("dve", lambda: nc.vector.reciprocal(out=ss[:, 2:3], in_=ss[:, 1:2]), ["ss1"], ["ss2"])
                    stt(ot[:], xa, ss[:, 2:3], fgbc[:], ALU.mult, ALU.mult, [("x1b", p, ti), "ss2", "fgbc"], ["ot"])
                    r0 = ob * 256 + ti * 128
                    S.dma("sp", "ost", out[r0:r0 + 128, :], ot[:], ["ot"], ["outd"])
            S.barrier()
    return nc


_NC = None


def kernel(**inputs):
    global _NC
    x = np.asarray(inputs["x"], np.float32)
    c = np.asarray(inputs["c"], np.float32)
    if _NC is None:
        _NC = build()
    g = lambda n: np.asarray(inputs[n], np.float32)[0]
    names = ["ada_w", "ada_b", "norm1_g", "pool_w", "pool_scale", "w_out", "norm2_g", "w_ffn_gu", "w_ffn_down"]
    shared = {n: np.ascontiguousarray(g(n)) for n in names}
    shared["final_g"] = np.ascontiguousarray(np.asarray(inputs["final_g"], np.float32))
    w_in, mu_shift = g("w_in"), g("mu_shift")
    in_maps = []
    wins = (2, 4, 8, 16)
    for core in range(8):
        b, qq = core // 4, core % 4
        xpad = np.zeros((T + 16, D), np.float32)
        xpad[16:] = x[b]
        fl = np.zeros((128, 65), np.float32)
        fl[:, 0] = 0.0 if qq == 0 else 1.0
        for gi, w in enumerate(wins):
            for t in range(16):
                fl[:, 1 + gi * 16 + t] = 1.0 / min(qq * TQ + t + 1, w)
        ps_ = slice(qq * 128, (qq + 1) * 128)
        cols = np.concatenate([np.arange(0, 512)] + [np.arange(512 + i * 512 + qq * 128, 512 + i * 512 + (qq + 1) * 128) for i in range(3)]
                              + [np.arange(2048, 2304)])
        mcols = np.stack([mu_shift[i * 512 + qq * 128:i * 512 + (qq + 1) * 128] for i in range(3)]
                         + [mu_shift[1536:1664], mu_shift[1664:1792]], axis=1)
        m = dict(shared)
        m["xb"] = xpad
        m["xo"] = np.ascontiguousarray(xpad[qq * TQ:qq * TQ + TQ + 16])
        m["cvec"] = np.ascontiguousarray(c[b])
        m["flags"] = fl
        m["w_in_sel"] = np.ascontiguousarray(w_in[:, cols])
        m["mu_sel"] = np.ascontiguousarray(mcols)
        m["pv_sel"] = np.ascontiguousarray(np.stack([g(n)[ps_] for n in ("w0", "a0", "k_k", "k_a", "r_k")], axis=1))
        m["ln2"] = np.ascontiguousarray(np.stack([g("lnx_w")[ps_].reshape(2, 64), g("lnx_b")[ps_].reshape(2, 64)], axis=0))
        m["Wd_sel"] = np.ascontiguousarray(g("w_decay_up")[:, ps_])
        m["Wa_sel"] = np.ascontiguousarray(g("w_iclr_up")[:, ps_])
        m["Wg_sel"] = np.ascontiguousarray(g("w_gate_up")[:, ps_])
        in_maps.append(m)
    res = run_bass_kernel_spmd(_NC, in_maps, core_ids=list(range(8)))
    if DEBUG:
        global DBG_OUT
        DBG_OUT = [res.results[core]["dbg"] for core in range(8)]
    outp = np.zeros((2, T, D), np.float32)
    for core in range(8):
        b, qq = core // 4, core % 4
        outp[b, qq * TQ:(qq + 1) * TQ] = res.results[core]["out"]
    return outp
```

```python
import numpy as np
import ml_dtypes
import concourse.bass as bass
import concourse.mybir as mybir
from concourse.bass_utils import run_bass_kernel_spmd

F32 = mybir.dt.float32
BF16 = mybir.dt.bfloat16
ALU = mybir.AluOpType
AF = mybir.ActivationFunctionType
AX = mybir.AxisListType

D = 1024
T = 8192
TQ = 2048
NBLK = T // 512
DFF = 2816
NFF = DFF // 128
GN_EPS = 64e-5
RMS_EPS = 1e-6
DEC = 0.6065306597126334


class Sched:
    def __init__(self, nc, es):
        self.nc = nc
        self.es = es
        self.eng = {"pe": nc.tensor, "act": nc.scalar, "dve": nc.vector, "pool": nc.gpsimd, "sp": nc.sync}
        self.sem = {}
        self.cnt = {}
        self.inc = {}
        self.waited = {}
        self.lastw = {}
        self.readers = {}
        for e in ("pe", "act", "dve", "pool"):
            self.stream(e, 1)

    def stream(self, name, inc):
        if name not in self.sem:
            self.sem[name] = self.es.enter_context(self.nc.semaphore("s_" + name))
            self.cnt[name] = 0
            self.inc[name] = inc
        return name

    def _wait(self, e, s, v):
        if s == e and e == "pe":
            return
        if self.waited.get((e, s), 0) >= v:
            return
        self.eng[e].wait_ge(self.sem[s], v)
        self.waited[(e, s)] = v

    @staticmethod
    def _bank(k):
        if isinstance(k, tuple) and k and k[0] == "TB":
            return "BANK_TB"
        if isinstance(k, str) and (k.startswith("s0_") or k.startswith("s1_")):
            return "BANK_" + k[:4]
        return None

    def _aug(self, keys):
        out = list(keys)
        for k in keys:
            b = self._bank(k)
            if b is not None and b not in out:
                out.append(b)
        return out

    def _deps(self, reads, writes):
        deps = set()
        for r in reads:
            if r in self.lastw:
                deps.add(self.lastw[r])
        for r in writes:
            if r in self.lastw:
                deps.add(self.lastw[r])
            for d in self.readers.get(r, ()):
                deps.add(d)
        return deps

    def _commit(self, s, reads, writes):
        v = self.cnt[s]
        for r in writes:
            self.lastw[r] = (s, v)
            self.readers[r] = []
        for r in reads:
            self.readers.setdefault(r, []).append((s, v))

    def op(self, e, fn, reads=(), writes=()):
        bk = [k for k in self._aug(list(reads) + list(writes)) if isinstance(k, str) and k.startswith("BANK_")]
        reads, writes = list(reads), list(writes) + bk
        for (s, v) in self._deps(reads, writes):
            self._wait(e, s, v)
        ins = fn()
        self.cnt[e] += 1
        ins.then_inc(self.sem[e], 1)
        self._commit(e, reads, writes)
        return ins

    def dma(self, q, s, out, in_, reads=(), writes=(), **kw):
        self.stream(s, 16)
        for (ss, v) in self._deps(reads, writes):
            self._wait(q, ss, v)
        ins = self.eng[q].dma_start(out=out, in_=in_, **kw)
        self.cnt[s] += 16
        ins.then_inc(self.sem[s], 16)
        self._commit(s, reads, writes)
        return ins

    def close(self, s):
        for r, (ss, v) in list(self.lastw.items()):
            if ss == s:
                self.lastw[r] = (s, self.cnt[s])

    def wait_all(self, e):
        for s in self.cnt:
            if self.cnt[s] > 0:
                self._wait(e, s, self.cnt[s])

    def barrier(self):
        for e in ("pe", "act", "dve", "pool", "sp"):
            self.wait_all(e)
        self.lastw.clear()
        self.readers.clear()


STOP = None
DEBUG = False
DBG_MAP = {}
DBG_OUT = None
_LAST_S = None


def build():
    import contextlib

    nc = bass.Bass("TRN2", target_bir_lowering=False)

    def din(name, shape, dt=F32):
        return nc.dram_tensor(name, list(shape), dt, kind="ExternalInput").ap()

    xb = din("xb", [T + 16, D])
    cvec = din("cvec", [D])
    flags = din("flags", [128, 65])
    ada_w = din("ada_w", [D, 6 * D])
    ada_b = din("ada_b", [6 * D])
    norm1_g = din("norm1_g", [D])
    xo = din("xo", [TQ + 16, D])
    w_in = din("w_in_sel", [D, 1152])
    mu_sel = din("mu_sel", [128, 5])
    pv_sel = din("pv_sel", [128, 5])
    ln2 = din("ln2", [2, 2, 64])
    pool_w = din("pool_w", [4, 128, 128])
    pool_scale = din("pool_scale", [512])
    w_decay_up = din("Wd_sel", [64, 128])
    w_iclr_up = din("Wa_sel", [64, 128])
    w_gate_up = din("Wg_sel", [128, 128])
    w_out = din("w_out", [D, D])
    norm2_g = din("norm2_g", [D])
    w_gu = din("w_ffn_gu", [D, 2 * DFF])
    w_down = din("w_ffn_down", [DFF, D])
    final_g = din("final_g", [D])
    out = nc.dram_tensor("out", [TQ, D], F32, kind="ExternalOutput").ap()
    ysrc = [nc.dram_tensor(f"ysrc{j}", [128, TQ], BF16) for j in range(4)]
    ydst = nc.dram_tensor("ydst", [4, 512, TQ], BF16)
    x1d = nc.dram_tensor("x1d", [TQ, D], F32)
    dbg = nc.dram_tensor("dbg", [128, 8192], F32, kind="ExternalOutput").ap() if DEBUG else None

    es = contextlib.ExitStack()
    with es:
        S = Sched(nc, es)
        global _LAST_S
        _LAST_S = S
        pid = nc.gpsimd.partition_id()
        q = pid % 4
        _n = [0]

        def sb(shape, dt=F32, stack=es, name=None):
            _n[0] += 1
            return stack.enter_context(nc.sbuf_tensor(name or f"t{_n[0]}", list(shape), dt))

        def ps(shape, dt=F32, name=None):
            _n[0] += 1
            return es.enter_context(nc.psum_tensor(name or f"p{_n[0]}", list(shape), dt))

        def mm(out, lhsT, rhs, r, w, start=True, stop=True):
            return S.op("pe", lambda: nc.tensor.matmul(out, lhsT, rhs, start=start, stop=stop), r, w)

        def tr(out, in_, ident, r, w):
            return S.op("pe", lambda: nc.tensor.transpose(out, in_, ident), r, w)

        def act(out, in_, func, r, w, bias=None, scale=None, accum_out=None):
            kw = {}
            if bias is not None:
                kw["bias"] = bias
            if scale is not None:
                kw["scale"] = scale
            if accum_out is not None:
                kw["accum_out"] = accum_out
            return S.op("act", lambda: nc.scalar.activation(out=out, in_=in_, func=func, **kw), r, w)

        def tt(e, out, in0, in1, op, r, w):
            eng = nc.vector if e == "dve" else nc.gpsimd
            return S.op(e, lambda: eng.tensor_tensor(out=out, in0=in0, in1=in1, op=op), r, w)

        def tsc(e, out, in0, s1, s2, op0, op1, r, w):
            eng = nc.vector if e == "dve" else nc.gpsimd
            if op1 is None:
                return S.op(e, lambda: eng.tensor_scalar(out=out, in0=in0, scalar1=s1, scalar2=None, op0=op0), r, w)
            return S.op(e, lambda: eng.tensor_scalar(out=out, in0=in0, scalar1=s1, scalar2=s2, op0=op0, op1=op1), r, w)

        def stt(out, in0, scalar, in1, op0, op1, r, w):
            return S.op("dve", lambda: nc.vector.scalar_tensor_tensor(out=out, in0=in0, scalar=scalar, in1=in1, op0=op0, op1=op1), r, w)

        def cp(e, out, in_, r, w):
            if e == "act":
                return S.op("act", lambda: nc.scalar.copy(out=out, in_=in_), r, w)
            eng = nc.vector if e == "dve" else nc.gpsimd
            return S.op(e, lambda: eng.tensor_copy(out=out, in_=in_), r, w)

        def mset(e, ap, val, w):
            eng = nc.vector if e == "dve" else nc.gpsimd
            return S.op(e, lambda: eng.memset(ap, val), (), w)

        NW = 3
        PS = [ps([128, 512], name=f"PS{i}") for i in range(8)]
        SB0 = [PS[p] for p in range(NW)]
        SB1 = [PS[NW + p] for p in range(NW)]
        TB = PS[6]
        GBL = [0, 1, 2, 3, 4, 5, 7]
        _gb = [0]

        def gbank():
            i = GBL[_gb[0] % len(GBL)]
            _gb[0] += 1
            return PS[i], f"GB{i}"

        ones_f = sb([128, 128])
        ident_f = sb([128, 128])
        ident_bf = sb([128, 128], BF16)
        flg = sb([128, 65])
        c_fm = sb([128, 8])
        n1g = sb([128, 8])
        n2g = sb([128, 8])
        mu = sb([128, 5])
        pscale = sb([128, 4])
        pv = sb([128, 8])
        fgbc = sb([128, D])
        omm = sb([128, 5])
        stage = [sb([128, 2048]) for _ in range(2)]
        g2bc = sb([128, D])
        modfm = sb([128, 48])
        gsc1 = sb([128, 8])
        gsc2 = sb([128, 8])
        xn = sb([128, D])
        junk = sb([128, D], BF16)
        ss = sb([128, 4])
        dbgt = sb([128, 512]) if DEBUG else None
        pA = contextlib.ExitStack()
        es.enter_context(pA)
        M_TS = sb([128, 2, 128], stack=pA)
        M_A3 = sb([128, 3, 128], stack=pA)
        Blk = sb([128, 128], stack=pA)
        E_bf = sb([128, 64], BF16, stack=pA)
        ones_bf = sb([128, 2], BF16, stack=pA)
        rmask = sb([128, 512], stack=pA)
        lnw_bc = sb([128, 64], stack=pA)
        lnb_bc = sb([128, 64], stack=pA)
        Wd = sb([128, 128], stack=pA)
        Wa = sb([128, 128], stack=pA)
        Wg = sb([128, 128], stack=pA)
        pw_f = sb([128, 4, 128], stack=pA)
        pw_bf = sb([128, 4, 128], BF16, stack=pA)
        g1bc = sb([128, D], stack=pA)
        xt = [sb([128, D], stack=pA) for _ in range(2)]
        mixp = sb([128, 4, TQ], BF16, stack=pA)
        mset("pool", ones_f[:], 1.0, ["ones_f"])

        def asel(out, pattern, cmul, cmp, w):
            return S.op("pool", lambda: nc.gpsimd.affine_select(out=out, in_=ones_f[:], pattern=pattern, compare_op=cmp,
                                                                fill=0.0, base=0, channel_multiplier=cmul), ["ones_f"], w)

        asel(ident_f[:], [[1, 128]], -1, ALU.is_equal, ["ident_f"])
        asel(M_TS[:, 0, :], [[1, 128]], -1, ALU.is_gt, ["M_TS"])
        asel(M_TS[:, 1, :], [[-1, 128]], 1, ALU.is_gt, ["M_TS"])
        asel(M_A3[:, 0, :], [[1, 128]], -1, ALU.is_ge, ["M_A3"])
        asel(M_A3[:, 1, :], [[1, 128]], -1, ALU.is_gt, ["M_A3"])
        asel(M_A3[:, 2, :], [[1, 128]], -1, ALU.is_ge, ["M_A3"])
        cp("pool", ident_bf[:], ident_f[:], ["ident_f"], ["ident_bf"])
        mset("pool", Blk[:], 0.0, ["Blk"])
        mset("pool", Blk[0:64, 0:64], 1.0, ["Blk"])
        mset("pool", Blk[64:128, 64:128], 1.0, ["Blk"])
        tt("pool", E_bf[:], ident_f[:, 0:64], ident_f[:, 64:128], ALU.add, ["ident_f"], ["E_bf"])
        mset("pool", ones_bf[:], 1.0, ["ones_bf"])
        mset("pool", rmask[:], 1.0, ["rmask"])
        mset("pool", rmask[:].rearrange("p (c t) -> p c t", t=64)[:, :, 0:1], 0.0, ["rmask"])

        def pl(dst, src, w, **kw):
            S.dma("pool", "init", dst, src, (), w, **kw)

        def fm(v, k):
            return v.rearrange("(k p) -> p k", p=128)

        pl(flg[:], flags[:, :], ["flg"])
        pl(c_fm[:], fm(cvec, 8), ["c_fm"], allow_slow_non_contiguous=True)
        pl(n1g[:], fm(norm1_g, 8), ["n1g"], allow_slow_non_contiguous=True)
        pl(n2g[:], fm(norm2_g, 8), ["n2g"], allow_slow_non_contiguous=True)
        pl(mu[:], mu_sel[:, :], ["mu"])
        pl(pscale[:], fm(pool_scale, 4), ["pscale"], allow_slow_non_contiguous=True)
        pl(pv[:, 0:5], pv_sel[:, :], ["pv"])
        for h in range(2):
            pl(lnw_bc[h * 64:(h + 1) * 64, :], ln2[0, h:h + 1, :].broadcast_to([64, 64]), ["lnw_bc"])
            pl(lnb_bc[h * 64:(h + 1) * 64, :], ln2[1, h:h + 1, :].broadcast_to([64, 64]), ["lnb_bc"])
        pl(Wd[0:64, :], w_decay_up[:, :], ["Wd"])
        pl(Wa[64:128, :], w_iclr_up[:, :], ["Wa"])
        pl(Wg[:], w_gate_up[:, :], ["Wg"])
        pl(pw_f[:], pool_w.rearrange("g c d -> c g d"), ["pw_f"])
        pl(fgbc[:], final_g.partition_broadcast(128), ["fgbc"])
        S.close("init")
        cp("pool", pw_bf[:], pw_f[:], ["pw_f"], ["pw_bf"])
        tsc("pool", omm[:], mu[:], -1.0, 1.0, ALU.mult, ALU.add, ["mu"], ["omm"])

        if STOP == "setup":
            S.barrier()
            return nc
        _st = [0]
        _ce = [0]

        def load_cast(dst_ap_fn, src_ap_fn, nparts, K, N, w, scale_bc=None, part0=0):
            ncol = max(1, 2048 // K)
            for n0 in range(0, N, ncol):
                n1 = min(N, n0 + ncol)
                i = _st[0] % 2
                _st[0] += 1
                sv = stage[i][part0:part0 + nparts, 0:K * (n1 - n0)].rearrange("p (k n) -> p k n", k=K)
                S.dma(("sp", "act")[i], f"stg{i}", sv, src_ap_fn(n0, n1), (), [f"stage{i}"])
                e = ("act", "dve")[_ce[0] % 2]
                _ce[0] += 1
                wk = w[0](n0, n1) if callable(w[0]) else [w[0]]
                if scale_bc is not None:
                    tt("dve" if e == "act" else e, dst_ap_fn(n0, n1), sv, scale_bc(n0, n1), ALU.mult, [f"stage{i}"] + w[1:], wk)
                else:
                    cp(e, dst_ap_fn(n0, n1), sv, [f"stage{i}"], wk)

        _dc = [0]

        def dump(name, ap, n, e="dve"):
            if not DEBUG:
                return
            c0 = _dc[0]
            _dc[0] += n
            DBG_MAP[name] = (c0, n)
            S.wait_all(e)
            npart = ap.shape[0]
            cp(e, dbgt[0:npart, 0:n], ap, [], ["dbgt"])
            S.dma("sp", "dbgs", dbg[0:npart, c0:c0 + n], dbgt[0:npart, 0:n], ["dbgt"], ["dbgd"])

        with contextlib.ExitStack() as p0:
            csil = sb([128, 8, 1], stack=p0)
            crep = sb([128, 8, 128], stack=p0)
            adab = [sb([128, 512], stack=p0) for _ in range(2)]
            mblk = sb([128, 512], stack=p0)
            tmp4 = sb([128, 4, 128], stack=p0)
            act(csil[:, :, 0], c_fm[:], AF.Silu, ["c_fm"], ["csil"])
            cp("dve", crep[:], csil[:].broadcast_to([128, 8, 128]), ["csil"], ["crep"])
            for cb in range(12):
                bank, bk = gbank()
                for k2 in range(4):
                    j = _st[0] % 2
                    _st[0] += 1
                    sv = stage[j][:, 0:1024].rearrange("p (k n) -> p k n", k=2)
                    S.dma(("sp", "act")[j], f"stg{j}", sv,
                          ada_w[k2 * 256:(k2 + 1) * 256, cb * 512:(cb + 1) * 512].rearrange("(k p) n -> p k n", p=128),
                          (), [f"stage{j}"])
                    for kk in range(2):
                        k = k2 * 2 + kk
                        mm(bank[:], crep[:, k, :], sv[:, kk, :], ["crep", f"stage{j}"], [bk], start=(k == 0), stop=(k == 7))
                a = cb % 2
                S.dma("sp", f"adab{a}", adab[a][:], ada_b[cb * 512:(cb + 1) * 512].partition_broadcast(128), (), [f"adab{a}"])
                sec = cb // 2
                if sec == 2:
                    dst, dk = g1bc[:, (cb % 2) * 512:(cb % 2 + 1) * 512], "g1bc"
                elif sec == 5:
                    dst, dk = g2bc[:, (cb % 2) * 512:(cb % 2 + 1) * 512], "g2bc"
                else:
                    dst, dk = mblk[:], "mblk"
                tt("dve", dst, bank[:], adab[a][:], ALU.add, [bk, f"adab{a}"], [dk])
                tt("dve", tmp4[:], dst.rearrange("p (a b) -> p a b", b=128),
                   ident_f[:].rearrange("p (a b) -> p a b", a=1).broadcast_to([128, 4, 128]), ALU.mult, [dk, "ident_f"], ["tmp4"])
                S.op("dve", lambda: nc.vector.tensor_reduce(out=modfm[:, cb * 4:(cb + 1) * 4], in_=tmp4[:], axis=AX.X, op=ALU.add),
                     ["tmp4"], ["modfm"])
            S.barrier()
        if STOP == "ada":
            return nc
        stt(gsc1[:], modfm[:, 8:16], 1.0, n1g[:], ALU.add, ALU.mult, ["modfm", "n1g"], ["gsc1"])
        stt(gsc2[:], modfm[:, 32:40], 1.0, n2g[:], ALU.add, ALU.mult, ["modfm", "n2g"], ["gsc2"])
        dump("modfm", modfm[:], 48)
        sh1 = modfm[:, 0:8]
        sh2 = modfm[:, 24:32]

        _xt = [0]

        def norm_to_fm(x_ap, nrow, gsc, sh, shk, dst_fn, dst_key, dq="sp", src_key=None, keep=None):
            if src_key is None:
                i = _xt[0] % 2
                _xt[0] += 1
                xs, xk = xt[i], f"xt{i}"
                S.dma(dq, f"xld{i}", xs[0:nrow, :], x_ap, (), [xk])
                xa = xs[0:nrow, :]
            else:
                xa, xk = x_ap, src_key
            act(junk[0:nrow, :], xa, AF.Square, [xk], ["junk", "ss"], accum_out=ss[0:nrow, 0:1])
            act(ss[0:nrow, 1:2], ss[0:nrow, 0:1], AF.Sqrt, ["ss"], ["ss1"], bias=RMS_EPS, scale=1.0 / D)
            S.op("dve", lambda: nc.vector.reciprocal(out=ss[0:nrow, 2:3], in_=ss[0:nrow, 1:2]), ["ss1"], ["ss2"])
            act(xn[0:nrow, :], xa, AF.Copy, [xk, "ss2"], ["xn"], scale=ss[0:nrow, 2:3])
            for half in range(2):
                bank, bk = gbank()
                for kk_ in range(4):
                    k = half * 4 + kk_
                    tr(bank[:, kk_ * 128:kk_ * 128 + nrow], xn[0:nrow, k * 128:(k + 1) * 128], ident_f[0:nrow, 0:nrow], ["xn", "ident_f"], [bk])
                for kk_ in range(4):
                    k = half * 4 + kk_
                    tsc("dve", dst_fn(k), bank[:, kk_ * 128:kk_ * 128 + nrow], gsc[:, k:k + 1], sh[:, k:k + 1], ALU.mult, ALU.add,
                        [bk, "gsc1", "gsc2", "modfm"], [dst_key])
            return xa, xk

        def norm_gen(x_ap, nrow, gsc, sh, shk, dst_fn, dst_key, dq="sp", src_key=None, keep=None):
            if src_key is None:
                i = _xt[0] % 2
                _xt[0] += 1
                xs, xk = xt[i], f"xt{i}"
                S.dma(dq, f"xld{i}", xs[0:nrow, :], x_ap, (), [xk])
                xa = xs[0:nrow, :]
            else:
                xa, xk = x_ap, src_key
            act(junk[0:nrow, :], xa, AF.Square, [xk], ["junk", "ss"], accum_out=ss[0:nrow, 0:1])
            yield
            act(ss[0:nrow, 1:2], ss[0:nrow, 0:1], AF.Sqrt, ["ss"], ["ss1"], bias=RMS_EPS, scale=1.0 / D)
            yield
            S.op("dve", lambda: nc.vector.reciprocal(out=ss[0:nrow, 2:3], in_=ss[0:nrow, 1:2]), ["ss1"], ["ss2"])
            yield
            act(xn[0:nrow, :], xa, AF.Copy, [xk, "ss2"], ["xn"], scale=ss[0:nrow, 2:3])
            yield
            for half in range(2):
                bank, bk = gbank()
                for kk_ in range(4):
                    k = half * 4 + kk_
                    tr(bank[:, kk_ * 128:kk_ * 128 + nrow], xn[0:nrow, k * 128:(k + 1) * 128], ident_f[0:nrow, 0:nrow], ["xn", "ident_f"], [bk])
                yield
                for kk_ in range(4):
                    k = half * 4 + kk_
                    tsc("dve", dst_fn(k), bank[:, kk_ * 128:kk_ * 128 + nrow], gsc[:, k:k + 1], sh[:, k:k + 1], ALU.mult, ALU.add,
                        [bk, "gsc1", "gsc2", "modfm"], [dst_key])
                yield

        with contextlib.ExitStack() as p1:
            w_in_bf = sb([128, 8, 9 * 128], BF16, stack=p1)
            w3 = w_in.rearrange("(k p) n -> p k n", p=128)
            load_cast(lambda a, b: w_in_bf[:, :, a:b], lambda a, b: w3[:, :, a:b], 128, 8, 1152, ["w_in_bf"])

            if STOP == "w_in":
                S.barrier()
                return nc
            hT = sb([128, 8, 528], BF16, stack=p1)
            dn = [sb([128, 528], stack=p1) for _ in range(12)]
            zT = [sb([128, 5, 513], stack=p1) for _ in range(2)]
            gT = [sb([128, 2, 512], stack=p1) for _ in range(2)]
            yTb = [sb([128, 2, 512], BF16, stack=p1) for _ in range(2)]
            gamC = [sb([128, 8], stack=p1) for _ in range(2)]
            Hs = [sb([128, 64], stack=p1) for _ in range(2)]
            AR_bd = [sb([128, 8, 256], BF16, stack=p1) for _ in range(2)]
            B_bd = [sb([128, 8, 128], BF16, stack=p1) for _ in range(2)]
            K_bd = [sb([128, 8, 128], BF16, stack=p1) for _ in range(2)]
            BH_bd = [sb([128, 8, 128], BF16, stack=p1) for _ in range(2)]
            KH_bd = [sb([128, 8, 128], BF16, stack=p1) for _ in range(2)]
            V_bd = [sb([128, 8, 128], BF16, stack=p1) for _ in range(2)]
            RRK_bd = [sb([128, 8, 128], BF16, stack=p1) for _ in range(2)]
            Xi = [sb([128, 3, 128], stack=p1) for _ in range(NW)]
            Mb = [sb([128, 128], BF16, stack=p1) for _ in range(NW)]
            A3 = [sb([128, 3, 128], BF16, stack=p1) for _ in range(NW)]
            R3 = [sb([128, 192], BF16, stack=p1) for _ in range(NW)]
            BK = [sb([128, 2, 128], BF16, stack=p1) for _ in range(NW)]
            Vst = [sb([128, 64], BF16, stack=p1) for _ in range(NW)]
            WU = [sb([128, 192], BF16, stack=p1) for _ in range(NW)]
            PT = [sb([128, 128], stack=p1) for _ in range(NW)]
            RH = [sb([128, 128], stack=p1) for _ in range(NW)]
            sm = [sb([128, 16], stack=p1) for _ in range(NW)]
            yfin = [sb([128, 64], stack=p1) for _ in range(NW)]
            yh = [sb([128, 64], stack=p1) for _ in range(NW)]
            for p in range(2):
                for tl, nm in ((AR_bd, "AR"), (B_bd, "B"), (K_bd, "K"), (BH_bd, "BH"), (KH_bd, "KH"), (V_bd, "V"), (RRK_bd, "RRK")):
                    mset("pool", tl[p][:], 0.0, [f"{nm}{p}"])
                mset("pool", Hs[p][:], 0.0, [f"H{p}"])

            uT = dn[0]
            for ob in range(4):
                norm_to_fm(xo[ob * 512:ob * 512 + 16, :], 16, gsc1, sh1, "sh1",
                           lambda k: hT[:, k, 0:16], "hT")
                if STOP == "norm16":
                    S.barrier()
                    return nc
                for ti in range(4):
                    norm_to_fm(xo[16 + ob * 512 + ti * 128:16 + ob * 512 + (ti + 1) * 128, :], 128, gsc1, sh1, "sh1",
                               lambda k, ti=ti: hT[:, k, 16 + ti * 128:16 + (ti + 1) * 128], "hT")
                if STOP == "norm":
                    S.barrier()
                    return nc
                if ob == 0:
                    for k_ in range(8):
                        dump(f"hT{k_}", hT[:, k_, 0:144], 144)
                for g in range(4):
                    if STOP == "g1" and g == 1:
                        S.barrier()
                        return nc
                    w = (2, 4, 8, 16)[g]
                    bank, bk = gbank()
                    for k in range(8):
                        mm(bank[:], w_in_bf[:, k, g * 128:(g + 1) * 128], hT[:, k, 16:528], ["w_in_bf", "hT"], [bk], start=(k == 0), stop=(k == 7))
                    cp("act", uT[:, 16:528], bank[:], [bk], ["dn0"])
                    bank2, bk2 = gbank()
                    for k in range(8):
                        mm(bank2[:, 0:16], w_in_bf[:, k, g * 128:(g + 1) * 128], hT[:, k, 0:16], ["w_in_bf", "hT"], [bk2], start=(k == 0), stop=(k == 7))
                    if ob == 0:
                        tsc("dve", uT[:, 0:16], bank2[:, 0:16], flg[:, 0:1], None, ALU.mult, None, [bk2, "flg"], ["dn0"])
                    else:
                        cp("dve", uT[:, 0:16], bank2[:, 0:16], [bk2], ["dn0"])
                    src, sk = uT, "dn0"
                    sh_ = 1
                    lvl = 0
                    while sh_ < w:
                        dst, dk = dn[1 + lvl % 2], f"dn{1 + lvl % 2}"
                        tt("pool", dst[:, sh_:528], src[:, sh_:528], src[:, 0:528 - sh_], ALU.add, [sk], [dk])
                        src, sk = dst, dk
                        sh_ *= 2
                        lvl += 1
                    dd = dn[3]
                    stt(dd[:, 16:528], src[:, 16:528], 1.0 / w, uT[:, 16:528], ALU.mult, ALU.subtract, [sk, "dn0"], ["dn3"])
                    if ob == 0:
                        tt("dve", dn[4][:, 0:16], src[:, 16:32], flg[:, 1 + g * 16:1 + (g + 1) * 16], ALU.mult, [sk, "flg"], ["dn4"])
                        tt("dve", dd[:, 16:32], dn[4][:, 0:16], uT[:, 16:32], ALU.subtract, ["dn4", "dn0", "dn3"], ["dn3"])
                    dbfT = dn[5][:, 0:256].bitcast(BF16)
                    cp("act", dbfT, dd[:, 16:528], ["dn3"], ["dn5"])
                    bank3, bk3 = gbank()
                    mm(bank3[:], pw_bf[:, g, :], dbfT, ["pw_bf", "dn5"], [bk3])
                    tsc("dve", mixp[:, g, ob * 512:(ob + 1) * 512], bank3[:], pscale[:, g:g + 1], None, ALU.mult, None, [bk3, "pscale"], ["mixp"])

            for g_ in range(4):
                dump(f"mixp{g_}", mixp[:, g_, 0:128], 128)
            if STOP == "pool":
                S.barrier()
                return nc
            S.barrier()
            GBL[:] = [7]
            ysv = [ysrc[j].ap().rearrange("(h i) t -> i h t", i=64) for j in range(4)]
            S.stream("cc", 1)

            ymark = {}

            def exchange(j):
                for st_, v_ in ymark[j]:
                    S._wait("pool", st_, v_)
                nc.gpsimd.collective_compute("AllGather", ALU.bypass, replica_groups=[[0, 1, 2, 3], [4, 5, 6, 7]],
                                             ins=[ysrc[j].ap().opt()], outs=[ydst.ap()[j].opt()]).then_inc(S.sem["cc"], 1)
                S.cnt["cc"] += 1
            state = {"pk": 0}

            def prep(blk):
                bp = blk % 2
                zt, zk = zT[bp], f"zT{bp}"
                for ti in range(4):
                    yield from norm_gen(xb[16 + blk * 512 + ti * 128:16 + blk * 512 + (ti + 1) * 128, :], 128, gsc1, sh1, "sh1",
                                        lambda k, ti=ti: hT[:, k, 16 + ti * 128:16 + (ti + 1) * 128], "hT")
                if blk == 0:
                    mset("pool", zt[:, :, 0:1], 0.0, [zk])
                else:
                    cp("pool", zt[:, :, 0:1], zT[1 - bp][:, :, 512:513], [f"zT{1 - bp}"], [zk])
                    yield
                for m in range(5):
                    bank, bk = gbank()
                    for k in range(8):
                        mm(bank[:], w_in_bf[:, k, 512 + m * 128:512 + (m + 1) * 128], hT[:, k, 16:528], ["w_in_bf", "hT"], [bk], start=(k == 0), stop=(k == 7))
                    yield
                    cp("act", zt[:, m, 1:513], bank[:], [bk], [zk])
                    yield
                zs = []
                for m in range(5):
                    tmp = dn[11]
                    act(tmp[:, 0:512], zt[:, m, 0:512], AF.Copy, [zk, "mu"], ["dn11"], scale=mu[:, m:m + 1])
                    yield
                    dst = dn[m]
                    stt(dst[:, 0:512], zt[:, m, 1:513], omm[:, m:m + 1], tmp[:, 0:512], ALU.mult, ALU.add, [zk, "omm", "dn11"], [f"dn{m}"])
                    yield
                    zs.append(dst)
                rT, kT, vT, xwa, xg = [z[:, 0:512] for z in zs]
                thx = dn[5]
                act(thx[0:64, 0:512], xwa[0:64, :], AF.Tanh, ["dn3"], ["dn5"])
                yield
                bank, bk = gbank()
                mm(bank[:], Wd[0:64, :], thx[0:64, 0:512], ["Wd", "dn5"], [bk])
                yield
                sg = dn[6]
                act(sg[:, 0:512], bank[:], AF.Sigmoid, [bk, "pv"], ["dn6"], bias=pv[:, 0:1])
                yield
                bank, bk = gbank()
                mm(bank[:], Wa[64:128, :], xwa[64:128, :], ["Wa", "dn3"], [bk])
                yield
                aT = dn[7]
                act(aT[:, 0:512], bank[:], AF.Sigmoid, [bk, "pv"], ["dn7"], bias=pv[:, 1:2])
                yield
                sgx = dn[5]
                act(sgx[:, 0:512], xg, AF.Sigmoid, ["dn4"], ["dn5"])
                yield
                for h in range(2):
                    bank, bk = gbank()
                    mm(bank[0:64, :], Wg[:, h * 64:(h + 1) * 64], sgx[:, 0:512], ["Wg", "dn5"], [bk])
                    yield
                    cp("act", gT[bp][0:64, h, :], bank[0:64, :], [bk], [f"gT{bp}"])
                    yield
                cs = dn[8]
                S.op("dve", lambda: nc.vector.tensor_tensor_scan(out=cs[:, 0:512], data0=rmask[:], data1=sg[:, 0:512], initial=0.0,
                                                                 op0=ALU.mult, op1=ALU.add), ["rmask", "dn6"], ["dn8"])
                yield
                epos = dn[9]
                act(epos[:, 0:512], cs[:, 0:512], AF.Exp, ["dn8"], ["dn9"], scale=-DEC)
                yield
                cp("pool", gamC[bp][:], epos[:, 0:512].rearrange("p (c t) -> p c t", t=64)[:, :, 63], ["dn9"], [f"gamC{bp}"])
                yield
                eneg = dn[10]
                act(eneg[:, 0:512], cs[:, 0:512], AF.Exp, ["dn8"], ["dn10"], scale=DEC)
                yield
                tt("dve", cs[:, 0:512], cs[:, 0:512], sg[:, 0:512], ALU.subtract, ["dn8", "dn6"], ["dn8"])
                yield
                eprev = dn[6]
                act(eprev[:, 0:512], cs[:, 0:512], AF.Exp, ["dn8"], ["dn6"], scale=-DEC)
                yield
                kkr = dn[3]
                tsc("pool", kkr[:, 0:512], kT, pv[:, 2:3], None, ALU.mult, None, ["dn1", "pv"], ["dn3"])
                yield
                sq = dn[4]
                tt("pool", sq[:, 0:512], kkr[:, 0:512], kkr[:, 0:512], ALU.mult, ["dn3"], ["dn4"])
                yield
                bank, bk = gbank()
                mm(bank[:], Blk[:], sq[:, 0:512], ["Blk", "dn4"], [bk])
                yield
                act(sq[:, 0:512], bank[:], AF.Sqrt, [bk], ["dn4"])
                yield
                tsc("dve", sq[:, 0:512], sq[:, 0:512], 1e-12, None, ALU.max, None, ["dn4"], ["dn4"])
                yield
                S.op("dve", lambda: nc.vector.reciprocal(out=sq[:, 0:512], in_=sq[:, 0:512]), ["dn4"], ["dn4"])
                yield
                kk = dn[3]
                tt("dve", kk[:, 0:512], kkr[:, 0:512], sq[:, 0:512], ALU.mult, ["dn3", "dn4"], ["dn3"])
                yield
                t1 = dn[4]
                tsc("dve", t1[:, 0:512], aT[:, 0:512], -1.0, pv[:, 3:4], ALU.add, ALU.mult, ["dn7", "pv"], ["dn4"])
                yield
                kp = dn[8]
                stt(kp[:, 0:512], t1[:, 0:512], 1.0, kT, ALU.add, ALU.mult, ["dn4", "dn1"], ["dn8"])
                yield
                tt("pool", aT[:, 0:512], aT[:, 0:512], kk[:, 0:512], ALU.mult, ["dn7", "dn3"], ["dn7"])
                yield
                stt(t1[:, 0:512], rT, pv[:, 4:5], kp[:, 0:512], ALU.mult, ALU.mult, ["dn0", "pv", "dn8"], ["dn4"])
                yield
                stt(kk[:, 0:512], kk[:, 0:512], -1.0, eprev[:, 0:512], ALU.mult, ALU.mult, ["dn3", "dn6"], ["dn3"])
                yield
                tt("pool", aT[:, 0:512], aT[:, 0:512], eneg[:, 0:512], ALU.mult, ["dn7", "dn10"], ["dn7"])
                yield
                tt("dve", kp[:, 0:512], kp[:, 0:512], eneg[:, 0:512], ALU.mult, ["dn8", "dn10"], ["dn8"])
                yield

                if blk == 0:
                    for nm_, t__ in (("rT", rT), ("vT", vT), ("atil", kk[:, 0:512]), ("btil", aT[:, 0:512]), ("ktil", kp[:, 0:512]),
                                     ("epos", epos[:, 0:512]), ("rrk", t1[:, 0:512])):
                        dump(nm_, t__[:, 0:128], 128)
                    for h_ in range(2):
                        dump(f"gT{h_}", gT[bp][0:64, h_, 0:64], 64)

                def c3(t_):
                    return t_.rearrange("p (c t) -> p c t", t=64)

                gam3 = gamC[bp][:].rearrange("p (c o) -> p c o", o=1)
                for h in range(2):
                    hs = slice(h * 64, (h + 1) * 64)
                    cs_ = slice(h * 64, (h + 1) * 64)
                    e1 = "dve" if h == 0 else "pool"
                    cp(e1, AR_bd[bp][hs, :, cs_], c3(kk[hs, 0:512]), ["dn3"], [f"AR{bp}"])
                    yield
                    tt(e1, AR_bd[bp][hs, :, 128 + h * 64:128 + (h + 1) * 64], c3(rT[hs, :]), c3(epos[hs, 0:512]), ALU.mult, ["dn0", "dn9"], [f"AR{bp}"])
                    yield
                    cp(e1, B_bd[bp][hs, :, cs_], c3(aT[hs, 0:512]), ["dn7"], [f"B{bp}"])
                    yield
                    cp(e1, K_bd[bp][hs, :, cs_], c3(kp[hs, 0:512]), ["dn8"], [f"K{bp}"])
                    yield
                    tt(e1, BH_bd[bp][hs, :, cs_], c3(aT[hs, 0:512]), gam3[hs].broadcast_to([64, 8, 64]), ALU.mult, ["dn7", f"gamC{bp}"], [f"BH{bp}"])
                    yield
                    tt(e1, KH_bd[bp][hs, :, cs_], c3(kp[hs, 0:512]), gam3[hs].broadcast_to([64, 8, 64]), ALU.mult, ["dn8", f"gamC{bp}"], [f"KH{bp}"])
                    yield
                    cp(e1, V_bd[bp][hs, :, cs_], c3(vT[hs, :]), ["dn2"], [f"V{bp}"])
                    yield
                    cp(e1, RRK_bd[bp][hs, :, cs_], c3(t1[hs, 0:512]), ["dn4"], [f"RRK{bp}"])
                    yield

            def pack(blk, c):
                bp = blk % 2
                g = state["pk"]
                state["pk"] += 1
                import os
                pp = (g + int(os.environ.get("PPX", "0"))) % NW
                hc, hn = g % 2, (g + 1) % 2
                s0, s1 = SB0[pp], SB1[pp]
                I0, I1, I2, KAV, KVS = [f"s0_{pp}_{i}" for i in range(5)]
                k1 = [f"s1_{pp}_{i}" for i in range(5)]
                tb = [("TB", j) for j in range(3)]
                X, a3, r3, bk_, vs, wu, pt, rh, smm = Xi[pp], A3[pp], R3[pp], BK[pp], Vst[pp], WU[pp], PT[pp], RH[pp], sm[pp]
                kX, kA3, kR3, kBK, kV, kWU, kPT, kRH = f"X{pp}", f"A3{pp}", f"R3{pp}", f"BK{pp}", f"Vst{pp}", f"WU{pp}", f"PT{pp}", f"RH{pp}"
                ar, bb, kb, bh, kh, vb, rrk = AR_bd[bp], B_bd[bp], K_bd[bp], BH_bd[bp], KH_bd[bp], V_bd[bp], RRK_bd[bp]
                s0v = s0[:, 0:384].rearrange("p (a b) -> p a b", b=128)
                mm(s0[:, 0:128], bb[:, c, :], ar[:, c, 0:128], [f"B{bp}", f"AR{bp}"], [I0])
                mm(s0[:, 256:384], ar[:, c, 0:128], bb[:, c, :], [f"B{bp}", f"AR{bp}"], [I2])
                mm(s1[:, 0:128], bb[:, c, :], ar[:, c, 128:256], [f"B{bp}", f"AR{bp}"], [k1[0]])
                mm(s1[:, 128:384], kb[:, c, :], ar[:, c, :], [f"K{bp}", f"AR{bp}"], [k1[1], k1[2], k1[3]])
                o = 0
                mm(TB[:, o:o + 128], ar[:, c, 0:128], ident_bf[:], [f"AR{bp}", "ident_bf"], [tb[0]])
                mm(TB[:, o + 128:o + 256], bh[:, c, :], ident_bf[:], [f"BH{bp}", "ident_bf"], [tb[1]])
                mm(TB[:, o + 256:o + 384], kh[:, c, :], ident_bf[:], [f"KH{bp}", "ident_bf"], [tb[2]])
                mm(s0[:, 448:512], vb[:, c, :], E_bf[:], [f"V{bp}", "E_bf"], [KVS])
                import os
                if os.environ.get("RKTB", "0") == "1":
                    mm(TB[:, 384:386], rrk[:, c, :], ones_bf[:], [f"RRK{bp}", "ones_bf"], [k1[4]])
                else:
                    mm(s1[:, 384:386], rrk[:, c, :], ones_bf[:], [f"RRK{bp}", "ones_bf"], [k1[4]])
                yield
                import os
                OPS = os.environ.get("OPS", "abcdefg")
                if "a" in OPS:
                    tt("dve", X[:, 0:3:2, :], s0v[:, 0:3:2, :], M_TS[:], ALU.mult, [I0, I2, "M_TS"], [kX])
                if "b" in OPS:
                    tt("dve", a3[:], s1[:, 0:384].rearrange("p (a b) -> p a b", b=128), M_A3[:], ALU.mult, [k1[0], k1[1], k1[2], k1[3], "M_A3"], [kA3])
                if "c" in OPS:
                    tt("pool", X[:, 1, :], X[:, 0, :], ident_f[:], ALU.add, [kX, "ident_f"], [kX])
                if "d" in OPS:
                    cp("act", r3[:, 0:128], TB[:, o:o + 128], [tb[0]], [kR3])
                if "e" in OPS:
                    cp("act", bk_[:], TB[:, o + 128:o + 384].rearrange("p (a b) -> p a b", b=128), [tb[1], tb[2]], [kBK])
                if "f" in OPS:
                    cp("act", vs[:], s0[:, 448:512], [KVS], [kV])
                if "g" in OPS:
                    cp("act", smm[:, 0:1], s1[:, 384:385], [k1[4]], [f"sm{pp}rk"])
                yield
                mm(s0[:, 0:128], X[:, 2, :], X[:, 0, :], [kX], [I0])
                mm(s0[:, 256:384], X[:, 0, :], X[:, 2, :], [kX], [I2])
                yield
                cp("act", X[:, 0:3:2, :], s0v[:, 0:3:2, :], [I0, I2], [kX])
                yield
                for lv in range(1, 5):
                    mm(s0[:, 0:256], X[:, 2, :], X[:, 0:2, :].rearrange("p a b -> p (a b)"), [kX], [I0, I1])
                    mm(s0[:, 256:384], X[:, 0, :], X[:, 2, :], [kX], [I2])
                    yield
                    tt("dve", X[:, 1, :], s0[:, 128:256], X[:, 1, :], ALU.add, [I1, kX], [kX])
                    cp("act", X[:, 0:3:2, :], s0v[:, 0:3:2, :], [I0, I2], [kX])
                    yield
                mm(s0[:, 128:256], X[:, 2, :], X[:, 1, :], [kX], [I1])
                mm(s0[:, 384:448], a3[:, 1, :], vs[:], [kA3, kV], [KAV])
                yield
                tt("dve", Mb[pp][:], s0[:, 128:256], X[:, 1, :], ALU.add, [I1, kX], [f"Mb{pp}"])
                cp("act", r3[:, 128:192], s0[:, 384:448], [KAV], [kR3])
                yield
                mm(s0[:, 0:192], Mb[pp][:], r3[:], [f"Mb{pp}", kR3], [I0, I1])
                yield
                cp("act", wu[:], s0[:, 0:192], [I0, I1], [kWU])
                yield
                mm(s0[:, 192:320], wu[:, 0:128], bk_[:, 0, :], [kWU, kBK], [I1, I2])
                mm(s1[:, 0:128], wu[:, 0:128], a3[:, 0, :], [kWU, kA3], [k1[0]])
                yield
                cp("act", pt[:], s0[:, 192:320], [I1, I2], [kPT])
                tt("dve", rh[:], s1[:, 0:128], ar[:, c, 128:256], ALU.add, [k1[0], f"AR{bp}"], [kRH])
                yield
                mm(s1[:, 256:320], a3[:, 0, :], wu[:, 128:192], [kA3, kWU], [k1[2]], start=True, stop=False)
                mm(s1[:, 256:320], a3[:, 2, :], vs[:], [kA3, kV], [k1[2]], start=False, stop=False)
                mm(s1[:, 256:320], rh[:], Hs[hc][:], [kRH, f"H{hc}"], [k1[2]], start=False, stop=True)
                mm(s1[:, 320:384], bk_[:, 0, :], wu[:, 128:192], [kBK, kWU], [k1[3]], start=True, stop=False)
                mm(s1[:, 320:384], bk_[:, 1, :], vs[:], [kBK, kV], [k1[3]], start=False, stop=False)
                mm(s1[:, 320:384], pt[:], Hs[hc][:], [kPT, f"H{hc}"], [k1[3]], start=False, stop=True)
                yield
                stt(Hs[hn][:], Hs[hc][:], gamC[bp][:, c:c + 1], s1[:, 320:384], ALU.mult, ALU.add, [f"H{hc}", f"gamC{bp}", k1[3]], [f"H{hn}"])
                S.op("dve", lambda: nc.vector.bn_stats(out=smm[:, 2:8], in_=s1[:, 256:320]), [k1[2]], [f"sm{pp}st"])
                S.op("dve", lambda: nc.vector.bn_aggr(out=smm[:, 8:10], in_=smm[:, 2:8]), [f"sm{pp}st"], [f"sm{pp}mv"])
                yield
                act(smm[:, 10:11], smm[:, 9:10], AF.Sqrt, [f"sm{pp}mv"], [f"sm{pp}sd"], bias=GN_EPS, scale=1.0)
                S.op("dve", lambda: nc.vector.reciprocal(out=smm[:, 11:12], in_=smm[:, 10:11]), [f"sm{pp}sd"], [f"sm{pp}rs"])
                yield
                tsc("dve", yh[pp][:], s1[:, 256:320], smm[:, 8:9], smm[:, 11:12], ALU.subtract, ALU.mult, [k1[2], f"sm{pp}mv", f"sm{pp}rs"], [f"yh{pp}"])
                yield
                tt("pool", yh[pp][:], yh[pp][:], lnw_bc[:], ALU.mult, [f"yh{pp}", "lnw_bc"], [f"yh{pp}"])
                tt("pool", yh[pp][:], yh[pp][:], lnb_bc[:], ALU.add, [f"yh{pp}", "lnb_bc"], [f"yh{pp}"])
                yield
                stt(yfin[pp][:], vs[:], smm[:, 0:1], yh[pp][:], ALU.mult, ALU.add, [kV, f"sm{pp}rk", f"yh{pp}"], [f"yfin{pp}"])
                yield
                tr(s1[0:64, 128:256], yfin[pp][:], ident_f[:], [f"yfin{pp}", "ident_f"], [k1[1]])
                yield
                tt("dve", yTb[bp][0:64, :, c * 64:(c + 1) * 64], s1[0:64, 128:256].rearrange("p (h t) -> p h t", t=64),
                   gT[bp][0:64, :, c * 64:(c + 1) * 64], ALU.mult, [k1[1], f"gT{bp}"], [f"yTb{bp}"])

            def run_packs(blk, extra=None):
                import os
                gens = [pack(blk, c) for c in range(int(os.environ.get("PKN", "8")))]
                maxs = int(STOP[2:]) if (STOP or "").startswith("pk") else 10 ** 9
                adv = {}
                active = []
                if extra is not None and maxs > 10 ** 8:
                    active.append(extra)
                nxt = 0
                stepc = 0
                NG = len(gens)
                while nxt < NG or active:
                    npk = len([a_ for a_ in active if a_ is not extra])
                    if nxt < NG and npk < NW and (npk == 0 or stepc % 5 == 0):
                        active.append(gens[nxt])
                        nxt += 1
                    for gkk in list(active):
                        try:
                            adv[id(gkk)] = adv.get(id(gkk), 0) + 1
                            if adv[id(gkk)] > maxs:
                                raise StopIteration
                            next(gkk)
                            if gkk is extra:
                                next(gkk)
                        except StopIteration:
                            active.remove(gkk)
                    stepc += 1

            for _ in prep(0):
                pass
            for blk in range(NBLK):
                if STOP == "prep":
                    S.barrier()
                    return nc
                run_packs(blk, prep(blk + 1) if blk + 1 < NBLK else None)
                if STOP == "blk1" or (STOP or "").startswith("pk"):
                    S.barrier()
                    return nc
                bp = blk % 2
                if blk == 0:
                    for h_ in range(2):
                        dump(f"yT{h_}", yTb[bp][0:64, h_, 0:128], 128)
                    dump("H1", Hs[0][:], 64)
                S.dma("sp", f"yst{bp}", ysv[blk // 4][:, :, (blk % 4) * 512:(blk % 4 + 1) * 512], yTb[bp][0:64, :, :], [f"yTb{bp}"], ["ysrc"])
                if blk % 4 == 3:
                    ymark[blk // 4] = [(st_, S.cnt[st_]) for st_ in ("yst0", "yst1")]
                if blk % 4 == 0 and blk > 0:
                    exchange(blk // 4 - 1)
            S.barrier()
            exchange(3)
            GBL[:] = [0, 1, 2, 3, 4, 5, 7]

        if STOP == "p1":
            return nc

        if STOP == "cc":
            return nc
        ydv = ydst.ap().rearrange("j (h i) t -> i j h t", i=64)
        with contextlib.ExitStack() as p2:
            wo_p = sb([128, 4, D], BF16, stack=p2)
            wo_r = sb([128, 8, D], BF16, stack=p2)
            yall = [sb([128, 8, 512], BF16, stack=p2) for _ in range(2)]
            x1t = [sb([128, D], stack=p2) for _ in range(2)]
            g1b3 = g1bc[:].rearrange("p (k n) -> p k n", k=1)
            load_cast(lambda a, b: wo_p[:, :, a:b], lambda a, b: w_out[0:512, a:b].rearrange("(k p) n -> p k n", p=128), 128, 4, D,
                      ["wo_p", "g1bc"], scale_bc=lambda a, b: g1b3[:, :, a:b].broadcast_to([128, 4, b - a]))
            load_cast(lambda a, b: wo_r[0:64, :, a:b], lambda a, b: w_out[512:1024, a:b].rearrange("(h i) n -> i h n", i=64), 64, 8, D,
                      ["wo_r", "g1bc"], scale_bc=lambda a, b: g1b3[0:64, :, a:b].broadcast_to([64, 8, b - a]))
            S.barrier()
            for ob in range(4):
                ya = yall[ob % 2]
                S.dma("pool", f"yld{ob % 2}", ya[0:64, :, :], ydv[:, bass.ds(q, 1), :, ob * 512:(ob + 1) * 512].rearrange("i j h t -> i (j h) t"),
                      ["wo_r", "wo_p"], [f"yall{ob % 2}"])
                for ti in range(4):
                    i = _xt[0] % 2
                    _xt[0] += 1
                    S.dma("sp", f"xld{i}", xt[i][:], xo[16 + ob * 512 + ti * 128:16 + ob * 512 + (ti + 1) * 128, :], (), [f"xt{i}"])
                    j = (ob * 4 + ti) % 2
                    for hf in range(2):
                        bank, bk = gbank()
                        for m in range(4):
                            mm(bank[:], mixp[:, m, ob * 512 + ti * 128:ob * 512 + (ti + 1) * 128], wo_p[:, m, hf * 512:(hf + 1) * 512],
                               ["mixp", "wo_p"], [bk], start=(m == 0), stop=False)
                        for h in range(8):
                            mm(bank[:], ya[0:64, h, ti * 128:(ti + 1) * 128], wo_r[0:64, h, hf * 512:(hf + 1) * 512],
                               [f"yall{ob % 2}", "wo_r"], [bk], start=False, stop=(h == 7))
                        tt("dve", x1t[j][:, hf * 512:(hf + 1) * 512], bank[:], xt[i][:, hf * 512:(hf + 1) * 512], ALU.add, [bk, f"xt{i}"], [f"x1t{j}"])
                    r0 = ob * 512 + ti * 128
                    if ob == 0 and ti == 0:
                        dump("x1", x1t[j][:, 0:256], 256)
                        for h_ in range(8):
                            dump(f"yall{h_}", ya[0:64, h_, 0:32], 32)
                    S.dma("sp", f"x1st{j}", x1d[r0:r0 + 128, :], x1t[j][:], [f"x1t{j}"], ["x1d"])
            S.barrier()
        pA.close()
        if STOP == "p2a":
            return nc

        with contextlib.ExitStack() as p3:
            wgu = sb([128, 8, 2 * DFF], BF16, stack=p3)
            wdn = sb([128, NFF, D], BF16, stack=p3)
            x1b = [sb([128, 2, D], stack=p3) for _ in range(2)]
            h2T = [sb([128, 8, 256], BF16, stack=p3) for _ in range(2)]
            actT = sb([128, NFF, 256], BF16, stack=p3)
            sgt = [sb([128, 256], stack=p3) for _ in range(2)]
            ot = sb([128, D], stack=p3)
            g2b3 = g2bc[:].rearrange("p (k n) -> p k n", k=1)

            def load_norm(ob):
                p = ob % 2
                for ti in range(2):
                    r0 = ob * 256 + ti * 128
                    S.dma("pool", f"x1ld{p}{ti}", x1b[p][:, ti, :], x1d[r0:r0 + 128, :], ["x1d"], [("x1b", p, ti)])
                    norm_to_fm(x1b[p][:, ti, :], 128, gsc2, sh2, "sh2", lambda k, ti=ti, p=p: h2T[p][:, k, ti * 128:(ti + 1) * 128], f"h2T{p}",
                               src_key=("x1b", p, ti))

            load_norm(0)
            rngs = []
            for i_ in range(6):
                rngs.append((i_ * 512, min((i_ + 1) * 512, DFF)))
                rngs.append((DFF + i_ * 512, min(DFF + (i_ + 1) * 512, 2 * DFF)))
            pieces = []
            for (a0, a1) in rngs:
                for kh in range(2):
                    pieces.append(lambda kh=kh, a0=a0, a1=a1: load_cast(
                        lambda a, b: wgu[:, kh * 4:(kh + 1) * 4, a0 + a:a0 + b],
                        lambda a, b: w_gu[kh * 512:(kh + 1) * 512, a0 + a:a0 + b].rearrange("(k p) n -> p k n", p=128),
                        128, 4, a1 - a0,
                        [lambda a, b: [("wgu", kh, c_) for c_ in range((a0 + a) // 256, (a0 + b + 255) // 256)]]))
            wd3 = w_down.rearrange("(f p) n -> p f n", p=128)
            for f0 in range(0, NFF, 2):
                pieces.append(lambda f0=f0: load_cast(
                    lambda a, b: wdn[:, f0:f0 + 2, a:b], lambda a, b: wd3[:, f0:f0 + 2, a:b], 128, 2, D,
                    [lambda a, b: [("wdn", f0 // 2)], "g2bc"], scale_bc=lambda a, b: g2b3[:, :, a:b].broadcast_to([128, 2, b - a])))
            _pe = [0]

            def emit_pieces(upto):
                while _pe[0] < min(upto, len(pieces)):
                    pieces[_pe[0]]()
                    _pe[0] += 1
            for ob in range(8):
                p = ob % 2
                for f in range(NFF):
                    emit_pieces(max(4 * (f // 4) + 4, 4 + 2 * f) if ob == 0 else len(pieces))
                    bg, kg = gbank()
                    for k in range(8):
                        mm(bg[:, 0:256], wgu[:, k, f * 128:(f + 1) * 128], h2T[p][:, k, :], [("wgu", k // 4, (f * 128) // 256), f"h2T{p}"], [kg], start=(k == 0), stop=(k == 7))
                    bu, ku = gbank()
                    for k in range(8):
                        mm(bu[:, 0:256], wgu[:, k, DFF + f * 128:DFF + (f + 1) * 128], h2T[p][:, k, :], [("wgu", k // 4, (DFF + f * 128) // 256), f"h2T{p}"], [ku], start=(k == 0), stop=(k == 7))
                    sj = f % 2
                    act(sgt[sj][:], bg[:, 0:256], AF.Silu, [kg], [f"sgt{sj}"])
                    tt("dve", actT[:, f, :], bu[:, 0:256], sgt[sj][:], ALU.mult, [ku, f"sgt{sj}"], ["actT"])
                    if f == NFF // 2 and ob + 1 < 8:
                        load_norm(ob + 1)
                emit_pieces(len(pieces))
                for ti in range(2):
                    for hf in range(2):
                        bank, bk = gbank()
                        for f in range(NFF):
                            mm(bank[:], actT[:, f, ti * 128:(ti + 1) * 128], wdn[:, f, hf * 512:(hf + 1) * 512], ["actT", ("wdn", f // 2)], [bk],
                               start=(f == 0), stop=(f == NFF - 1))
                        tt("dve", x1b[p][:, ti, hf * 512:(hf + 1) * 512], bank[:], x1b[p][:, ti, hf * 512:(hf + 1) * 512], ALU.add,
                           [bk, ("x1b", p, ti)], [("x1b", p, ti)])
                    xa = x1b[p][:, ti, :]
                    act(junk[:], xa, AF.Square, [("x1b", p, ti)], ["junk", "ss"], accum_out=ss[:, 0:1])
                    act(ss[:, 1:2], ss[:, 0:1], AF.Sqrt, ["ss"], ["ss1"], bias=RMS_EPS, scale=1.0 / D)
                    S.op("dve", lambda: nc.vector.reciprocal(out=ss[:, 2:3], in_=ss[:, 1:2]), ["ss1"], ["ss2"])
                    stt(ot[:], xa, ss[:, 2:3], fgbc[:], ALU.mult, ALU.mult, [("x1b", p, ti), "ss2", "fgbc"], ["ot"])
                    r0 = ob * 256 + ti * 128
                    S.dma("sp", "ost", out[r0:r0 + 128, :], ot[:], ["ot"], ["outd"])
            S.barrier()
    return nc


_NC = None


def kernel(**inputs):
    global _NC
    x = np.asarray(inputs["x"], np.float32)
    c = np.asarray(inputs["c"], np.float32)
    if _NC is None:
        _NC = build()
    g = lambda n: np.asarray(inputs[n], np.float32)[0]
    names = ["ada_w", "ada_b", "norm1_g", "pool_w", "pool_scale", "w_out", "norm2_g", "w_ffn_gu", "w_ffn_down"]
    shared = {n: np.ascontiguousarray(g(n)) for n in names}
    shared["final_g"] = np.ascontiguousarray(np.asarray(inputs["final_g"], np.float32))
    w_in, mu_shift = g("w_in"), g("mu_shift")
    in_maps = []
    wins = (2, 4, 8, 16)
    for core in range(8):
        b, qq = core // 4, core % 4
        xpad = np.zeros((T + 16, D), np.float32)
        xpad[16:] = x[b]
        fl = np.zeros((128, 65), np.float32)
        fl[:, 0] = 0.0 if qq == 0 else 1.0
        for gi, w in enumerate(wins):
            for t in range(16):
                fl[:, 1 + gi * 16 + t] = 1.0 / min(qq * TQ + t + 1, w)
        ps_ = slice(qq * 128, (qq + 1) * 128)
        cols = np.concatenate([np.arange(0, 512)] + [np.arange(512 + i * 512 + qq * 128, 512 + i * 512 + (qq + 1) * 128) for i in range(3)]
                              + [np.arange(2048, 2304)])
        mcols = np.stack([mu_shift[i * 512 + qq * 128:i * 512 + (qq + 1) * 128] for i in range(3)]
                         + [mu_shift[1536:1664], mu_shift[1664:1792]], axis=1)
        m = dict(shared)
        m["xb"] = xpad
        m["xo"] = np.ascontiguousarray(xpad[qq * TQ:qq * TQ + TQ + 16])
        m["cvec"] = np.ascontiguousarray(c[b])
        m["flags"] = fl
        m["w_in_sel"] = np.ascontiguousarray(w_in[:, cols])
        m["mu_sel"] = np.ascontiguousarray(mcols)
        m["pv_sel"] = np.ascontiguousarray(np.stack([g(n)[ps_] for n in ("w0", "a0", "k_k", "k_a", "r_k")], axis=1))
        m["ln2"] = np.ascontiguousarray(np.stack([g("lnx_w")[ps_].reshape(2, 64), g("lnx_b")[ps_].reshape(2, 64)], axis=0))
        m["Wd_sel"] = np.ascontiguousarray(g("w_decay_up")[:, ps_])
        m["Wa_sel"] = np.ascontiguousarray(g("w_iclr_up")[:, ps_])
        m["Wg_sel"] = np.ascontiguousarray(g("w_gate_up")[:, ps_])
        in_maps.append(m)
    res = run_bass_kernel_spmd(_NC, in_maps, core_ids=list(range(8)))
    if DEBUG:
        global DBG_OUT
        DBG_OUT = [res.results[core]["dbg"] for core in range(8)]
    outp = np.zeros((2, T, D), np.float32)
    for core in range(8):
        b, qq = core // 4, core % 4
        outp[b, qq * TQ:(qq + 1) * TQ] = res.results[core]["out"]
    return outp
```

```python
import numpy as np
import ml_dtypes
import concourse.bass as bass
import concourse.mybir as mybir
from concourse.bass_utils import run_bass_kernel_spmd

F32 = mybir.dt.float32
BF16 = mybir.dt.bfloat16
ALU = mybir.AluOpType
AF = mybir.ActivationFunctionType
AX = mybir.AxisListType

D = 1024
T = 8192
TQ = 2048
NBLK = T // 512
DFF = 2816
NFF = DFF // 128
GN_EPS = 64e-5
RMS_EPS = 1e-6
DEC = 0.6065306597126334


class Sched:
    def __init__(self, nc, es):
        self.nc = nc
        self.es = es
        self.eng = {"pe": nc.tensor, "act": nc.scalar, "dve": nc.vector, "pool": nc.gpsimd, "sp": nc.sync}
        self.sem = {}
        self.cnt = {}
        self.inc = {}
        self.waited = {}
        self.lastw = {}
        self.readers = {}
        for e in ("pe", "act", "dve", "pool"):
            self.stream(e, 1)

    def stream(self, name, inc):
        if name not in self.sem:
            self.sem[name] = self.es.enter_context(self.nc.semaphore("s_" + name))
            self.cnt[name] = 0
            self.inc[name] = inc
        return name

    def _wait(self, e, s, v):
        if s == e and e == "pe":
            return
        if self.waited.get((e, s), 0) >= v:
            return
        self.eng[e].wait_ge(self.sem[s], v)
        self.waited[(e, s)] = v

    @staticmethod
    def _bank(k):
        if isinstance(k, tuple) and k and k[0] == "TB":
            return "BANK_TB"
        if isinstance(k, str) and (k.startswith("s0_") or k.startswith("s1_")):
            return "BANK_" + k[:4]
        return None

    def _aug(self, keys):
        out = list(keys)
        for k in keys:
            b = self._bank(k)
            if b is not None and b not in out:
                out.append(b)
        return out

    def _deps(self, reads, writes):
        deps = set()
        for r in reads:
            if r in self.lastw:
                deps.add(self.lastw[r])
        for r in writes:
            if r in self.lastw:
                deps.add(self.lastw[r])
            for d in self.readers.get(r, ()):
                deps.add(d)
        return deps

    def _commit(self, s, reads, writes):
        v = self.cnt[s]
        for r in writes:
            self.lastw[r] = (s, v)
            self.readers[r] = []
        for r in reads:
            self.readers.setdefault(r, []).append((s, v))

    def op(self, e, fn, reads=(), writes=()):
        bk = [k for k in self._aug(list(reads) + list(writes)) if isinstance(k, str) and k.startswith("BANK_")]
        reads, writes = list(reads), list(writes) + bk
        for (s, v) in self._deps(reads, writes):
            self._wait(e, s, v)
        ins = fn()
        self.cnt[e] += 1
        ins.then_inc(self.sem[e], 1)
        self._commit(e, reads, writes)
        return ins

    def dma(self, q, s, out, in_, reads=(), writes=(), **kw):
        self.stream(s, 16)
        for (ss, v) in self._deps(reads, writes):
            self._wait(q, ss, v)
        ins = self.eng[q].dma_start(out=out, in_=in_, **kw)
        self.cnt[s] += 16
        ins.then_inc(self.sem[s], 16)
        self._commit(s, reads, writes)
        return ins

    def close(self, s):
        for r, (ss, v) in list(self.lastw.items()):
            if ss == s:
                self.lastw[r] = (s, self.cnt[s])

    def wait_all(self, e):
        for s in self.cnt:
            if self.cnt[s] > 0:
                self._wait(e, s, self.cnt[s])

    def barrier(self):
        for e in ("pe", "act", "dve", "pool", "sp"):
            self.wait_all(e)
        self.lastw.clear()
        self.readers.clear()


STOP = None
DEBUG = False
DBG_MAP = {}
DBG_OUT = None
_LAST_S = None


def build():
    import contextlib

    nc = bass.Bass("TRN2", target_bir_lowering=False)

    def din(name, shape, dt=F32):
        return nc.dram_tensor(name, list(shape), dt, kind="ExternalInput").ap()

    xb = din("xb", [T + 16, D])
    cvec = din("cvec", [D])
    flags = din("flags", [128, 65])
    ada_w = din("ada_w", [D, 6 * D])
    ada_b = din("ada_b", [6 * D])
    norm1_g = din("norm1_g", [D])
    xo = din("xo", [TQ + 16, D])
    w_in = din("w_in_sel", [D, 1152])
    mu_sel = din("mu_sel", [128, 5])
    pv_sel = din("pv_sel", [128, 5])
    ln2 = din("ln2", [2, 2, 64])
    pool_w = din("pool_w", [4, 128, 128])
    pool_scale = din("pool_scale", [512])
    w_decay_up = din("Wd_sel", [64, 128])
    w_iclr_up = din("Wa_sel", [64, 128])
    w_gate_up = din("Wg_sel", [128, 128])
    w_out = din("w_out", [D, D])
    norm2_g = din("norm2_g", [D])
    w_gu = din("w_ffn_gu", [D, 2 * DFF])
    w_down = din("w_ffn_down", [DFF, D])
    final_g = din("final_g", [D])
    out = nc.dram_tensor("out", [TQ, D], F32, kind="ExternalOutput").ap()
    ysrc = [nc.dram_tensor(f"ysrc{j}", [128, TQ], BF16) for j in range(4)]
    ydst = nc.dram_tensor("ydst", [4, 512, TQ], BF16)
    x1d = nc.dram_tensor("x1d", [TQ, D], F32)
    dbg = nc.dram_tensor("dbg", [128, 8192], F32, kind="ExternalOutput").ap() if DEBUG else None

    es = contextlib.ExitStack()
    with es:
        S = Sched(nc, es)
        global _LAST_S
        _LAST_S = S
        pid = nc.gpsimd.partition_id()
        q = pid % 4
        _n = [0]

        def sb(shape, dt=F32, stack=es, name=None):
            _n[0] += 1
            return stack.enter_context(nc.sbuf_tensor(name or f"t{_n[0]}", list(shape), dt))

        def ps(shape, dt=F32, name=None):
            _n[0] += 1
            return es.enter_context(nc.psum_tensor(name or f"p{_n[0]}", list(shape), dt))

        def mm(out, lhsT, rhs, r, w, start=True, stop=True):
            return S.op("pe", lambda: nc.tensor.matmul(out, lhsT, rhs, start=start, stop=stop), r, w)

        def tr(out, in_, ident, r, w):
            return S.op("pe", lambda: nc.tensor.transpose(out, in_, ident), r, w)

        def act(out, in_, func, r, w, bias=None, scale=None, accum_out=None):
            kw = {}
            if bias is not None:
                kw["bias"] = bias
            if scale is not None:
                kw["scale"] = scale
            if accum_out is not None:
                kw["accum_out"] = accum_out
            return S.op("act", lambda: nc.scalar.activation(out=out, in_=in_, func=func, **kw), r, w)

        def tt(e, out, in0, in1, op, r, w):
            eng = nc.vector if e == "dve" else nc.gpsimd
            return S.op(e, lambda: eng.tensor_tensor(out=out, in0=in0, in1=in1, op=op), r, w)

        def tsc(e, out, in0, s1, s2, op0, op1, r, w):
            eng = nc.vector if e == "dve" else nc.gpsimd
            if op1 is None:
                return S.op(e, lambda: eng.tensor_scalar(out=out, in0=in0, scalar1=s1, scalar2=None, op0=op0), r, w)
            return S.op(e, lambda: eng.tensor_scalar(out=out, in0=in0, scalar1=s1, scalar2=s2, op0=op0, op1=op1), r, w)

        def stt(out, in0, scalar, in1, op0, op1, r, w):
            return S.op("dve", lambda: nc.vector.scalar_tensor_tensor(out=out, in0=in0, scalar=scalar, in1=in1, op0=op0, op1=op1), r, w)

        def cp(e, out, in_, r, w):
            if e == "act":
                return S.op("act", lambda: nc.scalar.copy(out=out, in_=in_), r, w)
            eng = nc.vector if e == "dve" else nc.gpsimd
            return S.op(e, lambda: eng.tensor_copy(out=out, in_=in_), r, w)

        def mset(e, ap, val, w):
            eng = nc.vector if e == "dve" else nc.gpsimd
            return S.op(e, lambda: eng.memset(ap, val), (), w)

        NW = 3
        PS = [ps([128, 512], name=f"PS{i}") for i in range(8)]
        SB0 = [PS[p] for p in range(NW)]
        SB1 = [PS[NW + p] for p in range(NW)]
        TB = PS[6]
        GBL = [0, 1, 2, 3, 4, 5, 7]
        _gb = [0]

        def gbank():
            i = GBL[_gb[0] % len(GBL)]
            _gb[0] += 1
            return PS[i], f"GB{i}"

        ones_f = sb([128, 128])
        ident_f = sb([128, 128])
        ident_bf = sb([128, 128], BF16)
        flg = sb([128, 65])
        c_fm = sb([128, 8])
        n1g = sb([128, 8])
        n2g = sb([128, 8])
        mu = sb([128, 5])
        pscale = sb([128, 4])
        pv = sb([128, 8])
        fgbc = sb([128, D])
        omm = sb([128, 5])
        stage = [sb([128, 2048]) for _ in range(2)]
        g2bc = sb([128, D])
        modfm = sb([128, 48])
        gsc1 = sb([128, 8])
        gsc2 = sb([128, 8])
        xn = sb([128, D])
        junk = sb([128, D], BF16)
        ss = sb([128, 4])
        dbgt = sb([128, 512]) if DEBUG else None
        pA = contextlib.ExitStack()
        es.enter_context(pA)
        M_TS = sb([128, 2, 128], stack=pA)
        M_A3 = sb([128, 3, 128], stack=pA)
        Blk = sb([128, 128], stack=pA)
        E_bf = sb([128, 64], BF16, stack=pA)
        ones_bf = sb([128, 2], BF16, stack=pA)
        rmask = sb([128, 512], stack=pA)
        lnw_bc = sb([128, 64], stack=pA)
        lnb_bc = sb([128, 64], stack=pA)
        Wd = sb([128, 128], stack=pA)
        Wa = sb([128, 128], stack=pA)
        Wg = sb([128, 128], stack=pA)
        pw_f = sb([128, 4, 128], stack=pA)
        pw_bf = sb([128, 4, 128], BF16, stack=pA)
        g1bc = sb([128, D], stack=pA)
        xt = [sb([128, D], stack=pA) for _ in range(2)]
        mixp = sb([128, 4, TQ], BF16, stack=pA)
        mset("pool", ones_f[:], 1.0, ["ones_f"])

        def asel(out, pattern, cmul, cmp, w):
            return S.op("pool", lambda: nc.gpsimd.affine_select(out=out, in_=ones_f[:], pattern=pattern, compare_op=cmp,
                                                                fill=0.0, base=0, channel_multiplier=cmul), ["ones_f"], w)

        asel(ident_f[:], [[1, 128]], -1, ALU.is_equal, ["ident_f"])
        asel(M_TS[:, 0, :], [[1, 128]], -1, ALU.is_gt, ["M_TS"])
        asel(M_TS[:, 1, :], [[-1, 128]], 1, ALU.is_gt, ["M_TS"])
        asel(M_A3[:, 0, :], [[1, 128]], -1, ALU.is_ge, ["M_A3"])
        asel(M_A3[:, 1, :], [[1, 128]], -1, ALU.is_gt, ["M_A3"])
        asel(M_A3[:, 2, :], [[1, 128]], -1, ALU.is_ge, ["M_A3"])
        cp("pool", ident_bf[:], ident_f[:], ["ident_f"], ["ident_bf"])
        mset("pool", Blk[:], 0.0, ["Blk"])
        mset("pool", Blk[0:64, 0:64], 1.0, ["Blk"])
        mset("pool", Blk[64:128, 64:128], 1.0, ["Blk"])
        tt("pool", E_bf[:], ident_f[:, 0:64], ident_f[:, 64:128], ALU.add, ["ident_f"], ["E_bf"])
        mset("pool", ones_bf[:], 1.0, ["ones_bf"])
        mset("pool", rmask[:], 1.0, ["rmask"])
        mset("pool", rmask[:].rearrange("p (c t) -> p c t", t=64)[:, :, 0:1], 0.0, ["rmask"])

        def pl(dst, src, w, **kw):
            S.dma("pool", "init", dst, src, (), w, **kw)

        def fm(v, k):
            return v.rearrange("(k p) -> p k", p=128)

        pl(flg[:], flags[:, :], ["flg"])
        pl(c_fm[:], fm(cvec, 8), ["c_fm"], allow_slow_non_contiguous=True)
        pl(n1g[:], fm(norm1_g, 8), ["n1g"], allow_slow_non_contiguous=True)
        pl(n2g[:], fm(norm2_g, 8), ["n2g"], allow_slow_non_contiguous=True)
        pl(mu[:], mu_sel[:, :], ["mu"])
        pl(pscale[:], fm(pool_scale, 4), ["pscale"], allow_slow_non_contiguous=True)
        pl(pv[:, 0:5], pv_sel[:, :], ["pv"])
        for h in range(2):
            pl(lnw_bc[h * 64:(h + 1) * 64, :], ln2[0, h:h + 1, :].broadcast_to([64, 64]), [f"lnw_bc{h}"])
            pl(lnb_bc[h * 64:(h + 1) * 64, :], ln2[1, h:h + 1, :].broadcast_to([64, 64]), [f"lnb_bc{h}"])
        pl(Wd[0:64, :], w_decay_up[:, :], ["Wd"])
        pl(Wa[64:128, :], w_iclr_up[:, :], ["Wa"])
        pl(Wg[:], w_gate_up[:, :], ["Wg"])
        pl(pw_f[:], pool_w.rearrange("g c d -> c g d"), ["pw_f"])
        pl(fgbc[:], final_g.partition_broadcast(128), ["fgbc"])
        S.close("init")
        S.lastw["lnw_bc"] = S.lastw["lnw_bc1"]
        S.lastw["lnb_bc"] = S.lastw["lnb_bc1"]
        cp("pool", pw_bf[:], pw_f[:], ["pw_f"], ["pw_bf"])
        tsc("pool", omm[:], mu[:], -1.0, 1.0, ALU.mult, ALU.add, ["mu"], ["omm"])

        if STOP == "setup":
            S.barrier()
            return nc
        _st = [0]
        _ce = [0]

        def load_cast(dst_ap_fn, src_ap_fn, nparts, K, N, w, scale_bc=None, part0=0):
            ncol = max(1, 2048 // K)
            for n0 in range(0, N, ncol):
                n1 = min(N, n0 + ncol)
                i = _st[0] % 2
                _st[0] += 1
                sv = stage[i][part0:part0 + nparts, 0:K * (n1 - n0)].rearrange("p (k n) -> p k n", k=K)
                S.dma(("sp", "act")[i], f"stg{i}", sv, src_ap_fn(n0, n1), (), [f"stage{i}"])
                e = ("act", "dve")[_ce[0] % 2]
                _ce[0] += 1
                wk = w[0](n0, n1) if callable(w[0]) else [w[0]]
                if scale_bc is not None:
                    tt("dve" if e == "act" else e, dst_ap_fn(n0, n1), sv, scale_bc(n0, n1), ALU.mult, [f"stage{i}"] + w[1:], wk)
                else:
                    cp(e, dst_ap_fn(n0, n1), sv, [f"stage{i}"], wk)

        _dc = [0]

        def dump(name, ap, n, e="dve"):
            if not DEBUG:
                return
            c0 = _dc[0]
            _dc[0] += n
            DBG_MAP[name] = (c0, n)
            S.wait_all(e)
            npart = ap.shape[0]
            cp(e, dbgt[0:npart, 0:n], ap, [], ["dbgt"])
            S.dma("sp", "dbgs", dbg[0:npart, c0:c0 + n], dbgt[0:npart, 0:n], ["dbgt"], ["dbgd"])

        with contextlib.ExitStack() as p0:
            csil = sb([128, 8, 1], stack=p0)
            crep = sb([128, 8, 128], stack=p0)
            adab = [sb([128, 512], stack=p0) for _ in range(2)]
            mblk = sb([128, 512], stack=p0)
            tmp4 = sb([128, 4, 128], stack=p0)
            act(csil[:, :, 0], c_fm[:], AF.Silu, ["c_fm"], ["csil"])
            cp("dve", crep[:], csil[:].broadcast_to([128, 8, 128]), ["csil"], ["crep"])
            for cb in range(12):
                bank, bk = gbank()
                for k2 in range(4):
                    j = _st[0] % 2
                    _st[0] += 1
                    sv = stage[j][:, 0:1024].rearrange("p (k n) -> p k n", k=2)
                    S.dma(("sp", "act")[j], f"stg{j}", sv,
                          ada_w[k2 * 256:(k2 + 1) * 256, cb * 512:(cb + 1) * 512].rearrange("(k p) n -> p k n", p=128),
                          (), [f"stage{j}"])
                    for kk in range(2):
                        k = k2 * 2 + kk
                        mm(bank[:], crep[:, k, :], sv[:, kk, :], ["crep", f"stage{j}"], [bk], start=(k == 0), stop=(k == 7))
                a = cb % 2
                S.dma("sp", f"adab{a}", adab[a][:], ada_b[cb * 512:(cb + 1) * 512].partition_broadcast(128), (), [f"adab{a}"])
                sec = cb // 2
                if sec == 2:
                    dst, dk = g1bc[:, (cb % 2) * 512:(cb % 2 + 1) * 512], "g1bc"
                elif sec == 5:
                    dst, dk = g2bc[:, (cb % 2) * 512:(cb % 2 + 1) * 512], "g2bc"
                else:
                    dst, dk = mblk[:], "mblk"
                tt("dve", dst, bank[:], adab[a][:], ALU.add, [bk, f"adab{a}"], [dk])
                tt("dve", tmp4[:], dst.rearrange("p (a b) -> p a b", b=128),
                   ident_f[:].rearrange("p (a b) -> p a b", a=1).broadcast_to([128, 4, 128]), ALU.mult, [dk, "ident_f"], ["tmp4"])
                S.op("dve", lambda: nc.vector.tensor_reduce(out=modfm[:, cb * 4:(cb + 1) * 4], in_=tmp4[:], axis=AX.X, op=ALU.add),
                     ["tmp4"], ["modfm"])
            S.barrier()
        if STOP == "ada":
            return nc
        stt(gsc1[:], modfm[:, 8:16], 1.0, n1g[:], ALU.add, ALU.mult, ["modfm", "n1g"], ["gsc1"])
        stt(gsc2[:], modfm[:, 32:40], 1.0, n2g[:], ALU.add, ALU.mult, ["modfm", "n2g"], ["gsc2"])
        dump("modfm", modfm[:], 48)
        sh1 = modfm[:, 0:8]
        sh2 = modfm[:, 24:32]

        _xt = [0]

        def norm_to_fm(x_ap, nrow, gsc, sh, shk, dst_fn, dst_key, dq="sp", src_key=None, keep=None):
            if src_key is None:
                i = _xt[0] % 2
                _xt[0] += 1
                xs, xk = xt[i], f"xt{i}"
                S.dma(dq, f"xld{i}", xs[0:nrow, :], x_ap, (), [xk])
                xa = xs[0:nrow, :]
            else:
                xa, xk = x_ap, src_key
            act(junk[0:nrow, :], xa, AF.Square, [xk], ["junk", "ss"], accum_out=ss[0:nrow, 0:1])
            act(ss[0:nrow, 1:2], ss[0:nrow, 0:1], AF.Sqrt, ["ss"], ["ss1"], bias=RMS_EPS, scale=1.0 / D)
            S.op("dve", lambda: nc.vector.reciprocal(out=ss[0:nrow, 2:3], in_=ss[0:nrow, 1:2]), ["ss1"], ["ss2"])
            act(xn[0:nrow, :], xa, AF.Copy, [xk, "ss2"], ["xn"], scale=ss[0:nrow, 2:3])
            for half in range(2):
                bank, bk = gbank()
                for kk_ in range(4):
                    k = half * 4 + kk_
                    tr(bank[:, kk_ * 128:kk_ * 128 + nrow], xn[0:nrow, k * 128:(k + 1) * 128], ident_f[0:nrow, 0:nrow], ["xn", "ident_f"], [bk])
                for kk_ in range(4):
                    k = half * 4 + kk_
                    tsc("dve", dst_fn(k), bank[:, kk_ * 128:kk_ * 128 + nrow], gsc[:, k:k + 1], sh[:, k:k + 1], ALU.mult, ALU.add,
                        [bk, "gsc1", "gsc2", "modfm"], [dst_key])
            return xa, xk

        def norm_gen(x_ap, nrow, gsc, sh, shk, dst_fn, dst_key, dq="sp", src_key=None, keep=None):
            if src_key is None:
                i = _xt[0] % 2
                _xt[0] += 1
                xs, xk = xt[i], f"xt{i}"
                S.dma(dq, f"xld{i}", xs[0:nrow, :], x_ap, (), [xk])
                xa = xs[0:nrow, :]
            else:
                xa, xk = x_ap, src_key
            act(junk[0:nrow, :], xa, AF.Square, [xk], ["junk", "ss"], accum_out=ss[0:nrow, 0:1])
            yield
            act(ss[0:nrow, 1:2], ss[0:nrow, 0:1], AF.Sqrt, ["ss"], ["ss1"], bias=RMS_EPS, scale=1.0 / D)
            yield
            S.op("dve", lambda: nc.vector.reciprocal(out=ss[0:nrow, 2:3], in_=ss[0:nrow, 1:2]), ["ss1"], ["ss2"])
            yield
            act(xn[0:nrow, :], xa, AF.Copy, [xk, "ss2"], ["xn"], scale=ss[0:nrow, 2:3])
            yield
            for half in range(2):
                bank, bk = gbank()
                for kk_ in range(4):
                    k = half * 4 + kk_
                    tr(bank[:, kk_ * 128:kk_ * 128 + nrow], xn[0:nrow, k * 128:(k + 1) * 128], ident_f[0:nrow, 0:nrow], ["xn", "ident_f"], [bk])
                yield
                for kk_ in range(4):
                    k = half * 4 + kk_
                    tsc("dve", dst_fn(k), bank[:, kk_ * 128:kk_ * 128 + nrow], gsc[:, k:k + 1], sh[:, k:k + 1], ALU.mult, ALU.add,
                        [bk, "gsc1", "gsc2", "modfm"], [dst_key])
                yield

        with contextlib.ExitStack() as p1:
            w_in_bf = sb([128, 8, 9 * 128], BF16, stack=p1)
            w3 = w_in.rearrange("(k p) n -> p k n", p=128)
            load_cast(lambda a, b: w_in_bf[:, :, a:b], lambda a, b: w3[:, :, a:b], 128, 8, 1152, ["w_in_bf"])

            if STOP == "w_in":
                S.barrier()
                return nc
            hT = sb([128, 8, 528], BF16, stack=p1)
            dn = [sb([128, 528], stack=p1) for _ in range(12)]
            zT = [sb([128, 5, 513], stack=p1) for _ in range(2)]
            gT = [sb([128, 2, 512], stack=p1) for _ in range(2)]
            yTb = [sb([128, 2, 512], BF16, stack=p1) for _ in range(2)]
            gamC = [sb([128, 8], stack=p1) for _ in range(2)]
            Hs = [sb([128, 64], stack=p1) for _ in range(2)]
            AR_bd = [sb([128, 8, 256], BF16, stack=p1) for _ in range(2)]
            B_bd = [sb([128, 8, 128], BF16, stack=p1) for _ in range(2)]
            K_bd = [sb([128, 8, 128], BF16, stack=p1) for _ in range(2)]
            BH_bd = [sb([128, 8, 128], BF16, stack=p1) for _ in range(2)]
            KH_bd = [sb([128, 8, 128], BF16, stack=p1) for _ in range(2)]
            V_bd = [sb([128, 8, 128], BF16, stack=p1) for _ in range(2)]
            RRK_bd = [sb([128, 8, 128], BF16, stack=p1) for _ in range(2)]
            Xi = [sb([128, 3, 128], stack=p1) for _ in range(NW)]
            Mb = [sb([128, 128], BF16, stack=p1) for _ in range(NW)]
            A3 = [sb([128, 3, 128], BF16, stack=p1) for _ in range(NW)]
            R3 = [sb([128, 192], BF16, stack=p1) for _ in range(NW)]
            BK = [sb([128, 2, 128], BF16, stack=p1) for _ in range(NW)]
            Vst = [sb([128, 64], BF16, stack=p1) for _ in range(NW)]
            WU = [sb([128, 192], BF16, stack=p1) for _ in range(NW)]
            PT = [sb([128, 128], stack=p1) for _ in range(NW)]
            RH = [sb([128, 128], stack=p1) for _ in range(NW)]
            sm = [sb([128, 16], stack=p1) for _ in range(NW)]
            yfin = [sb([128, 64], stack=p1) for _ in range(NW)]
            yh = [sb([128, 64], stack=p1) for _ in range(NW)]
            for p in range(2):
                for tl, nm in ((AR_bd, "AR"), (B_bd, "B"), (K_bd, "K"), (BH_bd, "BH"), (KH_bd, "KH"), (V_bd, "V"), (RRK_bd, "RRK")):
                    mset("pool", tl[p][:], 0.0, [f"{nm}{p}"])
                mset("pool", Hs[p][:], 0.0, [f"H{p}"])

            uT = dn[0]
            for ob in range(4):
                norm_to_fm(xo[ob * 512:ob * 512 + 16, :], 16, gsc1, sh1, "sh1",
                           lambda k: hT[:, k, 0:16], "hT")
                if STOP == "norm16":
                    S.barrier()
                    return nc
                for ti in range(4):
                    norm_to_fm(xo[16 + ob * 512 + ti * 128:16 + ob * 512 + (ti + 1) * 128, :], 128, gsc1, sh1, "sh1",
                               lambda k, ti=ti: hT[:, k, 16 + ti * 128:16 + (ti + 1) * 128], "hT")
                if STOP == "norm":
                    S.barrier()
                    return nc
                if ob == 0:
                    for k_ in range(8):
                        dump(f"hT{k_}", hT[:, k_, 0:144], 144)
                for g in range(4):
                    if STOP == "g1" and g == 1:
                        S.barrier()
                        return nc
                    w = (2, 4, 8, 16)[g]
                    bank, bk = gbank()
                    for k in range(8):
                        mm(bank[:], w_in_bf[:, k, g * 128:(g + 1) * 128], hT[:, k, 16:528], ["w_in_bf", "hT"], [bk], start=(k == 0), stop=(k == 7))
                    cp("act", uT[:, 16:528], bank[:], [bk], ["dn0"])
                    bank2, bk2 = gbank()
                    for k in range(8):
                        mm(bank2[:, 0:16], w_in_bf[:, k, g * 128:(g + 1) * 128], hT[:, k, 0:16], ["w_in_bf", "hT"], [bk2], start=(k == 0), stop=(k == 7))
                    if ob == 0:
                        tsc("dve", uT[:, 0:16], bank2[:, 0:16], flg[:, 0:1], None, ALU.mult, None, [bk2, "flg"], ["dn0"])
                    else:
                        cp("dve", uT[:, 0:16], bank2[:, 0:16], [bk2], ["dn0"])
                    src, sk = uT, "dn0"
                    sh_ = 1
                    lvl = 0
                    while sh_ < w:
                        dst, dk = dn[1 + lvl % 2], f"dn{1 + lvl % 2}"
                        tt("pool", dst[:, sh_:528], src[:, sh_:528], src[:, 0:528 - sh_], ALU.add, [sk], [dk])
                        src, sk = dst, dk
                        sh_ *= 2
                        lvl += 1
                    dd = dn[3]
                    stt(dd[:, 16:528], src[:, 16:528], 1.0 / w, uT[:, 16:528], ALU.mult, ALU.subtract, [sk, "dn0"], ["dn3"])
                    if ob == 0:
                        tt("dve", dn[4][:, 0:16], src[:, 16:32], flg[:, 1 + g * 16:1 + (g + 1) * 16], ALU.mult, [sk, "flg"], ["dn4"])
                        tt("dve", dd[:, 16:32], dn[4][:, 0:16], uT[:, 16:32], ALU.subtract, ["dn4", "dn0", "dn3"], ["dn3"])
                    dbfT = dn[5][:, 0:256].bitcast(BF16)
                    cp("act", dbfT, dd[:, 16:528], ["dn3"], ["dn5"])
                    bank3, bk3 = gbank()
                    mm(bank3[:], pw_bf[:, g, :], dbfT, ["pw_bf", "dn5"], [bk3])
                    tsc("dve", mixp[:, g, ob * 512:(ob + 1) * 512], bank3[:], pscale[:, g:g + 1], None, ALU.mult, None, [bk3, "pscale"], ["mixp"])

            for g_ in range(4):
                dump(f"mixp{g_}", mixp[:, g_, 0:128], 128)
            if STOP == "pool":
                S.barrier()
                return nc
            S.barrier()
            GBL[:] = [7]
            ysv = [ysrc[j].ap().rearrange("(h i) t -> i h t", i=64) for j in range(4)]
            S.stream("cc", 1)

            ymark = {}

            def exchange(j):
                for st_, v_ in ymark[j]:
                    S._wait("pool", st_, v_)
                nc.gpsimd.collective_compute("AllGather", ALU.bypass, replica_groups=[[0, 1, 2, 3], [4, 5, 6, 7]],
                                             ins=[ysrc[j].ap().opt()], outs=[ydst.ap()[j].opt()]).then_inc(S.sem["cc"], 1)
                S.cnt["cc"] += 1
            state = {"pk": 0}

            def prep(blk):
                bp = blk % 2
                zt, zk = zT[bp], f"zT{bp}"
                for ti in range(4):
                    yield from norm_gen(xb[16 + blk * 512 + ti * 128:16 + blk * 512 + (ti + 1) * 128, :], 128, gsc1, sh1, "sh1",
                                        lambda k, ti=ti: hT[:, k, 16 + ti * 128:16 + (ti + 1) * 128], "hT")
                if blk == 0:
                    mset("pool", zt[:, :, 0:1], 0.0, [zk])
                else:
                    cp("pool", zt[:, :, 0:1], zT[1 - bp][:, :, 512:513], [f"zT{1 - bp}"], [zk])
                    yield
                for m in range(5):
                    bank, bk = gbank()
                    for k in range(8):
                        mm(bank[:], w_in_bf[:, k, 512 + m * 128:512 + (m + 1) * 128], hT[:, k, 16:528], ["w_in_bf", "hT"], [bk], start=(k == 0), stop=(k == 7))
                    yield
                    cp("act", zt[:, m, 1:513], bank[:], [bk], [zk])
                    yield
                zs = []
                for m in range(5):
                    tmp = dn[11]
                    act(tmp[:, 0:512], zt[:, m, 0:512], AF.Copy, [zk, "mu"], ["dn11"], scale=mu[:, m:m + 1])
                    yield
                    dst = dn[m]
                    stt(dst[:, 0:512], zt[:, m, 1:513], omm[:, m:m + 1], tmp[:, 0:512], ALU.mult, ALU.add, [zk, "omm", "dn11"], [f"dn{m}"])
                    yield
                    zs.append(dst)
                rT, kT, vT, xwa, xg = [z[:, 0:512] for z in zs]
                thx = dn[5]
                act(thx[0:64, 0:512], xwa[0:64, :], AF.Tanh, ["dn3"], ["dn5"])
                yield
                bank, bk = gbank()
                mm(bank[:], Wd[0:64, :], thx[0:64, 0:512], ["Wd", "dn5"], [bk])
                yield
                sg = dn[6]
                act(sg[:, 0:512], bank[:], AF.Sigmoid, [bk, "pv"], ["dn6"], bias=pv[:, 0:1])
                yield
                bank, bk = gbank()
                mm(bank[:], Wa[64:128, :], xwa[64:128, :], ["Wa", "dn3"], [bk])
                yield
                aT = dn[7]
                act(aT[:, 0:512], bank[:], AF.Sigmoid, [bk, "pv"], ["dn7"], bias=pv[:, 1:2])
                yield
                sgx = dn[5]
                act(sgx[:, 0:512], xg, AF.Sigmoid, ["dn4"], ["dn5"])
                yield
                for h in range(2):
                    bank, bk = gbank()
                    mm(bank[0:64, :], Wg[:, h * 64:(h + 1) * 64], sgx[:, 0:512], ["Wg", "dn5"], [bk])
                    yield
                    cp("act", gT[bp][0:64, h, :], bank[0:64, :], [bk], [f"gT{bp}"])
                    yield
                cs = dn[8]
                S.op("dve", lambda: nc.vector.tensor_tensor_scan(out=cs[:, 0:512], data0=rmask[:], data1=sg[:, 0:512], initial=0.0,
                                                                 op0=ALU.mult, op1=ALU.add), ["rmask", "dn6"], ["dn8"])
                yield
                epos = dn[9]
                act(epos[:, 0:512], cs[:, 0:512], AF.Exp, ["dn8"], ["dn9"], scale=-DEC)
                yield
                cp("pool", gamC[bp][:], epos[:, 0:512].rearrange("p (c t) -> p c t", t=64)[:, :, 63], ["dn9"], [f"gamC{bp}"])
                yield
                eneg = dn[10]
                act(eneg[:, 0:512], cs[:, 0:512], AF.Exp, ["dn8"], ["dn10"], scale=DEC)
                yield
                tt("dve", cs[:, 0:512], cs[:, 0:512], sg[:, 0:512], ALU.subtract, ["dn8", "dn6"], ["dn8"])
                yield
                eprev = dn[6]
                act(eprev[:, 0:512], cs[:, 0:512], AF.Exp, ["dn8"], ["dn6"], scale=-DEC)
                yield
                kkr = dn[3]
                tsc("pool", kkr[:, 0:512], kT, pv[:, 2:3], None, ALU.mult, None, ["dn1", "pv"], ["dn3"])
                yield
                sq = dn[4]
                tt("pool", sq[:, 0:512], kkr[:, 0:512], kkr[:, 0:512], ALU.mult, ["dn3"], ["dn4"])
                yield
                bank, bk = gbank()
                mm(bank[:], Blk[:], sq[:, 0:512], ["Blk", "dn4"], [bk])
                yield
                act(sq[:, 0:512], bank[:], AF.Sqrt, [bk], ["dn4"])
                yield
                tsc("dve", sq[:, 0:512], sq[:, 0:512], 1e-12, None, ALU.max, None, ["dn4"], ["dn4"])
                yield
                S.op("dve", lambda: nc.vector.reciprocal(out=sq[:, 0:512], in_=sq[:, 0:512]), ["dn4"], ["dn4"])
                yield
                kk = dn[3]
                tt("dve", kk[:, 0:512], kkr[:, 0:512], sq[:, 0:512], ALU.mult, ["dn3", "dn4"], ["dn3"])
                yield
                t1 = dn[4]
                tsc("dve", t1[:, 0:512], aT[:, 0:512], -1.0, pv[:, 3:4], ALU.add, ALU.mult, ["dn7", "pv"], ["dn4"])
                yield
                kp = dn[8]
                stt(kp[:, 0:512], t1[:, 0:512], 1.0, kT, ALU.add, ALU.mult, ["dn4", "dn1"], ["dn8"])
                yield
                tt("pool", aT[:, 0:512], aT[:, 0:512], kk[:, 0:512], ALU.mult, ["dn7", "dn3"], ["dn7"])
                yield
                stt(t1[:, 0:512], rT, pv[:, 4:5], kp[:, 0:512], ALU.mult, ALU.mult, ["dn0", "pv", "dn8"], ["dn4"])
                yield
                stt(kk[:, 0:512], kk[:, 0:512], -1.0, eprev[:, 0:512], ALU.mult, ALU.mult, ["dn3", "dn6"], ["dn3"])
                yield
                tt("pool", aT[:, 0:512], aT[:, 0:512], eneg[:, 0:512], ALU.mult, ["dn7", "dn10"], ["dn7"])
                yield
                tt("dve", kp[:, 0:512], kp[:, 0:512], eneg[:, 0:512], ALU.mult, ["dn8", "dn10"], ["dn8"])
                yield

                if blk == 0:
                    for nm_, t__ in (("rT", rT), ("vT", vT), ("atil", kk[:, 0:512]), ("btil", aT[:, 0:512]), ("ktil", kp[:, 0:512]),
                                     ("epos", epos[:, 0:512]), ("rrk", t1[:, 0:512])):
                        dump(nm_, t__[:, 0:128], 128)
                    for h_ in range(2):
                        dump(f"gT{h_}", gT[bp][0:64, h_, 0:64], 64)

                def c3(t_):
                    return t_.rearrange("p (c t) -> p c t", t=64)

                gam3 = gamC[bp][:].rearrange("p (c o) -> p c o", o=1)
                for h in range(2):
                    hs = slice(h * 64, (h + 1) * 64)
                    cs_ = slice(h * 64, (h + 1) * 64)
                    e1 = "dve" if h == 0 else "pool"
                    cp(e1, AR_bd[bp][hs, :, cs_], c3(kk[hs, 0:512]), ["dn3"], [f"AR{bp}"])
                    yield
                    tt(e1, AR_bd[bp][hs, :, 128 + h * 64:128 + (h + 1) * 64], c3(rT[hs, :]), c3(epos[hs, 0:512]), ALU.mult, ["dn0", "dn9"], [f"AR{bp}"])
                    yield
                    cp(e1, B_bd[bp][hs, :, cs_], c3(aT[hs, 0:512]), ["dn7"], [f"B{bp}"])
                    yield
                    cp(e1, K_bd[bp][hs, :, cs_], c3(kp[hs, 0:512]), ["dn8"], [f"K{bp}"])
                    yield
                    tt(e1, BH_bd[bp][hs, :, cs_], c3(aT[hs, 0:512]), gam3[hs].broadcast_to([64, 8, 64]), ALU.mult, ["dn7", f"gamC{bp}"], [f"BH{bp}"])
                    yield
                    tt(e1, KH_bd[bp][hs, :, cs_], c3(kp[hs, 0:512]), gam3[hs].broadcast_to([64, 8, 64]), ALU.mult, ["dn8", f"gamC{bp}"], [f"KH{bp}"])
                    yield
                    cp(e1, V_bd[bp][hs, :, cs_], c3(vT[hs, :]), ["dn2"], [f"V{bp}"])
                    yield
                    cp(e1, RRK_bd[bp][hs, :, cs_], c3(t1[hs, 0:512]), ["dn4"], [f"RRK{bp}"])
                    yield

            def pack(blk, c):
                bp = blk % 2
                g = state["pk"]
                state["pk"] += 1
                import os
                pp = (g + int(os.environ.get("PPX", "0"))) % NW
                hc, hn = g % 2, (g + 1) % 2
                s0, s1 = SB0[pp], SB1[pp]
                I0, I1, I2, KAV, KVS = [f"s0_{pp}_{i}" for i in range(5)]
                k1 = [f"s1_{pp}_{i}" for i in range(5)]
                tb = [("TB", j) for j in range(3)]
                X, a3, r3, bk_, vs, wu, pt, rh, smm = Xi[pp], A3[pp], R3[pp], BK[pp], Vst[pp], WU[pp], PT[pp], RH[pp], sm[pp]
                kX, kA3, kR3, kBK, kV, kWU, kPT, kRH = f"X{pp}", f"A3{pp}", f"R3{pp}", f"BK{pp}", f"Vst{pp}", f"WU{pp}", f"PT{pp}", f"RH{pp}"
                ar, bb, kb, bh, kh, vb, rrk = AR_bd[bp], B_bd[bp], K_bd[bp], BH_bd[bp], KH_bd[bp], V_bd[bp], RRK_bd[bp]
                s0v = s0[:, 0:384].rearrange("p (a b) -> p a b", b=128)
                mm(s0[:, 0:128], bb[:, c, :], ar[:, c, 0:128], [f"B{bp}", f"AR{bp}"], [I0])
                mm(s0[:, 256:384], ar[:, c, 0:128], bb[:, c, :], [f"B{bp}", f"AR{bp}"], [I2])
                mm(s1[:, 0:128], bb[:, c, :], ar[:, c, 128:256], [f"B{bp}", f"AR{bp}"], [k1[0]])
                mm(s1[:, 128:384], kb[:, c, :], ar[:, c, :], [f"K{bp}", f"AR{bp}"], [k1[1], k1[2], k1[3]])
                o = 0
                mm(TB[:, o:o + 128], ar[:, c, 0:128], ident_bf[:], [f"AR{bp}", "ident_bf"], [tb[0]])
                mm(TB[:, o + 128:o + 256], bh[:, c, :], ident_bf[:], [f"BH{bp}", "ident_bf"], [tb[1]])
                mm(TB[:, o + 256:o + 384], kh[:, c, :], ident_bf[:], [f"KH{bp}", "ident_bf"], [tb[2]])
                mm(s0[:, 448:512], vb[:, c, :], E_bf[:], [f"V{bp}", "E_bf"], [KVS])
                import os
                if os.environ.get("RKTB", "0") == "1":
                    mm(TB[:, 384:386], rrk[:, c, :], ones_bf[:], [f"RRK{bp}", "ones_bf"], [k1[4]])
                else:
                    mm(s1[:, 384:386], rrk[:, c, :], ones_bf[:], [f"RRK{bp}", "ones_bf"], [k1[4]])
                yield
                import os
                OPS = os.environ.get("OPS", "abcdefg")
                if "a" in OPS:
                    tt("dve", X[:, 0:3:2, :], s0v[:, 0:3:2, :], M_TS[:], ALU.mult, [I0, I2, "M_TS"], [kX])
                if "b" in OPS:
                    tt("dve", a3[:], s1[:, 0:384].rearrange("p (a b) -> p a b", b=128), M_A3[:], ALU.mult, [k1[0], k1[1], k1[2], k1[3], "M_A3"], [kA3])
                if "c" in OPS:
                    tt("pool", X[:, 1, :], X[:, 0, :], ident_f[:], ALU.add, [kX, "ident_f"], [kX])
                if "d" in OPS:
                    cp("act", r3[:, 0:128], TB[:, o:o + 128], [tb[0]], [kR3])
                if "e" in OPS:
                    cp("act", bk_[:], TB[:, o + 128:o + 384].rearrange("p (a b) -> p a b", b=128), [tb[1], tb[2]], [kBK])
                if "f" in OPS:
                    cp("act", vs[:], s0[:, 448:512], [KVS], [kV])
                if "g" in OPS:
                    cp("act", smm[:, 0:1], s1[:, 384:385], [k1[4]], [f"sm{pp}rk"])
                yield
                mm(s0[:, 0:128], X[:, 2, :], X[:, 0, :], [kX], [I0])
                mm(s0[:, 256:384], X[:, 0, :], X[:, 2, :], [kX], [I2])
                yield
                cp("act", X[:, 0:3:2, :], s0v[:, 0:3:2, :], [I0, I2], [kX])
                yield
                for lv in range(1, 5):
                    mm(s0[:, 0:256], X[:, 2, :], X[:, 0:2, :].rearrange("p a b -> p (a b)"), [kX], [I0, I1])
                    mm(s0[:, 256:384], X[:, 0, :], X[:, 2, :], [kX], [I2])
                    yield
                    tt("dve", X[:, 1, :], s0[:, 128:256], X[:, 1, :], ALU.add, [I1, kX], [kX])
                    cp("act", X[:, 0:3:2, :], s0v[:, 0:3:2, :], [I0, I2], [kX])
                    yield
                mm(s0[:, 128:256], X[:, 2, :], X[:, 1, :], [kX], [I1])
                mm(s0[:, 384:448], a3[:, 1, :], vs[:], [kA3, kV], [KAV])
                yield
                tt("dve", Mb[pp][:], s0[:, 128:256], X[:, 1, :], ALU.add, [I1, kX], [f"Mb{pp}"])
                cp("act", r3[:, 128:192], s0[:, 384:448], [KAV], [kR3])
                yield
                mm(s0[:, 0:192], Mb[pp][:], r3[:], [f"Mb{pp}", kR3], [I0, I1])
                yield
                cp("act", wu[:], s0[:, 0:192], [I0, I1], [kWU])
                yield
                mm(s0[:, 192:320], wu[:, 0:128], bk_[:, 0, :], [kWU, kBK], [I1, I2])
                mm(s1[:, 0:128], wu[:, 0:128], a3[:, 0, :], [kWU, kA3], [k1[0]])
                yield
                cp("act", pt[:], s0[:, 192:320], [I1, I2], [kPT])
                tt("dve", rh[:], s1[:, 0:128], ar[:, c, 128:256], ALU.add, [k1[0], f"AR{bp}"], [kRH])
                yield
                mm(s1[:, 256:320], a3[:, 0, :], wu[:, 128:192], [kA3, kWU], [k1[2]], start=True, stop=False)
                mm(s1[:, 256:320], a3[:, 2, :], vs[:], [kA3, kV], [k1[2]], start=False, stop=False)
                mm(s1[:, 256:320], rh[:], Hs[hc][:], [kRH, f"H{hc}"], [k1[2]], start=False, stop=True)
                mm(s1[:, 320:384], bk_[:, 0, :], wu[:, 128:192], [kBK, kWU], [k1[3]], start=True, stop=False)
                mm(s1[:, 320:384], bk_[:, 1, :], vs[:], [kBK, kV], [k1[3]], start=False, stop=False)
                mm(s1[:, 320:384], pt[:], Hs[hc][:], [kPT, f"H{hc}"], [k1[3]], start=False, stop=True)
                yield
                stt(Hs[hn][:], Hs[hc][:], gamC[bp][:, c:c + 1], s1[:, 320:384], ALU.mult, ALU.add, [f"H{hc}", f"gamC{bp}", k1[3]], [f"H{hn}"])
                S.op("dve", lambda: nc.vector.bn_stats(out=smm[:, 2:8], in_=s1[:, 256:320]), [k1[2]], [f"sm{pp}st"])
                S.op("dve", lambda: nc.vector.bn_aggr(out=smm[:, 8:10], in_=smm[:, 2:8]), [f"sm{pp}st"], [f"sm{pp}mv"])
                yield
                act(smm[:, 10:11], smm[:, 9:10], AF.Sqrt, [f"sm{pp}mv"], [f"sm{pp}sd"], bias=GN_EPS, scale=1.0)
                S.op("dve", lambda: nc.vector.reciprocal(out=smm[:, 11:12], in_=smm[:, 10:11]), [f"sm{pp}sd"], [f"sm{pp}rs"])
                yield
                tsc("dve", yh[pp][:], s1[:, 256:320], smm[:, 8:9], smm[:, 11:12], ALU.subtract, ALU.mult, [k1[2], f"sm{pp}mv", f"sm{pp}rs"], [f"yh{pp}"])
                yield
                tt("pool", yh[pp][:], yh[pp][:], lnw_bc[:], ALU.mult, [f"yh{pp}", "lnw_bc"], [f"yh{pp}"])
                tt("pool", yh[pp][:], yh[pp][:], lnb_bc[:], ALU.add, [f"yh{pp}", "lnb_bc"], [f"yh{pp}"])
                yield
                stt(yfin[pp][:], vs[:], smm[:, 0:1], yh[pp][:], ALU.mult, ALU.add, [kV, f"sm{pp}rk", f"yh{pp}"], [f"yfin{pp}"])
                yield
                tr(s1[0:64, 128:256], yfin[pp][:], ident_f[:], [f"yfin{pp}", "ident_f"], [k1[1]])
                yield
                tt("dve", yTb[bp][0:64, :, c * 64:(c + 1) * 64], s1[0:64, 128:256].rearrange("p (h t) -> p h t", t=64),
                   gT[bp][0:64, :, c * 64:(c + 1) * 64], ALU.mult, [k1[1], f"gT{bp}"], [f"yTb{bp}"])

            def run_packs(blk, extra=None):
                import os
                gens = [pack(blk, c) for c in range(int(os.environ.get("PKN", "8")))]
                maxs = int(STOP[2:]) if (STOP or "").startswith("pk") else 10 ** 9
                adv = {}
                active = []
                if extra is not None and maxs > 10 ** 8:
                    active.append(extra)
                nxt = 0
                stepc = 0
                NG = len(gens)
                while nxt < NG or active:
                    npk = len([a_ for a_ in active if a_ is not extra])
                    if nxt < NG and npk < NW and (npk == 0 or stepc % 5 == 0):
                        active.append(gens[nxt])
                        nxt += 1
                    for gkk in list(active):
                        try:
                            adv[id(gkk)] = adv.get(id(gkk), 0) + 1
                            if adv[id(gkk)] > maxs:
                                raise StopIteration
                            next(gkk)
                            if gkk is extra:
                                next(gkk)
                        except StopIteration:
                            active.remove(gkk)
                    stepc += 1

            for _ in prep(0):
                pass
            for blk in range(NBLK):
                if STOP == "prep":
                    S.barrier()
                    return nc
                run_packs(blk, prep(blk + 1) if blk + 1 < NBLK else None)
                if STOP == "blk1" or (STOP or "").startswith("pk"):
                    S.barrier()
                    return nc
                bp = blk % 2
                if blk == 0:
                    for h_ in range(2):
                        dump(f"yT{h_}", yTb[bp][0:64, h_, 0:128], 128)
                    dump("H1", Hs[0][:], 64)
                S.dma("sp", f"yst{bp}", ysv[blk // 4][:, :, (blk % 4) * 512:(blk % 4 + 1) * 512], yTb[bp][0:64, :, :], [f"yTb{bp}"], ["ysrc"])
                if blk % 4 == 3:
                    ymark[blk // 4] = [(st_, S.cnt[st_]) for st_ in ("yst0", "yst1")]
                if blk % 4 == 0 and blk > 0:
                    exchange(blk // 4 - 1)
            S.barrier()
            exchange(3)
            GBL[:] = [0, 1, 2, 3, 4, 5, 7]

        if STOP == "p1":
            return nc

        if STOP == "cc":
            return nc
        ydv = ydst.ap().rearrange("j (h i) t -> i j h t", i=64)
        with contextlib.ExitStack() as p2:
            wo_p = sb([128, 4, D], BF16, stack=p2)
            wo_r = sb([128, 8, D], BF16, stack=p2)
            yall = [sb([128, 8, 512], BF16, stack=p2) for _ in range(2)]
            x1t = [sb([128, D], stack=p2) for _ in range(2)]
            g1b3 = g1bc[:].rearrange("p (k n) -> p k n", k=1)
            load_cast(lambda a, b: wo_p[:, :, a:b], lambda a, b: w_out[0:512, a:b].rearrange("(k p) n -> p k n", p=128), 128, 4, D,
                      ["wo_p", "g1bc"], scale_bc=lambda a, b: g1b3[:, :, a:b].broadcast_to([128, 4, b - a]))
            load_cast(lambda a, b: wo_r[0:64, :, a:b], lambda a, b: w_out[512:1024, a:b].rearrange("(h i) n -> i h n", i=64), 64, 8, D,
                      ["wo_r", "g1bc"], scale_bc=lambda a, b: g1b3[0:64, :, a:b].broadcast_to([64, 8, b - a]))
            S.barrier()
            for ob in range(4):
                ya = yall[ob % 2]
                S.dma("pool", f"yld{ob % 2}", ya[0:64, :, :], ydv[:, bass.ds(q, 1), :, ob * 512:(ob + 1) * 512].rearrange("i j h t -> i (j h) t"),
                      ["wo_r", "wo_p"], [f"yall{ob % 2}"])
                for ti in range(4):
                    i = _xt[0] % 2
                    _xt[0] += 1
                    S.dma("sp", f"xld{i}", xt[i][:], xo[16 + ob * 512 + ti * 128:16 + ob * 512 + (ti + 1) * 128, :], (), [f"xt{i}"])
                    j = (ob * 4 + ti) % 2
                    for hf in range(2):
                        bank, bk = gbank()
                        for m in range(4):
                            mm(bank[:], mixp[:, m, ob * 512 + ti * 128:ob * 512 + (ti + 1) * 128], wo_p[:, m, hf * 512:(hf + 1) * 512],
                               ["mixp", "wo_p"], [bk], start=(m == 0), stop=False)
                        for h in range(8):
                            mm(bank[:], ya[0:64, h, ti * 128:(ti + 1) * 128], wo_r[0:64, h, hf * 512:(hf + 1) * 512],
                               [f"yall{ob % 2}", "wo_r"], [bk], start=False, stop=(h == 7))
                        tt("dve", x1t[j][:, hf * 512:(hf + 1) * 512], bank[:], xt[i][:, hf * 512:(hf + 1) * 512], ALU.add, [bk, f"xt{i}"], [f"x1t{j}"])
                    r0 = ob * 512 + ti * 128
                    if ob == 0 and ti == 0:
                        dump("x1", x1t[j][:, 0:256], 256)
                        for h_ in range(8):
                            dump(f"yall{h_}", ya[0:64, h_, 0:32], 32)
                    S.dma("sp", f"x1st{j}", x1d[r0:r0 + 128, :], x1t[j][:], [f"x1t{j}"], ["x1d"])
            S.barrier()
        pA.close()
        if STOP == "p2a":
            return nc

        with contextlib.ExitStack() as p3:
            wgu = sb([128, 8, 2 * DFF], BF16, stack=p3)
            wdn = sb([128, NFF, D], BF16, stack=p3)
            x1b = [sb([128, 2, D], stack=p3) for _ in range(2)]
            h2T = [sb([128, 8, 256], BF16, stack=p3) for _ in range(2)]
            actT = sb([128, NFF, 256], BF16, stack=p3)
            sgt = [sb([128, 256], stack=p3) for _ in range(2)]
            ot = sb([128, D], stack=p3)
            g2b3 = g2bc[:].rearrange("p (k n) -> p k n", k=1)

            def load_norm(ob):
                p = ob % 2
                for ti in range(2):
                    r0 = ob * 256 + ti * 128
                    S.dma("pool", f"x1ld{p}{ti}", x1b[p][:, ti, :], x1d[r0:r0 + 128, :], ["x1d"], [("x1b", p, ti)])
                    norm_to_fm(x1b[p][:, ti, :], 128, gsc2, sh2, "sh2", lambda k, ti=ti, p=p: h2T[p][:, k, ti * 128:(ti + 1) * 128], f"h2T{p}",
                               src_key=("x1b", p, ti))

            load_norm(0)
            rngs = []
            for i_ in range(6):
                rngs.append((i_ * 512, min((i_ + 1) * 512, DFF)))
                rngs.append((DFF + i_ * 512, min(DFF + (i_ + 1) * 512, 2 * DFF)))
            pieces = []
            for (a0, a1) in rngs:
                for kh in range(2):
                    pieces.append(lambda kh=kh, a0=a0, a1=a1: load_cast(
                        lambda a, b: wgu[:, kh * 4:(kh + 1) * 4, a0 + a:a0 + b],
                        lambda a, b: w_gu[kh * 512:(kh + 1) * 512, a0 + a:a0 + b].rearrange("(k p) n -> p k n", p=128),
                        128, 4, a1 - a0,
                        [lambda a, b: [("wgu", kh, c_) for c_ in range((a0 + a) // 256, (a0 + b + 255) // 256)]]))
            wd3 = w_down.rearrange("(f p) n -> p f n", p=128)
            for f0 in range(0, NFF, 2):
                pieces.append(lambda f0=f0: load_cast(
                    lambda a, b: wdn[:, f0:f0 + 2, a:b], lambda a, b: wd3[:, f0:f0 + 2, a:b], 128, 2, D,
                    [lambda a, b: [("wdn", f0 // 2)], "g2bc"], scale_bc=lambda a, b: g2b3[:, :, a:b].broadcast_to([128, 2, b - a])))
            _pe = [0]

            def emit_pieces(upto):
                while _pe[0] < min(upto, len(pieces)):
                    pieces[_pe[0]]()
                    _pe[0] += 1
            for ob in range(8):
                p = ob % 2
                for f in range(NFF):
                    emit_pieces(max(4 * (f // 4) + 4, 4 + 2 * f) if ob == 0 else len(pieces))
                    bg, kg = gbank()
                    for k in range(8):
                        mm(bg[:, 0:256], wgu[:, k, f * 128:(f + 1) * 128], h2T[p][:, k, :], [("wgu", k // 4, (f * 128) // 256), f"h2T{p}"], [kg], start=(k == 0), stop=(k == 7))
                    bu, ku = gbank()
                    for k in range(8):
                        mm(bu[:, 0:256], wgu[:, k, DFF + f * 128:DFF + (f + 1) * 128], h2T[p][:, k, :], [("wgu", k // 4, (DFF + f * 128) // 256), f"h2T{p}"], [ku], start=(k == 0), stop=(k == 7))
                    sj = f % 2
                    act(sgt[sj][:], bg[:, 0:256], AF.Silu, [kg], [f"sgt{sj}"])
                    tt("dve", actT[:, f, :], bu[:, 0:256], sgt[sj][:], ALU.mult, [ku, f"sgt{sj}"], ["actT"])
                    if f == NFF // 2 and ob + 1 < 8:
                        load_norm(ob + 1)
                emit_pieces(len(pieces))
                for ti in range(2):
                    for hf in range(2):
                        bank, bk = gbank()
                        for f in range(NFF):
                            mm(bank[:], actT[:, f, ti * 128:(ti + 1) * 128], wdn[:, f, hf * 512:(hf + 1) * 512], ["actT", ("wdn", f // 2)], [bk],
                               start=(f == 0), stop=(f == NFF - 1))
                        tt("dve", x1b[p][:, ti, hf * 512:(hf + 1) * 512], bank[:], x1b[p][:, ti, hf * 512:(hf + 1) * 512], ALU.add,
                           [bk, ("x1b", p, ti)], [("x1b", p, ti)])
                    xa = x1b[p][:, ti, :]
                    act(junk[:], xa, AF.Square, [("x1b", p, ti)], ["junk", "ss"], accum_out=ss[:, 0:1])
                    act(ss[:, 1:2], ss[:, 0:1], AF.Sqrt, ["ss"], ["ss1"], bias=RMS_EPS, scale=1.0 / D)
                    S.op("dve", lambda: nc.vector.reciprocal(out=ss[:, 2:3], in_=ss[:, 1:2]), ["ss1"], ["ss2"])
                    stt(ot[:], xa, ss[:, 2:3], fgbc[:], ALU.mult, ALU.mult, [("x1b", p, ti), "ss2", "fgbc"], ["ot"])
                    r0 = ob * 256 + ti * 128
                    S.dma("sp", "ost", out[r0:r0 + 128, :], ot[:], ["ot"], ["outd"])
            S.barrier()
    return nc


_NC = None


def kernel(**inputs):
    global _NC
    x = np.asarray(inputs["x"], np.float32)
    c = np.asarray(inputs["c"], np.float32)
    if _NC is None:
        _NC = build()
    g = lambda n: np.asarray(inputs[n], np.float32)[0]
    names = ["ada_w", "ada_b", "norm1_g", "pool_w", "pool_scale", "w_out", "norm2_g", "w_ffn_gu", "w_ffn_down"]
    shared = {n: np.ascontiguousarray(g(n)) for n in names}
    shared["final_g"] = np.ascontiguousarray(np.asarray(inputs["final_g"], np.float32))
    w_in, mu_shift = g("w_in"), g("mu_shift")
    in_maps = []
    wins = (2, 4, 8, 16)
    for core in range(8):
        b, qq = core // 4, core % 4
        xpad = np.zeros((T + 16, D), np.float32)
        xpad[16:] = x[b]
        fl = np.zeros((128, 65), np.float32)
        fl[:, 0] = 0.0 if qq == 0 else 1.0
        for gi, w in enumerate(wins):
            for t in range(16):
                fl[:, 1 + gi * 16 + t] = 1.0 / min(qq * TQ + t + 1, w)
        ps_ = slice(qq * 128, (qq + 1) * 128)
        cols = np.concatenate([np.arange(0, 512)] + [np.arange(512 + i * 512 + qq * 128, 512 + i * 512 + (qq + 1) * 128) for i in range(3)]
                              + [np.arange(2048, 2304)])
        mcols = np.stack([mu_shift[i * 512 + qq * 128:i * 512 + (qq + 1) * 128] for i in range(3)]
                         + [mu_shift[1536:1664], mu_shift[1664:1792]], axis=1)
        m = dict(shared)
        m["xb"] = xpad
        m["xo"] = np.ascontiguousarray(xpad[qq * TQ:qq * TQ + TQ + 16])
        m["cvec"] = np.ascontiguousarray(c[b])
        m["flags"] = fl
        m["w_in_sel"] = np.ascontiguousarray(w_in[:, cols])
        m["mu_sel"] = np.ascontiguousarray(mcols)
        m["pv_sel"] = np.ascontiguousarray(np.stack([g(n)[ps_] for n in ("w0", "a0", "k_k", "k_a", "r_k")], axis=1))
        m["ln2"] = np.ascontiguousarray(np.stack([g("lnx_w")[ps_].reshape(2, 64), g("lnx_b")[ps_].reshape(2, 64)], axis=0))
        m["Wd_sel"] = np.ascontiguousarray(g("w_decay_up")[:, ps_])
        m["Wa_sel"] = np.ascontiguousarray(g("w_iclr_up")[:, ps_])
        m["Wg_sel"] = np.ascontiguousarray(g("w_gate_up")[:, ps_])
        in_maps.append(m)
    res = run_bass_kernel_spmd(_NC, in_maps, core_ids=list(range(8)))
    if DEBUG:
        global DBG_OUT
        DBG_OUT = [res.results[core]["dbg"] for core in range(8)]
    outp = np.zeros((2, T, D), np.float32)
    for core in range(8):
        b, qq = core // 4, core % 4
        outp[b, qq * TQ:(qq + 1) * TQ] = res.results[core]["out"]
    return outp
```
